# Optimizing a Trainium2 kernel written in Bass

```python
import math
import jax, jax.numpy as jnp
from jax import lax
import numpy as np

D_MODEL = 4096
BATCH = 1
SEQ = 8192
DEPTH = 4

CTX_LEN = 256
GRID_W = 64

HY_WIDTH = D_MODEL // 2
HY_ORDER = 2
HY_SHORT = 3
HY_EMB = 33
HY_BANDS = (HY_EMB - 1) // 2
HY_FILTER_FF = 64
HY_TARGET = 1e-2
HY_FAST_DECAY_PCT = 0.3
HY_SLOW_DECAY_PCT = 1.5
HY_MAX_DECAY = math.log(HY_TARGET) / HY_FAST_DECAY_PCT
HY_MIN_DECAY = math.log(HY_TARGET) / HY_SLOW_DECAY_PCT
DN_HEADS = 16
DN_HEAD_DIM = 128
DN_WIDTH = DN_HEADS * DN_HEAD_DIM
DN_SHORT = 5
DN_CHUNK = 64
EVEN_IN = 3 * HY_WIDTH + 4 * DN_WIDTH + 4 * DN_HEADS
EVEN_MIX = HY_WIDTH + DN_WIDTH
S5_GROUP = 16
S5_GROUPS = D_MODEL // S5_GROUP
S5_STATE = 64
S5_BLOCK = 32
S5_MAX_RE = -1e-4
N_EXPERTS = 16
EXPERT_FF = 384
EC_CAPACITY = 2
DEEPNORM_ALPHA = (2 * DEPTH) ** 0.25
DEEPNORM_BETA = (8 * DEPTH) ** -0.25
LN_EPS = 1e-5
N_EVEN = (DEPTH + 1) // 2
N_ODD = DEPTH // 2

kernel_name = 'hybrid_hyena_deltanet_s5_ecmoe_dit'


def post_norm(h, y, g, b):
    z = (DEEPNORM_ALPHA * h + y).astype(jnp.float32)
    mu = jnp.mean(z, axis=-1, keepdims=True)
    var = jnp.mean(jnp.square(z - mu), axis=-1, keepdims=True)
    return ((z - mu) * lax.rsqrt(var + LN_EPS) * g.astype(jnp.float32) + b.astype(jnp.float32)).astype(h.dtype)


def short_conv(u, w):
    k, ch = w.shape
    return lax.conv_general_dilated(u, w.astype(u.dtype)[:, None, :], window_strides=(1,), padding=[(k // 2, k // 2)], dimension_numbers=('NWC', 'WIO', 'NWC'), feature_group_count=ch)


def cat_streams(a_ctx, a_lat, rev):
    if rev:
        return jnp.concatenate([a_ctx[:, ::-1], a_lat[:, ::-1]], axis=1)
    return jnp.concatenate([a_ctx, a_lat], axis=1)


def to_col_major(h):
    b_, n, d = h.shape
    rows = n // GRID_W
    return h.reshape(b_, rows, GRID_W, d).transpose(0, 2, 1, 3).reshape(b_, n, d)


def from_col_major(h):
    b_, n, d = h.shape
    rows = n // GRID_W
    return h.reshape(b_, GRID_W, rows, d).transpose(0, 2, 1, 3).reshape(b_, n, d)


def hyena_filters(L, w1, b1, w2, b2, w3, b3, w4, freq):
    f32 = jnp.float32
    pos = jnp.arange(L, dtype=f32)
    t = pos / max(L - 1, 1)
    ang = (2.0 * math.pi / L) * pos
    bands = jnp.linspace(1e-4, HY_BANDS - 1, HY_BANDS, dtype=f32)
    z = jnp.concatenate([t[:, None], jnp.cos(ang[:, None] * bands), -jnp.sin(ang[:, None] * bands)], axis=-1)
    freq = freq.astype(f32)
    h = jnp.sin(freq[0] * (z @ w1.astype(f32) + b1.astype(f32)))
    h = jnp.sin(freq[1] * (h @ w2.astype(f32) + b2.astype(f32)))
    h = jnp.sin(freq[2] * (h @ w3.astype(f32) + b3.astype(f32)))
    h = (h @ w4.astype(f32)).reshape(L, HY_ORDER, 2, HY_WIDTH)
    deltas = jnp.abs(jnp.linspace(HY_MIN_DECAY, HY_MAX_DECAY, HY_WIDTH, dtype=f32))
    h = h * jnp.exp(-t[:, None, None, None] * deltas)
    fwd, bwd = h[:, :, 0], h[:, :, 1]
    return jnp.concatenate([fwd, jnp.zeros_like(fwd[:1]), bwd[:0:-1]], axis=0)


def long_conv(u, filt, bias):
    L = u.shape[1]
    uf = jnp.fft.rfft(u.astype(jnp.float32), n=2 * L, axis=1)
    hf = jnp.fft.rfft(filt, axis=0)
    y = jnp.fft.irfft(uf * hf, n=2 * L, axis=1)[:, :L]
    return (y + u * bias).astype(u.dtype)


def hyena(hy, conv_w, filt_params, bias):
    L = hy.shape[1]
    v, x1, x2 = jnp.split(short_conv(hy, conv_w), 3, axis=-1)
    filt = hyena_filters(L, *filt_params)
    bias = bias.astype(jnp.float32)
    z = x1 * long_conv(v, filt[:, 0], bias[0])
    return x2 * long_conv(z, filt[:, 1], bias[1])


def l2norm(a):
    return a * lax.rsqrt(jnp.sum(a * a, axis=-1, keepdims=True) + 1e-6)


def gated_delta_chunked(q, k, v, g, beta):
    b_, t, h, dk = q.shape
    dv = v.shape[-1]
    n = t // DN_CHUNK

    def blocks(a):
        a = a.astype(jnp.float32).reshape((b_, n, DN_CHUNK, h) + a.shape[3:])
        return jnp.moveaxis(a, (1, 3), (0, 2))

    qc, kc, vc, bc = blocks(q), blocks(k), blocks(v), blocks(beta)
    gc = jnp.cumsum(blocks(g), axis=-1)
    causal = jnp.tril(jnp.ones((DN_CHUNK, DN_CHUNK), bool))
    strict = jnp.tril(jnp.ones((DN_CHUNK, DN_CHUNK), bool), -1)
    decay = jnp.exp(jnp.where(causal, gc[..., :, None] - gc[..., None, :], -jnp.inf))
    kb = kc * bc[..., None]
    a_mat = jnp.where(strict, jnp.einsum('nbhid,nbhjd->nbhij', kb, kc) * decay, 0.0)
    t_mat = a_mat + jnp.eye(DN_CHUNK, dtype=jnp.float32)
    u = lax.linalg.triangular_solve(t_mat, vc * bc[..., None], left_side=True, lower=True, unit_diagonal=True)
    w = lax.linalg.triangular_solve(t_mat, kb * jnp.exp(gc)[..., None], left_side=True, lower=True, unit_diagonal=True)
    qk = jnp.einsum('nbhid,nbhjd->nbhij', qc, kc) * decay

    def step(state, xs):
        q_i, k_i, u_i, w_i, g_i, qk_i = xs
        v_new = u_i - jnp.einsum('bhcd,bhde->bhce', w_i, state)
        o = jnp.einsum('bhcd,bhde->bhce', q_i * jnp.exp(g_i)[..., None], state) + jnp.einsum('bhij,bhje->bhie', qk_i, v_new)
        g_last = g_i[..., -1:]
        k_dec = k_i * jnp.exp(g_last - g_i)[..., None]
        state = state * jnp.exp(g_last)[..., None] + jnp.einsum('bhcd,bhce->bhde', k_dec, v_new)
        return state, o

    s0 = jnp.zeros((b_, h, dk, dv), jnp.float32)
    _, o = lax.scan(step, s0, (qc, kc, u, w, gc, qk))
    return jnp.moveaxis(o, (0, 2), (1, 3)).reshape(b_, t, h, dv)


def gated_head_norm(o, gate, w):
    b_, t_ = gate.shape[:2]
    o = o * lax.rsqrt(jnp.mean(o * o, axis=-1, keepdims=True) + 1e-6) * w.astype(jnp.float32)
    o = o * jax.nn.silu(gate.astype(jnp.float32)).reshape(b_, t_, DN_HEADS, DN_HEAD_DIM)
    return o.reshape(b_, t_, DN_WIDTH).astype(gate.dtype)


def deltanet(dn_ctx, dn_lat, ctx_out, conv_w, a_log, dt_bias, norm_w):
    a_rate = jnp.exp(a_log.astype(jnp.float32))
    dt_bias = dt_bias.astype(jnp.float32)

    def prep(dn):
        b_, t_, _ = dn.shape
        qkv = jax.nn.silu(short_conv(dn[..., :3 * DN_WIDTH], conv_w)).astype(jnp.float32)
        q, k, v = (a.reshape(b_, t_, DN_HEADS, DN_HEAD_DIM) for a in jnp.split(qkv, 3, axis=-1))
        ab = dn[..., 4 * DN_WIDTH:].astype(jnp.float32).reshape(b_, t_, 2, 2, DN_HEADS)
        beta = jax.nn.sigmoid(ab[:, :, 0])
        g = -a_rate * jax.nn.softplus(ab[:, :, 1] + dt_bias)
        return l2norm(q) * DN_HEAD_DIM ** -0.5, l2norm(k), v, beta, g

    pc, pl = prep(dn_ctx), prep(dn_lat)
    tc = dn_ctx.shape[1]
    o_ctx, o_lat = 0.0, 0.0
    for direction in (0, 1):
        rev = direction == 1
        o = gated_delta_chunked(cat_streams(pc[0], pl[0], rev), cat_streams(pc[1], pl[1], rev), cat_streams(pc[2], pl[2], rev), cat_streams(pc[4][:, :, direction], pl[4][:, :, direction], rev), cat_streams(pc[3][:, :, direction], pl[3][:, :, direction], rev))
        oc, ol = o[:, :tc], o[:, tc:]
        if rev:
            oc, ol = oc[:, ::-1], ol[:, ::-1]
        o_ctx = o_ctx + oc
        o_lat = o_lat + ol
    y_lat = gated_head_norm(o_lat, dn_lat[..., 3 * DN_WIDTH:4 * DN_WIDTH], norm_w)
    y_ctx = gated_head_norm(o_ctx, dn_ctx[..., 3 * DN_WIDTH:4 * DN_WIDTH], norm_w) if ctx_out else None
    return y_ctx, y_lat


def even_mixer(u_ctx, u_lat, ctx_out, w_in, hy_conv, hy_w1, hy_b1, hy_w2, hy_b2, hy_w3, hy_b3, hy_w4, hy_freq, hy_bias, dn_conv, dn_a_log, dn_dt_bias, dn_norm_w):
    hy_cols = 3 * HY_WIDTH
    filt = (hy_w1, hy_b1, hy_w2, hy_b2, hy_w3, hy_b3, hy_w4, hy_freq)
    z_lat = u_lat @ w_in
    z_ctx = u_ctx @ (w_in if ctx_out else w_in[:, hy_cols:])
    dn_ctx = z_ctx[..., hy_cols:] if ctx_out else z_ctx
    y_dn_ctx, y_dn_lat = deltanet(dn_ctx, z_lat[..., hy_cols:], ctx_out, dn_conv, dn_a_log, dn_dt_bias, dn_norm_w)
    y_lat = jnp.concatenate([hyena(z_lat[..., :hy_cols], hy_conv, filt, hy_bias), y_dn_lat], axis=-1)
    y_ctx = jnp.concatenate([hyena(z_ctx[..., :hy_cols], hy_conv, filt, hy_bias), y_dn_ctx], axis=-1) if ctx_out else None
    return y_ctx, y_lat


def s5_combine(left, right):
    a_l, b_l = left
    a_r, b_r = right
    return a_l * a_r, a_r * b_l + b_r


def s5_block(args):
    u, lam_re, lam_im, log_dt, b_re, b_im, c_re, c_im = args
    lam = lax.complex(jnp.minimum(lam_re, S5_MAX_RE), lam_im)
    lam_bar = jnp.exp(lam * jnp.exp(log_dt)[..., None])
    b_bar = ((lam_bar - 1.0) / lam)[..., None] * lax.complex(b_re, b_im)
    bu = jnp.einsum('dgpc,dbtgc->dbtgp', b_bar, u.astype(jnp.complex64))
    a = jnp.broadcast_to(lam_bar[:, None, None], bu.shape)
    _, states = lax.associative_scan(s5_combine, (a, bu), axis=2)
    return jnp.einsum('dgcp,dbtgp->dbtgc', lax.complex(c_re, c_im), states).real


def s5_mixer(u_ctx, u_lat, ctx_out, lam_re, lam_im, log_dt, b_re, b_im, c_re, c_im, d_skip):
    b_, tc, d = u_ctx.shape
    seqs = jnp.stack([cat_streams(u_ctx, u_lat, False), cat_streams(u_ctx, u_lat, True)]).astype(jnp.float32)
    t = seqs.shape[2]
    nb = S5_GROUPS // S5_BLOCK
    u_blk = seqs.reshape(2, b_, t, nb, S5_BLOCK, S5_GROUP).transpose(3, 0, 1, 2, 4, 5)

    def blk(p):
        p = p.astype(jnp.float32)
        return jnp.swapaxes(p.reshape((2, nb, S5_BLOCK) + p.shape[2:]), 0, 1)

    ys = lax.map(s5_block, (u_blk, blk(lam_re), blk(lam_im), blk(log_dt), blk(b_re), blk(b_im), blk(c_re), blk(c_im)))
    y = ys.transpose(1, 2, 3, 0, 4, 5).reshape(2, b_, t, d)
    d_skip = d_skip.astype(jnp.float32)

    def out(y_f, y_b_rev, u):
        return jax.nn.gelu(y_f + y_b_rev[:, ::-1] + d_skip * u).astype(u.dtype)

    f_lat = out(y[0, :, tc:], y[1, :, tc:], u_lat)
    f_ctx = out(y[0, :, :tc], y[1, :, :tc], u_ctx) if ctx_out else None
    return f_ctx, f_lat


def glu_out(f, w):
    gv = f @ w
    return gv[..., :D_MODEL] * jax.nn.sigmoid(gv[..., D_MODEL:])


def expert_choice_ffn(h, w_router, w_in, w_out):
    b_, n, d = h.shape
    cap = EC_CAPACITY * n // N_EXPERTS
    aff = jax.nn.softmax(jnp.einsum('bnd,de->bne', h, w_router).astype(jnp.float32), axis=-1)
    gate, idx = lax.top_k(jnp.swapaxes(aff, 1, 2), cap)
    xs = jax.vmap(lambda hb, ib: hb[ib])(h, idx)
    gu = jnp.einsum('becd,edf->becf', xs, w_in)
    act = jax.nn.silu(gu[..., :EXPERT_FF]) * gu[..., EXPERT_FF:]
    out = jnp.einsum('becf,efd->becd', act, w_out) * gate[..., None].astype(h.dtype)
    return jax.vmap(lambda ib, ob: jnp.zeros((n, d), ob.dtype).at[ib.reshape(-1)].add(ob.reshape(-1, d)))(idx, out)


def setup_inputs(seed: int = 0) -> dict:
    key = jax.random.key(seed)
    ks = iter(jax.random.split(key, 48))
    f32 = jnp.float32

    def nrm(shape, std):
        return std * jax.random.normal(next(ks), shape, f32)

    def unif(shape, lo, hi):
        return jax.random.uniform(next(ks), shape, f32, lo, hi)

    d = D_MODEL
    x = nrm((BATCH, SEQ, d), 1.0)
    c = nrm((BATCH, d), 1.0)
    ctx = nrm((BATCH, CTX_LEN, d), 1.0)
    c_ctx = nrm((d,), 1.0)
    ada_w = nrm((DEPTH, d, 6 * d), 0.5 * d ** -0.5)
    ada_b = nrm((DEPTH, 6 * d), 0.01)
    ln_g = 1.0 + nrm((DEPTH, 2, d), 0.01)
    ln_b = nrm((DEPTH, 2, d), 0.01)
    ev_w_in = nrm((N_EVEN, d, EVEN_IN), d ** -0.5)
    ev_w_out = nrm((N_EVEN, EVEN_MIX, d), EVEN_MIX ** -0.5 * DEEPNORM_BETA)
    hy_conv = nrm((N_EVEN, HY_SHORT, 3 * HY_WIDTH), HY_SHORT ** -0.5)
    hy_w1 = nrm((N_EVEN, HY_EMB, HY_FILTER_FF), HY_EMB ** -0.5)
    hy_b1 = nrm((N_EVEN, HY_FILTER_FF), 0.1)
    hy_w2 = nrm((N_EVEN, HY_FILTER_FF, HY_FILTER_FF), HY_FILTER_FF ** -0.5)
    hy_b2 = nrm((N_EVEN, HY_FILTER_FF), 0.1)
    hy_w3 = nrm((N_EVEN, HY_FILTER_FF, HY_FILTER_FF), HY_FILTER_FF ** -0.5)
    hy_b3 = nrm((N_EVEN, HY_FILTER_FF), 0.1)
    hy_w4 = nrm((N_EVEN, HY_FILTER_FF, HY_ORDER * 2 * HY_WIDTH), 0.1 * HY_FILTER_FF ** -0.5)
    hy_freq = 1.0 + nrm((N_EVEN, 3, HY_FILTER_FF), 0.01)
    hy_bias = nrm((N_EVEN, HY_ORDER, HY_WIDTH), 1.0)
    dn_conv = nrm((N_EVEN, DN_SHORT, 3 * DN_WIDTH), DN_SHORT ** -0.5)
    dn_a_log = jnp.log(unif((N_EVEN, 2, DN_HEADS), 1.0, 16.0))
    dt = jnp.exp(unif((N_EVEN, 2, DN_HEADS), math.log(1e-3), math.log(1e-1)))
    dn_dt_bias = dt + jnp.log(-jnp.expm1(-dt))
    dn_norm_w = 1.0 + nrm((N_EVEN, DN_HEAD_DIM), 0.01)
    s5_lam_re = -0.5 + nrm((N_ODD, 2, S5_GROUPS, S5_STATE), 0.01)
    s5_lam_im = math.pi * jnp.arange(S5_STATE, dtype=f32) + nrm((N_ODD, 2, S5_GROUPS, S5_STATE), 0.01)
    s5_log_dt = unif((N_ODD, 2, S5_GROUPS), math.log(1e-3), math.log(1e-1))
    s5_b_re = nrm((N_ODD, 2, S5_GROUPS, S5_STATE, S5_GROUP), (2 * S5_GROUP) ** -0.5)
    s5_b_im = nrm((N_ODD, 2, S5_GROUPS, S5_STATE, S5_GROUP), (2 * S5_GROUP) ** -0.5)
    s5_c_re = nrm((N_ODD, 2, S5_GROUPS, S5_GROUP, S5_STATE), S5_STATE ** -0.5)
    s5_c_im = nrm((N_ODD, 2, S5_GROUPS, S5_GROUP, S5_STATE), S5_STATE ** -0.5)
    s5_d = nrm((N_ODD, d), 0.5)
    od_w_glu = nrm((N_ODD, d, 2 * d), d ** -0.5) * jnp.where(jnp.arange(2 * d) < d, DEEPNORM_BETA, 1.0)
    moe_router = nrm((DEPTH, d, N_EXPERTS), d ** -0.5)
    moe_w_in = nrm((DEPTH, N_EXPERTS, d, 2 * EXPERT_FF), d ** -0.5)
    moe_w_out = nrm((DEPTH, N_EXPERTS, EXPERT_FF, d), EXPERT_FF ** -0.5 * DEEPNORM_BETA)
    return {'x': x, 'c': c, 'ctx': ctx, 'c_ctx': c_ctx, 'ada_w': ada_w, 'ada_b': ada_b, 'ln_g': ln_g, 'ln_b': ln_b, 'ev_w_in': ev_w_in, 'ev_w_out': ev_w_out, 'hy_conv': hy_conv, 'hy_w1': hy_w1, 'hy_b1': hy_b1, 'hy_w2': hy_w2, 'hy_b2': hy_b2, 'hy_w3': hy_w3, 'hy_b3': hy_b3, 'hy_w4': hy_w4, 'hy_freq': hy_freq, 'hy_bias': hy_bias, 'dn_conv': dn_conv, 'dn_a_log': dn_a_log, 'dn_dt_bias': dn_dt_bias, 'dn_norm_w': dn_norm_w, 's5_lam_re': s5_lam_re, 's5_lam_im': s5_lam_im, 's5_log_dt': s5_log_dt, 's5_b_re': s5_b_re, 's5_b_im': s5_b_im, 's5_c_re': s5_c_re, 's5_c_im': s5_c_im, 's5_d': s5_d, 'od_w_glu': od_w_glu, 'moe_router': moe_router, 'moe_w_in': moe_w_in, 'moe_w_out': moe_w_out}


def reference(x, c, ctx, c_ctx, ada_w, ada_b, ln_g, ln_b, ev_w_in, ev_w_out, hy_conv, hy_w1, hy_b1, hy_w2, hy_b2, hy_w3, hy_b3, hy_w4, hy_freq, hy_bias, dn_conv, dn_a_log, dn_dt_bias, dn_norm_w, s5_lam_re, s5_lam_im, s5_log_dt, s5_b_re, s5_b_im, s5_c_re, s5_c_im, s5_d, od_w_glu, moe_router, moe_w_in, moe_w_out):
    d = D_MODEL
    h_lat, h_ctx = x, ctx
    s_lat = jax.nn.silu(c)
    s_ctx = jax.nn.silu(c_ctx)[None]
    for l in range(DEPTH):
        last = l == DEPTH - 1
        col = (l // 2) % 2 == 1
        i = l // 2
        n_ctx_mod = 2 if last else 6
        mod_lat = jnp.split((s_lat @ ada_w[l] + ada_b[l])[:, None], 6, axis=-1)
        mod_ctx = jnp.split((s_ctx @ ada_w[l][:, :n_ctx_mod * d] + ada_b[l][:n_ctx_mod * d])[:, None], n_ctx_mod, axis=-1)
        u_lat = h_lat * (1.0 + mod_lat[1]) + mod_lat[0]
        u_ctx = h_ctx * (1.0 + mod_ctx[1]) + mod_ctx[0]
        if col:
            u_lat = to_col_major(u_lat)
        if l % 2 == 0:
            f_ctx, f_lat = even_mixer(u_ctx, u_lat, not last, ev_w_in[i], hy_conv[i], hy_w1[i], hy_b1[i], hy_w2[i], hy_b2[i], hy_w3[i], hy_b3[i], hy_w4[i], hy_freq[i], hy_bias[i], dn_conv[i], dn_a_log[i], dn_dt_bias[i], dn_norm_w[i])
            w_proj = ev_w_out[i]
            proj = lambda f, w: f @ w
        else:
            f_ctx, f_lat = s5_mixer(u_ctx, u_lat, not last, s5_lam_re[i], s5_lam_im[i], s5_log_dt[i], s5_b_re[i], s5_b_im[i], s5_c_re[i], s5_c_im[i], s5_d[i])
            w_proj = od_w_glu[i]
            proj = glu_out
        if col:
            f_lat = from_col_major(f_lat)
        h_lat = post_norm(h_lat, mod_lat[2] * proj(f_lat, w_proj), ln_g[l, 0], ln_b[l, 0])
        m_lat = expert_choice_ffn(h_lat * (1.0 + mod_lat[4]) + mod_lat[3], moe_router[l], moe_w_in[l], moe_w_out[l])
        h_lat = post_norm(h_lat, mod_lat[5] * m_lat, ln_g[l, 1], ln_b[l, 1])
        if not last:
            h_ctx = post_norm(h_ctx, mod_ctx[2] * proj(f_ctx, w_proj), ln_g[l, 0], ln_b[l, 0])
            m_ctx = expert_choice_ffn(h_ctx * (1.0 + mod_ctx[4]) + mod_ctx[3], moe_router[l], moe_w_in[l], moe_w_out[l])
            h_ctx = post_norm(h_ctx, mod_ctx[5] * m_ctx, ln_g[l, 1], ln_b[l, 1])
    return h_lat
```

```python
import contextlib
import math
import numpy as np
import concourse.bass as bass
import concourse.mybir as mybir
from concourse.bass_utils import run_bass_kernel_spmd

F32 = mybir.dt.float32
BF16 = mybir.dt.bfloat16
AF = mybir.ActivationFunctionType
ALU = mybir.AluOpType
AX = mybir.AxisListType

NCORES = 8
D = 4096
SEQ = 8192
CTX = 256
T = SEQ + CTX
DEPTH = 4
KC = D // 128


class _Stop(Exception):
    pass


class Prog:
    NDSEM = 6

    def __init__(self):
        self.nc = bass.Bass("TRN2", target_bir_lowering=False)
        self.st = contextlib.ExitStack()
        nc = self.nc
        self.eng = {"pe": nc.tensor, "act": nc.scalar, "dve": nc.vector, "pool": nc.gpsimd, "sp": nc.sync}
        self.sem = {}
        self.cnt = {}
        for e in ("pe", "act", "dve", "pool"):
            self.sem[e] = self.st.enter_context(nc.semaphore("s_" + e))
            self.cnt[e] = 0
        self.dsem = {}
        self.dcnt = {}
        for q in ("sp", "pool", "act"):
            self.dsem[q] = [self.st.enter_context(nc.semaphore("d_%s%d" % (q, i))) for i in range(self.NDSEM)]
            self.dcnt[q] = 0
        self.waited = {}
        self.last_w = {}
        self.readers = {}
        self.out_events = []
        self.n_ins = 0

    def sb(self, name, shape, dt=F32):
        return self.st.enter_context(self.nc.sbuf_tensor("sb_" + name, list(shape), dt))

    def ps(self, name, shape, dt=F32):
        return self.st.enter_context(self.nc.psum_tensor("pp_" + name, list(shape), dt))

    def dram_in(self, name, shape, dt=F32):
        return self.nc.dram_tensor(name, list(shape), dt, kind="ExternalInput").ap()

    def dram_out(self, name, shape, dt=F32):
        return self.nc.dram_tensor(name, list(shape), dt, kind="ExternalOutput").ap()

    def _wait(self, e, ev):
        if ev is None:
            return
        sem, val = ev
        k = (e, sem.name)
        if self.waited.get(k, 0) >= val:
            return
        self.waited[k] = val
        self.eng[e].wait_ge(sem, val)

    def _deps(self, e, reads, writes, pe_acc=False):
        for k in reads:
            self._wait(e, self.last_w.get(k))
        for k in writes:
            lw = self.last_w.get(k)
            if not (pe_acc and lw is not None and lw[0] is self.sem["pe"]):
                self._wait(e, lw)
            for ev in self.readers.get(k, ()):
                self._wait(e, ev)

    def _record(self, ev, reads, writes):
        for k in reads:
            self.readers.setdefault(k, []).append(ev)
            if len(self.readers[k]) > 24:
                self.readers[k] = self.readers[k][-24:]
        for k in writes:
            self.last_w[k] = ev
            self.readers[k] = []

    def op(self, e, ins_fn, reads=(), writes=(), pe_acc=False):
        self._deps(e, reads, writes, pe_acc)
        ins = ins_fn(self.eng[e])
        self.cnt[e] += 1
        ins.then_inc(self.sem[e], 1)
        ev = (self.sem[e], self.cnt[e])
        self._record(ev, reads, writes)
        self.n_ins += 1
        return ev

    def dma(self, q, out, in_, reads=(), writes=(), is_output=False, **kw):
        n = self.dcnt[q]
        sem = self.dsem[q][n % self.NDSEM]
        prev = 16 * (n // self.NDSEM)
        if prev > 0:
            self._wait(q, (sem, prev))
        self._deps(q, reads, writes)
        ins = self.eng[q].dma_start(out=out, in_=in_, **kw)
        ins.then_inc(sem, 16)
        self.dcnt[q] = n + 1
        ev = (sem, prev + 16)
        self._record(ev, reads, writes)
        if is_output:
            self.out_events.append(ev)
        self.n_ins += 1
        return ev

    def finish(self):
        for ev in self.out_events:
            self._wait("sp", ev)
        for e in ("pe", "act", "dve", "pool"):
            if self.cnt[e]:
                self._wait("sp", (self.sem[e], self.cnt[e]))
        for q in ("sp", "pool", "act"):
            n = self.dcnt[q]
            for i in range(self.NDSEM):
                uses = (n - i + self.NDSEM - 1) // self.NDSEM if n > i else 0
                if uses:
                    self._wait("sp", (self.dsem[q][i], 16 * uses))
        self.st.close()
        return self.nc


def run(prog_nc, in_maps):
    res = run_bass_kernel_spmd(prog_nc, in_maps, core_ids=list(range(NCORES)))
    return res.results


NCH_ADA = 6 * D // 128
NCH_ADA_CORE = NCH_ADA // NCORES


def build_k0():
    P = Prog()
    nc = P.nc
    cols_core = NCH_ADA_CORE * 128
    c_in = P.dram_in("c2", [2, 128, KC])
    w_in = P.dram_in("ada_w", [DEPTH, D, cols_core])
    b_in = P.dram_in("ada_b", [DEPTH, 1, cols_core])
    out = P.dram_out("modT", [DEPTH, 128, NCH_ADA_CORE, 2])

    craw = P.sb("craw", [128, 2, KC])
    S = P.sb("S", [128, KC, 2])
    ones = P.sb("ones", [1, 2])
    bias = P.sb("bias", [1, DEPTH, cols_core])
    wt = [P.sb("wt%d" % i, [128, KC, 512]) for i in range(2)]
    ot = [P.sb("ot%d" % i, [128, NCH_ADA_CORE, 2]) for i in range(2)]
    pst = [P.ps("ps%d" % i, [128, 2]) for i in range(4)]

    P.dma("sp", craw[:, 0, :], c_in[0], writes=["craw"])
    P.dma("sp", craw[:, 1, :], c_in[1], writes=["craw"])
    P.dma("sp", bias[:], b_in.rearrange("l o c -> o l c"), writes=["bias"])
    P.op("dve", lambda e: e.memset(ones[:], 1.0), writes=["ones"])
    for s in range(2):
        P.op("act", lambda e: e.activation(out=S[:, :, s], in_=craw[:, s, :], func=AF.Silu),
             reads=["craw"], writes=["S"])
    wv = w_in.rearrange("l (p kc) c -> l p kc c", kc=KC)
    blk = 0
    for l in range(DEPTH):
        o = ot[l % 2]
        for cb in range(cols_core // 512):
            w = wt[blk % 2]
            wk = "wt%d" % (blk % 2)
            P.dma("sp" if blk % 2 == 0 else "act", w[:], wv[l, :, :, cb * 512:(cb + 1) * 512], writes=[wk])
            for sub in range(4):
                ch = cb * 4 + sub
                pt = pst[ch % 4]
                pk = "ps%d" % (ch % 4)
                for kc in range(KC):
                    P.op("pe", lambda e: e.matmul(pt[:], w[:, kc, sub * 128:(sub + 1) * 128], S[:, kc, :],
                                                  start=(kc == 0), stop=False),
                         reads=[wk, "S"], writes=[pk], pe_acc=True)
                c0 = ch * 128
                P.op("pe", lambda e: e.matmul(pt[:], bias[:, l, c0:c0 + 128], ones[:], start=False, stop=True),
                     reads=["bias", "ones"], writes=[pk], pe_acc=True)
                P.op("dve", lambda e: e.tensor_copy(out=o[:, ch, :], in_=pt[:]), reads=[pk], writes=["ot%d" % (l % 2)])
            blk += 1
        P.dma("sp", out[l], o[:], reads=["ot%d" % (l % 2)], is_output=True)
    return P.finish()


def run_k0(inputs):
    cols_core = NCH_ADA_CORE * 128
    c2 = np.stack([np.asarray(inputs["c"], np.float32).reshape(128, KC),
                   np.asarray(inputs["c_ctx"], np.float32).reshape(128, KC)])
    ada_w = inputs["ada_w"]
    ada_b = inputs["ada_b"]
    in_maps = []
    for j in range(NCORES):
        sl = slice(j * cols_core, (j + 1) * cols_core)
        in_maps.append({"c2": c2,
                        "ada_w": np.ascontiguousarray(ada_w[:, :, sl]),
                        "ada_b": np.ascontiguousarray(ada_b[:, None, sl])})
    res = run(build_k0(), in_maps)
    return np.concatenate([r["modT"] for r in res], axis=2)


NTC = CTX // NCORES
NTL = SEQ // NCORES
NT = NTC + NTL


def emit_modulate(P, dst, src, mod, sc1, c, k_shift, k_scale, dkey, skey):
    for (a, b, s) in ((0, NTC, 1), (NTC, NT, 0)):
        P.op("dve", lambda e: e.tensor_scalar(out=dst[:, a:b], in0=src[:, a:b],
                                              scalar1=sc1[:, k_scale * KC + c, s:s + 1],
                                              scalar2=mod[:, k_shift * KC + c, s:s + 1],
                                              op0=ALU.mult, op1=ALU.add),
             reads=[skey, "sc1", "mod"], writes=[dkey])


def load_mod(P, mod_in):
    mod = P.sb("mod", [128, NCH_ADA, 2])
    sc1 = P.sb("sc1", [128, NCH_ADA, 2])
    P.dma("sp", mod[:], mod_in, writes=["mod"])
    P.op("dve", lambda e: e.tensor_scalar_add(out=sc1[:], in0=mod[:], scalar1=1.0), reads=["mod"], writes=["sc1"])
    return mod, sc1


def build_mod(k_shift=0, k_scale=1):
    P = Prog()
    h_in = P.dram_in("hT", [D, NT])
    mod_in = P.dram_in("modT", [128, NCH_ADA, 2])
    u_out = P.dram_out("uT", [D, NT], BF16)
    mod, sc1 = load_mod(P, mod_in)
    ht = [P.sb("ht%d" % i, [128, NT]) for i in range(3)]
    ut = [P.sb("ut%d" % i, [128, NT], BF16) for i in range(3)]
    for c in range(KC):
        i = c % 3
        P.dma("sp", ht[i][:], h_in[c * 128:(c + 1) * 128, :], writes=["ht%d" % i])
        emit_modulate(P, ut[i], ht[i], mod, sc1, c, k_shift, k_scale, "ut%d" % i, "ht%d" % i)
        P.dma("act", u_out[c * 128:(c + 1) * 128, :], ut[i][:], reads=["ut%d" % i], is_output=True)
    return P.finish()


def tok_slices(j):
    return slice(NTC * j, NTC * (j + 1)), slice(CTX + NTL * j, CTX + NTL * (j + 1))


def shard_tokens(aT):
    out = []
    for j in range(NCORES):
        sc, sl = tok_slices(j)
        out.append(np.ascontiguousarray(np.concatenate([aT[:, sc], aT[:, sl]], axis=1)))
    return out


def unshard_tokens(parts):
    rows = parts[0].shape[0]
    full = np.empty((rows, T), parts[0].dtype)
    for j in range(NCORES):
        sc, sl = tok_slices(j)
        full[:, sc] = parts[j][:, :NTC]
        full[:, sl] = parts[j][:, NTC:]
    return full


def run_mod(hT, modT_l):
    hs = shard_tokens(hT)
    res = run(build_mod(), [{"hT": hs[j], "modT": modT_l} for j in range(NCORES)])
    return unshard_tokens([r["uT"] for r in res])


EA_NCOLS = 1920
HYW = 2048
DNW = 2048
TB = 256


def build_ea1(ncols=EA_NCOLS):
    P = Prog()
    nm = ncols // 128
    u_in = P.dram_in("uT", [D, T], BF16)
    w_in = P.dram_in("w", [D, ncols])
    z_out = P.dram_out("zT", [ncols, T])
    W = P.sb("W", [128, KC, ncols], BF16)
    for kc in range(KC):
        P.dma("pool", W[:, kc, :], w_in[kc * 128:(kc + 1) * 128, :], writes=[("W", kc)])
    ub = [P.sb("ub%d" % i, [128, KC, TB], BF16) for i in range(2)]
    st = [P.sb("st%d" % i, [128, nm, TB]) for i in range(2)]
    ps = [P.ps("ps%d" % i, [128, TB]) for i in range(4)]
    uv = u_in.rearrange("(kc p) t -> p kc t", p=128)
    zv = z_out.rearrange("(m p) t -> p m t", p=128)
    nblk = T // TB
    ev = 0
    for tb in range(nblk):
        i = tb % 2
        t0 = tb * TB
        P.dma("sp", ub[i][:], uv[:, :, t0:t0 + TB], writes=["ub%d" % i])
        for m in range(nm):
            pt = ps[ev % 4]
            pk = "ps%d" % (ev % 4)
            for kc in range(KC):
                P.op("pe", lambda e: e.matmul(pt[:], W[:, kc, m * 128:(m + 1) * 128], ub[i][:, kc, :],
                                              start=(kc == 0), stop=(kc == KC - 1)),
                     reads=[("W", kc), "ub%d" % i], writes=[pk], pe_acc=True)
            if ev % 2 == 0:
                P.op("dve", lambda e: e.tensor_copy(out=st[i][:, m, :], in_=pt[:]), reads=[pk], writes=["st%d" % i])
            else:
                P.op("act", lambda e: e.activation(out=st[i][:, m, :], in_=pt[:], func=AF.Copy),
                     reads=[pk], writes=["st%d" % i])
            ev += 1
        P.dma("act", zv[:, :, t0:t0 + TB], st[i][:], reads=["st%d" % i], is_output=True)
    return P.finish()


def ea_cols(j):
    cols = []
    for part in range(3):
        cols += list(range(part * HYW + 256 * j, part * HYW + 256 * (j + 1)))
    base = 3 * HYW
    for part in range(4):
        cols += list(range(base + part * DNW + 256 * j, base + part * DNW + 256 * (j + 1)))
    base = 3 * HYW + 4 * DNW
    for part in range(4):
        cols += [base + part * 16 + 2 * j, base + part * 16 + 2 * j + 1]
    return np.array(cols)


def run_ea1(uT, w_in_l):
    in_maps = []
    for j in range(NCORES):
        cols = ea_cols(j)
        w = np.zeros((D, EA_NCOLS), np.float32)
        w[:, :len(cols)] = w_in_l[:, cols]
        in_maps.append({"uT": uT, "w": w})
    res = run(build_ea1(), in_maps)
    return [r["zT"] for r in res]


ALPHA = (2 * DEPTH) ** 0.25
LN_EPS = 1e-5
NBK = 3
BK = NT // NBK
NEXP = 16


def emit_layernorm(P, zb, outb, g, b, ones128, ps_s, ps_s2, sq, tmp, zkey, okey, width):
    for m in range(KC):
        P.op("pe", lambda e: e.matmul(ps_s[:, :width], ones128[:], zb[:, m, :width], start=(m == 0), stop=(m == KC - 1)),
             reads=[zkey, "ones128"], writes=["ps_s"], pe_acc=True)
    for m in range(KC):
        s = sq[m % 2]
        sk = "sq%d" % (m % 2)
        P.op("act", lambda e: e.activation(out=s[:, :width], in_=zb[:, m, :width], func=AF.Square), reads=[zkey], writes=[sk])
        P.op("pe", lambda e: e.matmul(ps_s2[:, :width], ones128[:], s[:, :width], start=(m == 0), stop=(m == KC - 1)),
             reads=[sk, "ones128"], writes=["ps_s2"], pe_acc=True)
    mean, msq, rstd = tmp
    P.op("dve", lambda e: e.tensor_scalar(out=mean[:, :width], in0=ps_s[:, :width], scalar1=1.0 / D, scalar2=None, op0=ALU.mult),
         reads=["ps_s"], writes=["mean"])
    P.op("dve", lambda e: e.tensor_tensor(out=msq[:, :width], in0=mean[:, :width], in1=mean[:, :width], op=ALU.mult),
         reads=["mean"], writes=["msq"])
    P.op("dve", lambda e: e.scalar_tensor_tensor(out=rstd[:, :width], in0=ps_s2[:, :width], scalar=1.0 / D, in1=msq[:, :width],
                                                 op0=ALU.mult, op1=ALU.subtract),
         reads=["ps_s2", "msq"], writes=["rstd"])
    P.op("dve", lambda e: e.tensor_scalar_add(out=rstd[:, :width], in0=rstd[:, :width], scalar1=LN_EPS),
         reads=["rstd"], writes=["rstd"])
    P.op("act", lambda e: e.activation(out=rstd[:, :width], in_=rstd[:, :width], func=AF.Sqrt),
         reads=["rstd"], writes=["rstd"])
    P.op("dve", lambda e: e.reciprocal(out=rstd[:, :width], in_=rstd[:, :width]),
         reads=["rstd"], writes=["rstd"])
    for m in range(KC):
        eng = "dve" if m % 2 == 0 else "pool"
        P.op(eng, lambda e: e.tensor_tensor(out=outb[:, m, :width], in0=zb[:, m, :width], in1=mean[:, :width], op=ALU.subtract),
             reads=[zkey, "mean"], writes=[(okey, m)])
        P.op(eng, lambda e: e.tensor_tensor(out=outb[:, m, :width], in0=outb[:, m, :width], in1=rstd[:, :width], op=ALU.mult),
             reads=[(okey, m), "rstd"], writes=[(okey, m)])
        P.op(eng, lambda e: e.tensor_scalar(out=outb[:, m, :width], in0=outb[:, m, :width], scalar1=g[:, m:m + 1],
                                            scalar2=b[:, m:m + 1], op0=ALU.mult, op1=ALU.add),
             reads=[(okey, m), "lng", "lnb"], writes=[(okey, m)])


def blk_streams(bk):
    if bk == 0:
        return [(0, NTC, 1), (NTC, BK, 0)]
    return [(0, BK, 0)]


def build_x3(glu):
    P = Prog()
    nout = 2 * D if glu else D
    f_in = P.dram_in("fT", [D, NT])
    h_in = P.dram_in("hT", [D, NT])
    mod_in = P.dram_in("modT", [128, NCH_ADA, 2])
    w_in = P.dram_in("w", [D, nout])
    g_in = P.dram_in("lng", [128, KC])
    b_in = P.dram_in("lnb", [128, KC])
    wr_in = P.dram_in("wr", [128, KC, NEXP])
    h1_out = P.dram_out("h1T", [D, NT])
    h2_out = P.dram_out("h2T", [D, NT], BF16)
    aff_out = P.dram_out("affT", [NEXP, NT])

    mod, sc1 = load_mod(P, mod_in)
    g = P.sb("lng", [128, KC]); b = P.sb("lnb", [128, KC]); wr = P.sb("wr", [128, KC, NEXP])
    P.dma("sp", g[:], g_in, writes=["lng"]); P.dma("sp", b[:], b_in, writes=["lnb"]); P.dma("sp", wr[:], wr_in, writes=["wr"])
    ones128 = P.sb("ones128", [128, 128])
    P.op("dve", lambda e: e.memset(ones128[:], 1.0), writes=["ones128"])
    fb = P.sb("fb", [128, KC, BK], BF16)
    hb = P.sb("hb", [128, KC, BK])
    zb = P.sb("zb", [128, KC, BK])
    h2b = P.sb("h2b", [128, KC, BK], BF16)
    nw = 4 if glu else 2
    wm = [P.sb("wm%d" % i, [128, KC, 128], BF16) for i in range(nw)]
    sq = [P.sb("sq%d" % i, [128, BK]) for i in range(2)]
    tmp = (P.sb("mean", [128, BK]), P.sb("msq", [128, BK]), P.sb("rstd", [128, BK]))
    ex = P.sb("ex", [NEXP, BK]); rs = P.sb("rs", [NEXP, BK]); af = P.sb("af", [NEXP, BK])
    psA = [P.ps("psA%d" % i, [128, BK]) for i in range(2)]
    psB = [P.ps("psB%d" % i, [128, BK]) for i in range(2)] if glu else None
    ps_s = P.ps("ps_s", [128, BK]); ps_s2 = P.ps("ps_s2", [128, BK])
    ps_r = P.ps("ps_r", [NEXP, BK])
    fv = f_in.rearrange("(kc p) t -> p kc t", p=128)
    hv = h_in.rearrange("(kc p) t -> p kc t", p=128)
    wv = w_in.rearrange("(kc p) c -> p kc c", p=128)
    h1v = h1_out.rearrange("(kc p) t -> p kc t", p=128)
    h2v = h2_out.rearrange("(kc p) t -> p kc t", p=128)
    wi = 0
    for bk in range(NBK):
        c0 = bk * BK
        P.dma("pool", fb[:], fv[:, :, c0:c0 + BK], writes=["fb"])
        P.dma("sp", hb[:], hv[:, :, c0:c0 + BK], writes=["hb"])
        P.op("act", lambda e: e.activation(out=hb[:], in_=hb[:], func=AF.Copy, scale=ALPHA), reads=["hb"], writes=["hb"])
        for m in range(KC):
            wa = wm[wi % nw]; wak = "wm%d" % (wi % nw); wi += 1
            P.dma("pool", wa[:], wv[:, :, m * 128:(m + 1) * 128], writes=[wak])
            pa = psA[m % 2]; pak = "psA%d" % (m % 2)
            for kc in range(KC):
                P.op("pe", lambda e: e.matmul(pa[:], wa[:, kc, :], fb[:, kc, :], start=(kc == 0), stop=(kc == KC - 1)),
                     reads=[wak, "fb"], writes=[pak], pe_acc=True)
            src = pa; srck = pak
            if glu:
                wb = wm[wi % nw]; wbk = "wm%d" % (wi % nw); wi += 1
                P.dma("pool", wb[:], wv[:, :, D + m * 128:D + (m + 1) * 128], writes=[wbk])
                pb = psB[m % 2]; pbk = "psB%d" % (m % 2)
                for kc in range(KC):
                    P.op("pe", lambda e: e.matmul(pb[:], wb[:, kc, :], fb[:, kc, :], start=(kc == 0), stop=(kc == KC - 1)),
                         reads=[wbk, "fb"], writes=[pbk], pe_acc=True)
                sg = sq[m % 2]; sgk = "sq%d" % (m % 2)
                P.op("act", lambda e: e.activation(out=sg[:], in_=pb[:], func=AF.Sigmoid), reads=[pbk], writes=[sgk])
                P.op("dve", lambda e: e.tensor_tensor(out=sg[:], in0=pa[:], in1=sg[:], op=ALU.mult), reads=[pak, sgk], writes=[sgk])
                src = sg; srck = sgk
            for (a, bb, s) in blk_streams(bk):
                P.op("dve", lambda e: e.scalar_tensor_tensor(out=zb[:, m, a:bb], in0=src[:, a:bb],
                                                             scalar=mod[:, 2 * KC + m, s:s + 1], in1=hb[:, m, a:bb],
                                                             op0=ALU.mult, op1=ALU.add),
                     reads=[srck, "hb", "mod"], writes=["zb"])
        emit_layernorm(P, zb, hb, g, b, ones128, ps_s, ps_s2, sq, tmp, "zb", "hb", BK)
        P.dma("sp", h1v[:, :, c0:c0 + BK], hb[:], reads=[("hb", m) for m in range(KC)] + ["hb"], is_output=True)
        for m in range(KC):
            for (a, bb, s) in blk_streams(bk):
                P.op("pool", lambda e: e.tensor_scalar(out=zb[:, m, a:bb], in0=hb[:, m, a:bb],
                                                       scalar1=sc1[:, 4 * KC + m, s:s + 1], scalar2=mod[:, 3 * KC + m, s:s + 1],
                                                       op0=ALU.mult, op1=ALU.add),
                     reads=[("hb", m), "sc1", "mod"], writes=[("zb2", m), "zb"])
            P.op("act", lambda e: e.activation(out=h2b[:, m, :], in_=zb[:, m, :], func=AF.Copy), reads=[("zb2", m)], writes=["h2b"])
            P.op("pe", lambda e: e.matmul(ps_r[:], wr[:, m, :], zb[:, m, :], start=(m == 0), stop=(m == KC - 1)),
                 reads=["wr", ("zb2", m)], writes=["ps_r"], pe_acc=True)
        P.dma("act", h2v[:, :, c0:c0 + BK], h2b[:], reads=["h2b"], is_output=True)
        P.op("act", lambda e: e.activation(out=ex[:], in_=ps_r[:], func=AF.Exp), reads=["ps_r"], writes=["ex"])
        P.op("pe", lambda e: e.matmul(ps_r[:], ones128[0:NEXP, 0:NEXP], ex[:], start=True, stop=True),
             reads=["ex", "ones128"], writes=["ps_r"])
        P.op("dve", lambda e: e.reciprocal(out=rs[:], in_=ps_r[:]), reads=["ps_r"], writes=["rs"])
        P.op("dve", lambda e: e.tensor_tensor(out=af[:], in0=ex[:], in1=rs[:], op=ALU.mult), reads=["ex", "rs"], writes=["af"])
        P.dma("sp", aff_out[:, c0:c0 + BK], af[:], reads=["af"], is_output=True)
    return P.finish()


def pvec(v):
    return np.ascontiguousarray(np.asarray(v, np.float32).reshape(KC, 128).T)


def run_x3(glu, fT, hT, modT_l, w, lng, lnb, wrouter):
    fs = shard_tokens(fT); hs = shard_tokens(hT)
    wr = np.ascontiguousarray(np.asarray(wrouter, np.float32).reshape(KC, 128, NEXP).transpose(1, 0, 2))
    g = pvec(lng); b = pvec(lnb)
    w = np.ascontiguousarray(w, dtype=np.float32)
    in_maps = [{"fT": fs[j], "hT": hs[j], "modT": modT_l, "w": w, "lng": g, "lnb": b, "wr": wr} for j in range(NCORES)]
    res = run(build_x3(glu), in_maps)
    return (unshard_tokens([r["h1T"] for r in res]), unshard_tokens([r["h2T"] for r in res]),
            unshard_tokens([r["affT"] for r in res]))


EFF = 384
NKF = EFF // 128
CAP_LAT = 2 * SEQ // NEXP
CAP_CTX = 2 * CTX // NEXP
NBIS = 28


def emit_threshold(P, aff_t, width, cap, blockones, lo, mid, cmp_t, cnt, ge, ps_c, tag):
    P.op("dve", lambda e: e.memset(lo[:], 0.0), writes=[tag + "lo"])
    for it in range(NBIS):
        h = 0.5 ** (it + 1)
        P.op("dve", lambda e: e.tensor_scalar_add(out=mid[:], in0=lo[:], scalar1=h), reads=[tag + "lo"], writes=[tag + "mid"])
        P.op("dve", lambda e: e.tensor_scalar(out=cmp_t[:, :width], in0=aff_t[:, :width], scalar1=mid[:, 0:1], scalar2=None,
                                              op0=ALU.is_ge),
             reads=[tag + "aff", tag + "mid"], writes=[tag + "cmp"])
        P.op("dve", lambda e: e.reduce_sum(out=cnt[:], in_=cmp_t[:, :width], axis=AX.X), reads=[tag + "cmp"], writes=[tag + "cnt"])
        P.op("pe", lambda e: e.matmul(ps_c, blockones[:], cnt[:], start=True, stop=True),
             reads=[tag + "cnt", "blockones"], writes=["ps_c"])
        P.op("dve", lambda e: e.tensor_scalar(out=ge[:], in0=ps_c, scalar1=float(cap) - 0.5, scalar2=None, op0=ALU.is_ge),
             reads=["ps_c"], writes=[tag + "ge"])
        P.op("dve", lambda e: e.scalar_tensor_tensor(out=lo[:], in0=ge[:], scalar=h, in1=lo[:], op0=ALU.mult, op1=ALU.add),
             reads=[tag + "ge", tag + "lo"], writes=[tag + "lo"])


def build_x4():
    P = Prog()
    h2_in = P.dram_in("h2T", [D, NT], BF16)
    h1_in = P.dram_in("h1T", [D, NT])
    affj_in = P.dram_in("affj", [NEXP, NT])
    affl_in = P.dram_in("affl", [128, SEQ // 8])
    affc_in = P.dram_in("affc", [128, CTX // 8])
    mod_in = P.dram_in("modT", [128, NCH_ADA, 2])
    g_in = P.dram_in("lng", [128, KC])
    b_in = P.dram_in("lnb", [128, KC])
    wi_in = P.dram_in("w_in", [NEXP, D, 2 * EFF])
    wo_in = P.dram_in("w_out", [NEXP, EFF, D])
    sel_in = P.dram_in("sel", [NEXP, NEXP, 128])
    bo_in = P.dram_in("blockones", [128, 128])
    pick_in = P.dram_in("pick", [128, NEXP])
    h_out = P.dram_out("hT", [D, NT])

    mod, sc1 = load_mod(P, mod_in)
    g = P.sb("lng", [128, KC]); b = P.sb("lnb", [128, KC])
    P.dma("sp", g[:], g_in, writes=["lng"]); P.dma("sp", b[:], b_in, writes=["lnb"])
    sel = P.sb("sel", [NEXP, NEXP, 128]); blockones = P.sb("blockones", [128, 128]); pick = P.sb("pick", [128, NEXP])
    P.dma("sp", sel[:], sel_in, writes=["sel"]); P.dma("sp", blockones[:], bo_in, writes=["blockones"])
    P.dma("sp", pick[:], pick_in, writes=["pick"])
    ones128 = P.sb("ones128", [128, 128])
    P.op("dve", lambda e: e.memset(ones128[:], 1.0), writes=["ones128"])

    affl = P.sb("affl", [128, SEQ // 8]); affc = P.sb("affc", [128, CTX // 8]); cmp_t = P.sb("cmp", [128, SEQ // 8])
    P.dma("sp", affl[:], affl_in, writes=["Laff"]); P.dma("sp", affc[:], affc_in, writes=["Caff"])
    lo_l = P.sb("lo_l", [128, 1]); lo_c = P.sb("lo_c", [128, 1]); mid = P.sb("mid", [128, 1]); cnt = P.sb("cnt", [128, 1])
    ge = P.sb("ge", [128, 1])
    ps_small = P.ps("ps_small", [128, 4])
    ps_c = ps_small[:, 0:1]
    emit_threshold(P, affl, SEQ // 8, CAP_LAT, blockones, lo_l, mid, cmp_t, cnt, ge, ps_c, "L")
    emit_threshold(P, affc, CTX // 8, CAP_CTX, blockones, lo_c, mid, cmp_t, cnt, ge, ps_c, "C")
    thr = P.sb("thr", [NEXP, 2])
    ps_t = ps_small[0:NEXP, 1:3]
    P.op("pe", lambda e: e.matmul(ps_t[:, 0:1], pick[:], lo_l[:], start=True, stop=True), reads=["pick", "Llo"], writes=["ps_c"])
    P.op("pe", lambda e: e.matmul(ps_t[:, 1:2], pick[:], lo_c[:], start=True, stop=True), reads=["pick", "Clo"], writes=["ps_c"])
    P.op("dve", lambda e: e.tensor_copy(out=thr[:], in_=ps_t), reads=["ps_c"], writes=["thr"])
    affj = P.sb("affj", [NEXP, NT]); G = P.sb("G", [NEXP, NT])
    P.dma("sp", affj[:], affj_in, writes=["affj"])
    for (a, bb, s) in ((0, NTC, 1), (NTC, NT, 0)):
        P.op("dve", lambda e: e.scalar_tensor_tensor(out=G[:, a:bb], in0=affj[:, a:bb], scalar=thr[:, s:s + 1], in1=affj[:, a:bb],
                                                     op0=ALU.is_ge, op1=ALU.mult),
             reads=["affj", "thr"], writes=["G"])

    h2b = P.sb("h2b", [128, KC, BK], BF16)
    act = P.sb("act", [128, NEXP * NKF, BK], BF16)
    zb = P.sb("zb", [128, KC, BK])
    h1c = [P.sb("h1c%d" % i, [128, BK]) for i in range(2)]
    wa = [P.sb("wa%d" % i, [128, KC, 128], BF16) for i in range(3)]
    wo = [P.sb("wo%d" % i, [128, NEXP * NKF, 128], BF16) for i in range(2)]
    sq = [P.sb("sq%d" % i, [128, BK]) for i in range(2)]
    tmp = (P.sb("mean", [128, BK]), P.sb("msq", [128, BK]), P.sb("rstd", [128, BK]))
    st = [P.sb("silu%d" % i, [128, BK]) for i in range(2)]
    psg = P.ps("psg", [128, BK])
    psa = [P.ps("psa%d" % i, [128, BK]) for i in range(2)]
    psb = [P.ps("psb%d" % i, [128, BK]) for i in range(2)]
    ps_s = P.ps("ps_s", [128, BK]); ps_s2 = P.ps("ps_s2", [128, BK])
    h2v = h2_in.rearrange("(kc p) t -> p kc t", p=128)
    h1v = h1_in.rearrange("(kc p) t -> p kc t", p=128)
    hov = h_out.rearrange("(kc p) t -> p kc t", p=128)
    wiv = wi_in.rearrange("e (kc p) c -> e p kc c", p=128)
    wov = wo_in.rearrange("e (kf p) c -> p e kf c", p=128)
    wai = 0
    for bk in range(NBK):
        c0 = bk * BK
        P.dma("sp", h2b[:], h2v[:, :, c0:c0 + BK], writes=["h2b"])
        for ex in range(NEXP):
            P.op("pe", lambda e: e.matmul(psg[:], sel[:, ex, :], G[:, c0:c0 + BK], start=True, stop=True),
                 reads=["sel", "G"], writes=["psg"])
            for kf in range(NKF):
                wts = []
                for half in range(2):
                    w = wa[wai % 3]; wk = "wa%d" % (wai % 3); wai += 1
                    cc = half * EFF + kf * 128
                    P.dma("pool", w[:], wiv[ex, :, :, cc:cc + 128], writes=[wk])
                    wts.append((w, wk))
                i2 = (ex * NKF + kf) % 2
                pa, pak = psa[i2], "psa%d" % i2
                pb, pbk = psb[i2], "psb%d" % i2
                for (pt, ptk, (w, wk)) in ((pa, pak, wts[0]), (pb, pbk, wts[1])):
                    for kc in range(KC):
                        P.op("pe", lambda e: e.matmul(pt[:], w[:, kc, :], h2b[:, kc, :], start=(kc == 0), stop=(kc == KC - 1)),
                             reads=[wk, "h2b"], writes=[ptk], pe_acc=True)
                s_t, sk = st[i2], "silu%d" % i2
                P.op("act", lambda e: e.activation(out=s_t[:], in_=pa[:], func=AF.Silu), reads=[pak], writes=[sk])
                P.op("dve", lambda e: e.tensor_tensor(out=s_t[:], in0=pb[:], in1=s_t[:], op=ALU.mult), reads=[pbk, sk], writes=[sk])
                P.op("dve", lambda e: e.tensor_tensor(out=act[:, ex * NKF + kf, :], in0=psg[:], in1=s_t[:], op=ALU.mult),
                     reads=["psg", sk], writes=["act"])
        for m in range(KC):
            w = wo[m % 2]; wk = "wo%d" % (m % 2)
            P.dma("pool", w[:].rearrange("p (e kf) c -> p e kf c", kf=NKF), wov[:, :, :, m * 128:(m + 1) * 128], writes=[wk])
            hc = h1c[m % 2]; hk = "h1c%d" % (m % 2)
            P.dma("sp", hc[:], h1v[:, m, c0:c0 + BK], writes=[hk])
            P.op("act", lambda e: e.activation(out=hc[:], in_=hc[:], func=AF.Copy, scale=ALPHA), reads=[hk], writes=[hk])
            pa, pak = psa[m % 2], "psa%d" % (m % 2)
            nk = NEXP * NKF
            for k in range(nk):
                P.op("pe", lambda e: e.matmul(pa[:], w[:, k, :], act[:, k, :], start=(k == 0), stop=(k == nk - 1)),
                     reads=[wk, "act"], writes=[pak], pe_acc=True)
            for (a, bb, s) in blk_streams(bk):
                P.op("dve", lambda e: e.scalar_tensor_tensor(out=zb[:, m, a:bb], in0=pa[:, a:bb],
                                                             scalar=mod[:, 5 * KC + m, s:s + 1], in1=hc[:, a:bb],
                                                             op0=ALU.mult, op1=ALU.add),
                     reads=[pak, hk, "mod"], writes=["zb"])
        emit_layernorm(P, zb, zb, g, b, ones128, ps_s, ps_s2, sq, tmp, "zb", "zb", BK)
        P.dma("sp", hov[:, :, c0:c0 + BK], zb[:], reads=[("zb", m) for m in range(KC)] + ["zb"], writes=["zb"], is_output=True)
    return P.finish()


def moe_consts():
    sel = np.zeros((NEXP, NEXP, 128), np.float32)
    for e in range(NEXP):
        sel[e, e, :] = 1.0
    bo = np.kron(np.eye(NEXP, dtype=np.float32), np.ones((8, 8), np.float32))
    pick = np.zeros((128, NEXP), np.float32)
    for e in range(NEXP):
        pick[8 * e, e] = 1.0
    return sel, bo, pick


def run_x4(h1T, h2T, affT, modT_l, lng, lnb, w_in, w_out):
    h1s = shard_tokens(h1T); h2s = shard_tokens(h2T); affs = shard_tokens(affT)
    affl = np.ascontiguousarray(affT[:, CTX:].reshape(128, SEQ // 8))
    affc = np.ascontiguousarray(affT[:, :CTX].reshape(128, CTX // 8))
    sel, bo, pick = moe_consts()
    g = pvec(lng); b = pvec(lnb)
    w_in = np.ascontiguousarray(w_in, dtype=np.float32); w_out = np.ascontiguousarray(w_out, dtype=np.float32)
    in_maps = [{"h2T": h2s[j], "h1T": h1s[j], "affj": affs[j], "affl": affl, "affc": affc, "modT": modT_l, "lng": g, "lnb": b,
                "w_in": w_in, "w_out": w_out, "sel": sel, "blockones": bo, "pick": pick} for j in range(NCORES)]
    res = run(build_x4(), in_maps)
    return unshard_tokens([r["hT"] for r in res])


S5G = 16
S5P = 64
NGC = 32
NOCT = 4
NA = T // 64
TWO_PI = 6.283180
GELU_C = 0.7978845608028654


def emit_cmul(P, yr, yi, xr, xi, c, s, conj, keys):
    ykr, yki, xkr, xki, tk = keys
    P.op("dve", lambda e: e.tensor_tensor(out=yr, in0=xr, in1=c, op=ALU.mult), reads=[xkr, tk], writes=[ykr])
    P.op("pool", lambda e: e.tensor_tensor(out=yi, in0=xi, in1=s, op=ALU.mult), reads=[xki, tk], writes=[yki])
    P.op("dve", lambda e: e.tensor_tensor(out=yr, in0=yr, in1=yi, op=(ALU.add if conj else ALU.subtract)),
         reads=[ykr, yki], writes=[ykr])
    P.op("pool", lambda e: e.tensor_tensor(out=yi, in0=xi, in1=c, op=ALU.mult), reads=[xki, tk, ykr], writes=[yki])
    P.op("dve", lambda e: e.tensor_tensor(out=xr, in0=xr, in1=s, op=ALU.mult), reads=[xkr, tk], writes=[xkr])
    P.op("pool", lambda e: e.tensor_tensor(out=yi, in0=yi, in1=xr, op=(ALU.subtract if conj else ALU.add)),
         reads=[yki, xkr], writes=[yki])


def kk2(k):
    return [k + "A", k + "B"]


def emit_cmul2(P, yr, yi, xr, xi, c, s, conj, keys, na):
    ykr, yki, xkr, xki, tk = keys
    for (eng, sl, sfx) in (("dve", slice(0, na), "A"), ("pool", slice(na, None), "B")):
        Yr, Yi, Xr, Xi, C, S = (t[:, sl, :] for t in (yr, yi, xr, xi, c, s))
        a_, b_, c_, d_ = ykr + sfx, yki + sfx, xkr + sfx, xki + sfx
        P.op(eng, lambda e: e.tensor_tensor(out=Yr, in0=Xr, in1=C, op=ALU.mult), reads=[c_, tk], writes=[a_])
        P.op(eng, lambda e: e.tensor_tensor(out=Yi, in0=Xi, in1=S, op=ALU.mult), reads=[d_, tk], writes=[b_])
        P.op(eng, lambda e: e.tensor_tensor(out=Yr, in0=Yr, in1=Yi, op=(ALU.add if conj else ALU.subtract)), reads=[a_, b_], writes=[a_])
        P.op(eng, lambda e: e.tensor_tensor(out=Yi, in0=Xi, in1=C, op=ALU.mult), reads=[d_, tk, a_], writes=[b_])
        P.op(eng, lambda e: e.tensor_tensor(out=Xr, in0=Xr, in1=S, op=ALU.mult), reads=[c_, tk], writes=[c_])
        P.op(eng, lambda e: e.tensor_tensor(out=Yi, in0=Yi, in1=Xr, op=(ALU.subtract if conj else ALU.add)), reads=[b_, c_], writes=[b_])


def emit_sincos(P, ph, tmp_i, out_s, out_c, key):
    I32 = mybir.dt.int32
    P.op("dve", lambda e: e.tensor_copy(out=tmp_i, in_=ph), reads=[key + "ph"], writes=[key + "i"])
    P.op("dve", lambda e: e.tensor_tensor(out=out_s, in0=ph, in1=tmp_i, op=ALU.subtract), reads=[key + "ph", key + "i"], writes=[key + "s"])
    P.op("act", lambda e: e.activation(out=out_s, in_=out_s, func=AF.Sin, scale=TWO_PI), reads=[key + "s"], writes=[key + "s"])
    P.op("dve", lambda e: e.tensor_scalar_add(out=ph, in0=ph, scalar1=0.25), reads=[key + "ph", key + "s"], writes=[key + "ph"])
    P.op("dve", lambda e: e.tensor_copy(out=tmp_i, in_=ph), reads=[key + "ph"], writes=[key + "i"])
    P.op("dve", lambda e: e.tensor_tensor(out=out_c, in0=ph, in1=tmp_i, op=ALU.subtract), reads=[key + "ph", key + "i"], writes=[key + "c"])
    P.op("act", lambda e: e.activation(out=out_c, in_=out_c, func=AF.Sin, scale=TWO_PI), reads=[key + "c"], writes=[key + "c"])


def build_o2(n_oct=NOCT, n_grp=8, stages=(1, 1, 1, 1, 1, 1)):
    P = Prog()
    nc = P.nc
    I32 = mybir.dt.int32
    h_in = P.dram_in("hT", [NOCT * 128, T])
    modo_in = P.dram_in("modo", [128, NOCT, 2, 2])
    dsk_in = P.dram_in("dsk", [128, NOCT])
    lre_in = P.dram_in("lam_re", [128, NGC]); lim_in = P.dram_in("lam_im", [128, NGC]); ldt_in = P.dram_in("log_dt", [128, NGC])
    bre_in = P.dram_in("bre", [NOCT, 128, 8, 128]); bim_in = P.dram_in("bim", [NOCT, 128, 8, 128])
    cre_in = P.dram_in("cre", [128, NGC, S5G]); cim_in = P.dram_in("cim", [128, NGC, S5G])
    av_in = P.dram_in("avals", [128, NA]); bv_in = P.dram_in("bvals", [128, 64])
    f_out = P.dram_out("fT", [NOCT * 128, T])
    yscr = nc.dram_tensor("yscr", [128, T], F32, kind="Internal").ap()

    def ld(name, src, shape, dt=F32, q="sp"):
        t = P.sb(name, shape, dt)
        P.dma(q, t[:], src, writes=[name])
        return t
    modo = ld("modo", modo_in, [128, NOCT, 2, 2]); dsk = ld("dsk", dsk_in, [128, NOCT])
    lre = ld("lre", lre_in, [128, NGC]); lim = ld("lim", lim_in, [128, NGC]); ldt = ld("ldt", ldt_in, [128, NGC])
    cre = ld("cre", cre_in, [128, NGC, S5G]); cim = ld("cim", cim_in, [128, NGC, S5G])
    avals = ld("avals", av_in, [128, NA]); bvals = ld("bvals", bv_in, [128, 64])
    sc1 = P.sb("sc1o", [128, NOCT, 2])
    P.op("dve", lambda e: e.tensor_scalar_add(out=sc1[:], in0=modo[:, :, 1, :], scalar1=1.0), reads=["modo"], writes=["sc1o"])

    def sm(name, dt=F32):
        return P.sb(name, [128, NGC], dt)
    dtv = sm("dtv"); r = sm("r"); fq = sm("fq"); F1 = sm("F1"); ti = sm("ti", I32)
    lbs = sm("lbs"); lbc = sm("lbc"); ph = sm("ph"); den = sm("den"); fre = sm("fre"); fim = sm("fim"); t1 = sm("t1"); t2 = sm("t2")
    K = "prm"
    def dv(fn, reads, writes):
        P.op("dve", fn, reads=reads, writes=writes)
    dv(lambda e: e.tensor_scalar_min(out=lre[:], in0=lre[:], scalar1=-1e-4), ["lre"], ["lre"])
    P.op("act", lambda e: e.activation(out=dtv[:], in_=ldt[:], func=AF.Exp), reads=["ldt"], writes=["dtv"])
    dv(lambda e: e.tensor_tensor(out=r[:], in0=lre[:], in1=dtv[:], op=ALU.mult), ["lre", "dtv"], ["r"])
    P.op("act", lambda e: e.activation(out=r[:], in_=r[:], func=AF.Exp), reads=["r"], writes=["r"])
    dv(lambda e: e.tensor_tensor(out=fq[:], in0=lim[:], in1=dtv[:], op=ALU.mult), ["lim", "dtv"], ["fq"])
    dv(lambda e: e.tensor_scalar(out=fq[:], in0=fq[:], scalar1=1.0 / (2 * math.pi), scalar2=None, op0=ALU.mult), ["fq"], ["fq"])
    dv(lambda e: e.tensor_scalar(out=t1[:], in0=fq[:], scalar1=64.0, scalar2=None, op0=ALU.mult), ["fq"], ["t1"])
    dv(lambda e: e.tensor_copy(out=ti[:], in_=t1[:]), ["t1"], ["ti"])
    dv(lambda e: e.tensor_tensor(out=F1[:], in0=t1[:], in1=ti[:], op=ALU.subtract), ["t1", "ti"], ["F1"])
    dv(lambda e: e.tensor_copy(out=ph[:], in_=fq[:]), ["fq"], ["lbph"])
    emit_sincos(P, ph[:], ti[:], lbs[:], lbc[:], "lb")
    dv(lambda e: e.tensor_tensor(out=lbs[:], in0=lbs[:], in1=r[:], op=ALU.mult), ["lbs", "r"], ["lbs"])
    dv(lambda e: e.tensor_tensor(out=lbc[:], in0=lbc[:], in1=r[:], op=ALU.mult), ["lbc", "r"], ["lbc"])
    dv(lambda e: e.tensor_scalar_add(out=lbc[:], in0=lbc[:], scalar1=-1.0), ["lbc"], ["lbc"])
    dv(lambda e: e.tensor_tensor(out=den[:], in0=lre[:], in1=lre[:], op=ALU.mult), ["lre"], ["den"])
    dv(lambda e: e.tensor_tensor(out=t1[:], in0=lim[:], in1=lim[:], op=ALU.mult), ["lim", "F1"], ["t1"])
    dv(lambda e: e.tensor_tensor(out=den[:], in0=den[:], in1=t1[:], op=ALU.add), ["den", "t1"], ["den"])
    dv(lambda e: e.reciprocal(out=den[:], in_=den[:]), ["den"], ["den"])
    dv(lambda e: e.tensor_tensor(out=fre[:], in0=lbc[:], in1=lre[:], op=ALU.mult), ["lbc", "lre"], ["fre"])
    dv(lambda e: e.tensor_tensor(out=t1[:], in0=lbs[:], in1=lim[:], op=ALU.mult), ["lbs", "lim", "den"], ["t1"])
    dv(lambda e: e.tensor_tensor(out=fre[:], in0=fre[:], in1=t1[:], op=ALU.add), ["fre", "t1"], ["fre"])
    dv(lambda e: e.tensor_tensor(out=fre[:], in0=fre[:], in1=den[:], op=ALU.mult), ["fre", "den"], ["fre"])
    dv(lambda e: e.tensor_tensor(out=fim[:], in0=lbs[:], in1=lre[:], op=ALU.mult), ["lbs", "lre"], ["fim"])
    dv(lambda e: e.tensor_tensor(out=t2[:], in0=lbc[:], in1=lim[:], op=ALU.mult), ["lbc", "lim"], ["t2"])
    dv(lambda e: e.tensor_tensor(out=fim[:], in0=fim[:], in1=t2[:], op=ALU.subtract), ["fim", "t2"], ["fim"])
    dv(lambda e: e.tensor_tensor(out=fim[:], in0=fim[:], in1=den[:], op=ALU.mult), ["fim", "den"], ["fim"])
    gre = P.sb("gre", [128, NGC, S5G]); gimn = P.sb("gimn", [128, NGC, S5G]); gt = P.sb("gt", [128, NGC, S5G])
    freb = fre[:].unsqueeze(2).to_broadcast([128, NGC, S5G]); fimb = fim[:].unsqueeze(2).to_broadcast([128, NGC, S5G])
    dv(lambda e: e.tensor_tensor(out=gre[:], in0=cre[:], in1=freb, op=ALU.mult), ["cre", "fre"], ["gre"])
    dv(lambda e: e.tensor_tensor(out=gt[:], in0=cim[:], in1=fimb, op=ALU.mult), ["cim", "fim"], ["gt"])
    dv(lambda e: e.tensor_tensor(out=gre[:], in0=gre[:], in1=gt[:], op=ALU.subtract), ["gre", "gt"], ["gre"])
    dv(lambda e: e.tensor_tensor(out=gimn[:], in0=cre[:], in1=fimb, op=ALU.mult), ["cre", "fim"], ["gimn"])
    dv(lambda e: e.tensor_tensor(out=gt[:], in0=cim[:], in1=freb, op=ALU.mult), ["cim", "fre", "gre"], ["gt"])
    dv(lambda e: e.tensor_tensor(out=gimn[:], in0=gimn[:], in1=gt[:], op=ALU.add), ["gimn", "gt"], ["gimn"])
    dv(lambda e: e.tensor_scalar(out=gimn[:], in0=gimn[:], scalar1=-1.0, scalar2=None, op0=ALU.mult), ["gimn"], ["gimn"])
    greb = P.sb("greb", [128, NGC, S5G], BF16); gimb = P.sb("gimb", [128, NGC, S5G], BF16)
    dv(lambda e: e.tensor_copy(out=greb[:], in_=gre[:]), ["gre"], ["greb"])
    dv(lambda e: e.tensor_copy(out=gimb[:], in_=gimn[:]), ["gimn"], ["gimb"])

    U = P.sb("U", [128, T], BF16)
    BR = P.sb("BR", [128, T], BF16); BI = P.sb("BI", [128, T], BF16); TR = P.sb("TR", [128, T]); TI = P.sb("TI", [128, T])
    ZR = P.sb("ZR", [128, T], BF16); ZI = P.sb("ZI", [128, T], BF16)
    Yg = P.sb("Yg", [S5G, 2112])
    bpr = P.sb("bpr", [128, 8, 128], BF16); bpi = P.sb("bpi", [128, 8, 128], BF16)
    eac = P.sb("eac", [128, 8, NA]); eas = P.sb("eas", [128, 8, NA]); ebc = P.sb("ebc", [128, 8, 64]); ebs = P.sb("ebs", [128, 8, 64])
    pha = P.sb("pha", [128, 8, NA]); phb = P.sb("phb", [128, 8, 64]); pia = P.sb("pia", [128, 8, NA], I32); pib = P.sb("pib", [128, 8, 64], I32)
    psr = [P.ps("psr%d" % i, [128, 512]) for i in range(2)]
    psi = [P.ps("psi%d" % i, [128, 512]) for i in range(2)]
    psy = [P.ps("psy%d" % i, [S5G, 512]) for i in range(2)]

    NSPL = 88

    def v3(t):
        return t[:].rearrange("p (a b) -> p a b", b=64)
    blocks = [(i * 512, min(512, T - i * 512)) for i in range((T + 511) // 512)]
    for oc in range(n_oct):
        for sg in range(4):
            c0 = sg * 2112
            hraw = TR[:, 0:2112]
            P.dma("sp", hraw, h_in[oc * 128:(oc + 1) * 128, c0:c0 + 2112], writes=kk2("TR"))
            pieces = [(0, CTX, 1), (CTX, 2112, 0)] if sg == 0 else [(0, 2112, 0)]
            for (a, bb, s) in pieces:
                P.op("dve", lambda e: e.tensor_scalar(out=U[:, c0 + a:c0 + bb], in0=hraw[:, a:bb], scalar1=sc1[:, oc, s:s + 1],
                                                      scalar2=modo[:, oc, 0, s:s + 1], op0=ALU.mult, op1=ALU.add),
                     reads=kk2("TR") + ["sc1o", "modo"], writes=["U"])
        P.dma("pool", bpr[:], bre_in[oc], writes=["bpr"]); P.dma("pool", bpi[:], bim_in[oc], writes=["bpi"])
        g0 = oc * 8
        dv(lambda e: e.tensor_tensor(out=pha[:], in0=F1[:, g0:g0 + 8].unsqueeze(2).to_broadcast([128, 8, NA]),
                                     in1=avals[:].unsqueeze(1).to_broadcast([128, 8, NA]), op=ALU.mult),
           ["F1", "avals"], ["EAph"])
        emit_sincos(P, pha[:], pia[:], eas[:], eac[:], "EA")
        dv(lambda e: e.tensor_tensor(out=phb[:], in0=fq[:, g0:g0 + 8].unsqueeze(2).to_broadcast([128, 8, 64]),
                                     in1=bvals[:].unsqueeze(1).to_broadcast([128, 8, 64]), op=ALU.mult),
           ["fq", "bvals"], ["EBph"])
        emit_sincos(P, phb[:], pib[:], ebs[:], ebc[:], "EB")
        for gl in range(n_grp):
            g = g0 + gl
            for bi_, (c0, w) in enumerate(blocks):
                pr, prk = psr[bi_ % 2], "psr%d" % (bi_ % 2)
                pi, pik = psi[bi_ % 2], "psi%d" % (bi_ % 2)
                P.op("pe", lambda e: e.matmul(pr[:, :w], bpr[:, gl, :], U[:, c0:c0 + w], start=True, stop=True),
                     reads=["bpr", "U"], writes=[prk])
                P.op("pe", lambda e: e.matmul(pi[:, :w], bpi[:, gl, :], U[:, c0:c0 + w], start=True, stop=True),
                     reads=["bpi", "U"], writes=[pik])
                P.op("act", lambda e: e.activation(out=BR[:, c0:c0 + w], in_=pr[:, :w], func=AF.Copy), reads=[prk], writes=kk2("BR"))
                P.op("act", lambda e: e.activation(out=BI[:, c0:c0 + w], in_=pi[:, :w], func=AF.Copy), reads=[pik], writes=kk2("BI"))
            ebcb = ebc[:, gl, :].unsqueeze(1).to_broadcast([128, NA, 64]); ebsb = ebs[:, gl, :].unsqueeze(1).to_broadcast([128, NA, 64])
            eacb = eac[:, gl, :].unsqueeze(2).to_broadcast([128, NA, 64]); easb = eas[:, gl, :].unsqueeze(2).to_broadcast([128, NA, 64])
            if stages[0]:
                emit_cmul2(P, v3(ZR), v3(ZI), v3(BR), v3(BI), ebcb, ebsb, True, ("ZR", "ZI", "BR", "BI", "EBc"), NSPL)
            if stages[1]:
                emit_cmul2(P, v3(BR), v3(BI), v3(ZR), v3(ZI), eacb, easb, True, ("BR", "BI", "ZR", "ZI", "EAc"), NSPL)
            rf = r[0:64, g:g + 1]; rb = r[64:128, g:g + 1]
            for (src, dst, sk, dk) in (((BR, TR, "BR", "TR"), (BI, TI, "BI", "TI")) if stages[2] else ()):
                P.op("dve", lambda e: e.tensor_tensor_scan(out=dst[0:64, :], data0=rf.to_broadcast([64, T]), data1=src[0:64, :],
                                                           initial=0.0, op0=ALU.mult, op1=ALU.add),
                     reads=kk2(sk) + ["r"], writes=kk2(dk))
                P.op("dve", lambda e: e.tensor_tensor_scan(out=dst[64:128, 0:CTX][:, ::-1], data0=rb.to_broadcast([64, CTX]),
                                                           data1=src[64:128, 0:CTX][:, ::-1], initial=0.0, op0=ALU.mult, op1=ALU.add),
                     reads=kk2(sk) + ["r"], writes=kk2(dk))
                P.op("dve", lambda e: e.tensor_tensor_scan(out=dst[64:128, CTX:T][:, ::-1], data0=rb.to_broadcast([64, SEQ]),
                                                           data1=src[64:128, CTX:T][:, ::-1], initial=dst[64:128, 0:1],
                                                           op0=ALU.mult, op1=ALU.add),
                     reads=kk2(sk) + ["r"] + kk2(dk), writes=kk2(dk))
            if stages[3]:
                emit_cmul2(P, v3(BR), v3(BI), v3(TR), v3(TI), ebcb, ebsb, False, ("BR", "BI", "TR", "TI", "EBc"), NSPL)
                emit_cmul2(P, v3(ZR), v3(ZI), v3(BR), v3(BI), eacb, easb, False, ("ZR", "ZI", "BR", "BI", "EAc"), NSPL)
            for sg in (range(4) if stages[4] else ()):
                for sb_ in range(5):
                    c0 = sg * 2112 + sb_ * 512
                    w = min(512, sg * 2112 + 2112 - c0)
                    if w <= 0:
                        continue
                    py, pyk = psy[sb_ % 2], "psy%d" % (sb_ % 2)
                    P.op("pe", lambda e: e.matmul(py[:, :w], greb[:, g, :], ZR[:, c0:c0 + w], start=True, stop=False),
                         reads=["greb"] + kk2("ZR"), writes=[pyk])
                    P.op("pe", lambda e: e.matmul(py[:, :w], gimb[:, g, :], ZI[:, c0:c0 + w], start=False, stop=True),
                         reads=["gimb"] + kk2("ZI"), writes=[pyk], pe_acc=True)
                    P.op("dve", lambda e: e.tensor_copy(out=Yg[:, sb_ * 512:sb_ * 512 + w], in_=py[:, :w]), reads=[pyk], writes=["Yg"])
                P.dma("sp", yscr[gl * S5G:(gl + 1) * S5G, sg * 2112:(sg + 1) * 2112], Yg[:], reads=["Yg"], writes=["yscr"])
        for sg in (range(4) if stages[5] else ()):
            c0 = sg * 2112
            X = TR[:, 0:2112]; X2 = TI[:, 0:2112]
            P.dma("sp", X, yscr[:, c0:c0 + 2112], reads=["yscr"], writes=kk2("TR"))
            P.op("dve", lambda e: e.scalar_tensor_tensor(out=X, in0=U[:, c0:c0 + 2112], scalar=dsk[:, oc:oc + 1], in1=X,
                                                         op0=ALU.mult, op1=ALU.add), reads=["U", "dsk"] + kk2("TR"), writes=kk2("TR"))
            P.op("pool", lambda e: e.tensor_tensor(out=X2, in0=X, in1=X, op=ALU.mult), reads=kk2("TR"), writes=kk2("TI"))
            P.op("pool", lambda e: e.tensor_scalar(out=X2, in0=X2, scalar1=0.044715 * GELU_C, scalar2=GELU_C, op0=ALU.mult, op1=ALU.add),
                 reads=kk2("TI"), writes=kk2("TI"))
            P.op("dve", lambda e: e.tensor_tensor(out=X2, in0=X2, in1=X, op=ALU.mult), reads=kk2("TI") + kk2("TR"), writes=kk2("TI"))
            P.op("act", lambda e: e.activation(out=X2, in_=X2, func=AF.Tanh), reads=kk2("TI"), writes=kk2("TI"))
            P.op("dve", lambda e: e.tensor_scalar(out=X2, in0=X2, scalar1=1.0, scalar2=0.5, op0=ALU.add, op1=ALU.mult),
                 reads=kk2("TI"), writes=kk2("TI"))
            P.op("dve", lambda e: e.tensor_tensor(out=X, in0=X, in1=X2, op=ALU.mult), reads=kk2("TI") + kk2("TR"), writes=kk2("TR"))
            P.dma("sp", f_out[oc * 128:(oc + 1) * 128, c0:c0 + 2112], X, reads=kk2("TR"), writes=kk2("TR"), is_output=True)
    return P.finish()


def s5_consts():
    a = np.arange(NA, dtype=np.float32)
    ab = np.where(a < 4, 3 - a, 135 - a).astype(np.float32)
    avals = np.concatenate([np.tile(a, (64, 1)), np.tile(ab, (64, 1))], axis=0)
    b = np.arange(64, dtype=np.float32)
    bvals = np.concatenate([np.tile(b, (64, 1)), np.tile(63 - b, (64, 1))], axis=0)
    return np.ascontiguousarray(avals), np.ascontiguousarray(bvals)


def run_o2(hT, modT_l, lam_re, lam_im, log_dt, b_re, b_im, c_re, c_im, d_skip, **bkw):
    avals, bvals = s5_consts()
    in_maps = []
    for j in range(NCORES):
        gs = slice(NGC * j, NGC * (j + 1))
        ch = slice(512 * j, 512 * (j + 1))
        def dp(a):
            return np.ascontiguousarray(np.asarray(a[:, gs], np.float32).transpose(0, 2, 1).reshape(128, NGC))
        ldt = np.ascontiguousarray(np.broadcast_to(np.asarray(log_dt[:, gs], np.float32)[:, None, :], (2, 64, NGC)).reshape(128, NGC))
        def bpad(bm):
            bm = np.asarray(bm[:, gs], np.float32)
            out = np.zeros((NOCT, 128, 8, 128), np.float32)
            for oc in range(NOCT):
                for gl in range(8):
                    blk = bm[:, oc * 8 + gl]
                    out[oc, gl * 16:(gl + 1) * 16, gl, :] = blk.transpose(2, 0, 1).reshape(16, 128)
            return out
        def cpad(cm):
            cm = np.asarray(cm[:, gs], np.float32)
            return np.ascontiguousarray(cm.transpose(0, 3, 1, 2).reshape(128, NGC, S5G))
        modo = np.zeros((128, NOCT, 2, 2), np.float32)
        for oc in range(NOCT):
            chunk = 4 * j + oc
            modo[:, oc, 0, :] = modT_l[:, 0 * KC + chunk, :]
            modo[:, oc, 1, :] = modT_l[:, 1 * KC + chunk, :]
        dsk = np.ascontiguousarray(np.asarray(d_skip[ch], np.float32).reshape(NOCT, 128).T)
        in_maps.append({"hT": np.ascontiguousarray(hT[ch]), "modo": modo, "dsk": dsk, "lam_re": dp(lam_re), "lam_im": dp(lam_im),
                        "log_dt": ldt, "bre": bpad(b_re), "bim": bpad(b_im), "cre": cpad(c_re), "cim": cpad(c_im),
                        "avals": avals, "bvals": bvals})
    res = run(build_o2(**bkw), in_maps)
    return np.concatenate([r["fT"] for r in res], axis=0)


CB = 16
NFFT = 16384
HYF = 64


def build_eh():
    P = Prog()
    nc = P.nc
    I32 = mybir.dt.int32
    zh_in = P.dram_in("zh", [3, 256, T])
    cw_in = P.dram_in("convw", [128, 2, 3, 3])
    hb_in = P.dram_in("hbias", [128, 2, 2])
    w1_in = P.dram_in("w1", [33, HYF]); w2_in = P.dram_in("w2", [HYF, HYF]); w3_in = P.dram_in("w3", [HYF, HYF])
    bfr_in = P.dram_in("bfr", [HYF, 2, 3])
    w4_in = P.dram_in("w4", [HYF, 2, 2, 256])
    zl_in = P.dram_in("zposl", [33, SEQ]); zc_in = P.dram_in("zposc", [33, CTX])
    dl_in = P.dram_in("decl", [256, SEQ]); dc_in = P.dram_in("decc", [256, CTX])
    tabs_in = P.dram_in("tabs", [128, 12, 128])
    y_out = P.dram_out("yh", [256, T])
    x1c = nc.dram_tensor("x1c", [128, T], F32, kind="Internal").ap()
    x2c = nc.dram_tensor("x2c", [128, T], F32, kind="Internal").ap()
    a_dram = nc.dram_tensor("a_dram", [128, SEQ], F32, kind="Internal").ap()
    g_dram = nc.dram_tensor("g_dram", [128, NFFT], F32, kind="Internal").ap()
    c_dram = nc.dram_tensor("c_dram", [128, SEQ], F32, kind="Internal").ap()

    def ld(name, src, shape, q="sp"):
        t = P.sb(name, shape)
        P.dma(q, t[:], src, writes=[name])
        return t
    cw = ld("cw", cw_in, [128, 2, 3, 3]); hbias = ld("hbias", hb_in, [128, 2, 2])
    w1 = ld("w1", w1_in, [33, HYF]); w2 = ld("w2", w2_in, [HYF, HYF]); w3 = ld("w3", w3_in, [HYF, HYF])
    bfr = ld("bfr", bfr_in, [HYF, 2, 3]); w4 = ld("w4", w4_in, [HYF, 2, 2, 256])
    tabs = ld("tabs", tabs_in, [128, 12, 128])
    T1 = tabs[:, 0:2, :].rearrange("p a b -> p (a b)")
    TI1a = tabs[:, 7:9, :].rearrange("p a b -> p (a b)")
    TI1b = tabs[:, 9:11, :].rearrange("p a b -> p (a b)")
    C128 = tabs[:, 0, :]; NEGS = tabs[:, 1, :]; S128 = tabs[:, 2, :]
    twc = tabs[:, 3, :]; tws = tabs[:, 4, :]
    CN = tabs[:, 5, 0:64]; NSN = tabs[:, 6, 0:64]
    frs = P.sb("frs", [HYF, 3])
    P.op("dve", lambda e: e.tensor_scalar(out=frs[:], in0=bfr[:, 1, :], scalar1=1.0 / (2 * math.pi), scalar2=None, op0=ALU.mult),
         reads=["bfr"], writes=["frs"])

    h3l = P.sb("h3l", [HYF, SEQ]); h3c = P.sb("h3c", [HYF, CTX])
    zp = P.sb("zp", [33, 512]); ha = P.sb("ha", [HYF, 512]); hbt = P.sb("hbt", [HYF, 512]); hi_ = P.sb("hi", [HYF, 512], I32)
    psm = P.ps("psm", [128, 512])

    def mlp_layer(li, wt, wk, src, srck, dst, dstk, w_):
        kdim = 33 if li == 0 else HYF
        P.op("pe", lambda e: e.matmul(psm[0:HYF, :w_], wt[0:kdim, :], src, start=True, stop=True), reads=[wk, srck], writes=["psm"])
        P.op("dve", lambda e: e.tensor_scalar(out=hbt[:, :w_], in0=psm[0:HYF, :w_], scalar1=bfr[:, 0, li:li + 1],
                                              scalar2=frs[:, li:li + 1], op0=ALU.add, op1=ALU.mult),
             reads=["psm", "bfr", "frs"], writes=["hbt"])
        P.op("dve", lambda e: e.tensor_copy(out=hi_[:, :w_], in_=hbt[:, :w_]), reads=["hbt"], writes=["hi"])
        P.op("dve", lambda e: e.tensor_tensor(out=hbt[:, :w_], in0=hbt[:, :w_], in1=hi_[:, :w_], op=ALU.subtract),
             reads=["hbt", "hi"], writes=["hbt"])
        P.op("act", lambda e: e.activation(out=dst, in_=hbt[:, :w_], func=AF.Sin, scale=TWO_PI), reads=["hbt"], writes=[dstk])

    for (zsrc, L, h3) in ((zl_in, SEQ, h3l), (zc_in, CTX, h3c)):
        for b0 in range(0, L, 512):
            w_ = min(512, L - b0)
            P.dma("sp", zp[:, :w_], zsrc[:, b0:b0 + w_], writes=["zp"])
            mlp_layer(0, w1, "w1", zp[:, :w_], "zp", ha[:, :w_], "ha", w_)
            mlp_layer(1, w2, "w2", ha[:, :w_], "ha", ha[:, :w_], "ha", w_)
            mlp_layer(2, w3, "w3", ha[:, :w_], "ha", h3[:, b0:b0 + w_], "h3", w_)

    Xt = P.sb("Xt", [128, CB, 128]); A = P.sb("A", [128, CB, 256])
    P1r = P.sb("P1r", [128, CB, 128]); P1i = P.sb("P1i", [128, CB, 128])
    P2r = P.sb("P2r", [128, CB, 128]); P2i = P.sb("P2i", [128, CB, 128])
    Hr = P.sb("Hr", [128, CB, 128]); Hi = P.sb("Hi", [128, CB, 128])
    Yo = P.sb("Yo", [64, CB, 128])
    ps1 = [P.ps("ps1_%d" % i, [128, 512]) for i in range(2)]
    psxr = P.ps("psxr", [128, 512]); psxi = P.ps("psxi", [128, 512])
    ps3 = [P.ps("ps3_%d" % i, [64, 512]) for i in range(2)]
    twcb = twc.unsqueeze(1).to_broadcast([128, CB, 128]); twsb = tws.unsqueeze(1).to_broadcast([128, CB, 128])

    def fl(t):
        return t[:].rearrange("p c k -> p (c k)")

    def fft_fwd(src_dram, ch0, kdim, outr, outi, ork, oik):
        P.dma("sp", Xt[0:kdim], src_dram[ch0:ch0 + CB, 0:kdim * 128].rearrange("c (n2 n1) -> n2 c n1", n1=128),
              reads=[src_dram.tensor.name], writes=["Xt"])
        for c2 in range(CB // 2):
            pt, pk = ps1[c2 % 2], "ps1_%d" % (c2 % 2)
            for h in range(2):
                ch = 2 * c2 + h
                P.op("pe", lambda e: e.matmul(pt[:, h * 256:(h + 1) * 256], Xt[0:kdim, ch, :], T1[0:kdim, :], start=True, stop=True),
                     reads=["Xt", "tabs"], writes=[pk])
            P.op("act", lambda e: e.activation(out=A[:, 2 * c2:2 * c2 + 2, :].rearrange("p c k -> p (c k)"), in_=pt[:], func=AF.Copy),
                 reads=[pk], writes=["A"])
        emit_cmul(P, P1r[:], P1i[:], A[:, :, 0:128], A[:, :, 128:256], twcb, twsb, True, ("P1r", "P1i", "A", "A", "tabs"))
        for b4 in range(CB * 128 // 512):
            sl = slice(b4 * 512, (b4 + 1) * 512)
            P.op("pe", lambda e: e.matmul(psxr[:], C128, fl(P1r)[:, sl], start=True, stop=False), reads=["tabs", "P1r"], writes=["psxr"])
            P.op("pe", lambda e: e.matmul(psxr[:], S128, fl(P1i)[:, sl], start=False, stop=True), reads=["tabs", "P1i"], writes=["psxr"],
                 pe_acc=True)
            P.op("pe", lambda e: e.matmul(psxi[:], C128, fl(P1i)[:, sl], start=True, stop=False), reads=["tabs", "P1i"], writes=["psxi"])
            P.op("pe", lambda e: e.matmul(psxi[:], NEGS, fl(P1r)[:, sl], start=False, stop=True), reads=["tabs", "P1r"], writes=["psxi"],
                 pe_acc=True)
            P.op("act", lambda e: e.activation(out=fl(outr)[:, sl], in_=psxr[:], func=AF.Copy), reads=["psxr"], writes=[ork])
            P.op("dve", lambda e: e.tensor_copy(out=fl(outi)[:, sl], in_=psxi[:]), reads=["psxi"], writes=[oik])

    def fft_conv_batch(ch0):
        fft_fwd(g_dram, ch0, 128, Hr, Hi, "Hr", "Hi")
        fft_fwd(a_dram, ch0, 64, P2r, P2i, "P2r", "P2i")
        emit_cmul(P, P1r[:], P1i[:], P2r[:], P2i[:], Hr[:], Hi[:], False, ("P1r", "P1i", "P2r", "P2i", "Hr"))
        P.op("dve", lambda e: e.tensor_copy(out=hi_[0:1, 0:1], in_=hi_[0:1, 0:1]), reads=["Hi"], writes=["hi"])
        for c2 in range(CB // 2):
            pt, pk = ps1[c2 % 2], "ps1_%d" % (c2 % 2)
            for h in range(2):
                ch = 2 * c2 + h
                P.op("pe", lambda e: e.matmul(pt[:, h * 256:(h + 1) * 256], P1r[:, ch, :], TI1a, start=True, stop=False),
                     reads=["P1r", "tabs"], writes=[pk])
                P.op("pe", lambda e: e.matmul(pt[:, h * 256:(h + 1) * 256], P1i[:, ch, :], TI1b, start=False, stop=True),
                     reads=["P1i", "tabs"], writes=[pk], pe_acc=True)
            P.op("act", lambda e: e.activation(out=A[:, 2 * c2:2 * c2 + 2, :].rearrange("p c k -> p (c k)"), in_=pt[:], func=AF.Copy),
                 reads=[pk], writes=["A"])
        emit_cmul(P, P2r[:], P2i[:], A[:, :, 0:128], A[:, :, 128:256], twcb, twsb, False, ("P2r", "P2i", "A", "A", "tabs"))
        for b4 in range(CB * 128 // 512):
            sl = slice(b4 * 512, (b4 + 1) * 512)
            pt, pk = ps3[b4 % 2], "ps3_%d" % (b4 % 2)
            P.op("pe", lambda e: e.matmul(pt[:], CN, fl(P2r)[:, sl], start=True, stop=False), reads=["tabs", "P2r"], writes=[pk])
            P.op("pe", lambda e: e.matmul(pt[:], NSN, fl(P2i)[:, sl], start=False, stop=True), reads=["tabs", "P2i"], writes=[pk], pe_acc=True)
            P.op("act", lambda e: e.activation(out=Yo[:].rearrange("p c k -> p (c k)")[:, sl], in_=pt[:], func=AF.Copy),
                 reads=[pk], writes=["Yo"])
        P.dma("sp", c_dram[ch0:ch0 + CB, :].rearrange("c (m1 m2) -> m1 c m2", m2=128), Yo[:], reads=["Yo"], writes=["c_dram"])

    SEGW = 2048
    R = P.sb("R", [128, SEGW + 2]); ACC = P.sb("ACC", [128, SEGW]); GT = P.sb("GT", [128, SEGW]); CT = P.sb("CT", [128, SEGW])
    a_ctx = P.sb("a_ctx", [128, CTX]); cc1 = P.sb("cc1", [128, CTX]); cc2 = P.sb("cc2", [128, CTX])
    fwc = P.sb("fwc", [128, CTX]); bwc = P.sb("bwc", [128, CTX])
    gblk = P.sb("gblk", [128, 512]); grev = P.sb("grev", [128, 512]); dblk = P.sb("dblk", [128, 512]); zcol = P.sb("zcol", [128, 1])
    P.op("dve", lambda e: e.memset(zcol[:], 0.0), writes=["zcol"])
    segs = [(0, CTX)] + [(CTX + i * SEGW, SEGW) for i in range(SEQ // SEGW)]

    for tl in range(2):
        rows = slice(tl * 128, (tl + 1) * 128)
        for part in range(3):
            for (c0, w_) in segs:
                first = c0 in (0, CTX); last = (c0 + w_) in (CTX, T)
                if first or last:
                    P.op("dve", lambda e: e.memset(R[:, 0:w_ + 2], 0.0), writes=["R"])
                lo = c0 - (0 if first else 1); hi = c0 + w_ + (0 if last else 1)
                P.dma("sp", R[:, (1 if first else 0):(1 if first else 0) + hi - lo], zh_in[part, rows, lo:hi], writes=["R"])
                P.op("dve", lambda e: e.tensor_scalar(out=ACC[:, :w_], in0=R[:, 1:w_ + 1], scalar1=cw[:, tl, part, 1:2], scalar2=None,
                                                      op0=ALU.mult), reads=["R", "cw"], writes=["ACC"])
                P.op("dve", lambda e: e.scalar_tensor_tensor(out=ACC[:, :w_], in0=R[:, 0:w_], scalar=cw[:, tl, part, 0:1], in1=ACC[:, :w_],
                                                             op0=ALU.mult, op1=ALU.add), reads=["R", "cw", "ACC"], writes=["ACC"])
                P.op("dve", lambda e: e.scalar_tensor_tensor(out=ACC[:, :w_], in0=R[:, 2:w_ + 2], scalar=cw[:, tl, part, 2:3], in1=ACC[:, :w_],
                                                             op0=ALU.mult, op1=ALU.add), reads=["R", "cw", "ACC"], writes=["ACC"])
                if part == 0:
                    if c0 == 0:
                        P.op("act", lambda e: e.activation(out=a_ctx[:], in_=ACC[:, :CTX], func=AF.Copy), reads=["ACC"], writes=["a_ctx"])
                    else:
                        P.dma("act", a_dram[:, c0 - CTX:c0 - CTX + w_], ACC[:, :w_], reads=["ACC"], writes=["a_dram"])
                else:
                    dst = x1c if part == 1 else x2c
                    P.dma("act", dst[:, c0:c0 + w_], ACC[:, :w_], reads=["ACC"], writes=[dst.tensor.name])
        for o in range(2):
            for d in range(2):
                for b0 in range(0, SEQ, 512):
                    P.op("pe", lambda e: e.matmul(psm[:], w4[:, o, d, rows], h3l[:, b0:b0 + 512], start=True, stop=True),
                         reads=["w4", "h3"], writes=["psm"])
                    P.dma("sp", dblk[:], dl_in[rows, b0:b0 + 512], writes=["dblk"])
                    P.op("dve", lambda e: e.tensor_tensor(out=gblk[:], in0=psm[:], in1=dblk[:], op=ALU.mult), reads=["psm", "dblk"], writes=["gblk"])
                    if d == 0:
                        P.dma("act", g_dram[:, b0:b0 + 512], gblk[:], reads=["gblk"], writes=["g_dram"])
                    else:
                        P.op("pool", lambda e: e.tensor_copy(out=grev[:], in_=gblk[:, ::-1]), reads=["gblk"], writes=["grev"])
                        if b0 == 0:
                            P.dma("act", g_dram[:, NFFT - 511:NFFT], grev[:, 0:511], reads=["grev"], writes=["g_dram"])
                        else:
                            P.dma("act", g_dram[:, NFFT - b0 - 511:NFFT - b0 + 1], grev[:], reads=["grev"], writes=["g_dram"])
                P.op("pe", lambda e: e.matmul(psm[:, :CTX], w4[:, o, d, rows], h3c[:], start=True, stop=True), reads=["w4", "h3"], writes=["psm"])
                P.dma("sp", dblk[:, :CTX], dc_in[rows, :], writes=["dblk"])
                fc = fwc if d == 0 else bwc
                P.op("dve", lambda e: e.tensor_tensor(out=fc[:], in0=psm[:, :CTX], in1=dblk[:, :CTX], op=ALU.mult),
                     reads=["psm", "dblk"], writes=["fwc" if d == 0 else "bwc"])
            P.dma("act", g_dram[:, SEQ:SEQ + 1], zcol[:], reads=["zcol"], writes=["g_dram"], allow_slow_non_contiguous=True)
            for ch0 in range(0, 128, CB):
                fft_conv_batch(ch0)
            P.op("dve", lambda e: e.memset(cc1[:], 0.0), writes=["cc1"])
            P.op("pool", lambda e: e.memset(cc2[:], 0.0), writes=["cc2"])
            for d in range(CTX):
                P.op("dve", lambda e: e.scalar_tensor_tensor(out=cc1[:, d:CTX], in0=a_ctx[:, 0:CTX - d], scalar=fwc[:, d:d + 1],
                                                             in1=cc1[:, d:CTX], op0=ALU.mult, op1=ALU.add),
                     reads=["a_ctx", "fwc", "cc1"], writes=["cc1"])
                if d >= 1:
                    P.op("dve", lambda e: e.scalar_tensor_tensor(out=cc2[:, 0:CTX - d], in0=a_ctx[:, d:CTX], scalar=bwc[:, d:d + 1],
                                                                  in1=cc2[:, 0:CTX - d], op0=ALU.mult, op1=ALU.add),
                         reads=["a_ctx", "bwc", "cc2"], writes=["cc2"])
            P.op("dve", lambda e: e.tensor_tensor(out=cc1[:], in0=cc1[:], in1=cc2[:], op=ALU.add), reads=["cc1", "cc2"], writes=["cc1"])
            gsrc = x1c if o == 0 else x2c
            for (c0, w_) in segs:
                P.dma("sp", GT[:, :w_], gsrc[:, c0:c0 + w_], reads=[gsrc.tensor.name], writes=["GT"])
                if c0 == 0:
                    csrc = cc1[:]; asrc = a_ctx[:]; ck = "cc1"; ak = "a_ctx"
                else:
                    P.dma("sp", CT[:, :w_], c_dram[:, c0 - CTX:c0 - CTX + w_], reads=["c_dram"], writes=["CT"])
                    P.dma("sp", R[:, :w_], a_dram[:, c0 - CTX:c0 - CTX + w_], reads=["a_dram"], writes=["R"])
                    csrc = CT[:, :w_]; asrc = R[:, :w_]; ck = "CT"; ak = "R"
                P.op("dve", lambda e: e.scalar_tensor_tensor(out=ACC[:, :w_], in0=asrc, scalar=hbias[:, tl, o:o + 1], in1=csrc,
                                                             op0=ALU.mult, op1=ALU.add), reads=[ak, ck, "hbias"], writes=["ACC"])
                P.op("pool", lambda e: e.tensor_tensor(out=ACC[:, :w_], in0=ACC[:, :w_], in1=GT[:, :w_], op=ALU.mult),
                     reads=["ACC", "GT"], writes=["ACC"])
                if o == 0:
                    if c0 == 0:
                        P.op("act", lambda e: e.activation(out=a_ctx[:], in_=ACC[:, :CTX], func=AF.Copy), reads=["ACC"], writes=["a_ctx"])
                    else:
                        P.dma("act", a_dram[:, c0 - CTX:c0 - CTX + w_], ACC[:, :w_], reads=["ACC"], writes=["a_dram"])
                else:
                    P.dma("act", y_out[rows, c0:c0 + w_], ACC[:, :w_], reads=["ACC"], is_output=True)
    return P.finish()


def hyena_consts(L):
    pos = np.arange(L, dtype=np.float32)
    t = pos / np.float32(max(L - 1, 1))
    ang = np.float32(2.0 * math.pi / L) * pos
    bands = np.linspace(1e-4, 15, 16, dtype=np.float32)
    z = np.concatenate([t[:, None], np.cos(ang[:, None] * bands), -np.sin(ang[:, None] * bands)], axis=-1).astype(np.float32)
    mx = math.log(1e-2) / 0.3
    mn = math.log(1e-2) / 1.5
    deltas = np.abs(np.linspace(mn, mx, HYW, dtype=np.float32))
    dec = np.exp(-t[None, :] * deltas[:, None]).astype(np.float32)
    return np.ascontiguousarray(z.T), dec


def fft_tabs():
    p = np.arange(128, dtype=np.float64)[:, None]; j = np.arange(128, dtype=np.float64)[None, :]
    a128 = 2 * np.pi * p * j / 128.0
    aN = 2 * np.pi * p * j / NFFT
    tabs = np.zeros((128, 12, 128), np.float64)
    tabs[:, 0] = np.cos(a128); tabs[:, 1] = -np.sin(a128); tabs[:, 2] = np.sin(a128)
    tabs[:, 3] = np.cos(aN); tabs[:, 4] = np.sin(aN)
    tabs[:, 5] = np.cos(a128) / NFFT; tabs[:, 6] = -np.sin(a128) / NFFT
    tabs[:, 7] = np.cos(a128); tabs[:, 8] = np.sin(a128)
    tabs[:, 9] = -np.sin(a128); tabs[:, 10] = np.cos(a128)
    return tabs.astype(np.float32)


def run_eh(zs, hy_conv, w1, b1, w2, b2, w3, b3, w4, freq, bias):
    zl, decl = hyena_consts(SEQ); zc, decc = hyena_consts(CTX)
    tabs = fft_tabs()
    bfr = np.zeros((HYF, 2, 3), np.float32)
    bfr[:, 0, 0] = b1; bfr[:, 0, 1] = b2; bfr[:, 0, 2] = b3
    bfr[:, 1, :] = np.asarray(freq, np.float32).T
    w4r = np.asarray(w4, np.float32).reshape(HYF, 2, 2, HYW)
    in_maps = []
    for j in range(NCORES):
        ch = slice(256 * j, 256 * (j + 1))
        cwj = np.zeros((128, 2, 3, 3), np.float32)
        for part in range(3):
            blk = np.asarray(hy_conv[:, part * HYW + 256 * j: part * HYW + 256 * (j + 1)], np.float32)
            cwj[:, :, part, :] = blk.T.reshape(2, 128, 3).transpose(1, 0, 2)
        hbj = np.ascontiguousarray(np.asarray(bias[:, ch], np.float32).T.reshape(2, 128, 2).transpose(1, 0, 2))
        in_maps.append({"zh": np.ascontiguousarray(zs[j][:768].reshape(3, 256, T)), "convw": cwj, "hbias": hbj,
                        "w1": np.asarray(w1, np.float32), "w2": np.asarray(w2, np.float32), "w3": np.asarray(w3, np.float32),
                        "bfr": bfr, "w4": np.ascontiguousarray(w4r[:, :, :, ch]), "zposl": zl, "zposc": zc,
                        "decl": np.ascontiguousarray(decl[ch]), "decc": np.ascontiguousarray(decc[ch]), "tabs": tabs})
    res = run(build_eh(), in_maps)
    return [r["yh"] for r in res]


DNC = 128
NCHK = T // DNC
DK = 128


def dn_masks():
    t = np.arange(128)[:, None]; i = np.arange(128)[None, :]
    m = np.zeros((10, 128, 128), np.float32)
    m[0] = (t <= i); m[1] = (t > i)
    m[2] = np.where(t >= i, 0.0, -30000.0)
    m[3] = np.where(i >= t, 0.0, -30000.0)
    m[4] = (t > i)
    m[5] = (t >= i); m[6] = (t < i)
    m[7] = np.where(t <= i, 0.0, -30000.0)
    m[8] = np.where(i <= t, 0.0, -30000.0)
    m[9] = (t < i)
    return np.ascontiguousarray(m.transpose(1, 0, 2))


def build_ed(dbg=None):
    P = Prog()
    dbg = dbg or {}
    nc = P.nc
    qkvz_in = P.dram_in("qkvz", [4, 2, 128, T])
    braw_in = P.dram_in("braw", [4, T]); araw_in = P.dram_in("araw", [4, T])
    cw_in = P.dram_in("dconv", [128, 3, 2, 5])
    aA_in = P.dram_in("aA", [4, 2])
    nw_in = P.dram_in("normw", [128, 128])
    mk_in = P.dram_in("masks", [128, 10, 128])
    id_in = P.dram_in("ident", [128, 128])
    y_out = P.dram_out("yd", [2, 128, T])
    qd = nc.dram_tensor("qd", [2, 128, T], F32, kind="Internal").ap()
    kd = nc.dram_tensor("kd", [2, 128, T], F32, kind="Internal").ap()
    vd = nc.dram_tensor("vd", [2, 128, T], F32, kind="Internal").ap()
    dsc = {0: qd, 1: kd, 2: vd}

    def ld(name, src, shape, q="sp"):
        t = P.sb(name, shape)
        P.dma(q, t[:], src, writes=[name])
        return t
    cw = ld("cw", cw_in, [128, 3, 2, 5]); aA = ld("aA", aA_in, [4, 2]); nw = ld("nw", nw_in, [128, 128])
    mk = ld("mk", mk_in, [128, 10, 128]); ident = ld("ident", id_in, [128, 128])
    ones128 = P.sb("ones128", [128, 128])
    P.op("dve", lambda e: e.memset(ones128[:], 1.0), writes=["ones128"])
    PS = [P.ps("PS%d" % i, [128, 512]) for i in range(8)]
    pctr = [0]

    def pslot():
        n = pctr[0]; pctr[0] += 1
        b, s = n % 8, (n // 8) % 4
        return PS[b][:, s * 128:(s + 1) * 128], ("PS", b, s)

    SG = 22 * DNC
    bg = P.sb("bg", [4, 2, SG])
    tA = P.sb("tA", [4, SG]); tB = P.sb("tB", [4, SG])
    nA = P.sb("nA", [4, 1])
    P.op("act", lambda e: e.activation(out=nA[:], in_=aA[:, 0:1], func=AF.Exp), reads=["aA"], writes=["nA"])
    P.op("dve", lambda e: e.tensor_scalar(out=nA[:], in0=nA[:], scalar1=-1.0, scalar2=None, op0=ALU.mult), reads=["nA"], writes=["nA"])
    BGT = P.sb("BGT", [128, NCHK, 2, 4])
    for sgi in range(3):
        s0 = sgi * SG
        P.dma("sp", tA[:], braw_in[:, s0:s0 + SG], writes=["tA"])
        P.op("act", lambda e: e.activation(out=bg[:, 0, :], in_=tA[:], func=AF.Sigmoid), reads=["tA"], writes=["bg"])
        P.dma("sp", tB[:], araw_in[:, s0:s0 + SG], writes=["tB"])
        P.op("dve", lambda e: e.tensor_scalar(out=tB[:], in0=tB[:], scalar1=aA[:, 1:2], scalar2=None, op0=ALU.add), reads=["tB", "aA"], writes=["tB"])
        P.op("act", lambda e: e.activation(out=tA[:], in_=tB[:], func=AF.Abs), reads=["tB", "bg"], writes=["tA"])
        P.op("act", lambda e: e.activation(out=tA[:], in_=tA[:], func=AF.Exp, scale=-1.0), reads=["tA"], writes=["tA"])
        P.op("dve", lambda e: e.tensor_scalar_add(out=tA[:], in0=tA[:], scalar1=1.0), reads=["tA"], writes=["tA"])
        P.op("act", lambda e: e.activation(out=tA[:], in_=tA[:], func=AF.Ln), reads=["tA"], writes=["tA"])
        P.op("dve", lambda e: e.tensor_scalar_max(out=tB[:], in0=tB[:], scalar1=0.0), reads=["tB"], writes=["tB"])
        P.op("dve", lambda e: e.tensor_tensor(out=tB[:], in0=tB[:], in1=tA[:], op=ALU.add), reads=["tA", "tB"], writes=["tB"])
        P.op("dve", lambda e: e.tensor_scalar(out=bg[:, 1, :], in0=tB[:], scalar1=nA[:, 0:1], scalar2=None, op0=ALU.mult),
             reads=["tB", "nA"], writes=["bg"])
        for cl in range(22):
            c = sgi * 22 + cl
            for w_ in range(2):
                pt, pk = pslot()
                P.op("pe", lambda e: e.matmul(pt[:, 0:4], bg[:, w_, cl * DNC:(cl + 1) * DNC], ident[0:4, 0:4], start=True, stop=True), reads=["bg", "ident"], writes=[pk])
                P.op("dve", lambda e: e.tensor_copy(out=BGT[:, c, w_, :], in_=pt[:, 0:4]), reads=[pk], writes=["BGT"])
    NBG = P.sb("NBG", [128, NCHK, 4])
    P.op("dve", lambda e: e.tensor_scalar(out=NBG[:], in0=BGT[:, :, 0, :], scalar1=-1.0, scalar2=None, op0=ALU.mult), reads=["BGT"], writes=["NBG"])

    SEGW = 2048
    R = P.sb("R", [128, SEGW + 4]); ACC = P.sb("ACC", [128, SEGW]); SQ = P.sb("SQ", [128, SEGW]); RS = P.sb("RS", [128, 512])
    segs = [(0, CTX)] + [(CTX + i * SEGW, SEGW) for i in range(SEQ // SEGW)]
    for hd in range(2):
        for part in range(3):
            for (c0, w_) in segs:
                first = c0 in (0, CTX); last = (c0 + w_) in (CTX, T)
                if first or last:
                    P.op("pool", lambda e: e.memset(R[:, 0:w_ + 4], 0.0), writes=["R"])
                lo = c0 - (0 if first else 2); hi = c0 + w_ + (0 if last else 2)
                off = 2 if first else 0
                P.dma("sp", R[:, off:off + hi - lo], qkvz_in[part, hd, :, lo:hi], writes=["R"])
                P.op("dve", lambda e: e.tensor_scalar(out=ACC[:, :w_], in0=R[:, 0:w_], scalar1=cw[:, part, hd, 0:1], scalar2=None, op0=ALU.mult),
                     reads=["R", "cw"], writes=["ACC"])
                for k in range(1, 5):
                    P.op("dve", lambda e: e.scalar_tensor_tensor(out=ACC[:, :w_], in0=R[:, k:k + w_], scalar=cw[:, part, hd, k:k + 1],
                                                                 in1=ACC[:, :w_], op0=ALU.mult, op1=ALU.add),
                         reads=["R", "cw", "ACC"], writes=["ACC"])
                P.op("act", lambda e: e.activation(out=ACC[:, :w_], in_=ACC[:, :w_], func=AF.Silu), reads=["ACC"], writes=["ACC"])
                if part < 2:
                    P.op("pool", lambda e: e.tensor_tensor(out=SQ[:, :w_], in0=ACC[:, :w_], in1=ACC[:, :w_], op=ALU.mult), reads=["ACC"], writes=["SQ"])
                    for b0 in range(0, w_, 512):
                        bw = min(512, w_ - b0)
                        pb = PS[pctr[0] % 8]; pbk = ("PS", pctr[0] % 8, 0); pctr[0] += 1
                        allk = [("PS", pbk[1], s_) for s_ in range(4)]
                        P.op("pe", lambda e: e.matmul(pb[:, :bw], ones128[:], SQ[:, b0:b0 + bw], start=True, stop=True),
                             reads=["SQ", "ones128"], writes=allk)
                        P.op("dve", lambda e: e.tensor_scalar_add(out=RS[:, :bw], in0=pb[:, :bw], scalar1=1e-6),
                             reads=allk, writes=["RS"])
                        P.op("act", lambda e: e.activation(out=RS[:, :bw], in_=RS[:, :bw], func=AF.Sqrt), reads=["RS"], writes=["RS"])
                        P.op("dve", lambda e: e.reciprocal(out=RS[:, :bw], in_=RS[:, :bw]), reads=["RS"], writes=["RS"])
                        if part == 0:
                            P.op("dve", lambda e: e.scalar_tensor_tensor(out=ACC[:, b0:b0 + bw], in0=ACC[:, b0:b0 + bw], scalar=DK ** -0.5,
                                                                         in1=RS[:, :bw], op0=ALU.mult, op1=ALU.mult),
                                 reads=["ACC", "RS"], writes=["ACC"])
                        else:
                            P.op("dve", lambda e: e.tensor_tensor(out=ACC[:, b0:b0 + bw], in0=ACC[:, b0:b0 + bw], in1=RS[:, :bw], op=ALU.mult),
                                 reads=["ACC", "RS"], writes=["ACC"])
                P.dma("act", dsc[part][hd, :, c0:c0 + w_], ACC[:, :w_], reads=["ACC"], writes=[dsc[part].tensor.name])

    if dbg.get('stop') == 'pre':
        P.dma('sp', y_out[0, :, 0:128], ident[:], reads=['ident'], is_output=True)
        return P.finish()
    Oh = [P.sb("O%d" % h, [128, NCHK, 128]) for h in range(2)]
    for h in range(2):
        P.op("pool", lambda e: e.memset(Oh[h][:], 0.0), writes=[("O", h, c) for c in range(NCHK)])

    def mm(lhsT, lk, rhs, rk):
        pt, pk = pslot()
        n = rhs.shape[-1]
        m = lhsT.shape[-1]
        pt = pt[0:m, 0:n]
        P.op("pe", lambda e: e.matmul(pt, lhsT, rhs, start=True, stop=True), reads=lk + rk, writes=[pk])
        return pt, pk

    def dvop(fn, reads, writes, eng="dve"):
        P.op(eng, fn, reads=reads, writes=writes)

    class St:
        pass
    streams = []
    for hd in range(2):
        for dr in range(2):
            st = St()
            sid = "s%d%d_" % (hd, dr)
            st.sid = sid; st.hd = hd; st.dr = dr
            def T_(name, shape=(128, 128), sid=sid):
                return P.sb(sid + name, list(shape))
            st.qT = [T_("qT%d" % i) for i in range(2)]; st.kT = [T_("kT%d" % i) for i in range(2)]; st.vT = [T_("vT%d" % i) for i in range(2)]
            st.ktm = T_("ktm"); st.bv = T_("bv"); st.kbg = T_("kbg"); st.kdec = T_("kdec"); st.gMC = T_("gMC")
            st.dec = T_("dec"); st.decT = T_("decT")
            st.Xs = [T_("X%d" % i) for i in range(2)]; st.Ys = [T_("Y%d" % i) for i in range(2)]
            st.Pm = [T_("Pm%d" % i) for i in range(2)]; st.PTm = [T_("PTm%d" % i) for i in range(2)]
            st.usb = T_("usb"); st.wTs = T_("wTs"); st.vnew = T_("vnew"); st.o1 = T_("o1"); st.qkm = T_("qkm")
            st.cols = T_("cols", (128, 8)); st.S = T_("S")
            st.order = list(range(NCHK)) if dr == 0 else [1, 0] + list(range(NCHK - 1, 1, -1))
            if 'nchunks' in dbg:
                st.order = st.order[:dbg['nchunks']]
            P.op("pool", lambda e: e.memset(st.S[:], 0.0), writes=[sid + "S"])
            streams.append(st)

    def chunk_gen(st, c, b):
        sid = st.sid; hd = st.hd; dr = st.dr
        def K_(n):
            return sid + n
        row = dr * 2 + hd
        mo = 5 * dr
        MC = mk[:, mo + 0, :]; MS = mk[:, mo + 1, :]; NEG = mk[:, mo + 2, :]; STRICT = mk[:, mo + 4, :]
        qT, kT, vT = st.qT[b], st.kT[b], st.vT[b]
        cols = st.cols; S = st.S
        t0 = c * DNC
        P.dma("sp", qT[:], qd[hd, :, t0:t0 + DNC], reads=["qd"], writes=[K_("qT%d" % b)])
        P.dma("sp", kT[:], kd[hd, :, t0:t0 + DNC], reads=["kd"], writes=[K_("kT%d" % b)])
        P.dma("sp", vT[:], vd[hd, :, t0:t0 + DNC], reads=["vd"], writes=[K_("vT%d" % b)])
        qk_, kk_, vk_ = [K_("qT%d" % b)], [K_("kT%d" % b)], [K_("vT%d" % b)]
        beta = BGT[:, c, 0, row:row + 1]; g = BGT[:, c, 1, row:row + 1]; nbeta = NBG[:, c, row:row + 1]
        yield
        pt, pk = mm(kT[:], kk_, ident[:], ["ident"])
        dvop(lambda e: e.activation(out=st.ktm[:], in_=pt, func=AF.Copy), [pk], [K_("ktm")], "act")
        pt, pk = mm(vT[:], vk_, ident[:], ["ident"])
        dvop(lambda e: e.tensor_scalar(out=st.bv[:], in0=pt, scalar1=beta, scalar2=None, op0=ALU.mult), [pk, "BGT"], [K_("bv")])
        yield
        pt, pk = mm(MC, ["mk"], g, ["BGT"])
        dvop(lambda e: e.tensor_copy(out=cols[:, 0:1], in_=pt[:, 0:1]), [pk], [K_("cols")])
        pt, pk = mm(ones128[:], ["ones128"], g, ["BGT"])
        dvop(lambda e: e.tensor_copy(out=cols[:, 3:4], in_=pt[:, 0:1]), [pk], [K_("cols")])
        dvop(lambda e: e.tensor_scalar(out=st.gMC[:], in0=MC, scalar1=g, scalar2=None, op0=ALU.mult), ["mk", "BGT"], [K_("gMC")])
        yield
        dvop(lambda e: e.activation(out=cols[:, 1:2], in_=cols[:, 0:1], func=AF.Exp), [K_("cols")], [K_("cols")], "act")
        dvop(lambda e: e.activation(out=cols[:, 4:5], in_=cols[:, 3:4], func=AF.Exp), [K_("cols")], [K_("cols")], "act")
        dvop(lambda e: e.activation(out=cols[:, 5:6], in_=cols[:, 0:1], func=AF.Exp, scale=-1.0, bias=cols[:, 3:4]), [K_("cols")], [K_("cols")], "act")
        pt, pk = mm(st.gMC[:], [K_("gMC")], MS, ["mk"])
        dvop(lambda e: e.tensor_tensor(out=st.dec[:], in0=pt, in1=NEG, op=ALU.add), [pk, "mk"], [K_("dec")])
        yield
        dvop(lambda e: e.tensor_tensor(out=cols[:, 2:3], in0=cols[:, 1:2], in1=beta, op=ALU.mult), [K_("cols"), "BGT"], [K_("cols")])
        dvop(lambda e: e.activation(out=st.dec[:], in_=st.dec[:], func=AF.Exp), [K_("dec")], [K_("dec")], "act")
        dvop(lambda e: e.tensor_scalar(out=st.kbg[:], in0=st.ktm[:], scalar1=cols[:, 2:3], scalar2=None, op0=ALU.mult), [K_("ktm"), K_("cols")], [K_("kbg")])
        dvop(lambda e: e.tensor_scalar(out=st.kdec[:], in0=st.ktm[:], scalar1=cols[:, 5:6], scalar2=0.0, op0=ALU.mult, op1=ALU.add),
             [K_("ktm"), K_("cols")], [K_("kdec")], "pool")
        yield
        pt, pk = mm(st.dec[:], [K_("dec")], ident[:], ["ident"])
        dvop(lambda e: e.tensor_copy(out=st.decT[:], in_=pt), [pk], [K_("decT")])
        X, Y = st.Xs[0], st.Ys[0]
        pt, pk = mm(kT[:], kk_, kT[:], kk_)
        dvop(lambda e: e.tensor_tensor(out=X[:], in0=pt, in1=st.dec[:], op=ALU.mult), [pk, K_("dec")], [K_("X0")])
        dvop(lambda e: e.scalar_tensor_tensor(out=X[:], in0=X[:], scalar=nbeta, in1=STRICT, op0=ALU.mult, op1=ALU.mult),
             [K_("X0"), "NBG", "mk"], [K_("X0")])
        yield
        pt, pk = mm(X[:], [K_("X0")], ident[:], ["ident"])
        dvop(lambda e: e.activation(out=Y[:], in_=pt, func=AF.Copy), [pk], [K_("Y0")], "act")
        dvop(lambda e: e.tensor_tensor(out=st.Pm[0][:], in0=X[:], in1=ident[:], op=ALU.add), [K_("X0"), "ident"], [K_("Pm0")])
        yield
        dvop(lambda e: e.tensor_tensor(out=st.PTm[0][:], in0=Y[:], in1=ident[:], op=ALU.add), [K_("Y0"), "ident"], [K_("PTm0")], "pool")
        cur = 0
        for lv in range(6):
            nx = 1 - cur
            ptx, pkx = mm(st.Ys[cur][:], [K_("Y%d" % cur)], st.Xs[cur][:], [K_("X%d" % cur)])
            pty, pky = mm(st.Xs[cur][:], [K_("X%d" % cur)], st.Ys[cur][:], [K_("Y%d" % cur)])
            dvop(lambda e: e.tensor_copy(out=st.Xs[nx][:], in_=ptx), [pkx], [K_("X%d" % nx)])
            dvop(lambda e: e.activation(out=st.Ys[nx][:], in_=pty, func=AF.Copy), [pky], [K_("Y%d" % nx)], "act")
            yield
            if lv < 5:
                ptp, pkp = mm(st.PTm[cur][:], [K_("PTm%d" % cur)], st.Xs[nx][:], [K_("X%d" % nx)])
                dvop(lambda e: e.tensor_tensor(out=st.Pm[nx][:], in0=ptp, in1=st.Pm[cur][:], op=ALU.add), [pkp, K_("Pm%d" % cur)], [K_("Pm%d" % nx)])
            ptq, pkq = mm(st.Pm[cur][:], [K_("Pm%d" % cur)], st.Ys[nx][:], [K_("Y%d" % nx)])
            dvop(lambda e: e.tensor_tensor(out=st.PTm[nx][:], in0=ptq, in1=st.PTm[cur][:], op=ALU.add), [pkq, K_("PTm%d" % cur)], [K_("PTm%d" % nx)])
            cur = nx
            yield
        TinvT = st.PTm[cur]; tk = [K_("PTm%d" % cur)]
        pt, pk = mm(TinvT[:], tk, st.bv[:], [K_("bv")])
        dvop(lambda e: e.tensor_copy(out=st.usb[:], in_=pt), [pk], [K_("usb")])
        pt, pk = mm(st.kbg[:], [K_("kbg")], TinvT[:], tk)
        dvop(lambda e: e.activation(out=st.wTs[:], in_=pt, func=AF.Copy), [pk], [K_("wTs")], "act")
        yield
        pt, pk = mm(st.wTs[:], [K_("wTs")], S[:], [K_("S")])
        dvop(lambda e: e.tensor_tensor(out=st.vnew[:], in0=st.usb[:], in1=pt, op=ALU.subtract), [K_("usb"), pk], [K_("vnew")])
        pt, pk = mm(qT[:], qk_, S[:], [K_("S")])
        dvop(lambda e: e.activation(out=st.o1[:], in_=pt, func=AF.Copy, scale=cols[:, 1:2]), [pk, K_("cols")], [K_("o1")], "act")
        pt, pk = mm(kT[:], kk_, qT[:], qk_)
        dvop(lambda e: e.tensor_tensor(out=st.qkm[:], in0=pt, in1=st.decT[:], op=ALU.mult), [pk, K_("decT")], [K_("qkm")])
        yield
        pt, pk = mm(st.qkm[:], [K_("qkm")], st.vnew[:], [K_("vnew")])
        dvop(lambda e: e.tensor_tensor(out=st.o1[:], in0=pt, in1=st.o1[:], op=ALU.add), [pk, K_("o1")], [K_("o1")])
        dvop(lambda e: e.tensor_tensor(out=Oh[hd][:, c, :], in0=Oh[hd][:, c, :], in1=st.o1[:], op=ALU.add), [("O", hd, c), K_("o1")], [("O", hd, c)], "pool")
        pt, pk = mm(st.kdec[:], [K_("kdec")], st.vnew[:], [K_("vnew")])
        dvop(lambda e: e.scalar_tensor_tensor(out=S[:], in0=S[:], scalar=cols[:, 4:5], in1=pt, op0=ALU.mult, op1=ALU.add),
             [K_("S"), K_("cols"), pk], [K_("S")])
        yield

    nsteps = len(streams[0].order)
    for k in range(nsteps):
        gens = [chunk_gen(st, st.order[k], k % 2) for st in streams]
        alive = list(gens)
        while alive:
            nxt = []
            for gen in alive:
                try:
                    next(gen)
                    nxt.append(gen)
                except StopIteration:
                    pass
            alive = nxt

    if dbg.get('stop') == 'scan':
        P.dma('sp', y_out[0, :, 0:128], ident[:], reads=['ident'], is_output=True)
        return P.finish()
    zt = [P.sb("zt%d" % i, [128, 128]) for i in range(3)]
    ob_ = [P.sb("ob%d" % i, [128, 128]) for i in range(3)]
    sqt = P.sb("sqt", [128, 128]); nrm = P.sb("nrm", [128, 128]); gz = P.sb("gz", [128, 128]); ncol = P.sb("ncol", [128, 2])
    for hd in range(2):
        O = Oh[hd]
        for c in range(NCHK):
            b = c % 3
            t0 = c * DNC
            P.dma("sp", zt[b][:], qkvz_in[3, hd, :, t0:t0 + DNC], writes=["zt%d" % b])
            dvop(lambda e: e.tensor_tensor(out=sqt[:], in0=O[:, c, :], in1=O[:, c, :], op=ALU.mult), [("O", hd, c)], ["sqt"], "pool")
            dvop(lambda e: e.reduce_sum(out=ncol[:, 0:1], in_=sqt[:], axis=AX.X), ["sqt"], ["ncol"])
            dvop(lambda e: e.tensor_scalar(out=ncol[:, 1:2], in0=ncol[:, 0:1], scalar1=1.0 / 128, scalar2=1e-6, op0=ALU.mult, op1=ALU.add),
                 ["ncol"], ["ncol"])
            dvop(lambda e: e.activation(out=ncol[:, 1:2], in_=ncol[:, 1:2], func=AF.Sqrt), ["ncol"], ["ncol"], "act")
            dvop(lambda e: e.reciprocal(out=ncol[:, 1:2], in_=ncol[:, 1:2]), ["ncol"], ["ncol"])
            dvop(lambda e: e.scalar_tensor_tensor(out=nrm[:], in0=O[:, c, :], scalar=ncol[:, 1:2], in1=nw[:], op0=ALU.mult, op1=ALU.mult),
                 [("O", hd, c), "ncol", "nw"], ["nrm"])
            pt, pk = mm(zt[b][:], ["zt%d" % b], ident[:], ["ident"])
            dvop(lambda e: e.activation(out=gz[:], in_=pt, func=AF.Silu), [pk], ["gz"], "act")
            dvop(lambda e: e.tensor_tensor(out=nrm[:], in0=nrm[:], in1=gz[:], op=ALU.mult), ["nrm", "gz"], ["nrm"])
            pt, pk = mm(nrm[:], ["nrm"], ident[:], ["ident"])
            dvop(lambda e: e.tensor_copy(out=ob_[b][:], in_=pt), [pk], ["ob%d" % b])
            P.dma("act", y_out[hd, :, t0:t0 + DNC], ob_[b][:], reads=["ob%d" % b], is_output=True)
    return P.finish()


def run_ed(zs, dn_conv, a_log, dt_bias, norm_w, dbg=None):
    masks = dn_masks()
    ident = np.eye(128, dtype=np.float32)
    nwr = np.ascontiguousarray(np.broadcast_to(np.asarray(norm_w, np.float32)[None, :], (128, 128)))
    in_maps = []
    for j in range(NCORES):
        zz = zs[j]
        qkvz = np.ascontiguousarray(zz[768:1792].reshape(4, 2, 128, T))
        ab = zz[1792:1800]
        braw = np.ascontiguousarray(ab[0:4]); araw = np.ascontiguousarray(ab[4:8])
        cwj = np.zeros((128, 3, 2, 5), np.float32)
        for part in range(3):
            blk = np.asarray(dn_conv[:, part * DNW + 256 * j: part * DNW + 256 * (j + 1)], np.float32)
            cwj[:, part, :, :] = blk.T.reshape(2, 128, 5).transpose(1, 0, 2)
        aA = np.zeros((4, 2), np.float32)
        for dr in range(2):
            for hd in range(2):
                aA[dr * 2 + hd, 0] = a_log[dr, 2 * j + hd]
                aA[dr * 2 + hd, 1] = dt_bias[dr, 2 * j + hd]
        in_maps.append({"qkvz": qkvz, "braw": braw, "araw": araw, "dconv": cwj, "aA": aA, "normw": nwr, "masks": masks, "ident": ident})
    res = run(build_ed(dbg), in_maps)
    return [r["yd"].reshape(256, T) for r in res]


GRID_W = 64


def lat_to_col_major(aT):
    out = aT.copy()
    lat = aT[:, CTX:]
    out[:, CTX:] = lat.reshape(lat.shape[0], SEQ // GRID_W, GRID_W).transpose(0, 2, 1).reshape(lat.shape[0], SEQ)
    return out


def lat_from_col_major(aT):
    out = aT.copy()
    lat = aT[:, CTX:]
    out[:, CTX:] = lat.reshape(lat.shape[0], GRID_W, SEQ // GRID_W).transpose(0, 2, 1).reshape(lat.shape[0], SEQ)
    return out


def kernel(x, c, ctx, c_ctx, ada_w, ada_b, ln_g, ln_b, ev_w_in, ev_w_out, hy_conv, hy_w1, hy_b1, hy_w2, hy_b2, hy_w3, hy_b3,
           hy_w4, hy_freq, hy_bias, dn_conv, dn_a_log, dn_dt_bias, dn_norm_w, s5_lam_re, s5_lam_im, s5_log_dt, s5_b_re, s5_b_im,
           s5_c_re, s5_c_im, s5_d, od_w_glu, moe_router, moe_w_in, moe_w_out):
    f32 = np.float32
    modT = run_k0({"c": c, "c_ctx": c_ctx, "ada_w": ada_w, "ada_b": ada_b})
    hT = np.ascontiguousarray(np.concatenate([np.asarray(ctx, f32)[0], np.asarray(x, f32)[0]], axis=0).T)
    for l in range(DEPTH):
        col = (l // 2) % 2 == 1
        i = l // 2
        mod_l = np.ascontiguousarray(modT[l])
        if l % 2 == 0:
            uT = run_mod(hT, mod_l)
            if col:
                uT = lat_to_col_major(uT)
            zs = run_ea1(uT, np.asarray(ev_w_in[i], f32))
            yh = run_eh(zs, hy_conv[i], hy_w1[i], hy_b1[i], hy_w2[i], hy_b2[i], hy_w3[i], hy_b3[i], hy_w4[i], hy_freq[i], hy_bias[i])
            yd = run_ed(zs, dn_conv[i], dn_a_log[i], dn_dt_bias[i], dn_norm_w[i])
            fT = np.concatenate(yh + yd, axis=0)
            if col:
                fT = lat_from_col_major(fT)
            h1T, h2T, affT = run_x3(False, fT, hT, mod_l, ev_w_out[i], ln_g[l, 0], ln_b[l, 0], moe_router[l])
        else:
            hp = lat_to_col_major(hT) if col else hT
            fT = run_o2(hp, mod_l, s5_lam_re[i], s5_lam_im[i], s5_log_dt[i], s5_b_re[i], s5_b_im[i], s5_c_re[i], s5_c_im[i], s5_d[i])
            if col:
                fT = lat_from_col_major(fT)
            h1T, h2T, affT = run_x3(True, fT, hT, mod_l, od_w_glu[i], ln_g[l, 0], ln_b[l, 0], moe_router[l])
        hT = run_x4(h1T, h2T, affT, mod_l, ln_g[l, 1], ln_b[l, 1], moe_w_in[l], moe_w_out[l])
    return np.ascontiguousarray(hT[:, CTX:].T)[None].astype(np.float32)
```

```python
import contextlib
import math
import numpy as np
import concourse.bass as bass
import concourse.mybir as mybir
from concourse.bass_utils import run_bass_kernel_spmd

F32 = mybir.dt.float32
BF16 = mybir.dt.bfloat16
AF = mybir.ActivationFunctionType
ALU = mybir.AluOpType
AX = mybir.AxisListType

NCORES = 8
D = 4096
SEQ = 8192
CTX = 256
T = SEQ + CTX
DEPTH = 4
KC = D // 128


class _Stop(Exception):
    pass


class Prog:
    NDSEM = 6

    def __init__(self):
        self.nc = bass.Bass("TRN2", target_bir_lowering=False)
        self.st = contextlib.ExitStack()
        nc = self.nc
        self.eng = {"pe": nc.tensor, "act": nc.scalar, "dve": nc.vector, "pool": nc.gpsimd, "sp": nc.sync}
        self.sem = {}
        self.cnt = {}
        for e in ("pe", "act", "dve", "pool"):
            self.sem[e] = self.st.enter_context(nc.semaphore("s_" + e))
            self.cnt[e] = 0
        self.dsem = {}
        self.dcnt = {}
        for q in ("sp", "pool", "act"):
            self.dsem[q] = [self.st.enter_context(nc.semaphore("d_%s%d" % (q, i))) for i in range(self.NDSEM)]
            self.dcnt[q] = 0
        self.waited = {}
        self.last_w = {}
        self.readers = {}
        self.out_events = []
        self.n_ins = 0

    def sb(self, name, shape, dt=F32):
        return self.st.enter_context(self.nc.sbuf_tensor("sb_" + name, list(shape), dt))

    def ps(self, name, shape, dt=F32):
        return self.st.enter_context(self.nc.psum_tensor("pp_" + name, list(shape), dt))

    def dram_in(self, name, shape, dt=F32):
        return self.nc.dram_tensor(name, list(shape), dt, kind="ExternalInput").ap()

    def dram_out(self, name, shape, dt=F32):
        return self.nc.dram_tensor(name, list(shape), dt, kind="ExternalOutput").ap()

    def _wait(self, e, ev):
        if ev is None:
            return
        sem, val = ev
        k = (e, sem.name)
        if self.waited.get(k, 0) >= val:
            return
        self.waited[k] = val
        self.eng[e].wait_ge(sem, val)

    def _deps(self, e, reads, writes, pe_acc=False):
        for k in reads:
            self._wait(e, self.last_w.get(k))
        for k in writes:
            lw = self.last_w.get(k)
            if not (pe_acc and lw is not None and lw[0] is self.sem["pe"]):
                self._wait(e, lw)
            for ev in self.readers.get(k, ()):
                self._wait(e, ev)

    def _record(self, ev, reads, writes):
        for k in reads:
            self.readers.setdefault(k, []).append(ev)
            if len(self.readers[k]) > 24:
                self.readers[k] = self.readers[k][-24:]
        for k in writes:
            self.last_w[k] = ev
            self.readers[k] = []

    def op(self, e, ins_fn, reads=(), writes=(), pe_acc=False):
        self._deps(e, reads, writes, pe_acc)
        ins = ins_fn(self.eng[e])
        self.cnt[e] += 1
        ins.then_inc(self.sem[e], 1)
        ev = (self.sem[e], self.cnt[e])
        self._record(ev, reads, writes)
        self.n_ins += 1
        return ev

    def dma(self, q, out, in_, reads=(), writes=(), is_output=False, **kw):
        n = self.dcnt[q]
        sem = self.dsem[q][n % self.NDSEM]
        prev = 16 * (n // self.NDSEM)
        if prev > 0:
            self._wait(q, (sem, prev))
        self._deps(q, reads, writes)
        ins = self.eng[q].dma_start(out=out, in_=in_, **kw)
        ins.then_inc(sem, 16)
        self.dcnt[q] = n + 1
        ev = (sem, prev + 16)
        self._record(ev, reads, writes)
        if is_output:
            self.out_events.append(ev)
        self.n_ins += 1
        return ev

    def finish(self):
        for ev in self.out_events:
            self._wait("sp", ev)
        for e in ("pe", "act", "dve", "pool"):
            if self.cnt[e]:
                self._wait("sp", (self.sem[e], self.cnt[e]))
        for q in ("sp", "pool", "act"):
            n = self.dcnt[q]
            for i in range(self.NDSEM):
                uses = (n - i + self.NDSEM - 1) // self.NDSEM if n > i else 0
                if uses:
                    self._wait("sp", (self.dsem[q][i], 16 * uses))
        self.st.close()
        return self.nc


def run(prog_nc, in_maps):
    res = run_bass_kernel_spmd(prog_nc, in_maps, core_ids=list(range(NCORES)))
    return res.results


NCH_ADA = 6 * D // 128
NCH_ADA_CORE = NCH_ADA // NCORES


def build_k0():
    P = Prog()
    nc = P.nc
    cols_core = NCH_ADA_CORE * 128
    c_in = P.dram_in("c2", [2, 128, KC])
    w_in = P.dram_in("ada_w", [DEPTH, D, cols_core])
    b_in = P.dram_in("ada_b", [DEPTH, 1, cols_core])
    out = P.dram_out("modT", [DEPTH, 128, NCH_ADA_CORE, 2])

    craw = P.sb("craw", [128, 2, KC])
    S = P.sb("S", [128, KC, 2])
    ones = P.sb("ones", [1, 2])
    bias = P.sb("bias", [1, DEPTH, cols_core])
    wt = [P.sb("wt%d" % i, [128, KC, 512]) for i in range(2)]
    ot = [P.sb("ot%d" % i, [128, NCH_ADA_CORE, 2]) for i in range(2)]
    pst = [P.ps("ps%d" % i, [128, 2]) for i in range(4)]

    P.dma("sp", craw[:, 0, :], c_in[0], writes=["craw"])
    P.dma("sp", craw[:, 1, :], c_in[1], writes=["craw"])
    P.dma("sp", bias[:], b_in.rearrange("l o c -> o l c"), writes=["bias"])
    P.op("dve", lambda e: e.memset(ones[:], 1.0), writes=["ones"])
    for s in range(2):
        P.op("act", lambda e: e.activation(out=S[:, :, s], in_=craw[:, s, :], func=AF.Silu),
             reads=["craw"], writes=["S"])
    wv = w_in.rearrange("l (p kc) c -> l p kc c", kc=KC)
    blk = 0
    for l in range(DEPTH):
        o = ot[l % 2]
        for cb in range(cols_core // 512):
            w = wt[blk % 2]
            wk = "wt%d" % (blk % 2)
            P.dma("sp" if blk % 2 == 0 else "act", w[:], wv[l, :, :, cb * 512:(cb + 1) * 512], writes=[wk])
            for sub in range(4):
                ch = cb * 4 + sub
                pt = pst[ch % 4]
                pk = "ps%d" % (ch % 4)
                for kc in range(KC):
                    P.op("pe", lambda e: e.matmul(pt[:], w[:, kc, sub * 128:(sub + 1) * 128], S[:, kc, :],
                                                  start=(kc == 0), stop=False),
                         reads=[wk, "S"], writes=[pk], pe_acc=True)
                c0 = ch * 128
                P.op("pe", lambda e: e.matmul(pt[:], bias[:, l, c0:c0 + 128], ones[:], start=False, stop=True),
                     reads=["bias", "ones"], writes=[pk], pe_acc=True)
                P.op("dve", lambda e: e.tensor_copy(out=o[:, ch, :], in_=pt[:]), reads=[pk], writes=["ot%d" % (l % 2)])
            blk += 1
        P.dma("sp", out[l], o[:], reads=["ot%d" % (l % 2)], is_output=True)
    return P.finish()


def run_k0(inputs):
    cols_core = NCH_ADA_CORE * 128
    c2 = np.stack([np.asarray(inputs["c"], np.float32).reshape(128, KC),
                   np.asarray(inputs["c_ctx"], np.float32).reshape(128, KC)])
    ada_w = inputs["ada_w"]
    ada_b = inputs["ada_b"]
    in_maps = []
    for j in range(NCORES):
        sl = slice(j * cols_core, (j + 1) * cols_core)
        in_maps.append({"c2": c2,
                        "ada_w": np.ascontiguousarray(ada_w[:, :, sl]),
                        "ada_b": np.ascontiguousarray(ada_b[:, None, sl])})
    res = run(build_k0(), in_maps)
    return np.concatenate([r["modT"] for r in res], axis=2)


NTC = CTX // NCORES
NTL = SEQ // NCORES
NT = NTC + NTL


def emit_modulate(P, dst, src, mod, sc1, c, k_shift, k_scale, dkey, skey):
    for (a, b, s) in ((0, NTC, 1), (NTC, NT, 0)):
        P.op("dve", lambda e: e.tensor_scalar(out=dst[:, a:b], in0=src[:, a:b],
                                              scalar1=sc1[:, k_scale * KC + c, s:s + 1],
                                              scalar2=mod[:, k_shift * KC + c, s:s + 1],
                                              op0=ALU.mult, op1=ALU.add),
             reads=[skey, "sc1", "mod"], writes=[dkey])


def load_mod(P, mod_in):
    mod = P.sb("mod", [128, NCH_ADA, 2])
    sc1 = P.sb("sc1", [128, NCH_ADA, 2])
    P.dma("sp", mod[:], mod_in, writes=["mod"])
    P.op("dve", lambda e: e.tensor_scalar_add(out=sc1[:], in0=mod[:], scalar1=1.0), reads=["mod"], writes=["sc1"])
    return mod, sc1


def build_mod(k_shift=0, k_scale=1):
    P = Prog()
    h_in = P.dram_in("hT", [D, NT])
    mod_in = P.dram_in("modT", [128, NCH_ADA, 2])
    u_out = P.dram_out("uT", [D, NT], BF16)
    mod, sc1 = load_mod(P, mod_in)
    ht = [P.sb("ht%d" % i, [128, NT]) for i in range(3)]
    ut = [P.sb("ut%d" % i, [128, NT], BF16) for i in range(3)]
    for c in range(KC):
        i = c % 3
        P.dma("sp", ht[i][:], h_in[c * 128:(c + 1) * 128, :], writes=["ht%d" % i])
        emit_modulate(P, ut[i], ht[i], mod, sc1, c, k_shift, k_scale, "ut%d" % i, "ht%d" % i)
        P.dma("act", u_out[c * 128:(c + 1) * 128, :], ut[i][:], reads=["ut%d" % i], is_output=True)
    return P.finish()


def tok_slices(j):
    return slice(NTC * j, NTC * (j + 1)), slice(CTX + NTL * j, CTX + NTL * (j + 1))


def shard_tokens(aT):
    out = []
    for j in range(NCORES):
        sc, sl = tok_slices(j)
        out.append(np.ascontiguousarray(np.concatenate([aT[:, sc], aT[:, sl]], axis=1)))
    return out


def unshard_tokens(parts):
    rows = parts[0].shape[0]
    full = np.empty((rows, T), parts[0].dtype)
    for j in range(NCORES):
        sc, sl = tok_slices(j)
        full[:, sc] = parts[j][:, :NTC]
        full[:, sl] = parts[j][:, NTC:]
    return full


def run_mod(hT, modT_l):
    hs = shard_tokens(hT)
    res = run(build_mod(), [{"hT": hs[j], "modT": modT_l} for j in range(NCORES)])
    return unshard_tokens([r["uT"] for r in res])


EA_NCOLS = 1920
HYW = 2048
DNW = 2048
TB = 256


def build_ea1(ncols=EA_NCOLS):
    P = Prog()
    nm = ncols // 128
    u_in = P.dram_in("uT", [D, T], BF16)
    w_in = P.dram_in("w", [D, ncols])
    z_out = P.dram_out("zT", [ncols, T])
    W = P.sb("W", [128, KC, ncols], BF16)
    for kc in range(KC):
        P.dma("pool", W[:, kc, :], w_in[kc * 128:(kc + 1) * 128, :], writes=[("W", kc)])
    ub = [P.sb("ub%d" % i, [128, KC, TB], BF16) for i in range(2)]
    st = [P.sb("st%d" % i, [128, nm, TB]) for i in range(2)]
    ps = [P.ps("ps%d" % i, [128, TB]) for i in range(4)]
    uv = u_in.rearrange("(kc p) t -> p kc t", p=128)
    zv = z_out.rearrange("(m p) t -> p m t", p=128)
    nblk = T // TB
    ev = 0
    for tb in range(nblk):
        i = tb % 2
        t0 = tb * TB
        P.dma("sp", ub[i][:], uv[:, :, t0:t0 + TB], writes=["ub%d" % i])
        for m in range(nm):
            pt = ps[ev % 4]
            pk = "ps%d" % (ev % 4)
            for kc in range(KC):
                P.op("pe", lambda e: e.matmul(pt[:], W[:, kc, m * 128:(m + 1) * 128], ub[i][:, kc, :],
                                              start=(kc == 0), stop=(kc == KC - 1)),
                     reads=[("W", kc), "ub%d" % i], writes=[pk], pe_acc=True)
            if ev % 2 == 0:
                P.op("dve", lambda e: e.tensor_copy(out=st[i][:, m, :], in_=pt[:]), reads=[pk], writes=["st%d" % i])
            else:
                P.op("act", lambda e: e.activation(out=st[i][:, m, :], in_=pt[:], func=AF.Copy),
                     reads=[pk], writes=["st%d" % i])
            ev += 1
        P.dma("act", zv[:, :, t0:t0 + TB], st[i][:], reads=["st%d" % i], is_output=True)
    return P.finish()


def ea_cols(j):
    cols = []
    for part in range(3):
        cols += list(range(part * HYW + 256 * j, part * HYW + 256 * (j + 1)))
    base = 3 * HYW
    for part in range(4):
        cols += list(range(base + part * DNW + 256 * j, base + part * DNW + 256 * (j + 1)))
    base = 3 * HYW + 4 * DNW
    for part in range(4):
        cols += [base + part * 16 + 2 * j, base + part * 16 + 2 * j + 1]
    return np.array(cols)


def run_ea1(uT, w_in_l):
    in_maps = []
    for j in range(NCORES):
        cols = ea_cols(j)
        w = np.zeros((D, EA_NCOLS), np.float32)
        w[:, :len(cols)] = w_in_l[:, cols]
        in_maps.append({"uT": uT, "w": w})
    res = run(build_ea1(), in_maps)
    return [r["zT"] for r in res]


ALPHA = (2 * DEPTH) ** 0.25
LN_EPS = 1e-5
NBK = 3
BK = NT // NBK
NEXP = 16


def emit_layernorm(P, zb, outb, g, b, ones128, ps_s, ps_s2, sq, tmp, zkey, okey, width):
    for m in range(KC):
        P.op("pe", lambda e: e.matmul(ps_s[:, :width], ones128[:], zb[:, m, :width], start=(m == 0), stop=(m == KC - 1)),
             reads=[zkey, "ones128"], writes=["ps_s"], pe_acc=True)
    for m in range(KC):
        s = sq[m % 2]
        sk = "sq%d" % (m % 2)
        P.op("act", lambda e: e.activation(out=s[:, :width], in_=zb[:, m, :width], func=AF.Square), reads=[zkey], writes=[sk])
        P.op("pe", lambda e: e.matmul(ps_s2[:, :width], ones128[:], s[:, :width], start=(m == 0), stop=(m == KC - 1)),
             reads=[sk, "ones128"], writes=["ps_s2"], pe_acc=True)
    mean, msq, rstd = tmp
    P.op("dve", lambda e: e.tensor_scalar(out=mean[:, :width], in0=ps_s[:, :width], scalar1=1.0 / D, scalar2=None, op0=ALU.mult),
         reads=["ps_s"], writes=["mean"])
    P.op("dve", lambda e: e.tensor_tensor(out=msq[:, :width], in0=mean[:, :width], in1=mean[:, :width], op=ALU.mult),
         reads=["mean"], writes=["msq"])
    P.op("dve", lambda e: e.scalar_tensor_tensor(out=rstd[:, :width], in0=ps_s2[:, :width], scalar=1.0 / D, in1=msq[:, :width],
                                                 op0=ALU.mult, op1=ALU.subtract),
         reads=["ps_s2", "msq"], writes=["rstd"])
    P.op("dve", lambda e: e.tensor_scalar_add(out=rstd[:, :width], in0=rstd[:, :width], scalar1=LN_EPS),
         reads=["rstd"], writes=["rstd"])
    P.op("act", lambda e: e.activation(out=rstd[:, :width], in_=rstd[:, :width], func=AF.Sqrt),
         reads=["rstd"], writes=["rstd"])
    P.op("dve", lambda e: e.reciprocal(out=rstd[:, :width], in_=rstd[:, :width]),
         reads=["rstd"], writes=["rstd"])
    for m in range(KC):
        eng = "dve" if m % 2 == 0 else "pool"
        P.op(eng, lambda e: e.tensor_tensor(out=outb[:, m, :width], in0=zb[:, m, :width], in1=mean[:, :width], op=ALU.subtract),
             reads=[zkey, "mean"], writes=[(okey, m)])
        P.op(eng, lambda e: e.tensor_tensor(out=outb[:, m, :width], in0=outb[:, m, :width], in1=rstd[:, :width], op=ALU.mult),
             reads=[(okey, m), "rstd"], writes=[(okey, m)])
        P.op(eng, lambda e: e.tensor_scalar(out=outb[:, m, :width], in0=outb[:, m, :width], scalar1=g[:, m:m + 1],
                                            scalar2=b[:, m:m + 1], op0=ALU.mult, op1=ALU.add),
             reads=[(okey, m), "lng", "lnb"], writes=[(okey, m)])


def blk_streams(bk):
    if bk == 0:
        return [(0, NTC, 1), (NTC, BK, 0)]
    return [(0, BK, 0)]


def build_x3(glu):
    P = Prog()
    nout = 2 * D if glu else D
    f_in = P.dram_in("fT", [D, NT])
    h_in = P.dram_in("hT", [D, NT])
    mod_in = P.dram_in("modT", [128, NCH_ADA, 2])
    w_in = P.dram_in("w", [D, nout])
    g_in = P.dram_in("lng", [128, KC])
    b_in = P.dram_in("lnb", [128, KC])
    wr_in = P.dram_in("wr", [128, KC, NEXP])
    h1_out = P.dram_out("h1T", [D, NT])
    h2_out = P.dram_out("h2T", [D, NT], BF16)
    aff_out = P.dram_out("affT", [NEXP, NT])

    mod, sc1 = load_mod(P, mod_in)
    g = P.sb("lng", [128, KC]); b = P.sb("lnb", [128, KC]); wr = P.sb("wr", [128, KC, NEXP])
    P.dma("sp", g[:], g_in, writes=["lng"]); P.dma("sp", b[:], b_in, writes=["lnb"]); P.dma("sp", wr[:], wr_in, writes=["wr"])
    ones128 = P.sb("ones128", [128, 128])
    P.op("dve", lambda e: e.memset(ones128[:], 1.0), writes=["ones128"])
    fb = P.sb("fb", [128, KC, BK], BF16)
    hb = P.sb("hb", [128, KC, BK])
    zb = P.sb("zb", [128, KC, BK])
    h2b = P.sb("h2b", [128, KC, BK], BF16)
    nw = 4 if glu else 2
    wm = [P.sb("wm%d" % i, [128, KC, 128], BF16) for i in range(nw)]
    sq = [P.sb("sq%d" % i, [128, BK]) for i in range(2)]
    tmp = (P.sb("mean", [128, BK]), P.sb("msq", [128, BK]), P.sb("rstd", [128, BK]))
    ex = P.sb("ex", [NEXP, BK]); rs = P.sb("rs", [NEXP, BK]); af = P.sb("af", [NEXP, BK])
    psA = [P.ps("psA%d" % i, [128, BK]) for i in range(2)]
    psB = [P.ps("psB%d" % i, [128, BK]) for i in range(2)] if glu else None
    ps_s = P.ps("ps_s", [128, BK]); ps_s2 = P.ps("ps_s2", [128, BK])
    ps_r = P.ps("ps_r", [NEXP, BK])
    fv = f_in.rearrange("(kc p) t -> p kc t", p=128)
    hv = h_in.rearrange("(kc p) t -> p kc t", p=128)
    wv = w_in.rearrange("(kc p) c -> p kc c", p=128)
    h1v = h1_out.rearrange("(kc p) t -> p kc t", p=128)
    h2v = h2_out.rearrange("(kc p) t -> p kc t", p=128)
    wi = 0
    for bk in range(NBK):
        c0 = bk * BK
        P.dma("pool", fb[:], fv[:, :, c0:c0 + BK], writes=["fb"])
        P.dma("sp", hb[:], hv[:, :, c0:c0 + BK], writes=["hb"])
        P.op("act", lambda e: e.activation(out=hb[:], in_=hb[:], func=AF.Copy, scale=ALPHA), reads=["hb"], writes=["hb"])
        for m in range(KC):
            wa = wm[wi % nw]; wak = "wm%d" % (wi % nw); wi += 1
            P.dma("pool", wa[:], wv[:, :, m * 128:(m + 1) * 128], writes=[wak])
            pa = psA[m % 2]; pak = "psA%d" % (m % 2)
            for kc in range(KC):
                P.op("pe", lambda e: e.matmul(pa[:], wa[:, kc, :], fb[:, kc, :], start=(kc == 0), stop=(kc == KC - 1)),
                     reads=[wak, "fb"], writes=[pak], pe_acc=True)
            src = pa; srck = pak
            if glu:
                wb = wm[wi % nw]; wbk = "wm%d" % (wi % nw); wi += 1
                P.dma("pool", wb[:], wv[:, :, D + m * 128:D + (m + 1) * 128], writes=[wbk])
                pb = psB[m % 2]; pbk = "psB%d" % (m % 2)
                for kc in range(KC):
                    P.op("pe", lambda e: e.matmul(pb[:], wb[:, kc, :], fb[:, kc, :], start=(kc == 0), stop=(kc == KC - 1)),
                         reads=[wbk, "fb"], writes=[pbk], pe_acc=True)
                sg = sq[m % 2]; sgk = "sq%d" % (m % 2)
                P.op("act", lambda e: e.activation(out=sg[:], in_=pb[:], func=AF.Sigmoid), reads=[pbk], writes=[sgk])
                P.op("dve", lambda e: e.tensor_tensor(out=sg[:], in0=pa[:], in1=sg[:], op=ALU.mult), reads=[pak, sgk], writes=[sgk])
                src = sg; srck = sgk
            for (a, bb, s) in blk_streams(bk):
                P.op("dve", lambda e: e.scalar_tensor_tensor(out=zb[:, m, a:bb], in0=src[:, a:bb],
                                                             scalar=mod[:, 2 * KC + m, s:s + 1], in1=hb[:, m, a:bb],
                                                             op0=ALU.mult, op1=ALU.add),
                     reads=[srck, "hb", "mod"], writes=["zb"])
        emit_layernorm(P, zb, hb, g, b, ones128, ps_s, ps_s2, sq, tmp, "zb", "hb", BK)
        P.dma("sp", h1v[:, :, c0:c0 + BK], hb[:], reads=[("hb", m) for m in range(KC)] + ["hb"], is_output=True)
        for m in range(KC):
            for (a, bb, s) in blk_streams(bk):
                P.op("pool", lambda e: e.tensor_scalar(out=zb[:, m, a:bb], in0=hb[:, m, a:bb],
                                                       scalar1=sc1[:, 4 * KC + m, s:s + 1], scalar2=mod[:, 3 * KC + m, s:s + 1],
                                                       op0=ALU.mult, op1=ALU.add),
                     reads=[("hb", m), "sc1", "mod"], writes=[("zb2", m), "zb"])
            P.op("act", lambda e: e.activation(out=h2b[:, m, :], in_=zb[:, m, :], func=AF.Copy), reads=[("zb2", m)], writes=["h2b"])
            P.op("pe", lambda e: e.matmul(ps_r[:], wr[:, m, :], zb[:, m, :], start=(m == 0), stop=(m == KC - 1)),
                 reads=["wr", ("zb2", m)], writes=["ps_r"], pe_acc=True)
        P.dma("act", h2v[:, :, c0:c0 + BK], h2b[:], reads=["h2b"], is_output=True)
        P.op("act", lambda e: e.activation(out=ex[:], in_=ps_r[:], func=AF.Exp), reads=["ps_r"], writes=["ex"])
        P.op("pe", lambda e: e.matmul(ps_r[:], ones128[0:NEXP, 0:NEXP], ex[:], start=True, stop=True),
             reads=["ex", "ones128"], writes=["ps_r"])
        P.op("dve", lambda e: e.reciprocal(out=rs[:], in_=ps_r[:]), reads=["ps_r"], writes=["rs"])
        P.op("dve", lambda e: e.tensor_tensor(out=af[:], in0=ex[:], in1=rs[:], op=ALU.mult), reads=["ex", "rs"], writes=["af"])
        P.dma("sp", aff_out[:, c0:c0 + BK], af[:], reads=["af"], is_output=True)
    return P.finish()


def pvec(v):
    return np.ascontiguousarray(np.asarray(v, np.float32).reshape(KC, 128).T)


def run_x3(glu, fT, hT, modT_l, w, lng, lnb, wrouter):
    fs = shard_tokens(fT); hs = shard_tokens(hT)
    wr = np.ascontiguousarray(np.asarray(wrouter, np.float32).reshape(KC, 128, NEXP).transpose(1, 0, 2))
    g = pvec(lng); b = pvec(lnb)
    w = np.ascontiguousarray(w, dtype=np.float32)
    in_maps = [{"fT": fs[j], "hT": hs[j], "modT": modT_l, "w": w, "lng": g, "lnb": b, "wr": wr} for j in range(NCORES)]
    res = run(build_x3(glu), in_maps)
    return (unshard_tokens([r["h1T"] for r in res]), unshard_tokens([r["h2T"] for r in res]),
            unshard_tokens([r["affT"] for r in res]))


EFF = 384
NKF = EFF // 128
CAP_LAT = 2 * SEQ // NEXP
CAP_CTX = 2 * CTX // NEXP
NBIS = 28


def emit_threshold(P, aff_t, width, cap, blockones, lo, mid, cmp_t, cnt, ge, ps_c, tag):
    P.op("dve", lambda e: e.memset(lo[:], 0.0), writes=[tag + "lo"])
    for it in range(NBIS):
        h = 0.5 ** (it + 1)
        P.op("dve", lambda e: e.tensor_scalar_add(out=mid[:], in0=lo[:], scalar1=h), reads=[tag + "lo"], writes=[tag + "mid"])
        P.op("dve", lambda e: e.tensor_scalar(out=cmp_t[:, :width], in0=aff_t[:, :width], scalar1=mid[:, 0:1], scalar2=None,
                                              op0=ALU.is_ge),
             reads=[tag + "aff", tag + "mid"], writes=[tag + "cmp"])
        P.op("dve", lambda e: e.reduce_sum(out=cnt[:], in_=cmp_t[:, :width], axis=AX.X), reads=[tag + "cmp"], writes=[tag + "cnt"])
        P.op("pe", lambda e: e.matmul(ps_c, blockones[:], cnt[:], start=True, stop=True),
             reads=[tag + "cnt", "blockones"], writes=["ps_c"])
        P.op("dve", lambda e: e.tensor_scalar(out=ge[:], in0=ps_c, scalar1=float(cap) - 0.5, scalar2=None, op0=ALU.is_ge),
             reads=["ps_c"], writes=[tag + "ge"])
        P.op("dve", lambda e: e.scalar_tensor_tensor(out=lo[:], in0=ge[:], scalar=h, in1=lo[:], op0=ALU.mult, op1=ALU.add),
             reads=[tag + "ge", tag + "lo"], writes=[tag + "lo"])


def build_x4():
    P = Prog()
    h2_in = P.dram_in("h2T", [D, NT], BF16)
    h1_in = P.dram_in("h1T", [D, NT])
    affj_in = P.dram_in("affj", [NEXP, NT])
    affl_in = P.dram_in("affl", [128, SEQ // 8])
    affc_in = P.dram_in("affc", [128, CTX // 8])
    mod_in = P.dram_in("modT", [128, NCH_ADA, 2])
    g_in = P.dram_in("lng", [128, KC])
    b_in = P.dram_in("lnb", [128, KC])
    wi_in = P.dram_in("w_in", [NEXP, D, 2 * EFF])
    wo_in = P.dram_in("w_out", [NEXP, EFF, D])
    sel_in = P.dram_in("sel", [NEXP, NEXP, 128])
    bo_in = P.dram_in("blockones", [128, 128])
    pick_in = P.dram_in("pick", [128, NEXP])
    h_out = P.dram_out("hT", [D, NT])

    mod, sc1 = load_mod(P, mod_in)
    g = P.sb("lng", [128, KC]); b = P.sb("lnb", [128, KC])
    P.dma("sp", g[:], g_in, writes=["lng"]); P.dma("sp", b[:], b_in, writes=["lnb"])
    sel = P.sb("sel", [NEXP, NEXP, 128]); blockones = P.sb("blockones", [128, 128]); pick = P.sb("pick", [128, NEXP])
    P.dma("sp", sel[:], sel_in, writes=["sel"]); P.dma("sp", blockones[:], bo_in, writes=["blockones"])
    P.dma("sp", pick[:], pick_in, writes=["pick"])
    ones128 = P.sb("ones128", [128, 128])
    P.op("dve", lambda e: e.memset(ones128[:], 1.0), writes=["ones128"])

    zb = P.sb("zb", [128, KC, BK])
    zflat = zb[:].rearrange("p a b -> p (a b)")
    affl = zflat[:, 0:SEQ // 8]; cmp_t = zflat[:, SEQ // 8:2 * (SEQ // 8)]
    affc = P.sb("affc", [128, CTX // 8])
    P.dma("sp", affl, affl_in, writes=["Laff"]); P.dma("sp", affc[:], affc_in, writes=["Caff"])
    lo_l = P.sb("lo_l", [128, 1]); lo_c = P.sb("lo_c", [128, 1]); mid = P.sb("mid", [128, 1]); cnt = P.sb("cnt", [128, 1])
    ge = P.sb("ge", [128, 1])
    ps_small = P.ps("ps_small", [128, 4])
    ps_c = ps_small[:, 0:1]
    emit_threshold(P, affl, SEQ // 8, CAP_LAT, blockones, lo_l, mid, cmp_t, cnt, ge, ps_c, "L")
    emit_threshold(P, affc, CTX // 8, CAP_CTX, blockones, lo_c, mid, cmp_t, cnt, ge, ps_c, "C")
    thr = P.sb("thr", [NEXP, 2])
    ps_t = ps_small[0:NEXP, 1:3]
    P.op("pe", lambda e: e.matmul(ps_t[:, 0:1], pick[:], lo_l[:], start=True, stop=True), reads=["pick", "Llo"], writes=["ps_c"])
    P.op("pe", lambda e: e.matmul(ps_t[:, 1:2], pick[:], lo_c[:], start=True, stop=True), reads=["pick", "Clo"], writes=["ps_c"])
    P.op("dve", lambda e: e.tensor_copy(out=thr[:], in_=ps_t), reads=["ps_c"], writes=["thr"])
    affj = P.sb("affj", [NEXP, NT]); G = P.sb("G", [NEXP, NT])
    P.dma("sp", affj[:], affj_in, writes=["affj"])
    for (a, bb, s) in ((0, NTC, 1), (NTC, NT, 0)):
        P.op("dve", lambda e: e.scalar_tensor_tensor(out=G[:, a:bb], in0=affj[:, a:bb], scalar=thr[:, s:s + 1], in1=affj[:, a:bb],
                                                     op0=ALU.is_ge, op1=ALU.mult),
             reads=["affj", "thr"], writes=["G"])

    h2all = P.sb("h2all", [128, KC, NT], BF16)
    act = h2all[:].rearrange("p a b -> p (a b)")[:, 0:NEXP * NKF * BK].rearrange("p (k t) -> p k t", t=BK)
    actc = [P.sb("actc%d" % i, [128, BK], BF16) for i in range(3)]
    gbs = [P.sb("gbs%d" % i, [128, BK]) for i in range(NBK)]
    h1c = [P.sb("h1c%d" % i, [128, BK]) for i in range(2)]
    wa = [P.sb("wa%d" % i, [128, KC, 128], BF16) for i in range(3)]
    wo = [P.sb("wo%d" % i, [128, NEXP * NKF, 128], BF16) for i in range(2)]
    sq = [P.sb("sq%d" % i, [128, BK]) for i in range(2)]
    tmp = (P.sb("mean", [128, BK]), P.sb("msq", [128, BK]), P.sb("rstd", [128, BK]))
    st = [P.sb("silu%d" % i, [128, BK]) for i in range(2)]
    psg = P.ps("psg", [128, BK])
    psa = [P.ps("psa%d" % i, [128, BK]) for i in range(2)]
    psb = [P.ps("psb%d" % i, [128, BK]) for i in range(2)]
    ps_s = P.ps("ps_s", [128, BK]); ps_s2 = P.ps("ps_s2", [128, BK])
    actd = P.nc.dram_tensor("actd", [NEXP * NKF * 128, NT], BF16, kind="Internal").ap()
    h2v = h2_in.rearrange("(kc p) t -> p kc t", p=128)
    h1v = h1_in.rearrange("(kc p) t -> p kc t", p=128)
    hov = h_out.rearrange("(kc p) t -> p kc t", p=128)
    wiv = wi_in.rearrange("e (kc p) c -> e p kc c", p=128)
    wov = wo_in.rearrange("e (kf p) c -> p e kf c", p=128)
    actv = actd.rearrange("(k p) t -> p k t", p=128)
    for bk in range(NBK):
        c0 = bk * BK
        P.dma("sp", h2all[:, :, c0:c0 + BK], h2v[:, :, c0:c0 + BK], writes=[("h2all", bk)])
    wai = 0
    it = 0
    for ex in range(NEXP):
        for kf in range(NKF):
            wts = []
            for half in range(2):
                w = wa[wai % 3]; wk = "wa%d" % (wai % 3); wai += 1
                cc = half * EFF + kf * 128
                P.dma("pool", w[:], wiv[ex, :, :, cc:cc + 128], writes=[wk])
                wts.append((w, wk))
            for bk in range(NBK):
                c0 = bk * BK
                if kf == 0:
                    P.op("pe", lambda e: e.matmul(psg[:], sel[:, ex, :], G[:, c0:c0 + BK], start=True, stop=True),
                         reads=["sel", "G"], writes=["psg"])
                    P.op("act", lambda e: e.activation(out=gbs[bk][:], in_=psg[:], func=AF.Copy), reads=["psg"], writes=["gbs%d" % bk])
                i2 = it % 2
                pa, pak = psa[i2], "psa%d" % i2
                pb, pbk = psb[i2], "psb%d" % i2
                for (pt, ptk, (w, wk)) in ((pa, pak, wts[0]), (pb, pbk, wts[1])):
                    for kc in range(KC):
                        P.op("pe", lambda e: e.matmul(pt[:], w[:, kc, :], h2all[:, kc, c0:c0 + BK], start=(kc == 0), stop=(kc == KC - 1)),
                             reads=[wk, ("h2all", bk)], writes=[ptk], pe_acc=True)
                s_t, sk = st[i2], "silu%d" % i2
                ac = actc[it % 3]; ack = "actc%d" % (it % 3)
                P.op("act", lambda e: e.activation(out=s_t[:], in_=pa[:], func=AF.Silu), reads=[pak], writes=[sk])
                P.op("dve", lambda e: e.tensor_tensor(out=s_t[:], in0=pb[:], in1=s_t[:], op=ALU.mult), reads=[pbk, sk], writes=[sk])
                P.op("dve", lambda e: e.tensor_tensor(out=ac[:], in0=s_t[:], in1=gbs[bk][:], op=ALU.mult), reads=[sk, "gbs%d" % bk], writes=[ack])
                P.dma("sp", actd[(ex * NKF + kf) * 128:(ex * NKF + kf + 1) * 128, c0:c0 + BK], ac[:], reads=[ack], writes=["actd"])
                it += 1
    for bk in range(NBK):
        c0 = bk * BK
        P.dma("sp", act, actv[:, :, c0:c0 + BK], reads=["actd"], writes=["act"] + [("h2all", q) for q in range(NBK)])
        for m in range(KC):
            w = wo[m % 2]; wk = "wo%d" % (m % 2)
            P.dma("pool", w[:].rearrange("p (e kf) c -> p e kf c", kf=NKF), wov[:, :, :, m * 128:(m + 1) * 128], writes=[wk])
            hc = h1c[m % 2]; hk = "h1c%d" % (m % 2)
            P.dma("act", hc[:], h1v[:, m, c0:c0 + BK], writes=[hk])
            P.op("act", lambda e: e.activation(out=hc[:], in_=hc[:], func=AF.Copy, scale=ALPHA), reads=[hk], writes=[hk])
            pa, pak = psa[m % 2], "psa%d" % (m % 2)
            nk = NEXP * NKF
            for k in range(nk):
                P.op("pe", lambda e: e.matmul(pa[:], w[:, k, :], act[:, k, :], start=(k == 0), stop=(k == nk - 1)),
                     reads=[wk, "act"], writes=[pak], pe_acc=True)
            for (a_, bb, s_) in blk_streams(bk):
                P.op("dve", lambda e: e.scalar_tensor_tensor(out=zb[:, m, a_:bb], in0=pa[:, a_:bb],
                                                             scalar=mod[:, 5 * KC + m, s_:s_ + 1], in1=hc[:, a_:bb],
                                                             op0=ALU.mult, op1=ALU.add),
                     reads=[pak, hk, "mod"], writes=["zb"])
        emit_layernorm(P, zb, zb, g, b, ones128, ps_s, ps_s2, sq, tmp, "zb", "zb", BK)
        P.dma("sp", hov[:, :, c0:c0 + BK], zb[:], reads=[("zb", m) for m in range(KC)] + ["zb"], writes=["zb"], is_output=True)
    return P.finish()


def moe_consts():
    sel = np.zeros((NEXP, NEXP, 128), np.float32)
    for e in range(NEXP):
        sel[e, e, :] = 1.0
    bo = np.kron(np.eye(NEXP, dtype=np.float32), np.ones((8, 8), np.float32))
    pick = np.zeros((128, NEXP), np.float32)
    for e in range(NEXP):
        pick[8 * e, e] = 1.0
    return sel, bo, pick


def run_x4(h1T, h2T, affT, modT_l, lng, lnb, w_in, w_out):
    h1s = shard_tokens(h1T); h2s = shard_tokens(h2T); affs = shard_tokens(affT)
    affl = np.ascontiguousarray(affT[:, CTX:].reshape(128, SEQ // 8))
    affc = np.ascontiguousarray(affT[:, :CTX].reshape(128, CTX // 8))
    sel, bo, pick = moe_consts()
    g = pvec(lng); b = pvec(lnb)
    w_in = np.ascontiguousarray(w_in, dtype=np.float32); w_out = np.ascontiguousarray(w_out, dtype=np.float32)
    in_maps = [{"h2T": h2s[j], "h1T": h1s[j], "affj": affs[j], "affl": affl, "affc": affc, "modT": modT_l, "lng": g, "lnb": b,
                "w_in": w_in, "w_out": w_out, "sel": sel, "blockones": bo, "pick": pick} for j in range(NCORES)]
    res = run(build_x4(), in_maps)
    return unshard_tokens([r["hT"] for r in res])


S5G = 16
S5P = 64
NGC = 32
NOCT = 4
NA = T // 64
TWO_PI = 6.283180
GELU_C = 0.7978845608028654


def emit_cmul(P, yr, yi, xr, xi, c, s, conj, keys):
    ykr, yki, xkr, xki, tk = keys
    P.op("dve", lambda e: e.tensor_tensor(out=yr, in0=xr, in1=c, op=ALU.mult), reads=[xkr, tk], writes=[ykr])
    P.op("pool", lambda e: e.tensor_tensor(out=yi, in0=xi, in1=s, op=ALU.mult), reads=[xki, tk], writes=[yki])
    P.op("dve", lambda e: e.tensor_tensor(out=yr, in0=yr, in1=yi, op=(ALU.add if conj else ALU.subtract)),
         reads=[ykr, yki], writes=[ykr])
    P.op("pool", lambda e: e.tensor_tensor(out=yi, in0=xi, in1=c, op=ALU.mult), reads=[xki, tk, ykr], writes=[yki])
    P.op("dve", lambda e: e.tensor_tensor(out=xr, in0=xr, in1=s, op=ALU.mult), reads=[xkr, tk], writes=[xkr])
    P.op("pool", lambda e: e.tensor_tensor(out=yi, in0=yi, in1=xr, op=(ALU.subtract if conj else ALU.add)),
         reads=[yki, xkr], writes=[yki])


def kk2(k):
    return [k + "A", k + "B"]


def emit_cmul2(P, yr, yi, xr, xi, c, s, conj, keys, na):
    ykr, yki, xkr, xki, tk = keys
    for (eng, sl, sfx) in (("dve", slice(0, na), "A"), ("pool", slice(na, None), "B")):
        Yr, Yi, Xr, Xi, C, S = (t[:, sl, :] for t in (yr, yi, xr, xi, c, s))
        a_, b_, c_, d_ = ykr + sfx, yki + sfx, xkr + sfx, xki + sfx
        P.op(eng, lambda e: e.tensor_tensor(out=Yr, in0=Xr, in1=C, op=ALU.mult), reads=[c_, tk], writes=[a_])
        P.op(eng, lambda e: e.tensor_tensor(out=Yi, in0=Xi, in1=S, op=ALU.mult), reads=[d_, tk], writes=[b_])
        P.op(eng, lambda e: e.tensor_tensor(out=Yr, in0=Yr, in1=Yi, op=(ALU.add if conj else ALU.subtract)), reads=[a_, b_], writes=[a_])
        P.op(eng, lambda e: e.tensor_tensor(out=Yi, in0=Xi, in1=C, op=ALU.mult), reads=[d_, tk, a_], writes=[b_])
        P.op(eng, lambda e: e.tensor_tensor(out=Xr, in0=Xr, in1=S, op=ALU.mult), reads=[c_, tk], writes=[c_])
        P.op(eng, lambda e: e.tensor_tensor(out=Yi, in0=Yi, in1=Xr, op=(ALU.subtract if conj else ALU.add)), reads=[b_, c_], writes=[b_])


def emit_sincos(P, ph, tmp_i, out_s, out_c, key):
    I32 = mybir.dt.int32
    P.op("dve", lambda e: e.tensor_copy(out=tmp_i, in_=ph), reads=[key + "ph"], writes=[key + "i"])
    P.op("dve", lambda e: e.tensor_tensor(out=out_s, in0=ph, in1=tmp_i, op=ALU.subtract), reads=[key + "ph", key + "i"], writes=[key + "s"])
    P.op("act", lambda e: e.activation(out=out_s, in_=out_s, func=AF.Sin, scale=TWO_PI), reads=[key + "s"], writes=[key + "s"])
    P.op("dve", lambda e: e.tensor_scalar_add(out=ph, in0=ph, scalar1=0.25), reads=[key + "ph", key + "s"], writes=[key + "ph"])
    P.op("dve", lambda e: e.tensor_copy(out=tmp_i, in_=ph), reads=[key + "ph"], writes=[key + "i"])
    P.op("dve", lambda e: e.tensor_tensor(out=out_c, in0=ph, in1=tmp_i, op=ALU.subtract), reads=[key + "ph", key + "i"], writes=[key + "c"])
    P.op("act", lambda e: e.activation(out=out_c, in_=out_c, func=AF.Sin, scale=TWO_PI), reads=[key + "c"], writes=[key + "c"])


def build_o2(n_oct=NOCT, n_grp=8, stages=(1, 1, 1, 1, 1, 1)):
    P = Prog()
    nc = P.nc
    I32 = mybir.dt.int32
    h_in = P.dram_in("hT", [NOCT * 128, T])
    modo_in = P.dram_in("modo", [128, NOCT, 2, 2])
    dsk_in = P.dram_in("dsk", [128, NOCT])
    lre_in = P.dram_in("lam_re", [128, NGC]); lim_in = P.dram_in("lam_im", [128, NGC]); ldt_in = P.dram_in("log_dt", [128, NGC])
    bre_in = P.dram_in("bre", [NOCT, 128, 8, 128]); bim_in = P.dram_in("bim", [NOCT, 128, 8, 128])
    cre_in = P.dram_in("cre", [128, NGC, S5G]); cim_in = P.dram_in("cim", [128, NGC, S5G])
    av_in = P.dram_in("avals", [128, NA]); bv_in = P.dram_in("bvals", [128, 64])
    f_out = P.dram_out("fT", [NOCT * 128, T])
    yscr = nc.dram_tensor("yscr", [128, T], F32, kind="Internal").ap()

    def ld(name, src, shape, dt=F32, q="sp"):
        t = P.sb(name, shape, dt)
        P.dma(q, t[:], src, writes=[name])
        return t
    modo = ld("modo", modo_in, [128, NOCT, 2, 2]); dsk = ld("dsk", dsk_in, [128, NOCT])
    lre = ld("lre", lre_in, [128, NGC]); lim = ld("lim", lim_in, [128, NGC]); ldt = ld("ldt", ldt_in, [128, NGC])
    cre = ld("cre", cre_in, [128, NGC, S5G]); cim = ld("cim", cim_in, [128, NGC, S5G])
    avals = ld("avals", av_in, [128, NA]); bvals = ld("bvals", bv_in, [128, 64])
    sc1 = P.sb("sc1o", [128, NOCT, 2])
    P.op("dve", lambda e: e.tensor_scalar_add(out=sc1[:], in0=modo[:, :, 1, :], scalar1=1.0), reads=["modo"], writes=["sc1o"])

    def sm(name, dt=F32):
        return P.sb(name, [128, NGC], dt)
    dtv = sm("dtv"); r = sm("r"); fq = sm("fq"); F1 = sm("F1"); ti = sm("ti", I32)
    lbs = sm("lbs"); lbc = sm("lbc"); ph = sm("ph"); den = sm("den"); fre = sm("fre"); fim = sm("fim"); t1 = sm("t1"); t2 = sm("t2")
    K = "prm"
    def dv(fn, reads, writes):
        P.op("dve", fn, reads=reads, writes=writes)
    dv(lambda e: e.tensor_scalar_min(out=lre[:], in0=lre[:], scalar1=-1e-4), ["lre"], ["lre"])
    P.op("act", lambda e: e.activation(out=dtv[:], in_=ldt[:], func=AF.Exp), reads=["ldt"], writes=["dtv"])
    dv(lambda e: e.tensor_tensor(out=r[:], in0=lre[:], in1=dtv[:], op=ALU.mult), ["lre", "dtv"], ["r"])
    P.op("act", lambda e: e.activation(out=r[:], in_=r[:], func=AF.Exp), reads=["r"], writes=["r"])
    dv(lambda e: e.tensor_tensor(out=fq[:], in0=lim[:], in1=dtv[:], op=ALU.mult), ["lim", "dtv"], ["fq"])
    dv(lambda e: e.tensor_scalar(out=fq[:], in0=fq[:], scalar1=1.0 / (2 * math.pi), scalar2=None, op0=ALU.mult), ["fq"], ["fq"])
    dv(lambda e: e.tensor_scalar(out=t1[:], in0=fq[:], scalar1=64.0, scalar2=None, op0=ALU.mult), ["fq"], ["t1"])
    dv(lambda e: e.tensor_copy(out=ti[:], in_=t1[:]), ["t1"], ["ti"])
    dv(lambda e: e.tensor_tensor(out=F1[:], in0=t1[:], in1=ti[:], op=ALU.subtract), ["t1", "ti"], ["F1"])
    dv(lambda e: e.tensor_copy(out=ph[:], in_=fq[:]), ["fq"], ["lbph"])
    emit_sincos(P, ph[:], ti[:], lbs[:], lbc[:], "lb")
    dv(lambda e: e.tensor_tensor(out=lbs[:], in0=lbs[:], in1=r[:], op=ALU.mult), ["lbs", "r"], ["lbs"])
    dv(lambda e: e.tensor_tensor(out=lbc[:], in0=lbc[:], in1=r[:], op=ALU.mult), ["lbc", "r"], ["lbc"])
    dv(lambda e: e.tensor_scalar_add(out=lbc[:], in0=lbc[:], scalar1=-1.0), ["lbc"], ["lbc"])
    dv(lambda e: e.tensor_tensor(out=den[:], in0=lre[:], in1=lre[:], op=ALU.mult), ["lre"], ["den"])
    dv(lambda e: e.tensor_tensor(out=t1[:], in0=lim[:], in1=lim[:], op=ALU.mult), ["lim", "F1"], ["t1"])
    dv(lambda e: e.tensor_tensor(out=den[:], in0=den[:], in1=t1[:], op=ALU.add), ["den", "t1"], ["den"])
    dv(lambda e: e.reciprocal(out=den[:], in_=den[:]), ["den"], ["den"])
    dv(lambda e: e.tensor_tensor(out=fre[:], in0=lbc[:], in1=lre[:], op=ALU.mult), ["lbc", "lre"], ["fre"])
    dv(lambda e: e.tensor_tensor(out=t1[:], in0=lbs[:], in1=lim[:], op=ALU.mult), ["lbs", "lim", "den"], ["t1"])
    dv(lambda e: e.tensor_tensor(out=fre[:], in0=fre[:], in1=t1[:], op=ALU.add), ["fre", "t1"], ["fre"])
    dv(lambda e: e.tensor_tensor(out=fre[:], in0=fre[:], in1=den[:], op=ALU.mult), ["fre", "den"], ["fre"])
    dv(lambda e: e.tensor_tensor(out=fim[:], in0=lbs[:], in1=lre[:], op=ALU.mult), ["lbs", "lre"], ["fim"])
    dv(lambda e: e.tensor_tensor(out=t2[:], in0=lbc[:], in1=lim[:], op=ALU.mult), ["lbc", "lim"], ["t2"])
    dv(lambda e: e.tensor_tensor(out=fim[:], in0=fim[:], in1=t2[:], op=ALU.subtract), ["fim", "t2"], ["fim"])
    dv(lambda e: e.tensor_tensor(out=fim[:], in0=fim[:], in1=den[:], op=ALU.mult), ["fim", "den"], ["fim"])
    gre = P.sb("gre", [128, NGC, S5G]); gimn = P.sb("gimn", [128, NGC, S5G]); gt = P.sb("gt", [128, NGC, S5G])
    freb = fre[:].unsqueeze(2).to_broadcast([128, NGC, S5G]); fimb = fim[:].unsqueeze(2).to_broadcast([128, NGC, S5G])
    dv(lambda e: e.tensor_tensor(out=gre[:], in0=cre[:], in1=freb, op=ALU.mult), ["cre", "fre"], ["gre"])
    dv(lambda e: e.tensor_tensor(out=gt[:], in0=cim[:], in1=fimb, op=ALU.mult), ["cim", "fim"], ["gt"])
    dv(lambda e: e.tensor_tensor(out=gre[:], in0=gre[:], in1=gt[:], op=ALU.subtract), ["gre", "gt"], ["gre"])
    dv(lambda e: e.tensor_tensor(out=gimn[:], in0=cre[:], in1=fimb, op=ALU.mult), ["cre", "fim"], ["gimn"])
    dv(lambda e: e.tensor_tensor(out=gt[:], in0=cim[:], in1=freb, op=ALU.mult), ["cim", "fre", "gre"], ["gt"])
    dv(lambda e: e.tensor_tensor(out=gimn[:], in0=gimn[:], in1=gt[:], op=ALU.add), ["gimn", "gt"], ["gimn"])
    dv(lambda e: e.tensor_scalar(out=gimn[:], in0=gimn[:], scalar1=-1.0, scalar2=None, op0=ALU.mult), ["gimn"], ["gimn"])
    greb = P.sb("greb", [128, NGC, S5G], BF16); gimb = P.sb("gimb", [128, NGC, S5G], BF16)
    dv(lambda e: e.tensor_copy(out=greb[:], in_=gre[:]), ["gre"], ["greb"])
    dv(lambda e: e.tensor_copy(out=gimb[:], in_=gimn[:]), ["gimn"], ["gimb"])

    U = P.sb("U", [128, T], BF16)
    BR = P.sb("BR", [128, T], BF16); BI = P.sb("BI", [128, T], BF16); TR = P.sb("TR", [128, T]); TI = P.sb("TI", [128, T])
    ZR = P.sb("ZR", [128, T], BF16); ZI = P.sb("ZI", [128, T], BF16)
    Yg = P.sb("Yg", [S5G, 2112])
    bpr = P.sb("bpr", [128, 8, 128], BF16); bpi = P.sb("bpi", [128, 8, 128], BF16)
    eac = P.sb("eac", [128, 8, NA]); eas = P.sb("eas", [128, 8, NA]); ebc = P.sb("ebc", [128, 8, 64]); ebs = P.sb("ebs", [128, 8, 64])
    pha = P.sb("pha", [128, 8, NA]); phb = P.sb("phb", [128, 8, 64]); pia = P.sb("pia", [128, 8, NA], I32); pib = P.sb("pib", [128, 8, 64], I32)
    psr = [P.ps("psr%d" % i, [128, 512]) for i in range(2)]
    psi = [P.ps("psi%d" % i, [128, 512]) for i in range(2)]
    psy = [P.ps("psy%d" % i, [S5G, 512]) for i in range(2)]

    NSPL = 88

    def v3(t):
        return t[:].rearrange("p (a b) -> p a b", b=64)
    blocks = [(i * 512, min(512, T - i * 512)) for i in range((T + 511) // 512)]
    for oc in range(n_oct):
        for sg in range(4):
            c0 = sg * 2112
            hraw = TR[:, 0:2112]
            P.dma("sp", hraw, h_in[oc * 128:(oc + 1) * 128, c0:c0 + 2112], writes=kk2("TR"))
            pieces = [(0, CTX, 1), (CTX, 2112, 0)] if sg == 0 else [(0, 2112, 0)]
            for (a, bb, s) in pieces:
                P.op("dve", lambda e: e.tensor_scalar(out=U[:, c0 + a:c0 + bb], in0=hraw[:, a:bb], scalar1=sc1[:, oc, s:s + 1],
                                                      scalar2=modo[:, oc, 0, s:s + 1], op0=ALU.mult, op1=ALU.add),
                     reads=kk2("TR") + ["sc1o", "modo"], writes=["U"])
        P.dma("pool", bpr[:], bre_in[oc], writes=["bpr"]); P.dma("pool", bpi[:], bim_in[oc], writes=["bpi"])
        g0 = oc * 8
        dv(lambda e: e.tensor_tensor(out=pha[:], in0=F1[:, g0:g0 + 8].unsqueeze(2).to_broadcast([128, 8, NA]),
                                     in1=avals[:].unsqueeze(1).to_broadcast([128, 8, NA]), op=ALU.mult),
           ["F1", "avals"], ["EAph"])
        emit_sincos(P, pha[:], pia[:], eas[:], eac[:], "EA")
        dv(lambda e: e.tensor_tensor(out=phb[:], in0=fq[:, g0:g0 + 8].unsqueeze(2).to_broadcast([128, 8, 64]),
                                     in1=bvals[:].unsqueeze(1).to_broadcast([128, 8, 64]), op=ALU.mult),
           ["fq", "bvals"], ["EBph"])
        emit_sincos(P, phb[:], pib[:], ebs[:], ebc[:], "EB")
        for gl in range(n_grp):
            g = g0 + gl
            for bi_, (c0, w) in enumerate(blocks):
                pr, prk = psr[bi_ % 2], "psr%d" % (bi_ % 2)
                pi, pik = psi[bi_ % 2], "psi%d" % (bi_ % 2)
                P.op("pe", lambda e: e.matmul(pr[:, :w], bpr[:, gl, :], U[:, c0:c0 + w], start=True, stop=True),
                     reads=["bpr", "U"], writes=[prk])
                P.op("pe", lambda e: e.matmul(pi[:, :w], bpi[:, gl, :], U[:, c0:c0 + w], start=True, stop=True),
                     reads=["bpi", "U"], writes=[pik])
                P.op("act", lambda e: e.activation(out=BR[:, c0:c0 + w], in_=pr[:, :w], func=AF.Copy), reads=[prk], writes=kk2("BR"))
                P.op("act", lambda e: e.activation(out=BI[:, c0:c0 + w], in_=pi[:, :w], func=AF.Copy), reads=[pik], writes=kk2("BI"))
            ebcb = ebc[:, gl, :].unsqueeze(1).to_broadcast([128, NA, 64]); ebsb = ebs[:, gl, :].unsqueeze(1).to_broadcast([128, NA, 64])
            eacb = eac[:, gl, :].unsqueeze(2).to_broadcast([128, NA, 64]); easb = eas[:, gl, :].unsqueeze(2).to_broadcast([128, NA, 64])
            if stages[0]:
                emit_cmul2(P, v3(ZR), v3(ZI), v3(BR), v3(BI), ebcb, ebsb, True, ("ZR", "ZI", "BR", "BI", "EBc"), NSPL)
            if stages[1]:
                emit_cmul2(P, v3(BR), v3(BI), v3(ZR), v3(ZI), eacb, easb, True, ("BR", "BI", "ZR", "ZI", "EAc"), NSPL)
            rf = r[0:64, g:g + 1]; rb = r[64:128, g:g + 1]
            for (src, dst, sk, dk) in (((BR, TR, "BR", "TR"), (BI, TI, "BI", "TI")) if stages[2] else ()):
                P.op("dve", lambda e: e.tensor_tensor_scan(out=dst[0:64, :], data0=rf.to_broadcast([64, T]), data1=src[0:64, :],
                                                           initial=0.0, op0=ALU.mult, op1=ALU.add),
                     reads=kk2(sk) + ["r"], writes=kk2(dk))
                P.op("dve", lambda e: e.tensor_tensor_scan(out=dst[64:128, 0:CTX][:, ::-1], data0=rb.to_broadcast([64, CTX]),
                                                           data1=src[64:128, 0:CTX][:, ::-1], initial=0.0, op0=ALU.mult, op1=ALU.add),
                     reads=kk2(sk) + ["r"], writes=kk2(dk))
                P.op("dve", lambda e: e.tensor_tensor_scan(out=dst[64:128, CTX:T][:, ::-1], data0=rb.to_broadcast([64, SEQ]),
                                                           data1=src[64:128, CTX:T][:, ::-1], initial=dst[64:128, 0:1],
                                                           op0=ALU.mult, op1=ALU.add),
                     reads=kk2(sk) + ["r"] + kk2(dk), writes=kk2(dk))
            if stages[3]:
                emit_cmul2(P, v3(BR), v3(BI), v3(TR), v3(TI), ebcb, ebsb, False, ("BR", "BI", "TR", "TI", "EBc"), NSPL)
                emit_cmul2(P, v3(ZR), v3(ZI), v3(BR), v3(BI), eacb, easb, False, ("ZR", "ZI", "BR", "BI", "EAc"), NSPL)
            for sg in (range(4) if stages[4] else ()):
                for sb_ in range(5):
                    c0 = sg * 2112 + sb_ * 512
                    w = min(512, sg * 2112 + 2112 - c0)
                    if w <= 0:
                        continue
                    py, pyk = psy[sb_ % 2], "psy%d" % (sb_ % 2)
                    P.op("pe", lambda e: e.matmul(py[:, :w], greb[:, g, :], ZR[:, c0:c0 + w], start=True, stop=False),
                         reads=["greb"] + kk2("ZR"), writes=[pyk])
                    P.op("pe", lambda e: e.matmul(py[:, :w], gimb[:, g, :], ZI[:, c0:c0 + w], start=False, stop=True),
                         reads=["gimb"] + kk2("ZI"), writes=[pyk], pe_acc=True)
                    P.op("dve", lambda e: e.tensor_copy(out=Yg[:, sb_ * 512:sb_ * 512 + w], in_=py[:, :w]), reads=[pyk], writes=["Yg"])
                P.dma("sp", yscr[gl * S5G:(gl + 1) * S5G, sg * 2112:(sg + 1) * 2112], Yg[:], reads=["Yg"], writes=["yscr"])
        for sg in (range(4) if stages[5] else ()):
            c0 = sg * 2112
            X = TR[:, 0:2112]; X2 = TI[:, 0:2112]
            P.dma("sp", X, yscr[:, c0:c0 + 2112], reads=["yscr"], writes=kk2("TR"))
            P.op("dve", lambda e: e.scalar_tensor_tensor(out=X, in0=U[:, c0:c0 + 2112], scalar=dsk[:, oc:oc + 1], in1=X,
                                                         op0=ALU.mult, op1=ALU.add), reads=["U", "dsk"] + kk2("TR"), writes=kk2("TR"))
            P.op("pool", lambda e: e.tensor_tensor(out=X2, in0=X, in1=X, op=ALU.mult), reads=kk2("TR"), writes=kk2("TI"))
            P.op("pool", lambda e: e.tensor_scalar(out=X2, in0=X2, scalar1=0.044715 * GELU_C, scalar2=GELU_C, op0=ALU.mult, op1=ALU.add),
                 reads=kk2("TI"), writes=kk2("TI"))
            P.op("dve", lambda e: e.tensor_tensor(out=X2, in0=X2, in1=X, op=ALU.mult), reads=kk2("TI") + kk2("TR"), writes=kk2("TI"))
            P.op("act", lambda e: e.activation(out=X2, in_=X2, func=AF.Tanh), reads=kk2("TI"), writes=kk2("TI"))
            P.op("dve", lambda e: e.tensor_scalar(out=X2, in0=X2, scalar1=1.0, scalar2=0.5, op0=ALU.add, op1=ALU.mult),
                 reads=kk2("TI"), writes=kk2("TI"))
            P.op("dve", lambda e: e.tensor_tensor(out=X, in0=X, in1=X2, op=ALU.mult), reads=kk2("TI") + kk2("TR"), writes=kk2("TR"))
            P.dma("sp", f_out[oc * 128:(oc + 1) * 128, c0:c0 + 2112], X, reads=kk2("TR"), writes=kk2("TR"), is_output=True)
    return P.finish()


def s5_consts():
    a = np.arange(NA, dtype=np.float32)
    ab = np.where(a < 4, 3 - a, 135 - a).astype(np.float32)
    avals = np.concatenate([np.tile(a, (64, 1)), np.tile(ab, (64, 1))], axis=0)
    b = np.arange(64, dtype=np.float32)
    bvals = np.concatenate([np.tile(b, (64, 1)), np.tile(63 - b, (64, 1))], axis=0)
    return np.ascontiguousarray(avals), np.ascontiguousarray(bvals)


def run_o2(hT, modT_l, lam_re, lam_im, log_dt, b_re, b_im, c_re, c_im, d_skip, **bkw):
    avals, bvals = s5_consts()
    in_maps = []
    for j in range(NCORES):
        gs = slice(NGC * j, NGC * (j + 1))
        ch = slice(512 * j, 512 * (j + 1))
        def dp(a):
            return np.ascontiguousarray(np.asarray(a[:, gs], np.float32).transpose(0, 2, 1).reshape(128, NGC))
        ldt = np.ascontiguousarray(np.broadcast_to(np.asarray(log_dt[:, gs], np.float32)[:, None, :], (2, 64, NGC)).reshape(128, NGC))
        def bpad(bm):
            bm = np.asarray(bm[:, gs], np.float32)
            out = np.zeros((NOCT, 128, 8, 128), np.float32)
            for oc in range(NOCT):
                for gl in range(8):
                    blk = bm[:, oc * 8 + gl]
                    out[oc, gl * 16:(gl + 1) * 16, gl, :] = blk.transpose(2, 0, 1).reshape(16, 128)
            return out
        def cpad(cm):
            cm = np.asarray(cm[:, gs], np.float32)
            return np.ascontiguousarray(cm.transpose(0, 3, 1, 2).reshape(128, NGC, S5G))
        modo = np.zeros((128, NOCT, 2, 2), np.float32)
        for oc in range(NOCT):
            chunk = 4 * j + oc
            modo[:, oc, 0, :] = modT_l[:, 0 * KC + chunk, :]
            modo[:, oc, 1, :] = modT_l[:, 1 * KC + chunk, :]
        dsk = np.ascontiguousarray(np.asarray(d_skip[ch], np.float32).reshape(NOCT, 128).T)
        in_maps.append({"hT": np.ascontiguousarray(hT[ch]), "modo": modo, "dsk": dsk, "lam_re": dp(lam_re), "lam_im": dp(lam_im),
                        "log_dt": ldt, "bre": bpad(b_re), "bim": bpad(b_im), "cre": cpad(c_re), "cim": cpad(c_im),
                        "avals": avals, "bvals": bvals})
    res = run(build_o2(**bkw), in_maps)
    return np.concatenate([r["fT"] for r in res], axis=0)


CB = 16
NFFT = 16384
HYF = 64


def build_eh():
    P = Prog()
    nc = P.nc
    I32 = mybir.dt.int32
    zh_in = P.dram_in("zh", [3, 256, T])
    cw_in = P.dram_in("convw", [128, 2, 3, 3])
    hb_in = P.dram_in("hbias", [128, 2, 2])
    w1_in = P.dram_in("w1", [33, HYF]); w2_in = P.dram_in("w2", [HYF, HYF]); w3_in = P.dram_in("w3", [HYF, HYF])
    bfr_in = P.dram_in("bfr", [HYF, 2, 3])
    w4_in = P.dram_in("w4", [HYF, 2, 2, 256])
    zl_in = P.dram_in("zposl", [33, SEQ]); zc_in = P.dram_in("zposc", [33, CTX])
    dl_in = P.dram_in("decl", [256, SEQ]); dc_in = P.dram_in("decc", [256, CTX])
    tabs_in = P.dram_in("tabs", [128, 12, 128])
    y_out = P.dram_out("yh", [256, T])
    x1c = nc.dram_tensor("x1c", [128, T], F32, kind="Internal").ap()
    x2c = nc.dram_tensor("x2c", [128, T], F32, kind="Internal").ap()
    a_dram = nc.dram_tensor("a_dram", [128, SEQ], F32, kind="Internal").ap()
    g_dram = nc.dram_tensor("g_dram", [128, NFFT], F32, kind="Internal").ap()
    c_dram = nc.dram_tensor("c_dram", [128, SEQ], F32, kind="Internal").ap()

    def ld(name, src, shape, q="sp"):
        t = P.sb(name, shape)
        P.dma(q, t[:], src, writes=[name])
        return t
    cw = ld("cw", cw_in, [128, 2, 3, 3]); hbias = ld("hbias", hb_in, [128, 2, 2])
    w1 = ld("w1", w1_in, [33, HYF]); w2 = ld("w2", w2_in, [HYF, HYF]); w3 = ld("w3", w3_in, [HYF, HYF])
    bfr = ld("bfr", bfr_in, [HYF, 2, 3]); w4 = ld("w4", w4_in, [HYF, 2, 2, 256])
    tabs = ld("tabs", tabs_in, [128, 12, 128])
    T1 = tabs[:, 0:2, :].rearrange("p a b -> p (a b)")
    TI1a = tabs[:, 7:9, :].rearrange("p a b -> p (a b)")
    TI1b = tabs[:, 9:11, :].rearrange("p a b -> p (a b)")
    C128 = tabs[:, 0, :]; NEGS = tabs[:, 1, :]; S128 = tabs[:, 2, :]
    twc = tabs[:, 3, :]; tws = tabs[:, 4, :]
    CN = tabs[:, 5, 0:64]; NSN = tabs[:, 6, 0:64]
    frs = P.sb("frs", [HYF, 3])
    P.op("dve", lambda e: e.tensor_scalar(out=frs[:], in0=bfr[:, 1, :], scalar1=1.0 / (2 * math.pi), scalar2=None, op0=ALU.mult),
         reads=["bfr"], writes=["frs"])

    h3l = P.sb("h3l", [HYF, SEQ]); h3c = P.sb("h3c", [HYF, CTX])
    zp = P.sb("zp", [33, 512]); ha = P.sb("ha", [HYF, 512]); hbt = P.sb("hbt", [HYF, 512]); hi_ = P.sb("hi", [HYF, 512], I32)
    psm = P.ps("psm", [128, 512])

    def mlp_layer(li, wt, wk, src, srck, dst, dstk, w_):
        kdim = 33 if li == 0 else HYF
        P.op("pe", lambda e: e.matmul(psm[0:HYF, :w_], wt[0:kdim, :], src, start=True, stop=True), reads=[wk, srck], writes=["psm"])
        P.op("dve", lambda e: e.tensor_scalar(out=hbt[:, :w_], in0=psm[0:HYF, :w_], scalar1=bfr[:, 0, li:li + 1],
                                              scalar2=frs[:, li:li + 1], op0=ALU.add, op1=ALU.mult),
             reads=["psm", "bfr", "frs"], writes=["hbt"])
        P.op("dve", lambda e: e.tensor_copy(out=hi_[:, :w_], in_=hbt[:, :w_]), reads=["hbt"], writes=["hi"])
        P.op("dve", lambda e: e.tensor_tensor(out=hbt[:, :w_], in0=hbt[:, :w_], in1=hi_[:, :w_], op=ALU.subtract),
             reads=["hbt", "hi"], writes=["hbt"])
        P.op("act", lambda e: e.activation(out=dst, in_=hbt[:, :w_], func=AF.Sin, scale=TWO_PI), reads=["hbt"], writes=[dstk])

    for (zsrc, L, h3) in ((zl_in, SEQ, h3l), (zc_in, CTX, h3c)):
        for b0 in range(0, L, 512):
            w_ = min(512, L - b0)
            P.dma("sp", zp[:, :w_], zsrc[:, b0:b0 + w_], writes=["zp"])
            mlp_layer(0, w1, "w1", zp[:, :w_], "zp", ha[:, :w_], "ha", w_)
            mlp_layer(1, w2, "w2", ha[:, :w_], "ha", ha[:, :w_], "ha", w_)
            mlp_layer(2, w3, "w3", ha[:, :w_], "ha", h3[:, b0:b0 + w_], "h3", w_)

    CBH = 8

    class BS:
        pass
    bsets = []
    for si in range(2):
        bs = BS(); bs.k = "f%d_" % si
        def T_(name, shape, k=bs.k):
            return P.sb(k + name, shape)
        bs.Xt = T_("Xt", [128, CBH, 128]); bs.A = T_("A", [128, CBH, 256])
        bs.P1r = T_("P1r", [128, CBH, 128]); bs.P1i = T_("P1i", [128, CBH, 128])
        bs.P2r = T_("P2r", [128, CBH, 128]); bs.P2i = T_("P2i", [128, CBH, 128])
        bs.Hr = T_("Hr", [128, CBH, 128]); bs.Hi = T_("Hi", [128, CBH, 128])
        bs.Yo = T_("Yo", [64, CBH, 128])
        bs.ps1 = P.ps(bs.k + "ps1", [128, 512]); bs.psxr = P.ps(bs.k + "psxr", [128, 512]); bs.psxi = P.ps(bs.k + "psxi", [128, 512])
        bsets.append(bs)
    twcb = twc.unsqueeze(1).to_broadcast([128, CBH, 128]); twsb = tws.unsqueeze(1).to_broadcast([128, CBH, 128])

    def fl(t):
        return t[:].rearrange("p c k -> p (c k)")

    def fft_fwd(bs, src_dram, ch0, kdim, outr, outi, ork, oik):
        k = bs.k
        P.dma("sp", bs.Xt[0:kdim], src_dram[ch0:ch0 + CBH, 0:kdim * 128].rearrange("c (n2 n1) -> n2 c n1", n1=128),
              reads=[src_dram.tensor.name], writes=[k + "Xt"])
        yield
        for c2 in range(CBH // 2):
            for h in range(2):
                ch = 2 * c2 + h
                P.op("pe", lambda e: e.matmul(bs.ps1[:, h * 256:(h + 1) * 256], bs.Xt[0:kdim, ch, :], T1[0:kdim, :], start=True, stop=True),
                     reads=[k + "Xt", "tabs"], writes=[k + "ps1"])
            P.op("act", lambda e: e.activation(out=bs.A[:, 2 * c2:2 * c2 + 2, :].rearrange("p c k -> p (c k)"), in_=bs.ps1[:], func=AF.Copy),
                 reads=[k + "ps1"], writes=[k + "A"])
            yield
        emit_cmul(P, bs.P1r[:], bs.P1i[:], bs.A[:, :, 0:128], bs.A[:, :, 128:256], twcb, twsb, True,
                  (k + "P1r", k + "P1i", k + "A", k + "A", "tabs"))
        yield
        for b4 in range(CBH * 128 // 512):
            sl = slice(b4 * 512, (b4 + 1) * 512)
            P.op("pe", lambda e: e.matmul(bs.psxr[:], C128, fl(bs.P1r)[:, sl], start=True, stop=False), reads=["tabs", k + "P1r"], writes=[k + "psxr"])
            P.op("pe", lambda e: e.matmul(bs.psxr[:], S128, fl(bs.P1i)[:, sl], start=False, stop=True), reads=["tabs", k + "P1i"], writes=[k + "psxr"],
                 pe_acc=True)
            P.op("pe", lambda e: e.matmul(bs.psxi[:], C128, fl(bs.P1i)[:, sl], start=True, stop=False), reads=["tabs", k + "P1i"], writes=[k + "psxi"])
            P.op("pe", lambda e: e.matmul(bs.psxi[:], NEGS, fl(bs.P1r)[:, sl], start=False, stop=True), reads=["tabs", k + "P1r"], writes=[k + "psxi"],
                 pe_acc=True)
            P.op("act", lambda e: e.activation(out=fl(outr)[:, sl], in_=bs.psxr[:], func=AF.Copy), reads=[k + "psxr"], writes=[ork])
            P.op("dve", lambda e: e.tensor_copy(out=fl(outi)[:, sl], in_=bs.psxi[:]), reads=[k + "psxi"], writes=[oik])
            yield

    def fft_conv_batch(bs, ch0):
        k = bs.k
        yield from fft_fwd(bs, g_dram, ch0, 128, bs.Hr, bs.Hi, k + "Hr", k + "Hi")
        yield from fft_fwd(bs, a_dram, ch0, 64, bs.P2r, bs.P2i, k + "P2r", k + "P2i")
        P.op("pool", lambda e: e.tensor_tensor(out=bs.P1r[:], in0=bs.P2r[:], in1=bs.Hr[:], op=ALU.mult), reads=[k + "P2r", k + "Hr"], writes=[k + "P1r"])
        P.op("dve", lambda e: e.tensor_tensor(out=bs.P1i[:], in0=bs.P2i[:], in1=bs.Hi[:], op=ALU.mult), reads=[k + "P2i", k + "Hi"], writes=[k + "P1i"])
        P.op("dve", lambda e: e.tensor_tensor(out=bs.P1r[:], in0=bs.P1r[:], in1=bs.P1i[:], op=ALU.subtract), reads=[k + "P1r", k + "P1i"], writes=[k + "P1r"])
        yield
        P.op("pool", lambda e: e.tensor_tensor(out=bs.P1i[:], in0=bs.P2i[:], in1=bs.Hr[:], op=ALU.mult), reads=[k + "P2i", k + "Hr", k + "P1r"], writes=[k + "P1i"])
        P.op("dve", lambda e: e.tensor_tensor(out=bs.P2r[:], in0=bs.P2r[:], in1=bs.Hi[:], op=ALU.mult), reads=[k + "P2r", k + "Hi"], writes=[k + "P2r"])
        P.op("dve", lambda e: e.tensor_tensor(out=bs.P1i[:], in0=bs.P1i[:], in1=bs.P2r[:], op=ALU.add), reads=[k + "P1i", k + "P2r"], writes=[k + "P1i"])
        yield
        for c2 in range(CBH // 2):
            for h in range(2):
                ch = 2 * c2 + h
                P.op("pe", lambda e: e.matmul(bs.ps1[:, h * 256:(h + 1) * 256], bs.P1r[:, ch, :], TI1a, start=True, stop=False),
                     reads=[k + "P1r", "tabs"], writes=[k + "ps1"])
                P.op("pe", lambda e: e.matmul(bs.ps1[:, h * 256:(h + 1) * 256], bs.P1i[:, ch, :], TI1b, start=False, stop=True),
                     reads=[k + "P1i", "tabs"], writes=[k + "ps1"], pe_acc=True)
            P.op("act", lambda e: e.activation(out=bs.A[:, 2 * c2:2 * c2 + 2, :].rearrange("p c k -> p (c k)"), in_=bs.ps1[:], func=AF.Copy),
                 reads=[k + "ps1"], writes=[k + "A"])
            yield
        emit_cmul(P, bs.P2r[:], bs.P2i[:], bs.A[:, :, 0:128], bs.A[:, :, 128:256], twcb, twsb, False,
                  (k + "P2r", k + "P2i", k + "A", k + "A", "tabs"))
        yield
        for b4 in range(CBH * 128 // 512):
            sl = slice(b4 * 512, (b4 + 1) * 512)
            pt = bs.ps1[0:64, :]
            P.op("pe", lambda e: e.matmul(pt, CN, fl(bs.P2r)[:, sl], start=True, stop=False), reads=["tabs", k + "P2r"], writes=[k + "ps1"])
            P.op("pe", lambda e: e.matmul(pt, NSN, fl(bs.P2i)[:, sl], start=False, stop=True), reads=["tabs", k + "P2i"], writes=[k + "ps1"], pe_acc=True)
            P.op("act", lambda e: e.activation(out=bs.Yo[:].rearrange("p c k -> p (c k)")[:, sl], in_=pt, func=AF.Copy),
                 reads=[k + "ps1"], writes=[k + "Yo"])
            yield
        P.dma("sp", c_dram[ch0:ch0 + CBH, :].rearrange("c (m1 m2) -> m1 c m2", m2=128), bs.Yo[:], reads=[k + "Yo"], writes=["c_dram"])
        yield

    def run_fft_conv():
        batches = list(range(0, 128, CBH))
        for i0 in range(0, len(batches), 2):
            gens = [fft_conv_batch(bsets[j], batches[i0 + j]) for j in range(2) if i0 + j < len(batches)]
            alive = gens
            while alive:
                nxt = []
                for gen in alive:
                    try:
                        next(gen)
                        nxt.append(gen)
                    except StopIteration:
                        pass
                alive = nxt

    SEGW = 2048
    R = P.sb("R", [128, SEGW + 2]); ACC = P.sb("ACC", [128, SEGW]); GT = P.sb("GT", [128, SEGW]); CT = P.sb("CT", [128, SEGW])
    a_ctx = P.sb("a_ctx", [128, CTX]); cc1 = P.sb("cc1", [128, CTX]); cc2 = P.sb("cc2", [128, CTX])
    fwc = P.sb("fwc", [128, CTX]); bwc = P.sb("bwc", [128, CTX])
    gblk = P.sb("gblk", [128, 512]); grev = P.sb("grev", [128, 512]); dblk = P.sb("dblk", [128, 512]); zcol = P.sb("zcol", [128, 1])
    P.op("dve", lambda e: e.memset(zcol[:], 0.0), writes=["zcol"])
    segs = [(0, CTX)] + [(CTX + i * SEGW, SEGW) for i in range(SEQ // SEGW)]

    for tl in range(2):
        rows = slice(tl * 128, (tl + 1) * 128)
        for part in range(3):
            for (c0, w_) in segs:
                first = c0 in (0, CTX); last = (c0 + w_) in (CTX, T)
                if first or last:
                    P.op("dve", lambda e: e.memset(R[:, 0:w_ + 2], 0.0), writes=["R"])
                lo = c0 - (0 if first else 1); hi = c0 + w_ + (0 if last else 1)
                P.dma("sp", R[:, (1 if first else 0):(1 if first else 0) + hi - lo], zh_in[part, rows, lo:hi], writes=["R"])
                P.op("dve", lambda e: e.tensor_scalar(out=ACC[:, :w_], in0=R[:, 1:w_ + 1], scalar1=cw[:, tl, part, 1:2], scalar2=None,
                                                      op0=ALU.mult), reads=["R", "cw"], writes=["ACC"])
                P.op("dve", lambda e: e.scalar_tensor_tensor(out=ACC[:, :w_], in0=R[:, 0:w_], scalar=cw[:, tl, part, 0:1], in1=ACC[:, :w_],
                                                             op0=ALU.mult, op1=ALU.add), reads=["R", "cw", "ACC"], writes=["ACC"])
                P.op("dve", lambda e: e.scalar_tensor_tensor(out=ACC[:, :w_], in0=R[:, 2:w_ + 2], scalar=cw[:, tl, part, 2:3], in1=ACC[:, :w_],
                                                             op0=ALU.mult, op1=ALU.add), reads=["R", "cw", "ACC"], writes=["ACC"])
                if part == 0:
                    if c0 == 0:
                        P.op("act", lambda e: e.activation(out=a_ctx[:], in_=ACC[:, :CTX], func=AF.Copy), reads=["ACC"], writes=["a_ctx"])
                    else:
                        P.dma("act", a_dram[:, c0 - CTX:c0 - CTX + w_], ACC[:, :w_], reads=["ACC"], writes=["a_dram"])
                else:
                    dst = x1c if part == 1 else x2c
                    P.dma("act", dst[:, c0:c0 + w_], ACC[:, :w_], reads=["ACC"], writes=[dst.tensor.name])
        for o in range(2):
            for d in range(2):
                for b0 in range(0, SEQ, 512):
                    P.op("pe", lambda e: e.matmul(psm[:], w4[:, o, d, rows], h3l[:, b0:b0 + 512], start=True, stop=True),
                         reads=["w4", "h3"], writes=["psm"])
                    P.dma("sp", dblk[:], dl_in[rows, b0:b0 + 512], writes=["dblk"])
                    P.op("dve", lambda e: e.tensor_tensor(out=gblk[:], in0=psm[:], in1=dblk[:], op=ALU.mult), reads=["psm", "dblk"], writes=["gblk"])
                    if d == 0:
                        P.dma("act", g_dram[:, b0:b0 + 512], gblk[:], reads=["gblk"], writes=["g_dram"])
                    else:
                        P.op("pool", lambda e: e.tensor_copy(out=grev[:], in_=gblk[:, ::-1]), reads=["gblk"], writes=["grev"])
                        if b0 == 0:
                            P.dma("act", g_dram[:, NFFT - 511:NFFT], grev[:, 0:511], reads=["grev"], writes=["g_dram"])
                        else:
                            P.dma("act", g_dram[:, NFFT - b0 - 511:NFFT - b0 + 1], grev[:], reads=["grev"], writes=["g_dram"])
                P.op("pe", lambda e: e.matmul(psm[:, :CTX], w4[:, o, d, rows], h3c[:], start=True, stop=True), reads=["w4", "h3"], writes=["psm"])
                P.dma("sp", dblk[:, :CTX], dc_in[rows, :], writes=["dblk"])
                fc = fwc if d == 0 else bwc
                P.op("dve", lambda e: e.tensor_tensor(out=fc[:], in0=psm[:, :CTX], in1=dblk[:, :CTX], op=ALU.mult),
                     reads=["psm", "dblk"], writes=["fwc" if d == 0 else "bwc"])
            P.dma("act", g_dram[:, SEQ:SEQ + 1], zcol[:], reads=["zcol"], writes=["g_dram"], allow_slow_non_contiguous=True)
            run_fft_conv()
            P.op("dve", lambda e: e.memset(cc1[:], 0.0), writes=["cc1"])
            P.op("pool", lambda e: e.memset(cc2[:], 0.0), writes=["cc2"])
            for d in range(CTX):
                P.op("dve", lambda e: e.scalar_tensor_tensor(out=cc1[:, d:CTX], in0=a_ctx[:, 0:CTX - d], scalar=fwc[:, d:d + 1],
                                                             in1=cc1[:, d:CTX], op0=ALU.mult, op1=ALU.add),
                     reads=["a_ctx", "fwc", "cc1"], writes=["cc1"])
                if d >= 1:
                    P.op("dve", lambda e: e.scalar_tensor_tensor(out=cc2[:, 0:CTX - d], in0=a_ctx[:, d:CTX], scalar=bwc[:, d:d + 1],
                                                                  in1=cc2[:, 0:CTX - d], op0=ALU.mult, op1=ALU.add),
                         reads=["a_ctx", "bwc", "cc2"], writes=["cc2"])
            P.op("dve", lambda e: e.tensor_tensor(out=cc1[:], in0=cc1[:], in1=cc2[:], op=ALU.add), reads=["cc1", "cc2"], writes=["cc1"])
            gsrc = x1c if o == 0 else x2c
            for (c0, w_) in segs:
                P.dma("sp", GT[:, :w_], gsrc[:, c0:c0 + w_], reads=[gsrc.tensor.name], writes=["GT"])
                if c0 == 0:
                    csrc = cc1[:]; asrc = a_ctx[:]; ck = "cc1"; ak = "a_ctx"
                else:
                    P.dma("sp", CT[:, :w_], c_dram[:, c0 - CTX:c0 - CTX + w_], reads=["c_dram"], writes=["CT"])
                    P.dma("sp", R[:, :w_], a_dram[:, c0 - CTX:c0 - CTX + w_], reads=["a_dram"], writes=["R"])
                    csrc = CT[:, :w_]; asrc = R[:, :w_]; ck = "CT"; ak = "R"
                P.op("dve", lambda e: e.scalar_tensor_tensor(out=ACC[:, :w_], in0=asrc, scalar=hbias[:, tl, o:o + 1], in1=csrc,
                                                             op0=ALU.mult, op1=ALU.add), reads=[ak, ck, "hbias"], writes=["ACC"])
                P.op("pool", lambda e: e.tensor_tensor(out=ACC[:, :w_], in0=ACC[:, :w_], in1=GT[:, :w_], op=ALU.mult),
                     reads=["ACC", "GT"], writes=["ACC"])
                if o == 0:
                    if c0 == 0:
                        P.op("act", lambda e: e.activation(out=a_ctx[:], in_=ACC[:, :CTX], func=AF.Copy), reads=["ACC"], writes=["a_ctx"])
                    else:
                        P.dma("act", a_dram[:, c0 - CTX:c0 - CTX + w_], ACC[:, :w_], reads=["ACC"], writes=["a_dram"])
                else:
                    P.dma("act", y_out[rows, c0:c0 + w_], ACC[:, :w_], reads=["ACC"], is_output=True)
    return P.finish()


def hyena_consts(L):
    pos = np.arange(L, dtype=np.float32)
    t = pos / np.float32(max(L - 1, 1))
    ang = np.float32(2.0 * math.pi / L) * pos
    bands = np.linspace(1e-4, 15, 16, dtype=np.float32)
    z = np.concatenate([t[:, None], np.cos(ang[:, None] * bands), -np.sin(ang[:, None] * bands)], axis=-1).astype(np.float32)
    mx = math.log(1e-2) / 0.3
    mn = math.log(1e-2) / 1.5
    deltas = np.abs(np.linspace(mn, mx, HYW, dtype=np.float32))
    dec = np.exp(-t[None, :] * deltas[:, None]).astype(np.float32)
    return np.ascontiguousarray(z.T), dec


def fft_tabs():
    p = np.arange(128, dtype=np.float64)[:, None]; j = np.arange(128, dtype=np.float64)[None, :]
    a128 = 2 * np.pi * p * j / 128.0
    aN = 2 * np.pi * p * j / NFFT
    tabs = np.zeros((128, 12, 128), np.float64)
    tabs[:, 0] = np.cos(a128); tabs[:, 1] = -np.sin(a128); tabs[:, 2] = np.sin(a128)
    tabs[:, 3] = np.cos(aN); tabs[:, 4] = np.sin(aN)
    tabs[:, 5] = np.cos(a128) / NFFT; tabs[:, 6] = -np.sin(a128) / NFFT
    tabs[:, 7] = np.cos(a128); tabs[:, 8] = np.sin(a128)
    tabs[:, 9] = -np.sin(a128); tabs[:, 10] = np.cos(a128)
    return tabs.astype(np.float32)


def run_eh(zs, hy_conv, w1, b1, w2, b2, w3, b3, w4, freq, bias):
    zl, decl = hyena_consts(SEQ); zc, decc = hyena_consts(CTX)
    tabs = fft_tabs()
    bfr = np.zeros((HYF, 2, 3), np.float32)
    bfr[:, 0, 0] = b1; bfr[:, 0, 1] = b2; bfr[:, 0, 2] = b3
    bfr[:, 1, :] = np.asarray(freq, np.float32).T
    w4r = np.asarray(w4, np.float32).reshape(HYF, 2, 2, HYW)
    in_maps = []
    for j in range(NCORES):
        ch = slice(256 * j, 256 * (j + 1))
        cwj = np.zeros((128, 2, 3, 3), np.float32)
        for part in range(3):
            blk = np.asarray(hy_conv[:, part * HYW + 256 * j: part * HYW + 256 * (j + 1)], np.float32)
            cwj[:, :, part, :] = blk.T.reshape(2, 128, 3).transpose(1, 0, 2)
        hbj = np.ascontiguousarray(np.asarray(bias[:, ch], np.float32).T.reshape(2, 128, 2).transpose(1, 0, 2))
        in_maps.append({"zh": np.ascontiguousarray(zs[j][:768].reshape(3, 256, T)), "convw": cwj, "hbias": hbj,
                        "w1": np.asarray(w1, np.float32), "w2": np.asarray(w2, np.float32), "w3": np.asarray(w3, np.float32),
                        "bfr": bfr, "w4": np.ascontiguousarray(w4r[:, :, :, ch]), "zposl": zl, "zposc": zc,
                        "decl": np.ascontiguousarray(decl[ch]), "decc": np.ascontiguousarray(decc[ch]), "tabs": tabs})
    res = run(build_eh(), in_maps)
    return [r["yh"] for r in res]


DNC = 128
NCHK = T // DNC
DK = 128


def dn_masks():
    t = np.arange(128)[:, None]; i = np.arange(128)[None, :]
    m = np.zeros((10, 128, 128), np.float32)
    m[0] = (t <= i); m[1] = (t > i)
    m[2] = np.where(t >= i, 0.0, -30000.0)
    m[3] = np.where(i >= t, 0.0, -30000.0)
    m[4] = (t > i)
    m[5] = (t >= i); m[6] = (t < i)
    m[7] = np.where(t <= i, 0.0, -30000.0)
    m[8] = np.where(i <= t, 0.0, -30000.0)
    m[9] = (t < i)
    return np.ascontiguousarray(m.transpose(1, 0, 2))


def build_ed(dbg=None):
    P = Prog()
    dbg = dbg or {}
    nc = P.nc
    qkvz_in = P.dram_in("qkvz", [4, 2, 128, T])
    braw_in = P.dram_in("braw", [4, T]); araw_in = P.dram_in("araw", [4, T])
    cw_in = P.dram_in("dconv", [128, 3, 2, 5])
    aA_in = P.dram_in("aA", [4, 2])
    nw_in = P.dram_in("normw", [128, 128])
    mk_in = P.dram_in("masks", [128, 10, 128])
    id_in = P.dram_in("ident", [128, 128])
    y_out = P.dram_out("yd", [2, 128, T])
    qd = nc.dram_tensor("qd", [2, 128, T], F32, kind="Internal").ap()
    kd = nc.dram_tensor("kd", [2, 128, T], F32, kind="Internal").ap()
    vd = nc.dram_tensor("vd", [2, 128, T], F32, kind="Internal").ap()
    dsc = {0: qd, 1: kd, 2: vd}

    def ld(name, src, shape, q="sp"):
        t = P.sb(name, shape)
        P.dma(q, t[:], src, writes=[name])
        return t
    cw = ld("cw", cw_in, [128, 3, 2, 5]); aA = ld("aA", aA_in, [4, 2]); nw = ld("nw", nw_in, [128, 128])
    mk = ld("mk", mk_in, [128, 10, 128]); ident = ld("ident", id_in, [128, 128])
    ones128 = P.sb("ones128", [128, 128])
    P.op("dve", lambda e: e.memset(ones128[:], 1.0), writes=["ones128"])
    PS = [P.ps("PS%d" % i, [128, 512]) for i in range(8)]
    pctr = [0]

    def pslot():
        n = pctr[0]; pctr[0] += 1
        b, s = n % 8, (n // 8) % 4
        return PS[b][:, s * 128:(s + 1) * 128], ("PS", b, s)

    SG = 22 * DNC
    bg = P.sb("bg", [4, 2, SG])
    tA = P.sb("tA", [4, SG]); tB = P.sb("tB", [4, SG])
    nA = P.sb("nA", [4, 1])
    P.op("act", lambda e: e.activation(out=nA[:], in_=aA[:, 0:1], func=AF.Exp), reads=["aA"], writes=["nA"])
    P.op("dve", lambda e: e.tensor_scalar(out=nA[:], in0=nA[:], scalar1=-1.0, scalar2=None, op0=ALU.mult), reads=["nA"], writes=["nA"])
    BGT = P.sb("BGT", [128, NCHK, 2, 4])
    for sgi in range(3):
        s0 = sgi * SG
        P.dma("sp", tA[:], braw_in[:, s0:s0 + SG], writes=["tA"])
        P.op("act", lambda e: e.activation(out=bg[:, 0, :], in_=tA[:], func=AF.Sigmoid), reads=["tA"], writes=["bg"])
        P.dma("sp", tB[:], araw_in[:, s0:s0 + SG], writes=["tB"])
        P.op("dve", lambda e: e.tensor_scalar(out=tB[:], in0=tB[:], scalar1=aA[:, 1:2], scalar2=None, op0=ALU.add), reads=["tB", "aA"], writes=["tB"])
        P.op("act", lambda e: e.activation(out=tA[:], in_=tB[:], func=AF.Abs), reads=["tB", "bg"], writes=["tA"])
        P.op("act", lambda e: e.activation(out=tA[:], in_=tA[:], func=AF.Exp, scale=-1.0), reads=["tA"], writes=["tA"])
        P.op("dve", lambda e: e.tensor_scalar_add(out=tA[:], in0=tA[:], scalar1=1.0), reads=["tA"], writes=["tA"])
        P.op("act", lambda e: e.activation(out=tA[:], in_=tA[:], func=AF.Ln), reads=["tA"], writes=["tA"])
        P.op("dve", lambda e: e.tensor_scalar_max(out=tB[:], in0=tB[:], scalar1=0.0), reads=["tB"], writes=["tB"])
        P.op("dve", lambda e: e.tensor_tensor(out=tB[:], in0=tB[:], in1=tA[:], op=ALU.add), reads=["tA", "tB"], writes=["tB"])
        P.op("dve", lambda e: e.tensor_scalar(out=bg[:, 1, :], in0=tB[:], scalar1=nA[:, 0:1], scalar2=None, op0=ALU.mult),
             reads=["tB", "nA"], writes=["bg"])
        for cl in range(22):
            c = sgi * 22 + cl
            for w_ in range(2):
                pt, pk = pslot()
                P.op("pe", lambda e: e.matmul(pt[:, 0:4], bg[:, w_, cl * DNC:(cl + 1) * DNC], ident[0:4, 0:4], start=True, stop=True), reads=["bg", "ident"], writes=[pk])
                P.op("dve", lambda e: e.tensor_copy(out=BGT[:, c, w_, :], in_=pt[:, 0:4]), reads=[pk], writes=["BGT"])
    NBG = P.sb("NBG", [128, NCHK, 4])
    P.op("dve", lambda e: e.tensor_scalar(out=NBG[:], in0=BGT[:, :, 0, :], scalar1=-1.0, scalar2=None, op0=ALU.mult), reads=["BGT"], writes=["NBG"])

    SEGW = 2048
    R = P.sb("R", [128, SEGW + 4]); ACC = P.sb("ACC", [128, SEGW]); SQ = P.sb("SQ", [128, SEGW]); RS = P.sb("RS", [128, 512])
    segs = [(0, CTX)] + [(CTX + i * SEGW, SEGW) for i in range(SEQ // SEGW)]
    for hd in range(2):
        for part in range(3):
            for (c0, w_) in segs:
                first = c0 in (0, CTX); last = (c0 + w_) in (CTX, T)
                if first or last:
                    P.op("pool", lambda e: e.memset(R[:, 0:w_ + 4], 0.0), writes=["R"])
                lo = c0 - (0 if first else 2); hi = c0 + w_ + (0 if last else 2)
                off = 2 if first else 0
                P.dma("sp", R[:, off:off + hi - lo], qkvz_in[part, hd, :, lo:hi], writes=["R"])
                P.op("dve", lambda e: e.tensor_scalar(out=ACC[:, :w_], in0=R[:, 0:w_], scalar1=cw[:, part, hd, 0:1], scalar2=None, op0=ALU.mult),
                     reads=["R", "cw"], writes=["ACC"])
                for k in range(1, 5):
                    P.op("dve", lambda e: e.scalar_tensor_tensor(out=ACC[:, :w_], in0=R[:, k:k + w_], scalar=cw[:, part, hd, k:k + 1],
                                                                 in1=ACC[:, :w_], op0=ALU.mult, op1=ALU.add),
                         reads=["R", "cw", "ACC"], writes=["ACC"])
                P.op("act", lambda e: e.activation(out=ACC[:, :w_], in_=ACC[:, :w_], func=AF.Silu), reads=["ACC"], writes=["ACC"])
                if part < 2:
                    P.op("pool", lambda e: e.tensor_tensor(out=SQ[:, :w_], in0=ACC[:, :w_], in1=ACC[:, :w_], op=ALU.mult), reads=["ACC"], writes=["SQ"])
                    for b0 in range(0, w_, 512):
                        bw = min(512, w_ - b0)
                        pb = PS[pctr[0] % 8]; pbk = ("PS", pctr[0] % 8, 0); pctr[0] += 1
                        allk = [("PS", pbk[1], s_) for s_ in range(4)]
                        P.op("pe", lambda e: e.matmul(pb[:, :bw], ones128[:], SQ[:, b0:b0 + bw], start=True, stop=True),
                             reads=["SQ", "ones128"], writes=allk)
                        P.op("dve", lambda e: e.tensor_scalar_add(out=RS[:, :bw], in0=pb[:, :bw], scalar1=1e-6),
                             reads=allk, writes=["RS"])
                        P.op("act", lambda e: e.activation(out=RS[:, :bw], in_=RS[:, :bw], func=AF.Sqrt), reads=["RS"], writes=["RS"])
                        P.op("dve", lambda e: e.reciprocal(out=RS[:, :bw], in_=RS[:, :bw]), reads=["RS"], writes=["RS"])
                        if part == 0:
                            P.op("dve", lambda e: e.scalar_tensor_tensor(out=ACC[:, b0:b0 + bw], in0=ACC[:, b0:b0 + bw], scalar=DK ** -0.5,
                                                                         in1=RS[:, :bw], op0=ALU.mult, op1=ALU.mult),
                                 reads=["ACC", "RS"], writes=["ACC"])
                        else:
                            P.op("dve", lambda e: e.tensor_tensor(out=ACC[:, b0:b0 + bw], in0=ACC[:, b0:b0 + bw], in1=RS[:, :bw], op=ALU.mult),
                                 reads=["ACC", "RS"], writes=["ACC"])
                P.dma("act", dsc[part][hd, :, c0:c0 + w_], ACC[:, :w_], reads=["ACC"], writes=[dsc[part].tensor.name])

    if dbg.get('stop') == 'pre':
        P.dma('sp', y_out[0, :, 0:128], ident[:], reads=['ident'], is_output=True)
        return P.finish()
    Oh = [P.sb("O%d" % h, [128, NCHK, 128]) for h in range(2)]
    for h in range(2):
        P.op("pool", lambda e: e.memset(Oh[h][:], 0.0), writes=[("O", h, c) for c in range(NCHK)])

    def mm(lhsT, lk, rhs, rk):
        pt, pk = pslot()
        n = rhs.shape[-1]
        m = lhsT.shape[-1]
        pt = pt[0:m, 0:n]
        P.op("pe", lambda e: e.matmul(pt, lhsT, rhs, start=True, stop=True), reads=lk + rk, writes=[pk])
        return pt, pk

    def dvop(fn, reads, writes, eng="dve"):
        P.op(eng, fn, reads=reads, writes=writes)

    class St:
        pass
    streams = []
    for hd in range(2):
        for dr in range(2):
            st = St()
            sid = "s%d%d_" % (hd, dr)
            st.sid = sid; st.hd = hd; st.dr = dr
            def T_(name, shape=(128, 128), sid=sid):
                return P.sb(sid + name, list(shape))
            st.qT = [T_("qT%d" % i) for i in range(2)]; st.kT = [T_("kT%d" % i) for i in range(2)]; st.vT = [T_("vT%d" % i) for i in range(2)]
            st.ktm = T_("ktm"); st.bv = T_("bv"); st.kbg = T_("kbg"); st.kdec = T_("kdec"); st.gMC = T_("gMC")
            st.dec = T_("dec"); st.decT = T_("decT")
            st.Xs = [T_("X%d" % i) for i in range(2)]; st.Ys = [T_("Y%d" % i) for i in range(2)]
            st.Pm = [T_("Pm%d" % i) for i in range(2)]; st.PTm = [T_("PTm%d" % i) for i in range(2)]
            st.usb = T_("usb"); st.wTs = T_("wTs"); st.vnew = T_("vnew"); st.o1 = T_("o1"); st.qkm = T_("qkm")
            st.cols = T_("cols", (128, 8)); st.S = T_("S")
            st.order = list(range(NCHK)) if dr == 0 else [1, 0] + list(range(NCHK - 1, 1, -1))
            if 'nchunks' in dbg:
                st.order = st.order[:dbg['nchunks']]
            P.op("pool", lambda e: e.memset(st.S[:], 0.0), writes=[sid + "S"])
            streams.append(st)

    def chunk_gen(st, c, b):
        sid = st.sid; hd = st.hd; dr = st.dr
        def K_(n):
            return sid + n
        row = dr * 2 + hd
        mo = 5 * dr
        MC = mk[:, mo + 0, :]; MS = mk[:, mo + 1, :]; NEG = mk[:, mo + 2, :]; STRICT = mk[:, mo + 4, :]
        qT, kT, vT = st.qT[b], st.kT[b], st.vT[b]
        cols = st.cols; S = st.S
        t0 = c * DNC
        P.dma("sp", qT[:], qd[hd, :, t0:t0 + DNC], reads=["qd"], writes=[K_("qT%d" % b)])
        P.dma("sp", kT[:], kd[hd, :, t0:t0 + DNC], reads=["kd"], writes=[K_("kT%d" % b)])
        P.dma("sp", vT[:], vd[hd, :, t0:t0 + DNC], reads=["vd"], writes=[K_("vT%d" % b)])
        qk_, kk_, vk_ = [K_("qT%d" % b)], [K_("kT%d" % b)], [K_("vT%d" % b)]
        beta = BGT[:, c, 0, row:row + 1]; g = BGT[:, c, 1, row:row + 1]; nbeta = NBG[:, c, row:row + 1]
        yield
        pt, pk = mm(kT[:], kk_, ident[:], ["ident"])
        dvop(lambda e: e.activation(out=st.ktm[:], in_=pt, func=AF.Copy), [pk], [K_("ktm")], "act")
        pt, pk = mm(vT[:], vk_, ident[:], ["ident"])
        dvop(lambda e: e.tensor_scalar(out=st.bv[:], in0=pt, scalar1=beta, scalar2=None, op0=ALU.mult), [pk, "BGT"], [K_("bv")])
        yield
        pt, pk = mm(MC, ["mk"], g, ["BGT"])
        dvop(lambda e: e.tensor_copy(out=cols[:, 0:1], in_=pt[:, 0:1]), [pk], [K_("cols")])
        pt, pk = mm(ones128[:], ["ones128"], g, ["BGT"])
        dvop(lambda e: e.tensor_copy(out=cols[:, 3:4], in_=pt[:, 0:1]), [pk], [K_("cols")])
        dvop(lambda e: e.tensor_scalar(out=st.gMC[:], in0=MC, scalar1=g, scalar2=None, op0=ALU.mult), ["mk", "BGT"], [K_("gMC")])
        yield
        dvop(lambda e: e.activation(out=cols[:, 1:2], in_=cols[:, 0:1], func=AF.Exp), [K_("cols")], [K_("cols")], "act")
        dvop(lambda e: e.activation(out=cols[:, 4:5], in_=cols[:, 3:4], func=AF.Exp), [K_("cols")], [K_("cols")], "act")
        dvop(lambda e: e.activation(out=cols[:, 5:6], in_=cols[:, 0:1], func=AF.Exp, scale=-1.0, bias=cols[:, 3:4]), [K_("cols")], [K_("cols")], "act")
        pt, pk = mm(st.gMC[:], [K_("gMC")], MS, ["mk"])
        dvop(lambda e: e.tensor_tensor(out=st.dec[:], in0=pt, in1=NEG, op=ALU.add), [pk, "mk"], [K_("dec")])
        yield
        dvop(lambda e: e.tensor_tensor(out=cols[:, 2:3], in0=cols[:, 1:2], in1=beta, op=ALU.mult), [K_("cols"), "BGT"], [K_("cols")])
        dvop(lambda e: e.activation(out=st.dec[:], in_=st.dec[:], func=AF.Exp), [K_("dec")], [K_("dec")], "act")
        dvop(lambda e: e.tensor_scalar(out=st.kbg[:], in0=st.ktm[:], scalar1=cols[:, 2:3], scalar2=None, op0=ALU.mult), [K_("ktm"), K_("cols")], [K_("kbg")])
        dvop(lambda e: e.tensor_scalar(out=st.kdec[:], in0=st.ktm[:], scalar1=cols[:, 5:6], scalar2=0.0, op0=ALU.mult, op1=ALU.add),
             [K_("ktm"), K_("cols")], [K_("kdec")], "pool")
        yield
        pt, pk = mm(st.dec[:], [K_("dec")], ident[:], ["ident"])
        dvop(lambda e: e.tensor_copy(out=st.decT[:], in_=pt), [pk], [K_("decT")])
        X, Y = st.Xs[0], st.Ys[0]
        pt, pk = mm(kT[:], kk_, kT[:], kk_)
        dvop(lambda e: e.tensor_tensor(out=X[:], in0=pt, in1=st.dec[:], op=ALU.mult), [pk, K_("dec")], [K_("X0")])
        dvop(lambda e: e.scalar_tensor_tensor(out=X[:], in0=X[:], scalar=nbeta, in1=STRICT, op0=ALU.mult, op1=ALU.mult),
             [K_("X0"), "NBG", "mk"], [K_("X0")])
        yield
        pt, pk = mm(X[:], [K_("X0")], ident[:], ["ident"])
        dvop(lambda e: e.activation(out=Y[:], in_=pt, func=AF.Copy), [pk], [K_("Y0")], "act")
        dvop(lambda e: e.tensor_tensor(out=st.Pm[0][:], in0=X[:], in1=ident[:], op=ALU.add), [K_("X0"), "ident"], [K_("Pm0")])
        yield
        dvop(lambda e: e.tensor_tensor(out=st.PTm[0][:], in0=Y[:], in1=ident[:], op=ALU.add), [K_("Y0"), "ident"], [K_("PTm0")], "pool")
        cur = 0
        for lv in range(6):
            nx = 1 - cur
            ptx, pkx = mm(st.Ys[cur][:], [K_("Y%d" % cur)], st.Xs[cur][:], [K_("X%d" % cur)])
            pty, pky = mm(st.Xs[cur][:], [K_("X%d" % cur)], st.Ys[cur][:], [K_("Y%d" % cur)])
            dvop(lambda e: e.tensor_copy(out=st.Xs[nx][:], in_=ptx), [pkx], [K_("X%d" % nx)])
            dvop(lambda e: e.activation(out=st.Ys[nx][:], in_=pty, func=AF.Copy), [pky], [K_("Y%d" % nx)], "act")
            yield
            if lv < 5:
                ptp, pkp = mm(st.PTm[cur][:], [K_("PTm%d" % cur)], st.Xs[nx][:], [K_("X%d" % nx)])
                dvop(lambda e: e.tensor_tensor(out=st.Pm[nx][:], in0=ptp, in1=st.Pm[cur][:], op=ALU.add), [pkp, K_("Pm%d" % cur)], [K_("Pm%d" % nx)])
            ptq, pkq = mm(st.Pm[cur][:], [K_("Pm%d" % cur)], st.Ys[nx][:], [K_("Y%d" % nx)])
            dvop(lambda e: e.tensor_tensor(out=st.PTm[nx][:], in0=ptq, in1=st.PTm[cur][:], op=ALU.add), [pkq, K_("PTm%d" % cur)], [K_("PTm%d" % nx)])
            cur = nx
            yield
        TinvT = st.PTm[cur]; tk = [K_("PTm%d" % cur)]
        pt, pk = mm(TinvT[:], tk, st.bv[:], [K_("bv")])
        dvop(lambda e: e.tensor_copy(out=st.usb[:], in_=pt), [pk], [K_("usb")])
        pt, pk = mm(st.kbg[:], [K_("kbg")], TinvT[:], tk)
        dvop(lambda e: e.activation(out=st.wTs[:], in_=pt, func=AF.Copy), [pk], [K_("wTs")], "act")
        yield
        pt, pk = mm(st.wTs[:], [K_("wTs")], S[:], [K_("S")])
        dvop(lambda e: e.tensor_tensor(out=st.vnew[:], in0=st.usb[:], in1=pt, op=ALU.subtract), [K_("usb"), pk], [K_("vnew")])
        pt, pk = mm(qT[:], qk_, S[:], [K_("S")])
        dvop(lambda e: e.activation(out=st.o1[:], in_=pt, func=AF.Copy, scale=cols[:, 1:2]), [pk, K_("cols")], [K_("o1")], "act")
        pt, pk = mm(kT[:], kk_, qT[:], qk_)
        dvop(lambda e: e.tensor_tensor(out=st.qkm[:], in0=pt, in1=st.decT[:], op=ALU.mult), [pk, K_("decT")], [K_("qkm")])
        yield
        pt, pk = mm(st.qkm[:], [K_("qkm")], st.vnew[:], [K_("vnew")])
        dvop(lambda e: e.tensor_tensor(out=st.o1[:], in0=pt, in1=st.o1[:], op=ALU.add), [pk, K_("o1")], [K_("o1")])
        dvop(lambda e: e.tensor_tensor(out=Oh[hd][:, c, :], in0=Oh[hd][:, c, :], in1=st.o1[:], op=ALU.add), [("O", hd, c), K_("o1")], [("O", hd, c)], "pool")
        pt, pk = mm(st.kdec[:], [K_("kdec")], st.vnew[:], [K_("vnew")])
        dvop(lambda e: e.scalar_tensor_tensor(out=S[:], in0=S[:], scalar=cols[:, 4:5], in1=pt, op0=ALU.mult, op1=ALU.add),
             [K_("S"), K_("cols"), pk], [K_("S")])
        yield

    nsteps = len(streams[0].order)
    for k in range(nsteps):
        gens = [chunk_gen(st, st.order[k], k % 2) for st in streams]
        alive = list(gens)
        while alive:
            nxt = []
            for gen in alive:
                try:
                    next(gen)
                    nxt.append(gen)
                except StopIteration:
                    pass
            alive = nxt

    if dbg.get('stop') == 'scan':
        P.dma('sp', y_out[0, :, 0:128], ident[:], reads=['ident'], is_output=True)
        return P.finish()
    zt = [P.sb("zt%d" % i, [128, 128]) for i in range(3)]
    ob_ = [P.sb("ob%d" % i, [128, 128]) for i in range(3)]
    sqt = P.sb("sqt", [128, 128]); nrm = P.sb("nrm", [128, 128]); gz = P.sb("gz", [128, 128]); ncol = P.sb("ncol", [128, 2])
    for hd in range(2):
        O = Oh[hd]
        for c in range(NCHK):
            b = c % 3
            t0 = c * DNC
            P.dma("sp", zt[b][:], qkvz_in[3, hd, :, t0:t0 + DNC], writes=["zt%d" % b])
            dvop(lambda e: e.tensor_tensor(out=sqt[:], in0=O[:, c, :], in1=O[:, c, :], op=ALU.mult), [("O", hd, c)], ["sqt"], "pool")
            dvop(lambda e: e.reduce_sum(out=ncol[:, 0:1], in_=sqt[:], axis=AX.X), ["sqt"], ["ncol"])
            dvop(lambda e: e.tensor_scalar(out=ncol[:, 1:2], in0=ncol[:, 0:1], scalar1=1.0 / 128, scalar2=1e-6, op0=ALU.mult, op1=ALU.add),
                 ["ncol"], ["ncol"])
            dvop(lambda e: e.activation(out=ncol[:, 1:2], in_=ncol[:, 1:2], func=AF.Sqrt), ["ncol"], ["ncol"], "act")
            dvop(lambda e: e.reciprocal(out=ncol[:, 1:2], in_=ncol[:, 1:2]), ["ncol"], ["ncol"])
            dvop(lambda e: e.scalar_tensor_tensor(out=nrm[:], in0=O[:, c, :], scalar=ncol[:, 1:2], in1=nw[:], op0=ALU.mult, op1=ALU.mult),
                 [("O", hd, c), "ncol", "nw"], ["nrm"])
            pt, pk = mm(zt[b][:], ["zt%d" % b], ident[:], ["ident"])
            dvop(lambda e: e.activation(out=gz[:], in_=pt, func=AF.Silu), [pk], ["gz"], "act")
            dvop(lambda e: e.tensor_tensor(out=nrm[:], in0=nrm[:], in1=gz[:], op=ALU.mult), ["nrm", "gz"], ["nrm"])
            pt, pk = mm(nrm[:], ["nrm"], ident[:], ["ident"])
            dvop(lambda e: e.tensor_copy(out=ob_[b][:], in_=pt), [pk], ["ob%d" % b])
            P.dma("act", y_out[hd, :, t0:t0 + DNC], ob_[b][:], reads=["ob%d" % b], is_output=True)
    return P.finish()


def run_ed(zs, dn_conv, a_log, dt_bias, norm_w, dbg=None):
    masks = dn_masks()
    ident = np.eye(128, dtype=np.float32)
    nwr = np.ascontiguousarray(np.broadcast_to(np.asarray(norm_w, np.float32)[None, :], (128, 128)))
    in_maps = []
    for j in range(NCORES):
        zz = zs[j]
        qkvz = np.ascontiguousarray(zz[768:1792].reshape(4, 2, 128, T))
        ab = zz[1792:1800]
        braw = np.ascontiguousarray(ab[0:4]); araw = np.ascontiguousarray(ab[4:8])
        cwj = np.zeros((128, 3, 2, 5), np.float32)
        for part in range(3):
            blk = np.asarray(dn_conv[:, part * DNW + 256 * j: part * DNW + 256 * (j + 1)], np.float32)
            cwj[:, part, :, :] = blk.T.reshape(2, 128, 5).transpose(1, 0, 2)
        aA = np.zeros((4, 2), np.float32)
        for dr in range(2):
            for hd in range(2):
                aA[dr * 2 + hd, 0] = a_log[dr, 2 * j + hd]
                aA[dr * 2 + hd, 1] = dt_bias[dr, 2 * j + hd]
        in_maps.append({"qkvz": qkvz, "braw": braw, "araw": araw, "dconv": cwj, "aA": aA, "normw": nwr, "masks": masks, "ident": ident})
    res = run(build_ed(dbg), in_maps)
    return [r["yd"].reshape(256, T) for r in res]


GRID_W = 64


def lat_to_col_major(aT):
    out = aT.copy()
    lat = aT[:, CTX:]
    out[:, CTX:] = lat.reshape(lat.shape[0], SEQ // GRID_W, GRID_W).transpose(0, 2, 1).reshape(lat.shape[0], SEQ)
    return out


def lat_from_col_major(aT):
    out = aT.copy()
    lat = aT[:, CTX:]
    out[:, CTX:] = lat.reshape(lat.shape[0], GRID_W, SEQ // GRID_W).transpose(0, 2, 1).reshape(lat.shape[0], SEQ)
    return out


def kernel(x, c, ctx, c_ctx, ada_w, ada_b, ln_g, ln_b, ev_w_in, ev_w_out, hy_conv, hy_w1, hy_b1, hy_w2, hy_b2, hy_w3, hy_b3,
           hy_w4, hy_freq, hy_bias, dn_conv, dn_a_log, dn_dt_bias, dn_norm_w, s5_lam_re, s5_lam_im, s5_log_dt, s5_b_re, s5_b_im,
           s5_c_re, s5_c_im, s5_d, od_w_glu, moe_router, moe_w_in, moe_w_out):
    f32 = np.float32
    modT = run_k0({"c": c, "c_ctx": c_ctx, "ada_w": ada_w, "ada_b": ada_b})
    hT = np.ascontiguousarray(np.concatenate([np.asarray(ctx, f32)[0], np.asarray(x, f32)[0]], axis=0).T)
    for l in range(DEPTH):
        col = (l // 2) % 2 == 1
        i = l // 2
        mod_l = np.ascontiguousarray(modT[l])
        if l % 2 == 0:
            uT = run_mod(hT, mod_l)
            if col:
                uT = lat_to_col_major(uT)
            zs = run_ea1(uT, np.asarray(ev_w_in[i], f32))
            yh = run_eh(zs, hy_conv[i], hy_w1[i], hy_b1[i], hy_w2[i], hy_b2[i], hy_w3[i], hy_b3[i], hy_w4[i], hy_freq[i], hy_bias[i])
            yd = run_ed(zs, dn_conv[i], dn_a_log[i], dn_dt_bias[i], dn_norm_w[i])
            fT = np.concatenate(yh + yd, axis=0)
            if col:
                fT = lat_from_col_major(fT)
            h1T, h2T, affT = run_x3(False, fT, hT, mod_l, ev_w_out[i], ln_g[l, 0], ln_b[l, 0], moe_router[l])
        else:
            hp = lat_to_col_major(hT) if col else hT
            fT = run_o2(hp, mod_l, s5_lam_re[i], s5_lam_im[i], s5_log_dt[i], s5_b_re[i], s5_b_im[i], s5_c_re[i], s5_c_im[i], s5_d[i])
            if col:
                fT = lat_from_col_major(fT)
            h1T, h2T, affT = run_x3(True, fT, hT, mod_l, od_w_glu[i], ln_g[l, 0], ln_b[l, 0], moe_router[l])
        hT = run_x4(h1T, h2T, affT, mod_l, ln_g[l, 1], ln_b[l, 1], moe_w_in[l], moe_w_out[l])
    return np.ascontiguousarray(hT[:, CTX:].T)[None].astype(np.float32)
```

```python
import contextlib
import math
import numpy as np
import concourse.bass as bass
import concourse.mybir as mybir
from concourse.bass_utils import run_bass_kernel_spmd

F32 = mybir.dt.float32
BF16 = mybir.dt.bfloat16
AF = mybir.ActivationFunctionType
ALU = mybir.AluOpType
AX = mybir.AxisListType

NCORES = 8
D = 4096
SEQ = 8192
CTX = 256
T = SEQ + CTX
DEPTH = 4
KC = D // 128


class _Stop(Exception):
    pass


class Prog:
    NDSEM = 6

    def __init__(self):
        self.nc = bass.Bass("TRN2", target_bir_lowering=False)
        self.st = contextlib.ExitStack()
        nc = self.nc
        self.eng = {"pe": nc.tensor, "act": nc.scalar, "dve": nc.vector, "pool": nc.gpsimd, "sp": nc.sync}
        self.sem = {}
        self.cnt = {}
        for e in ("pe", "act", "dve", "pool"):
            self.sem[e] = self.st.enter_context(nc.semaphore("s_" + e))
            self.cnt[e] = 0
        self.dsem = {}
        self.dcnt = {}
        for q in ("sp", "pool", "act"):
            self.dsem[q] = [self.st.enter_context(nc.semaphore("d_%s%d" % (q, i))) for i in range(self.NDSEM)]
            self.dcnt[q] = 0
        self.waited = {}
        self.last_w = {}
        self.readers = {}
        self.out_events = []
        self.n_ins = 0

    def sb(self, name, shape, dt=F32):
        return self.st.enter_context(self.nc.sbuf_tensor("sb_" + name, list(shape), dt))

    def ps(self, name, shape, dt=F32):
        return self.st.enter_context(self.nc.psum_tensor("pp_" + name, list(shape), dt))

    def dram_in(self, name, shape, dt=F32):
        return self.nc.dram_tensor(name, list(shape), dt, kind="ExternalInput").ap()

    def dram_out(self, name, shape, dt=F32):
        return self.nc.dram_tensor(name, list(shape), dt, kind="ExternalOutput").ap()

    def _wait(self, e, ev):
        if ev is None:
            return
        sem, val = ev
        k = (e, sem.name)
        if self.waited.get(k, 0) >= val:
            return
        self.waited[k] = val
        self.eng[e].wait_ge(sem, val)

    def _deps(self, e, reads, writes, pe_acc=False):
        for k in reads:
            self._wait(e, self.last_w.get(k))
        for k in writes:
            lw = self.last_w.get(k)
            if not (pe_acc and lw is not None and lw[0] is self.sem["pe"]):
                self._wait(e, lw)
            for ev in self.readers.get(k, ()):
                self._wait(e, ev)

    def _record(self, ev, reads, writes):
        for k in reads:
            self.readers.setdefault(k, []).append(ev)
            if len(self.readers[k]) > 24:
                self.readers[k] = self.readers[k][-24:]
        for k in writes:
            self.last_w[k] = ev
            self.readers[k] = []

    def op(self, e, ins_fn, reads=(), writes=(), pe_acc=False):
        self._deps(e, reads, writes, pe_acc)
        ins = ins_fn(self.eng[e])
        self.cnt[e] += 1
        ins.then_inc(self.sem[e], 1)
        ev = (self.sem[e], self.cnt[e])
        self._record(ev, reads, writes)
        self.n_ins += 1
        return ev

    def dma(self, q, out, in_, reads=(), writes=(), is_output=False, **kw):
        n = self.dcnt[q]
        sem = self.dsem[q][n % self.NDSEM]
        prev = 16 * (n // self.NDSEM)
        if prev > 0:
            self._wait(q, (sem, prev))
        self._deps(q, reads, writes)
        ins = self.eng[q].dma_start(out=out, in_=in_, **kw)
        ins.then_inc(sem, 16)
        self.dcnt[q] = n + 1
        ev = (sem, prev + 16)
        self._record(ev, reads, writes)
        if is_output:
            self.out_events.append(ev)
        self.n_ins += 1
        return ev

    def finish(self):
        for ev in self.out_events:
            self._wait("sp", ev)
        for e in ("pe", "act", "dve", "pool"):
            if self.cnt[e]:
                self._wait("sp", (self.sem[e], self.cnt[e]))
        for q in ("sp", "pool", "act"):
            n = self.dcnt[q]
            for i in range(self.NDSEM):
                uses = (n - i + self.NDSEM - 1) // self.NDSEM if n > i else 0
                if uses:
                    self._wait("sp", (self.dsem[q][i], 16 * uses))
        self.st.close()
        return self.nc


def run(prog_nc, in_maps):
    res = run_bass_kernel_spmd(prog_nc, in_maps, core_ids=list(range(NCORES)))
    return res.results


NCH_ADA = 6 * D // 128
NCH_ADA_CORE = NCH_ADA // NCORES


def build_k0():
    P = Prog()
    nc = P.nc
    cols_core = NCH_ADA_CORE * 128
    c_in = P.dram_in("c2", [2, 128, KC])
    w_in = P.dram_in("ada_w", [DEPTH, D, cols_core])
    b_in = P.dram_in("ada_b", [DEPTH, 1, cols_core])
    out = P.dram_out("modT", [DEPTH, 128, NCH_ADA_CORE, 2])

    craw = P.sb("craw", [128, 2, KC])
    S = P.sb("S", [128, KC, 2])
    ones = P.sb("ones", [1, 2])
    bias = P.sb("bias", [1, DEPTH, cols_core])
    wt = [P.sb("wt%d" % i, [128, KC, 512]) for i in range(2)]
    ot = [P.sb("ot%d" % i, [128, NCH_ADA_CORE, 2]) for i in range(2)]
    pst = [P.ps("ps%d" % i, [128, 2]) for i in range(4)]

    P.dma("sp", craw[:, 0, :], c_in[0], writes=["craw"])
    P.dma("sp", craw[:, 1, :], c_in[1], writes=["craw"])
    P.dma("sp", bias[:], b_in.rearrange("l o c -> o l c"), writes=["bias"])
    P.op("dve", lambda e: e.memset(ones[:], 1.0), writes=["ones"])
    for s in range(2):
        P.op("act", lambda e: e.activation(out=S[:, :, s], in_=craw[:, s, :], func=AF.Silu),
             reads=["craw"], writes=["S"])
    wv = w_in.rearrange("l (p kc) c -> l p kc c", kc=KC)
    blk = 0
    for l in range(DEPTH):
        o = ot[l % 2]
        for cb in range(cols_core // 512):
            w = wt[blk % 2]
            wk = "wt%d" % (blk % 2)
            P.dma("sp" if blk % 2 == 0 else "act", w[:], wv[l, :, :, cb * 512:(cb + 1) * 512], writes=[wk])
            for sub in range(4):
                ch = cb * 4 + sub
                pt = pst[ch % 4]
                pk = "ps%d" % (ch % 4)
                for kc in range(KC):
                    P.op("pe", lambda e: e.matmul(pt[:], w[:, kc, sub * 128:(sub + 1) * 128], S[:, kc, :],
                                                  start=(kc == 0), stop=False),
                         reads=[wk, "S"], writes=[pk], pe_acc=True)
                c0 = ch * 128
                P.op("pe", lambda e: e.matmul(pt[:], bias[:, l, c0:c0 + 128], ones[:], start=False, stop=True),
                     reads=["bias", "ones"], writes=[pk], pe_acc=True)
                P.op("dve", lambda e: e.tensor_copy(out=o[:, ch, :], in_=pt[:]), reads=[pk], writes=["ot%d" % (l % 2)])
            blk += 1
        P.dma("sp", out[l], o[:], reads=["ot%d" % (l % 2)], is_output=True)
    return P.finish()


def run_k0(inputs):
    cols_core = NCH_ADA_CORE * 128
    c2 = np.stack([np.asarray(inputs["c"], np.float32).reshape(128, KC),
                   np.asarray(inputs["c_ctx"], np.float32).reshape(128, KC)])
    ada_w = inputs["ada_w"]
    ada_b = inputs["ada_b"]
    in_maps = []
    for j in range(NCORES):
        sl = slice(j * cols_core, (j + 1) * cols_core)
        in_maps.append({"c2": c2,
                        "ada_w": np.ascontiguousarray(ada_w[:, :, sl]),
                        "ada_b": np.ascontiguousarray(ada_b[:, None, sl])})
    res = run(build_k0(), in_maps)
    return np.concatenate([r["modT"] for r in res], axis=2)


NTC = CTX // NCORES
NTL = SEQ // NCORES
NT = NTC + NTL


def emit_modulate(P, dst, src, mod, sc1, c, k_shift, k_scale, dkey, skey):
    for (a, b, s) in ((0, NTC, 1), (NTC, NT, 0)):
        P.op("dve", lambda e: e.tensor_scalar(out=dst[:, a:b], in0=src[:, a:b],
                                              scalar1=sc1[:, k_scale * KC + c, s:s + 1],
                                              scalar2=mod[:, k_shift * KC + c, s:s + 1],
                                              op0=ALU.mult, op1=ALU.add),
             reads=[skey, "sc1", "mod"], writes=[dkey])


def load_mod(P, mod_in):
    mod = P.sb("mod", [128, NCH_ADA, 2])
    sc1 = P.sb("sc1", [128, NCH_ADA, 2])
    P.dma("sp", mod[:], mod_in, writes=["mod"])
    P.op("dve", lambda e: e.tensor_scalar_add(out=sc1[:], in0=mod[:], scalar1=1.0), reads=["mod"], writes=["sc1"])
    return mod, sc1


def build_mod(k_shift=0, k_scale=1):
    P = Prog()
    h_in = P.dram_in("hT", [D, NT])
    mod_in = P.dram_in("modT", [128, NCH_ADA, 2])
    u_out = P.dram_out("uT", [D, NT], BF16)
    mod, sc1 = load_mod(P, mod_in)
    ht = [P.sb("ht%d" % i, [128, NT]) for i in range(3)]
    ut = [P.sb("ut%d" % i, [128, NT], BF16) for i in range(3)]
    for c in range(KC):
        i = c % 3
        P.dma("sp", ht[i][:], h_in[c * 128:(c + 1) * 128, :], writes=["ht%d" % i])
        emit_modulate(P, ut[i], ht[i], mod, sc1, c, k_shift, k_scale, "ut%d" % i, "ht%d" % i)
        P.dma("act", u_out[c * 128:(c + 1) * 128, :], ut[i][:], reads=["ut%d" % i], is_output=True)
    return P.finish()


def tok_slices(j):
    return slice(NTC * j, NTC * (j + 1)), slice(CTX + NTL * j, CTX + NTL * (j + 1))


def shard_tokens(aT):
    out = []
    for j in range(NCORES):
        sc, sl = tok_slices(j)
        out.append(np.ascontiguousarray(np.concatenate([aT[:, sc], aT[:, sl]], axis=1)))
    return out


def unshard_tokens(parts):
    rows = parts[0].shape[0]
    full = np.empty((rows, T), parts[0].dtype)
    for j in range(NCORES):
        sc, sl = tok_slices(j)
        full[:, sc] = parts[j][:, :NTC]
        full[:, sl] = parts[j][:, NTC:]
    return full


def run_mod(hT, modT_l):
    hs = shard_tokens(hT)
    res = run(build_mod(), [{"hT": hs[j], "modT": modT_l} for j in range(NCORES)])
    return unshard_tokens([r["uT"] for r in res])


EA_NCOLS = 1920
HYW = 2048
DNW = 2048
TB = 256


def build_ea1(ncols=EA_NCOLS):
    P = Prog()
    nm = ncols // 128
    u_in = P.dram_in("uT", [D, T], BF16)
    w_in = P.dram_in("w", [D, ncols])
    z_out = P.dram_out("zT", [ncols, T])
    W = P.sb("W", [128, KC, ncols], BF16)
    for kc in range(KC):
        P.dma("pool", W[:, kc, :], w_in[kc * 128:(kc + 1) * 128, :], writes=[("W", kc)])
    ub = [P.sb("ub%d" % i, [128, KC, TB], BF16) for i in range(2)]
    st = [P.sb("st%d" % i, [128, nm, TB]) for i in range(2)]
    ps = [P.ps("ps%d" % i, [128, TB]) for i in range(4)]
    uv = u_in.rearrange("(kc p) t -> p kc t", p=128)
    zv = z_out.rearrange("(m p) t -> p m t", p=128)
    nblk = T // TB
    ev = 0
    for tb in range(nblk):
        i = tb % 2
        t0 = tb * TB
        P.dma("sp", ub[i][:], uv[:, :, t0:t0 + TB], writes=["ub%d" % i])
        for m in range(nm):
            pt = ps[ev % 4]
            pk = "ps%d" % (ev % 4)
            for kc in range(KC):
                P.op("pe", lambda e: e.matmul(pt[:], W[:, kc, m * 128:(m + 1) * 128], ub[i][:, kc, :],
                                              start=(kc == 0), stop=(kc == KC - 1)),
                     reads=[("W", kc), "ub%d" % i], writes=[pk], pe_acc=True)
            if ev % 2 == 0:
                P.op("dve", lambda e: e.tensor_copy(out=st[i][:, m, :], in_=pt[:]), reads=[pk], writes=["st%d" % i])
            else:
                P.op("act", lambda e: e.activation(out=st[i][:, m, :], in_=pt[:], func=AF.Copy),
                     reads=[pk], writes=["st%d" % i])
            ev += 1
        P.dma("act", zv[:, :, t0:t0 + TB], st[i][:], reads=["st%d" % i], is_output=True)
    return P.finish()


def ea_cols(j):
    cols = []
    for part in range(3):
        cols += list(range(part * HYW + 256 * j, part * HYW + 256 * (j + 1)))
    base = 3 * HYW
    for part in range(4):
        cols += list(range(base + part * DNW + 256 * j, base + part * DNW + 256 * (j + 1)))
    base = 3 * HYW + 4 * DNW
    for part in range(4):
        cols += [base + part * 16 + 2 * j, base + part * 16 + 2 * j + 1]
    return np.array(cols)


def run_ea1(uT, w_in_l):
    in_maps = []
    for j in range(NCORES):
        cols = ea_cols(j)
        w = np.zeros((D, EA_NCOLS), np.float32)
        w[:, :len(cols)] = w_in_l[:, cols]
        in_maps.append({"uT": uT, "w": w})
    res = run(build_ea1(), in_maps)
    return [r["zT"] for r in res]


ALPHA = (2 * DEPTH) ** 0.25
LN_EPS = 1e-5
NBK = 3
BK = NT // NBK
NEXP = 16


def emit_layernorm(P, zb, outb, g, b, ones128, ps_s, ps_s2, sq, tmp, zkey, okey, width):
    for m in range(KC):
        P.op("pe", lambda e: e.matmul(ps_s[:, :width], ones128[:], zb[:, m, :width], start=(m == 0), stop=(m == KC - 1)),
             reads=[zkey, "ones128"], writes=["ps_s"], pe_acc=True)
    for m in range(KC):
        s = sq[m % 2]
        sk = "sq%d" % (m % 2)
        P.op("act", lambda e: e.activation(out=s[:, :width], in_=zb[:, m, :width], func=AF.Square), reads=[zkey], writes=[sk])
        P.op("pe", lambda e: e.matmul(ps_s2[:, :width], ones128[:], s[:, :width], start=(m == 0), stop=(m == KC - 1)),
             reads=[sk, "ones128"], writes=["ps_s2"], pe_acc=True)
    mean, msq, rstd = tmp
    P.op("dve", lambda e: e.tensor_scalar(out=mean[:, :width], in0=ps_s[:, :width], scalar1=1.0 / D, scalar2=None, op0=ALU.mult),
         reads=["ps_s"], writes=["mean"])
    P.op("dve", lambda e: e.tensor_tensor(out=msq[:, :width], in0=mean[:, :width], in1=mean[:, :width], op=ALU.mult),
         reads=["mean"], writes=["msq"])
    P.op("dve", lambda e: e.scalar_tensor_tensor(out=rstd[:, :width], in0=ps_s2[:, :width], scalar=1.0 / D, in1=msq[:, :width],
                                                 op0=ALU.mult, op1=ALU.subtract),
         reads=["ps_s2", "msq"], writes=["rstd"])
    P.op("dve", lambda e: e.tensor_scalar_add(out=rstd[:, :width], in0=rstd[:, :width], scalar1=LN_EPS),
         reads=["rstd"], writes=["rstd"])
    P.op("act", lambda e: e.activation(out=rstd[:, :width], in_=rstd[:, :width], func=AF.Sqrt),
         reads=["rstd"], writes=["rstd"])
    P.op("dve", lambda e: e.reciprocal(out=rstd[:, :width], in_=rstd[:, :width]),
         reads=["rstd"], writes=["rstd"])
    for m in range(KC):
        eng = "dve" if m % 2 == 0 else "pool"
        P.op(eng, lambda e: e.tensor_tensor(out=outb[:, m, :width], in0=zb[:, m, :width], in1=mean[:, :width], op=ALU.subtract),
             reads=[zkey, "mean"], writes=[(okey, m)])
        P.op(eng, lambda e: e.tensor_tensor(out=outb[:, m, :width], in0=outb[:, m, :width], in1=rstd[:, :width], op=ALU.mult),
             reads=[(okey, m), "rstd"], writes=[(okey, m)])
        P.op(eng, lambda e: e.tensor_scalar(out=outb[:, m, :width], in0=outb[:, m, :width], scalar1=g[:, m:m + 1],
                                            scalar2=b[:, m:m + 1], op0=ALU.mult, op1=ALU.add),
             reads=[(okey, m), "lng", "lnb"], writes=[(okey, m)])


def blk_streams(bk):
    if bk == 0:
        return [(0, NTC, 1), (NTC, BK, 0)]
    return [(0, BK, 0)]


def build_x3(glu):
    P = Prog()
    nout = 2 * D if glu else D
    f_in = P.dram_in("fT", [D, NT])
    h_in = P.dram_in("hT", [D, NT])
    mod_in = P.dram_in("modT", [128, NCH_ADA, 2])
    w_in = P.dram_in("w", [D, nout])
    g_in = P.dram_in("lng", [128, KC])
    b_in = P.dram_in("lnb", [128, KC])
    wr_in = P.dram_in("wr", [128, KC, NEXP])
    h1_out = P.dram_out("h1T", [D, NT])
    h2_out = P.dram_out("h2T", [D, NT], BF16)
    aff_out = P.dram_out("affT", [NEXP, NT])

    mod, sc1 = load_mod(P, mod_in)
    g = P.sb("lng", [128, KC]); b = P.sb("lnb", [128, KC]); wr = P.sb("wr", [128, KC, NEXP])
    P.dma("sp", g[:], g_in, writes=["lng"]); P.dma("sp", b[:], b_in, writes=["lnb"]); P.dma("sp", wr[:], wr_in, writes=["wr"])
    ones128 = P.sb("ones128", [128, 128])
    P.op("dve", lambda e: e.memset(ones128[:], 1.0), writes=["ones128"])
    fb = P.sb("fb", [128, KC, NT], BF16)
    ystg = [P.sb("ystg%d" % i, [128, BK]) for i in range(3)]
    yscr = P.nc.dram_tensor("yscr3", [D, NT], F32, kind="Internal").ap()
    yv = yscr.rearrange("(kc p) t -> p kc t", p=128)
    hb = P.sb("hb", [128, KC, BK])
    zb = P.sb("zb", [128, KC, BK])
    h2b = fb[:].rearrange("p a b -> p (a b)")[:, 0:KC * BK].rearrange("p (k t) -> p k t", t=BK)
    nw = 4 if glu else 2
    wm = [P.sb("wm%d" % i, [128, KC, 128], BF16) for i in range(nw)]
    sq = [P.sb("sq%d" % i, [128, BK]) for i in range(2)]
    tmp = (P.sb("mean", [128, BK]), P.sb("msq", [128, BK]), P.sb("rstd", [128, BK]))
    ex = P.sb("ex", [NEXP, BK]); rs = P.sb("rs", [NEXP, BK]); af = P.sb("af", [NEXP, BK])
    psA = [P.ps("psA%d" % i, [128, BK]) for i in range(2)]
    psB = [P.ps("psB%d" % i, [128, BK]) for i in range(2)] if glu else None
    ps_s = P.ps("ps_s", [128, BK]); ps_s2 = P.ps("ps_s2", [128, BK])
    ps_r = P.ps("ps_r", [NEXP, BK])
    fv = f_in.rearrange("(kc p) t -> p kc t", p=128)
    hv = h_in.rearrange("(kc p) t -> p kc t", p=128)
    wv = w_in.rearrange("(kc p) c -> p kc c", p=128)
    h1v = h1_out.rearrange("(kc p) t -> p kc t", p=128)
    h2v = h2_out.rearrange("(kc p) t -> p kc t", p=128)
    wi = 0
    for bk in range(NBK):
        c0 = bk * BK
        P.dma("pool", fb[:, :, c0:c0 + BK], fv[:, :, c0:c0 + BK], writes=[("fb", bk)])
    it = 0
    for m in range(KC):
        wa = wm[wi % nw]; wak = "wm%d" % (wi % nw); wi += 1
        P.dma("pool", wa[:], wv[:, :, m * 128:(m + 1) * 128], writes=[wak])
        if glu:
            wb = wm[wi % nw]; wbk = "wm%d" % (wi % nw); wi += 1
            P.dma("pool", wb[:], wv[:, :, D + m * 128:D + (m + 1) * 128], writes=[wbk])
        for bk in range(NBK):
            c0 = bk * BK
            pa = psA[it % 2]; pak = "psA%d" % (it % 2)
            for kc in range(KC):
                P.op("pe", lambda e: e.matmul(pa[:], wa[:, kc, :], fb[:, kc, c0:c0 + BK], start=(kc == 0), stop=(kc == KC - 1)),
                     reads=[wak, ("fb", bk)], writes=[pak], pe_acc=True)
            ys = ystg[it % 3]; ysk = "ystg%d" % (it % 3)
            if glu:
                pb = psB[it % 2]; pbk = "psB%d" % (it % 2)
                for kc in range(KC):
                    P.op("pe", lambda e: e.matmul(pb[:], wb[:, kc, :], fb[:, kc, c0:c0 + BK], start=(kc == 0), stop=(kc == KC - 1)),
                         reads=[wbk, ("fb", bk)], writes=[pbk], pe_acc=True)
                P.op("act", lambda e: e.activation(out=ys[:], in_=pb[:], func=AF.Sigmoid), reads=[pbk], writes=[ysk])
                P.op("dve", lambda e: e.tensor_tensor(out=ys[:], in0=pa[:], in1=ys[:], op=ALU.mult), reads=[pak, ysk], writes=[ysk])
            else:
                if it % 2 == 0:
                    P.op("act", lambda e: e.activation(out=ys[:], in_=pa[:], func=AF.Copy), reads=[pak], writes=[ysk])
                else:
                    P.op("dve", lambda e: e.tensor_copy(out=ys[:], in_=pa[:]), reads=[pak], writes=[ysk])
            P.dma("sp", yv[:, m, c0:c0 + BK], ys[:], reads=[ysk], writes=["yscr"])
            it += 1
    for bk in range(NBK):
        c0 = bk * BK
        P.dma("sp", zb[:], yv[:, :, c0:c0 + BK], reads=["yscr"], writes=["zb"] + ([("fb", q) for q in range(NBK)] if bk == 0 else []))
        P.dma("act", hb[:], hv[:, :, c0:c0 + BK], writes=["hb"] + [("hb", m_) for m_ in range(KC)])
        P.op("act", lambda e: e.activation(out=hb[:], in_=hb[:], func=AF.Copy, scale=ALPHA), reads=["hb"], writes=["hb"])
        for m in range(KC):
            for (a, bb, s) in blk_streams(bk):
                P.op("dve", lambda e: e.scalar_tensor_tensor(out=zb[:, m, a:bb], in0=zb[:, m, a:bb],
                                                             scalar=mod[:, 2 * KC + m, s:s + 1], in1=hb[:, m, a:bb],
                                                             op0=ALU.mult, op1=ALU.add),
                     reads=["zb", "hb", "mod"], writes=["zb"])
        emit_layernorm(P, zb, hb, g, b, ones128, ps_s, ps_s2, sq, tmp, "zb", "hb", BK)
        P.dma("sp", h1v[:, :, c0:c0 + BK], hb[:], reads=[("hb", m) for m in range(KC)] + ["hb"], is_output=True)
        for m in range(KC):
            for (a, bb, s) in blk_streams(bk):
                P.op("pool", lambda e: e.tensor_scalar(out=zb[:, m, a:bb], in0=hb[:, m, a:bb],
                                                       scalar1=sc1[:, 4 * KC + m, s:s + 1], scalar2=mod[:, 3 * KC + m, s:s + 1],
                                                       op0=ALU.mult, op1=ALU.add),
                     reads=[("hb", m), "sc1", "mod"], writes=[("zb2", m), "zb"])
            P.op("act", lambda e: e.activation(out=h2b[:, m, :], in_=zb[:, m, :], func=AF.Copy), reads=[("zb2", m)], writes=["h2b"])
            P.op("pe", lambda e: e.matmul(ps_r[:], wr[:, m, :], zb[:, m, :], start=(m == 0), stop=(m == KC - 1)),
                 reads=["wr", ("zb2", m)], writes=["ps_r"], pe_acc=True)
        P.dma("act", h2v[:, :, c0:c0 + BK], h2b, reads=["h2b"], is_output=True)
        P.op("act", lambda e: e.activation(out=ex[:], in_=ps_r[:], func=AF.Exp), reads=["ps_r"], writes=["ex"])
        P.op("pe", lambda e: e.matmul(ps_r[:], ones128[0:NEXP, 0:NEXP], ex[:], start=True, stop=True),
             reads=["ex", "ones128"], writes=["ps_r"])
        P.op("dve", lambda e: e.reciprocal(out=rs[:], in_=ps_r[:]), reads=["ps_r"], writes=["rs"])
        P.op("dve", lambda e: e.tensor_tensor(out=af[:], in0=ex[:], in1=rs[:], op=ALU.mult), reads=["ex", "rs"], writes=["af"])
        P.dma("sp", aff_out[:, c0:c0 + BK], af[:], reads=["af"], is_output=True)
    return P.finish()


def pvec(v):
    return np.ascontiguousarray(np.asarray(v, np.float32).reshape(KC, 128).T)


def run_x3(glu, fT, hT, modT_l, w, lng, lnb, wrouter):
    fs = shard_tokens(fT); hs = shard_tokens(hT)
    wr = np.ascontiguousarray(np.asarray(wrouter, np.float32).reshape(KC, 128, NEXP).transpose(1, 0, 2))
    g = pvec(lng); b = pvec(lnb)
    w = np.ascontiguousarray(w, dtype=np.float32)
    in_maps = [{"fT": fs[j], "hT": hs[j], "modT": modT_l, "w": w, "lng": g, "lnb": b, "wr": wr} for j in range(NCORES)]
    res = run(build_x3(glu), in_maps)
    return (unshard_tokens([r["h1T"] for r in res]), unshard_tokens([r["h2T"] for r in res]),
            unshard_tokens([r["affT"] for r in res]))


EFF = 384
NKF = EFF // 128
CAP_LAT = 2 * SEQ // NEXP
CAP_CTX = 2 * CTX // NEXP
NBIS = 28


def emit_threshold(P, aff_t, width, cap, blockones, lo, mid, cmp_t, cnt, ge, ps_c, tag):
    P.op("dve", lambda e: e.memset(lo[:], 0.0), writes=[tag + "lo"])
    for it in range(NBIS):
        h = 0.5 ** (it + 1)
        P.op("dve", lambda e: e.tensor_scalar_add(out=mid[:], in0=lo[:], scalar1=h), reads=[tag + "lo"], writes=[tag + "mid"])
        P.op("dve", lambda e: e.tensor_scalar(out=cmp_t[:, :width], in0=aff_t[:, :width], scalar1=mid[:, 0:1], scalar2=None,
                                              op0=ALU.is_ge),
             reads=[tag + "aff", tag + "mid"], writes=[tag + "cmp"])
        P.op("dve", lambda e: e.reduce_sum(out=cnt[:], in_=cmp_t[:, :width], axis=AX.X), reads=[tag + "cmp"], writes=[tag + "cnt"])
        P.op("pe", lambda e: e.matmul(ps_c, blockones[:], cnt[:], start=True, stop=True),
             reads=[tag + "cnt", "blockones"], writes=["ps_c"])
        P.op("dve", lambda e: e.tensor_scalar(out=ge[:], in0=ps_c, scalar1=float(cap) - 0.5, scalar2=None, op0=ALU.is_ge),
             reads=["ps_c"], writes=[tag + "ge"])
        P.op("dve", lambda e: e.scalar_tensor_tensor(out=lo[:], in0=ge[:], scalar=h, in1=lo[:], op0=ALU.mult, op1=ALU.add),
             reads=[tag + "ge", tag + "lo"], writes=[tag + "lo"])


def build_x4():
    P = Prog()
    h2_in = P.dram_in("h2T", [D, NT], BF16)
    h1_in = P.dram_in("h1T", [D, NT])
    affj_in = P.dram_in("affj", [NEXP, NT])
    affl_in = P.dram_in("affl", [128, SEQ // 8])
    affc_in = P.dram_in("affc", [128, CTX // 8])
    mod_in = P.dram_in("modT", [128, NCH_ADA, 2])
    g_in = P.dram_in("lng", [128, KC])
    b_in = P.dram_in("lnb", [128, KC])
    wi_in = P.dram_in("w_in", [NEXP, D, 2 * EFF])
    wo_in = P.dram_in("w_out", [NEXP, EFF, D])
    sel_in = P.dram_in("sel", [NEXP, NEXP, 128])
    bo_in = P.dram_in("blockones", [128, 128])
    pick_in = P.dram_in("pick", [128, NEXP])
    h_out = P.dram_out("hT", [D, NT])

    mod, sc1 = load_mod(P, mod_in)
    g = P.sb("lng", [128, KC]); b = P.sb("lnb", [128, KC])
    P.dma("sp", g[:], g_in, writes=["lng"]); P.dma("sp", b[:], b_in, writes=["lnb"])
    sel = P.sb("sel", [NEXP, NEXP, 128]); blockones = P.sb("blockones", [128, 128]); pick = P.sb("pick", [128, NEXP])
    P.dma("sp", sel[:], sel_in, writes=["sel"]); P.dma("sp", blockones[:], bo_in, writes=["blockones"])
    P.dma("sp", pick[:], pick_in, writes=["pick"])
    ones128 = P.sb("ones128", [128, 128])
    P.op("dve", lambda e: e.memset(ones128[:], 1.0), writes=["ones128"])

    zb = P.sb("zb", [128, KC, BK])
    zflat = zb[:].rearrange("p a b -> p (a b)")
    affl = zflat[:, 0:SEQ // 8]; cmp_t = zflat[:, SEQ // 8:2 * (SEQ // 8)]
    affc = P.sb("affc", [128, CTX // 8])
    P.dma("sp", affl, affl_in, writes=["Laff"]); P.dma("sp", affc[:], affc_in, writes=["Caff"])
    lo_l = P.sb("lo_l", [128, 1]); lo_c = P.sb("lo_c", [128, 1]); mid = P.sb("mid", [128, 1]); cnt = P.sb("cnt", [128, 1])
    ge = P.sb("ge", [128, 1])
    ps_small = P.ps("ps_small", [128, 4])
    ps_c = ps_small[:, 0:1]
    emit_threshold(P, affl, SEQ // 8, CAP_LAT, blockones, lo_l, mid, cmp_t, cnt, ge, ps_c, "L")
    emit_threshold(P, affc, CTX // 8, CAP_CTX, blockones, lo_c, mid, cmp_t, cnt, ge, ps_c, "C")
    thr = P.sb("thr", [NEXP, 2])
    ps_t = ps_small[0:NEXP, 1:3]
    P.op("pe", lambda e: e.matmul(ps_t[:, 0:1], pick[:], lo_l[:], start=True, stop=True), reads=["pick", "Llo"], writes=["ps_c"])
    P.op("pe", lambda e: e.matmul(ps_t[:, 1:2], pick[:], lo_c[:], start=True, stop=True), reads=["pick", "Clo"], writes=["ps_c"])
    P.op("dve", lambda e: e.tensor_copy(out=thr[:], in_=ps_t), reads=["ps_c"], writes=["thr"])
    affj = P.sb("affj", [NEXP, NT]); G = P.sb("G", [NEXP, NT])
    P.dma("sp", affj[:], affj_in, writes=["affj"])
    for (a, bb, s) in ((0, NTC, 1), (NTC, NT, 0)):
        P.op("dve", lambda e: e.scalar_tensor_tensor(out=G[:, a:bb], in0=affj[:, a:bb], scalar=thr[:, s:s + 1], in1=affj[:, a:bb],
                                                     op0=ALU.is_ge, op1=ALU.mult),
             reads=["affj", "thr"], writes=["G"])

    h2all = P.sb("h2all", [128, KC, NT], BF16)
    act = h2all[:].rearrange("p a b -> p (a b)")[:, 0:NEXP * NKF * BK].rearrange("p (k t) -> p k t", t=BK)
    actc = [P.sb("actc%d" % i, [128, BK], BF16) for i in range(3)]
    gbs = [P.sb("gbs%d" % i, [128, BK]) for i in range(NBK)]
    h1c = [P.sb("h1c%d" % i, [128, BK]) for i in range(2)]
    wa = [P.sb("wa%d" % i, [128, KC, 128], BF16) for i in range(3)]
    wo = [P.sb("wo%d" % i, [128, NEXP * NKF, 128], BF16) for i in range(2)]
    sq = [P.sb("sq%d" % i, [128, BK]) for i in range(2)]
    tmp = (P.sb("mean", [128, BK]), P.sb("msq", [128, BK]), P.sb("rstd", [128, BK]))
    st = [P.sb("silu%d" % i, [128, BK]) for i in range(2)]
    psg = P.ps("psg", [128, BK])
    psa = [P.ps("psa%d" % i, [128, BK]) for i in range(2)]
    psb = [P.ps("psb%d" % i, [128, BK]) for i in range(2)]
    ps_s = P.ps("ps_s", [128, BK]); ps_s2 = P.ps("ps_s2", [128, BK])
    actd = P.nc.dram_tensor("actd", [NEXP * NKF * 128, NT], BF16, kind="Internal").ap()
    h2v = h2_in.rearrange("(kc p) t -> p kc t", p=128)
    h1v = h1_in.rearrange("(kc p) t -> p kc t", p=128)
    hov = h_out.rearrange("(kc p) t -> p kc t", p=128)
    wiv = wi_in.rearrange("e (kc p) c -> e p kc c", p=128)
    wov = wo_in.rearrange("e (kf p) c -> p e kf c", p=128)
    actv = actd.rearrange("(k p) t -> p k t", p=128)
    for bk in range(NBK):
        c0 = bk * BK
        P.dma("sp", h2all[:, :, c0:c0 + BK], h2v[:, :, c0:c0 + BK], writes=[("h2all", bk)])
    wai = 0
    it = 0
    for ex in range(NEXP):
        for kf in range(NKF):
            wts = []
            for half in range(2):
                w = wa[wai % 3]; wk = "wa%d" % (wai % 3); wai += 1
                cc = half * EFF + kf * 128
                P.dma("pool", w[:], wiv[ex, :, :, cc:cc + 128], writes=[wk])
                wts.append((w, wk))
            for bk in range(NBK):
                c0 = bk * BK
                if kf == 0:
                    P.op("pe", lambda e: e.matmul(psg[:], sel[:, ex, :], G[:, c0:c0 + BK], start=True, stop=True),
                         reads=["sel", "G"], writes=["psg"])
                    P.op("act", lambda e: e.activation(out=gbs[bk][:], in_=psg[:], func=AF.Copy), reads=["psg"], writes=["gbs%d" % bk])
                i2 = it % 2
                pa, pak = psa[i2], "psa%d" % i2
                pb, pbk = psb[i2], "psb%d" % i2
                for (pt, ptk, (w, wk)) in ((pa, pak, wts[0]), (pb, pbk, wts[1])):
                    for kc in range(KC):
                        P.op("pe", lambda e: e.matmul(pt[:], w[:, kc, :], h2all[:, kc, c0:c0 + BK], start=(kc == 0), stop=(kc == KC - 1)),
                             reads=[wk, ("h2all", bk)], writes=[ptk], pe_acc=True)
                s_t, sk = st[i2], "silu%d" % i2
                ac = actc[it % 3]; ack = "actc%d" % (it % 3)
                P.op("act", lambda e: e.activation(out=s_t[:], in_=pa[:], func=AF.Silu), reads=[pak], writes=[sk])
                P.op("dve", lambda e: e.tensor_tensor(out=s_t[:], in0=pb[:], in1=s_t[:], op=ALU.mult), reads=[pbk, sk], writes=[sk])
                P.op("dve", lambda e: e.tensor_tensor(out=ac[:], in0=s_t[:], in1=gbs[bk][:], op=ALU.mult), reads=[sk, "gbs%d" % bk], writes=[ack])
                P.dma("sp", actd[(ex * NKF + kf) * 128:(ex * NKF + kf + 1) * 128, c0:c0 + BK], ac[:], reads=[ack], writes=["actd"])
                it += 1
    for bk in range(NBK):
        c0 = bk * BK
        P.dma("sp", act, actv[:, :, c0:c0 + BK], reads=["actd"], writes=["act"] + [("h2all", q) for q in range(NBK)])
        for m in range(KC):
            w = wo[m % 2]; wk = "wo%d" % (m % 2)
            P.dma("pool", w[:].rearrange("p (e kf) c -> p e kf c", kf=NKF), wov[:, :, :, m * 128:(m + 1) * 128], writes=[wk])
            hc = h1c[m % 2]; hk = "h1c%d" % (m % 2)
            P.dma("act", hc[:], h1v[:, m, c0:c0 + BK], writes=[hk])
            P.op("act", lambda e: e.activation(out=hc[:], in_=hc[:], func=AF.Copy, scale=ALPHA), reads=[hk], writes=[hk])
            pa, pak = psa[m % 2], "psa%d" % (m % 2)
            nk = NEXP * NKF
            for k in range(nk):
                P.op("pe", lambda e: e.matmul(pa[:], w[:, k, :], act[:, k, :], start=(k == 0), stop=(k == nk - 1)),
                     reads=[wk, "act"], writes=[pak], pe_acc=True)
            for (a_, bb, s_) in blk_streams(bk):
                P.op("dve", lambda e: e.scalar_tensor_tensor(out=zb[:, m, a_:bb], in0=pa[:, a_:bb],
                                                             scalar=mod[:, 5 * KC + m, s_:s_ + 1], in1=hc[:, a_:bb],
                                                             op0=ALU.mult, op1=ALU.add),
                     reads=[pak, hk, "mod"], writes=["zb"])
        emit_layernorm(P, zb, zb, g, b, ones128, ps_s, ps_s2, sq, tmp, "zb", "zb", BK)
        P.dma("sp", hov[:, :, c0:c0 + BK], zb[:], reads=[("zb", m) for m in range(KC)] + ["zb"], writes=["zb"], is_output=True)
    return P.finish()


def moe_consts():
    sel = np.zeros((NEXP, NEXP, 128), np.float32)
    for e in range(NEXP):
        sel[e, e, :] = 1.0
    bo = np.kron(np.eye(NEXP, dtype=np.float32), np.ones((8, 8), np.float32))
    pick = np.zeros((128, NEXP), np.float32)
    for e in range(NEXP):
        pick[8 * e, e] = 1.0
    return sel, bo, pick


def run_x4(h1T, h2T, affT, modT_l, lng, lnb, w_in, w_out):
    h1s = shard_tokens(h1T); h2s = shard_tokens(h2T); affs = shard_tokens(affT)
    affl = np.ascontiguousarray(affT[:, CTX:].reshape(128, SEQ // 8))
    affc = np.ascontiguousarray(affT[:, :CTX].reshape(128, CTX // 8))
    sel, bo, pick = moe_consts()
    g = pvec(lng); b = pvec(lnb)
    w_in = np.ascontiguousarray(w_in, dtype=np.float32); w_out = np.ascontiguousarray(w_out, dtype=np.float32)
    in_maps = [{"h2T": h2s[j], "h1T": h1s[j], "affj": affs[j], "affl": affl, "affc": affc, "modT": modT_l, "lng": g, "lnb": b,
                "w_in": w_in, "w_out": w_out, "sel": sel, "blockones": bo, "pick": pick} for j in range(NCORES)]
    res = run(build_x4(), in_maps)
    return unshard_tokens([r["hT"] for r in res])


S5G = 16
S5P = 64
NGC = 32
NOCT = 4
NA = T // 64
TWO_PI = 6.283180
GELU_C = 0.7978845608028654


def emit_cmul(P, yr, yi, xr, xi, c, s, conj, keys):
    ykr, yki, xkr, xki, tk = keys
    P.op("dve", lambda e: e.tensor_tensor(out=yr, in0=xr, in1=c, op=ALU.mult), reads=[xkr, tk], writes=[ykr])
    P.op("pool", lambda e: e.tensor_tensor(out=yi, in0=xi, in1=s, op=ALU.mult), reads=[xki, tk], writes=[yki])
    P.op("dve", lambda e: e.tensor_tensor(out=yr, in0=yr, in1=yi, op=(ALU.add if conj else ALU.subtract)),
         reads=[ykr, yki], writes=[ykr])
    P.op("pool", lambda e: e.tensor_tensor(out=yi, in0=xi, in1=c, op=ALU.mult), reads=[xki, tk, ykr], writes=[yki])
    P.op("dve", lambda e: e.tensor_tensor(out=xr, in0=xr, in1=s, op=ALU.mult), reads=[xkr, tk], writes=[xkr])
    P.op("pool", lambda e: e.tensor_tensor(out=yi, in0=yi, in1=xr, op=(ALU.subtract if conj else ALU.add)),
         reads=[yki, xkr], writes=[yki])


def kk2(k):
    return [k + "A", k + "B"]


def emit_cmul2(P, yr, yi, xr, xi, c, s, conj, keys, na):
    ykr, yki, xkr, xki, tk = keys
    for (eng, sl, sfx) in (("dve", slice(0, na), "A"), ("pool", slice(na, None), "B")):
        Yr, Yi, Xr, Xi, C, S = (t[:, sl, :] for t in (yr, yi, xr, xi, c, s))
        a_, b_, c_, d_ = ykr + sfx, yki + sfx, xkr + sfx, xki + sfx
        P.op(eng, lambda e: e.tensor_tensor(out=Yr, in0=Xr, in1=C, op=ALU.mult), reads=[c_, tk], writes=[a_])
        P.op(eng, lambda e: e.tensor_tensor(out=Yi, in0=Xi, in1=S, op=ALU.mult), reads=[d_, tk], writes=[b_])
        P.op(eng, lambda e: e.tensor_tensor(out=Yr, in0=Yr, in1=Yi, op=(ALU.add if conj else ALU.subtract)), reads=[a_, b_], writes=[a_])
        P.op(eng, lambda e: e.tensor_tensor(out=Yi, in0=Xi, in1=C, op=ALU.mult), reads=[d_, tk, a_], writes=[b_])
        P.op(eng, lambda e: e.tensor_tensor(out=Xr, in0=Xr, in1=S, op=ALU.mult), reads=[c_, tk], writes=[c_])
        P.op(eng, lambda e: e.tensor_tensor(out=Yi, in0=Yi, in1=Xr, op=(ALU.subtract if conj else ALU.add)), reads=[b_, c_], writes=[b_])


def emit_sincos(P, ph, tmp_i, out_s, out_c, key):
    I32 = mybir.dt.int32
    P.op("dve", lambda e: e.tensor_copy(out=tmp_i, in_=ph), reads=[key + "ph"], writes=[key + "i"])
    P.op("dve", lambda e: e.tensor_tensor(out=out_s, in0=ph, in1=tmp_i, op=ALU.subtract), reads=[key + "ph", key + "i"], writes=[key + "s"])
    P.op("act", lambda e: e.activation(out=out_s, in_=out_s, func=AF.Sin, scale=TWO_PI), reads=[key + "s"], writes=[key + "s"])
    P.op("dve", lambda e: e.tensor_scalar_add(out=ph, in0=ph, scalar1=0.25), reads=[key + "ph", key + "s"], writes=[key + "ph"])
    P.op("dve", lambda e: e.tensor_copy(out=tmp_i, in_=ph), reads=[key + "ph"], writes=[key + "i"])
    P.op("dve", lambda e: e.tensor_tensor(out=out_c, in0=ph, in1=tmp_i, op=ALU.subtract), reads=[key + "ph", key + "i"], writes=[key + "c"])
    P.op("act", lambda e: e.activation(out=out_c, in_=out_c, func=AF.Sin, scale=TWO_PI), reads=[key + "c"], writes=[key + "c"])


def build_o2(n_oct=NOCT, n_grp=8, stages=(1, 1, 1, 1, 1, 1)):
    P = Prog()
    nc = P.nc
    I32 = mybir.dt.int32
    h_in = P.dram_in("hT", [NOCT * 128, T])
    modo_in = P.dram_in("modo", [128, NOCT, 2, 2])
    dsk_in = P.dram_in("dsk", [128, NOCT])
    lre_in = P.dram_in("lam_re", [128, NGC]); lim_in = P.dram_in("lam_im", [128, NGC]); ldt_in = P.dram_in("log_dt", [128, NGC])
    bre_in = P.dram_in("bre", [NOCT, 128, 8, 128]); bim_in = P.dram_in("bim", [NOCT, 128, 8, 128])
    cre_in = P.dram_in("cre", [128, NGC, S5G]); cim_in = P.dram_in("cim", [128, NGC, S5G])
    av_in = P.dram_in("avals", [128, NA]); bv_in = P.dram_in("bvals", [128, 64])
    f_out = P.dram_out("fT", [NOCT * 128, T])
    yscr = nc.dram_tensor("yscr", [128, T], F32, kind="Internal").ap()

    def ld(name, src, shape, dt=F32, q="sp"):
        t = P.sb(name, shape, dt)
        P.dma(q, t[:], src, writes=[name])
        return t
    modo = ld("modo", modo_in, [128, NOCT, 2, 2]); dsk = ld("dsk", dsk_in, [128, NOCT])
    lre = ld("lre", lre_in, [128, NGC]); lim = ld("lim", lim_in, [128, NGC]); ldt = ld("ldt", ldt_in, [128, NGC])
    cre = ld("cre", cre_in, [128, NGC, S5G]); cim = ld("cim", cim_in, [128, NGC, S5G])
    avals = ld("avals", av_in, [128, NA]); bvals = ld("bvals", bv_in, [128, 64])
    sc1 = P.sb("sc1o", [128, NOCT, 2])
    P.op("dve", lambda e: e.tensor_scalar_add(out=sc1[:], in0=modo[:, :, 1, :], scalar1=1.0), reads=["modo"], writes=["sc1o"])

    def sm(name, dt=F32):
        return P.sb(name, [128, NGC], dt)
    dtv = sm("dtv"); r = sm("r"); fq = sm("fq"); F1 = sm("F1"); ti = sm("ti", I32)
    lbs = sm("lbs"); lbc = sm("lbc"); ph = sm("ph"); den = sm("den"); fre = sm("fre"); fim = sm("fim"); t1 = sm("t1"); t2 = sm("t2")
    K = "prm"
    def dv(fn, reads, writes):
        P.op("dve", fn, reads=reads, writes=writes)
    dv(lambda e: e.tensor_scalar_min(out=lre[:], in0=lre[:], scalar1=-1e-4), ["lre"], ["lre"])
    P.op("act", lambda e: e.activation(out=dtv[:], in_=ldt[:], func=AF.Exp), reads=["ldt"], writes=["dtv"])
    dv(lambda e: e.tensor_tensor(out=r[:], in0=lre[:], in1=dtv[:], op=ALU.mult), ["lre", "dtv"], ["r"])
    P.op("act", lambda e: e.activation(out=r[:], in_=r[:], func=AF.Exp), reads=["r"], writes=["r"])
    dv(lambda e: e.tensor_tensor(out=fq[:], in0=lim[:], in1=dtv[:], op=ALU.mult), ["lim", "dtv"], ["fq"])
    dv(lambda e: e.tensor_scalar(out=fq[:], in0=fq[:], scalar1=1.0 / (2 * math.pi), scalar2=None, op0=ALU.mult), ["fq"], ["fq"])
    dv(lambda e: e.tensor_scalar(out=t1[:], in0=fq[:], scalar1=64.0, scalar2=None, op0=ALU.mult), ["fq"], ["t1"])
    dv(lambda e: e.tensor_copy(out=ti[:], in_=t1[:]), ["t1"], ["ti"])
    dv(lambda e: e.tensor_tensor(out=F1[:], in0=t1[:], in1=ti[:], op=ALU.subtract), ["t1", "ti"], ["F1"])
    dv(lambda e: e.tensor_copy(out=ph[:], in_=fq[:]), ["fq"], ["lbph"])
    emit_sincos(P, ph[:], ti[:], lbs[:], lbc[:], "lb")
    dv(lambda e: e.tensor_tensor(out=lbs[:], in0=lbs[:], in1=r[:], op=ALU.mult), ["lbs", "r"], ["lbs"])
    dv(lambda e: e.tensor_tensor(out=lbc[:], in0=lbc[:], in1=r[:], op=ALU.mult), ["lbc", "r"], ["lbc"])
    dv(lambda e: e.tensor_scalar_add(out=lbc[:], in0=lbc[:], scalar1=-1.0), ["lbc"], ["lbc"])
    dv(lambda e: e.tensor_tensor(out=den[:], in0=lre[:], in1=lre[:], op=ALU.mult), ["lre"], ["den"])
    dv(lambda e: e.tensor_tensor(out=t1[:], in0=lim[:], in1=lim[:], op=ALU.mult), ["lim", "F1"], ["t1"])
    dv(lambda e: e.tensor_tensor(out=den[:], in0=den[:], in1=t1[:], op=ALU.add), ["den", "t1"], ["den"])
    dv(lambda e: e.reciprocal(out=den[:], in_=den[:]), ["den"], ["den"])
    dv(lambda e: e.tensor_tensor(out=fre[:], in0=lbc[:], in1=lre[:], op=ALU.mult), ["lbc", "lre"], ["fre"])
    dv(lambda e: e.tensor_tensor(out=t1[:], in0=lbs[:], in1=lim[:], op=ALU.mult), ["lbs", "lim", "den"], ["t1"])
    dv(lambda e: e.tensor_tensor(out=fre[:], in0=fre[:], in1=t1[:], op=ALU.add), ["fre", "t1"], ["fre"])
    dv(lambda e: e.tensor_tensor(out=fre[:], in0=fre[:], in1=den[:], op=ALU.mult), ["fre", "den"], ["fre"])
    dv(lambda e: e.tensor_tensor(out=fim[:], in0=lbs[:], in1=lre[:], op=ALU.mult), ["lbs", "lre"], ["fim"])
    dv(lambda e: e.tensor_tensor(out=t2[:], in0=lbc[:], in1=lim[:], op=ALU.mult), ["lbc", "lim"], ["t2"])
    dv(lambda e: e.tensor_tensor(out=fim[:], in0=fim[:], in1=t2[:], op=ALU.subtract), ["fim", "t2"], ["fim"])
    dv(lambda e: e.tensor_tensor(out=fim[:], in0=fim[:], in1=den[:], op=ALU.mult), ["fim", "den"], ["fim"])
    gre = P.sb("gre", [128, NGC, S5G]); gimn = P.sb("gimn", [128, NGC, S5G]); gt = P.sb("gt", [128, NGC, S5G])
    freb = fre[:].unsqueeze(2).to_broadcast([128, NGC, S5G]); fimb = fim[:].unsqueeze(2).to_broadcast([128, NGC, S5G])
    dv(lambda e: e.tensor_tensor(out=gre[:], in0=cre[:], in1=freb, op=ALU.mult), ["cre", "fre"], ["gre"])
    dv(lambda e: e.tensor_tensor(out=gt[:], in0=cim[:], in1=fimb, op=ALU.mult), ["cim", "fim"], ["gt"])
    dv(lambda e: e.tensor_tensor(out=gre[:], in0=gre[:], in1=gt[:], op=ALU.subtract), ["gre", "gt"], ["gre"])
    dv(lambda e: e.tensor_tensor(out=gimn[:], in0=cre[:], in1=fimb, op=ALU.mult), ["cre", "fim"], ["gimn"])
    dv(lambda e: e.tensor_tensor(out=gt[:], in0=cim[:], in1=freb, op=ALU.mult), ["cim", "fre", "gre"], ["gt"])
    dv(lambda e: e.tensor_tensor(out=gimn[:], in0=gimn[:], in1=gt[:], op=ALU.add), ["gimn", "gt"], ["gimn"])
    dv(lambda e: e.tensor_scalar(out=gimn[:], in0=gimn[:], scalar1=-1.0, scalar2=None, op0=ALU.mult), ["gimn"], ["gimn"])
    greb = P.sb("greb", [128, NGC, S5G], BF16); gimb = P.sb("gimb", [128, NGC, S5G], BF16)
    dv(lambda e: e.tensor_copy(out=greb[:], in_=gre[:]), ["gre"], ["greb"])
    dv(lambda e: e.tensor_copy(out=gimb[:], in_=gimn[:]), ["gimn"], ["gimb"])

    U = P.sb("U", [128, T], BF16)
    BR = P.sb("BR", [128, T], BF16); BI = P.sb("BI", [128, T], BF16); TR = P.sb("TR", [128, T]); TI = P.sb("TI", [128, T])
    ZR = P.sb("ZR", [128, T], BF16); ZI = P.sb("ZI", [128, T], BF16)
    Yg = P.sb("Yg", [S5G, 2112])
    bpr = P.sb("bpr", [128, 8, 128], BF16); bpi = P.sb("bpi", [128, 8, 128], BF16)
    eac = P.sb("eac", [128, 8, NA]); eas = P.sb("eas", [128, 8, NA]); ebc = P.sb("ebc", [128, 8, 64]); ebs = P.sb("ebs", [128, 8, 64])
    pha = P.sb("pha", [128, 8, NA]); phb = P.sb("phb", [128, 8, 64]); pia = P.sb("pia", [128, 8, NA], I32); pib = P.sb("pib", [128, 8, 64], I32)
    psr = [P.ps("psr%d" % i, [128, 512]) for i in range(2)]
    psi = [P.ps("psi%d" % i, [128, 512]) for i in range(2)]
    psy = [P.ps("psy%d" % i, [S5G, 512]) for i in range(2)]

    NSPL = 88

    def v3(t):
        return t[:].rearrange("p (a b) -> p a b", b=64)
    blocks = [(i * 512, min(512, T - i * 512)) for i in range((T + 511) // 512)]
    for oc in range(n_oct):
        for sg in range(4):
            c0 = sg * 2112
            hraw = TR[:, 0:2112]
            P.dma("sp", hraw, h_in[oc * 128:(oc + 1) * 128, c0:c0 + 2112], writes=kk2("TR"))
            pieces = [(0, CTX, 1), (CTX, 2112, 0)] if sg == 0 else [(0, 2112, 0)]
            for (a, bb, s) in pieces:
                P.op("dve", lambda e: e.tensor_scalar(out=U[:, c0 + a:c0 + bb], in0=hraw[:, a:bb], scalar1=sc1[:, oc, s:s + 1],
                                                      scalar2=modo[:, oc, 0, s:s + 1], op0=ALU.mult, op1=ALU.add),
                     reads=kk2("TR") + ["sc1o", "modo"], writes=["U"])
        P.dma("pool", bpr[:], bre_in[oc], writes=["bpr"]); P.dma("pool", bpi[:], bim_in[oc], writes=["bpi"])
        g0 = oc * 8
        dv(lambda e: e.tensor_tensor(out=pha[:], in0=F1[:, g0:g0 + 8].unsqueeze(2).to_broadcast([128, 8, NA]),
                                     in1=avals[:].unsqueeze(1).to_broadcast([128, 8, NA]), op=ALU.mult),
           ["F1", "avals"], ["EAph"])
        emit_sincos(P, pha[:], pia[:], eas[:], eac[:], "EA")
        dv(lambda e: e.tensor_tensor(out=phb[:], in0=fq[:, g0:g0 + 8].unsqueeze(2).to_broadcast([128, 8, 64]),
                                     in1=bvals[:].unsqueeze(1).to_broadcast([128, 8, 64]), op=ALU.mult),
           ["fq", "bvals"], ["EBph"])
        emit_sincos(P, phb[:], pib[:], ebs[:], ebc[:], "EB")
        for gl in range(n_grp):
            g = g0 + gl
            for bi_, (c0, w) in enumerate(blocks):
                pr, prk = psr[bi_ % 2], "psr%d" % (bi_ % 2)
                pi, pik = psi[bi_ % 2], "psi%d" % (bi_ % 2)
                P.op("pe", lambda e: e.matmul(pr[:, :w], bpr[:, gl, :], U[:, c0:c0 + w], start=True, stop=True),
                     reads=["bpr", "U"], writes=[prk])
                P.op("pe", lambda e: e.matmul(pi[:, :w], bpi[:, gl, :], U[:, c0:c0 + w], start=True, stop=True),
                     reads=["bpi", "U"], writes=[pik])
                P.op("act", lambda e: e.activation(out=BR[:, c0:c0 + w], in_=pr[:, :w], func=AF.Copy), reads=[prk], writes=kk2("BR"))
                P.op("act", lambda e: e.activation(out=BI[:, c0:c0 + w], in_=pi[:, :w], func=AF.Copy), reads=[pik], writes=kk2("BI"))
            ebcb = ebc[:, gl, :].unsqueeze(1).to_broadcast([128, NA, 64]); ebsb = ebs[:, gl, :].unsqueeze(1).to_broadcast([128, NA, 64])
            eacb = eac[:, gl, :].unsqueeze(2).to_broadcast([128, NA, 64]); easb = eas[:, gl, :].unsqueeze(2).to_broadcast([128, NA, 64])
            if stages[0]:
                emit_cmul2(P, v3(ZR), v3(ZI), v3(BR), v3(BI), ebcb, ebsb, True, ("ZR", "ZI", "BR", "BI", "EBc"), NSPL)
            if stages[1]:
                emit_cmul2(P, v3(BR), v3(BI), v3(ZR), v3(ZI), eacb, easb, True, ("BR", "BI", "ZR", "ZI", "EAc"), NSPL)
            rf = r[0:64, g:g + 1]; rb = r[64:128, g:g + 1]
            for (src, dst, sk, dk) in (((BR, TR, "BR", "TR"), (BI, TI, "BI", "TI")) if stages[2] else ()):
                P.op("dve", lambda e: e.tensor_tensor_scan(out=dst[0:64, :], data0=rf.to_broadcast([64, T]), data1=src[0:64, :],
                                                           initial=0.0, op0=ALU.mult, op1=ALU.add),
                     reads=kk2(sk) + ["r"], writes=kk2(dk))
                P.op("dve", lambda e: e.tensor_tensor_scan(out=dst[64:128, 0:CTX][:, ::-1], data0=rb.to_broadcast([64, CTX]),
                                                           data1=src[64:128, 0:CTX][:, ::-1], initial=0.0, op0=ALU.mult, op1=ALU.add),
                     reads=kk2(sk) + ["r"], writes=kk2(dk))
                P.op("dve", lambda e: e.tensor_tensor_scan(out=dst[64:128, CTX:T][:, ::-1], data0=rb.to_broadcast([64, SEQ]),
                                                           data1=src[64:128, CTX:T][:, ::-1], initial=dst[64:128, 0:1],
                                                           op0=ALU.mult, op1=ALU.add),
                     reads=kk2(sk) + ["r"] + kk2(dk), writes=kk2(dk))
            if stages[3]:
                emit_cmul2(P, v3(BR), v3(BI), v3(TR), v3(TI), ebcb, ebsb, False, ("BR", "BI", "TR", "TI", "EBc"), NSPL)
                emit_cmul2(P, v3(ZR), v3(ZI), v3(BR), v3(BI), eacb, easb, False, ("ZR", "ZI", "BR", "BI", "EAc"), NSPL)
            for sg in (range(4) if stages[4] else ()):
                for sb_ in range(5):
                    c0 = sg * 2112 + sb_ * 512
                    w = min(512, sg * 2112 + 2112 - c0)
                    if w <= 0:
                        continue
                    py, pyk = psy[sb_ % 2], "psy%d" % (sb_ % 2)
                    P.op("pe", lambda e: e.matmul(py[:, :w], greb[:, g, :], ZR[:, c0:c0 + w], start=True, stop=False),
                         reads=["greb"] + kk2("ZR"), writes=[pyk])
                    P.op("pe", lambda e: e.matmul(py[:, :w], gimb[:, g, :], ZI[:, c0:c0 + w], start=False, stop=True),
                         reads=["gimb"] + kk2("ZI"), writes=[pyk], pe_acc=True)
                    P.op("dve", lambda e: e.tensor_copy(out=Yg[:, sb_ * 512:sb_ * 512 + w], in_=py[:, :w]), reads=[pyk], writes=["Yg"])
                P.dma("sp", yscr[gl * S5G:(gl + 1) * S5G, sg * 2112:(sg + 1) * 2112], Yg[:], reads=["Yg"], writes=["yscr"])
        for sg in (range(4) if stages[5] else ()):
            c0 = sg * 2112
            X = TR[:, 0:2112]; X2 = TI[:, 0:2112]
            P.dma("sp", X, yscr[:, c0:c0 + 2112], reads=["yscr"], writes=kk2("TR"))
            P.op("dve", lambda e: e.scalar_tensor_tensor(out=X, in0=U[:, c0:c0 + 2112], scalar=dsk[:, oc:oc + 1], in1=X,
                                                         op0=ALU.mult, op1=ALU.add), reads=["U", "dsk"] + kk2("TR"), writes=kk2("TR"))
            P.op("pool", lambda e: e.tensor_tensor(out=X2, in0=X, in1=X, op=ALU.mult), reads=kk2("TR"), writes=kk2("TI"))
            P.op("pool", lambda e: e.tensor_scalar(out=X2, in0=X2, scalar1=0.044715 * GELU_C, scalar2=GELU_C, op0=ALU.mult, op1=ALU.add),
                 reads=kk2("TI"), writes=kk2("TI"))
            P.op("dve", lambda e: e.tensor_tensor(out=X2, in0=X2, in1=X, op=ALU.mult), reads=kk2("TI") + kk2("TR"), writes=kk2("TI"))
            P.op("act", lambda e: e.activation(out=X2, in_=X2, func=AF.Tanh), reads=kk2("TI"), writes=kk2("TI"))
            P.op("dve", lambda e: e.tensor_scalar(out=X2, in0=X2, scalar1=1.0, scalar2=0.5, op0=ALU.add, op1=ALU.mult),
                 reads=kk2("TI"), writes=kk2("TI"))
            P.op("dve", lambda e: e.tensor_tensor(out=X, in0=X, in1=X2, op=ALU.mult), reads=kk2("TI") + kk2("TR"), writes=kk2("TR"))
            P.dma("sp", f_out[oc * 128:(oc + 1) * 128, c0:c0 + 2112], X, reads=kk2("TR"), writes=kk2("TR"), is_output=True)
    return P.finish()


def s5_consts():
    a = np.arange(NA, dtype=np.float32)
    ab = np.where(a < 4, 3 - a, 135 - a).astype(np.float32)
    avals = np.concatenate([np.tile(a, (64, 1)), np.tile(ab, (64, 1))], axis=0)
    b = np.arange(64, dtype=np.float32)
    bvals = np.concatenate([np.tile(b, (64, 1)), np.tile(63 - b, (64, 1))], axis=0)
    return np.ascontiguousarray(avals), np.ascontiguousarray(bvals)


def run_o2(hT, modT_l, lam_re, lam_im, log_dt, b_re, b_im, c_re, c_im, d_skip, **bkw):
    avals, bvals = s5_consts()
    in_maps = []
    for j in range(NCORES):
        gs = slice(NGC * j, NGC * (j + 1))
        ch = slice(512 * j, 512 * (j + 1))
        def dp(a):
            return np.ascontiguousarray(np.asarray(a[:, gs], np.float32).transpose(0, 2, 1).reshape(128, NGC))
        ldt = np.ascontiguousarray(np.broadcast_to(np.asarray(log_dt[:, gs], np.float32)[:, None, :], (2, 64, NGC)).reshape(128, NGC))
        def bpad(bm):
            bm = np.asarray(bm[:, gs], np.float32)
            out = np.zeros((NOCT, 128, 8, 128), np.float32)
            for oc in range(NOCT):
                for gl in range(8):
                    blk = bm[:, oc * 8 + gl]
                    out[oc, gl * 16:(gl + 1) * 16, gl, :] = blk.transpose(2, 0, 1).reshape(16, 128)
            return out
        def cpad(cm):
            cm = np.asarray(cm[:, gs], np.float32)
            return np.ascontiguousarray(cm.transpose(0, 3, 1, 2).reshape(128, NGC, S5G))
        modo = np.zeros((128, NOCT, 2, 2), np.float32)
        for oc in range(NOCT):
            chunk = 4 * j + oc
            modo[:, oc, 0, :] = modT_l[:, 0 * KC + chunk, :]
            modo[:, oc, 1, :] = modT_l[:, 1 * KC + chunk, :]
        dsk = np.ascontiguousarray(np.asarray(d_skip[ch], np.float32).reshape(NOCT, 128).T)
        in_maps.append({"hT": np.ascontiguousarray(hT[ch]), "modo": modo, "dsk": dsk, "lam_re": dp(lam_re), "lam_im": dp(lam_im),
                        "log_dt": ldt, "bre": bpad(b_re), "bim": bpad(b_im), "cre": cpad(c_re), "cim": cpad(c_im),
                        "avals": avals, "bvals": bvals})
    res = run(build_o2(**bkw), in_maps)
    return np.concatenate([r["fT"] for r in res], axis=0)


CB = 16
NFFT = 16384
HYF = 64


def build_eh():
    P = Prog()
    nc = P.nc
    I32 = mybir.dt.int32
    zh_in = P.dram_in("zh", [3, 256, T])
    cw_in = P.dram_in("convw", [128, 2, 3, 3])
    hb_in = P.dram_in("hbias", [128, 2, 2])
    w1_in = P.dram_in("w1", [33, HYF]); w2_in = P.dram_in("w2", [HYF, HYF]); w3_in = P.dram_in("w3", [HYF, HYF])
    bfr_in = P.dram_in("bfr", [HYF, 2, 3])
    w4_in = P.dram_in("w4", [HYF, 2, 2, 256])
    zl_in = P.dram_in("zposl", [33, SEQ]); zc_in = P.dram_in("zposc", [33, CTX])
    dl_in = P.dram_in("decl", [256, SEQ]); dc_in = P.dram_in("decc", [256, CTX])
    tabs_in = P.dram_in("tabs", [128, 12, 128])
    y_out = P.dram_out("yh", [256, T])
    x1c = nc.dram_tensor("x1c", [128, T], F32, kind="Internal").ap()
    x2c = nc.dram_tensor("x2c", [128, T], F32, kind="Internal").ap()
    a_dram = nc.dram_tensor("a_dram", [128, SEQ], F32, kind="Internal").ap()
    g_dram = nc.dram_tensor("g_dram", [128, NFFT], F32, kind="Internal").ap()
    c_dram = nc.dram_tensor("c_dram", [128, SEQ], F32, kind="Internal").ap()

    def ld(name, src, shape, q="sp"):
        t = P.sb(name, shape)
        P.dma(q, t[:], src, writes=[name])
        return t
    cw = ld("cw", cw_in, [128, 2, 3, 3]); hbias = ld("hbias", hb_in, [128, 2, 2])
    w1 = ld("w1", w1_in, [33, HYF]); w2 = ld("w2", w2_in, [HYF, HYF]); w3 = ld("w3", w3_in, [HYF, HYF])
    bfr = ld("bfr", bfr_in, [HYF, 2, 3]); w4 = ld("w4", w4_in, [HYF, 2, 2, 256])
    tabs = ld("tabs", tabs_in, [128, 12, 128])
    T1 = tabs[:, 0:2, :].rearrange("p a b -> p (a b)")
    TI1a = tabs[:, 7:9, :].rearrange("p a b -> p (a b)")
    TI1b = tabs[:, 9:11, :].rearrange("p a b -> p (a b)")
    C128 = tabs[:, 0, :]; NEGS = tabs[:, 1, :]; S128 = tabs[:, 2, :]
    twc = tabs[:, 3, :]; tws = tabs[:, 4, :]
    CN = tabs[:, 5, 0:64]; NSN = tabs[:, 6, 0:64]
    frs = P.sb("frs", [HYF, 3])
    P.op("dve", lambda e: e.tensor_scalar(out=frs[:], in0=bfr[:, 1, :], scalar1=1.0 / (2 * math.pi), scalar2=None, op0=ALU.mult),
         reads=["bfr"], writes=["frs"])

    h3l = P.sb("h3l", [HYF, SEQ]); h3c = P.sb("h3c", [HYF, CTX])
    zp = P.sb("zp", [33, 512]); ha = P.sb("ha", [HYF, 512]); hbt = P.sb("hbt", [HYF, 512]); hi_ = P.sb("hi", [HYF, 512], I32)
    psm = P.ps("psm", [128, 512])

    def mlp_layer(li, wt, wk, src, srck, dst, dstk, w_):
        kdim = 33 if li == 0 else HYF
        P.op("pe", lambda e: e.matmul(psm[0:HYF, :w_], wt[0:kdim, :], src, start=True, stop=True), reads=[wk, srck], writes=["psm"])
        P.op("dve", lambda e: e.tensor_scalar(out=hbt[:, :w_], in0=psm[0:HYF, :w_], scalar1=bfr[:, 0, li:li + 1],
                                              scalar2=frs[:, li:li + 1], op0=ALU.add, op1=ALU.mult),
             reads=["psm", "bfr", "frs"], writes=["hbt"])
        P.op("dve", lambda e: e.tensor_copy(out=hi_[:, :w_], in_=hbt[:, :w_]), reads=["hbt"], writes=["hi"])
        P.op("dve", lambda e: e.tensor_tensor(out=hbt[:, :w_], in0=hbt[:, :w_], in1=hi_[:, :w_], op=ALU.subtract),
             reads=["hbt", "hi"], writes=["hbt"])
        P.op("act", lambda e: e.activation(out=dst, in_=hbt[:, :w_], func=AF.Sin, scale=TWO_PI), reads=["hbt"], writes=[dstk])

    for (zsrc, L, h3) in ((zl_in, SEQ, h3l), (zc_in, CTX, h3c)):
        for b0 in range(0, L, 512):
            w_ = min(512, L - b0)
            P.dma("sp", zp[:, :w_], zsrc[:, b0:b0 + w_], writes=["zp"])
            mlp_layer(0, w1, "w1", zp[:, :w_], "zp", ha[:, :w_], "ha", w_)
            mlp_layer(1, w2, "w2", ha[:, :w_], "ha", ha[:, :w_], "ha", w_)
            mlp_layer(2, w3, "w3", ha[:, :w_], "ha", h3[:, b0:b0 + w_], "h3", w_)

    CBH = 8

    class BS:
        pass
    bsets = []
    for si in range(2):
        bs = BS(); bs.k = "f%d_" % si
        def T_(name, shape, k=bs.k):
            return P.sb(k + name, shape)
        bs.Xt = T_("Xt", [128, CBH, 128]); bs.A = T_("A", [128, CBH, 256])
        bs.P1r = T_("P1r", [128, CBH, 128]); bs.P1i = T_("P1i", [128, CBH, 128])
        bs.P2r = T_("P2r", [128, CBH, 128]); bs.P2i = T_("P2i", [128, CBH, 128])
        bs.Hr = T_("Hr", [128, CBH, 128]); bs.Hi = T_("Hi", [128, CBH, 128])
        bs.Yo = T_("Yo", [64, CBH, 128])
        bs.ps1 = P.ps(bs.k + "ps1", [128, 512]); bs.psxr = P.ps(bs.k + "psxr", [128, 512]); bs.psxi = P.ps(bs.k + "psxi", [128, 512])
        bsets.append(bs)
    twcb = twc.unsqueeze(1).to_broadcast([128, CBH, 128]); twsb = tws.unsqueeze(1).to_broadcast([128, CBH, 128])

    def fl(t):
        return t[:].rearrange("p c k -> p (c k)")

    def fft_fwd(bs, src_dram, ch0, kdim, outr, outi, ork, oik):
        k = bs.k
        P.dma("sp", bs.Xt[0:kdim], src_dram[ch0:ch0 + CBH, 0:kdim * 128].rearrange("c (n2 n1) -> n2 c n1", n1=128),
              reads=[src_dram.tensor.name], writes=[k + "Xt"])
        yield
        for c2 in range(CBH // 2):
            for h in range(2):
                ch = 2 * c2 + h
                P.op("pe", lambda e: e.matmul(bs.ps1[:, h * 256:(h + 1) * 256], bs.Xt[0:kdim, ch, :], T1[0:kdim, :], start=True, stop=True),
                     reads=[k + "Xt", "tabs"], writes=[k + "ps1"])
            P.op("act", lambda e: e.activation(out=bs.A[:, 2 * c2:2 * c2 + 2, :].rearrange("p c k -> p (c k)"), in_=bs.ps1[:], func=AF.Copy),
                 reads=[k + "ps1"], writes=[k + "A"])
            yield
        emit_cmul(P, bs.P1r[:], bs.P1i[:], bs.A[:, :, 0:128], bs.A[:, :, 128:256], twcb, twsb, True,
                  (k + "P1r", k + "P1i", k + "A", k + "A", "tabs"))
        yield
        for b4 in range(CBH * 128 // 512):
            sl = slice(b4 * 512, (b4 + 1) * 512)
            P.op("pe", lambda e: e.matmul(bs.psxr[:], C128, fl(bs.P1r)[:, sl], start=True, stop=False), reads=["tabs", k + "P1r"], writes=[k + "psxr"])
            P.op("pe", lambda e: e.matmul(bs.psxr[:], S128, fl(bs.P1i)[:, sl], start=False, stop=True), reads=["tabs", k + "P1i"], writes=[k + "psxr"],
                 pe_acc=True)
            P.op("pe", lambda e: e.matmul(bs.psxi[:], C128, fl(bs.P1i)[:, sl], start=True, stop=False), reads=["tabs", k + "P1i"], writes=[k + "psxi"])
            P.op("pe", lambda e: e.matmul(bs.psxi[:], NEGS, fl(bs.P1r)[:, sl], start=False, stop=True), reads=["tabs", k + "P1r"], writes=[k + "psxi"],
                 pe_acc=True)
            P.op("act", lambda e: e.activation(out=fl(outr)[:, sl], in_=bs.psxr[:], func=AF.Copy), reads=[k + "psxr"], writes=[ork])
            P.op("dve", lambda e: e.tensor_copy(out=fl(outi)[:, sl], in_=bs.psxi[:]), reads=[k + "psxi"], writes=[oik])
            yield

    def fft_conv_batch(bs, ch0):
        k = bs.k
        yield from fft_fwd(bs, g_dram, ch0, 128, bs.Hr, bs.Hi, k + "Hr", k + "Hi")
        yield from fft_fwd(bs, a_dram, ch0, 64, bs.P2r, bs.P2i, k + "P2r", k + "P2i")
        P.op("pool", lambda e: e.tensor_tensor(out=bs.P1r[:], in0=bs.P2r[:], in1=bs.Hr[:], op=ALU.mult), reads=[k + "P2r", k + "Hr"], writes=[k + "P1r"])
        P.op("dve", lambda e: e.tensor_tensor(out=bs.P1i[:], in0=bs.P2i[:], in1=bs.Hi[:], op=ALU.mult), reads=[k + "P2i", k + "Hi"], writes=[k + "P1i"])
        P.op("dve", lambda e: e.tensor_tensor(out=bs.P1r[:], in0=bs.P1r[:], in1=bs.P1i[:], op=ALU.subtract), reads=[k + "P1r", k + "P1i"], writes=[k + "P1r"])
        yield
        P.op("pool", lambda e: e.tensor_tensor(out=bs.P1i[:], in0=bs.P2i[:], in1=bs.Hr[:], op=ALU.mult), reads=[k + "P2i", k + "Hr", k + "P1r"], writes=[k + "P1i"])
        P.op("dve", lambda e: e.tensor_tensor(out=bs.P2r[:], in0=bs.P2r[:], in1=bs.Hi[:], op=ALU.mult), reads=[k + "P2r", k + "Hi"], writes=[k + "P2r"])
        P.op("dve", lambda e: e.tensor_tensor(out=bs.P1i[:], in0=bs.P1i[:], in1=bs.P2r[:], op=ALU.add), reads=[k + "P1i", k + "P2r"], writes=[k + "P1i"])
        yield
        for c2 in range(CBH // 2):
            for h in range(2):
                ch = 2 * c2 + h
                P.op("pe", lambda e: e.matmul(bs.ps1[:, h * 256:(h + 1) * 256], bs.P1r[:, ch, :], TI1a, start=True, stop=False),
                     reads=[k + "P1r", "tabs"], writes=[k + "ps1"])
                P.op("pe", lambda e: e.matmul(bs.ps1[:, h * 256:(h + 1) * 256], bs.P1i[:, ch, :], TI1b, start=False, stop=True),
                     reads=[k + "P1i", "tabs"], writes=[k + "ps1"], pe_acc=True)
            P.op("act", lambda e: e.activation(out=bs.A[:, 2 * c2:2 * c2 + 2, :].rearrange("p c k -> p (c k)"), in_=bs.ps1[:], func=AF.Copy),
                 reads=[k + "ps1"], writes=[k + "A"])
            yield
        emit_cmul(P, bs.P2r[:], bs.P2i[:], bs.A[:, :, 0:128], bs.A[:, :, 128:256], twcb, twsb, False,
                  (k + "P2r", k + "P2i", k + "A", k + "A", "tabs"))
        yield
        for b4 in range(CBH * 128 // 512):
            sl = slice(b4 * 512, (b4 + 1) * 512)
            pt = bs.ps1[0:64, :]
            P.op("pe", lambda e: e.matmul(pt, CN, fl(bs.P2r)[:, sl], start=True, stop=False), reads=["tabs", k + "P2r"], writes=[k + "ps1"])
            P.op("pe", lambda e: e.matmul(pt, NSN, fl(bs.P2i)[:, sl], start=False, stop=True), reads=["tabs", k + "P2i"], writes=[k + "ps1"], pe_acc=True)
            P.op("act", lambda e: e.activation(out=bs.Yo[:].rearrange("p c k -> p (c k)")[:, sl], in_=pt, func=AF.Copy),
                 reads=[k + "ps1"], writes=[k + "Yo"])
            yield
        P.dma("sp", c_dram[ch0:ch0 + CBH, :].rearrange("c (m1 m2) -> m1 c m2", m2=128), bs.Yo[:], reads=[k + "Yo"], writes=["c_dram"])
        yield

    def run_fft_conv():
        batches = list(range(0, 128, CBH))
        for i0 in range(0, len(batches), 2):
            gens = [fft_conv_batch(bsets[j], batches[i0 + j]) for j in range(2) if i0 + j < len(batches)]
            alive = gens
            while alive:
                nxt = []
                for gen in alive:
                    try:
                        next(gen)
                        nxt.append(gen)
                    except StopIteration:
                        pass
                alive = nxt

    SEGW = 2048
    R = P.sb("R", [128, SEGW + 2]); ACC = P.sb("ACC", [128, SEGW]); GT = P.sb("GT", [128, SEGW]); CT = P.sb("CT", [128, SEGW])
    a_ctx = P.sb("a_ctx", [128, CTX]); cc1 = P.sb("cc1", [128, CTX]); cc2 = P.sb("cc2", [128, CTX])
    fwc = P.sb("fwc", [128, CTX]); bwc = P.sb("bwc", [128, CTX])
    gblk = P.sb("gblk", [128, 512]); grev = P.sb("grev", [128, 512]); dblk = P.sb("dblk", [128, 512]); zcol = P.sb("zcol", [128, 1])
    P.op("dve", lambda e: e.memset(zcol[:], 0.0), writes=["zcol"])
    segs = [(0, CTX)] + [(CTX + i * SEGW, SEGW) for i in range(SEQ // SEGW)]

    for tl in range(2):
        rows = slice(tl * 128, (tl + 1) * 128)
        for part in range(3):
            for (c0, w_) in segs:
                first = c0 in (0, CTX); last = (c0 + w_) in (CTX, T)
                if first or last:
                    P.op("dve", lambda e: e.memset(R[:, 0:w_ + 2], 0.0), writes=["R"])
                lo = c0 - (0 if first else 1); hi = c0 + w_ + (0 if last else 1)
                P.dma("sp", R[:, (1 if first else 0):(1 if first else 0) + hi - lo], zh_in[part, rows, lo:hi], writes=["R"])
                P.op("dve", lambda e: e.tensor_scalar(out=ACC[:, :w_], in0=R[:, 1:w_ + 1], scalar1=cw[:, tl, part, 1:2], scalar2=None,
                                                      op0=ALU.mult), reads=["R", "cw"], writes=["ACC"])
                P.op("dve", lambda e: e.scalar_tensor_tensor(out=ACC[:, :w_], in0=R[:, 0:w_], scalar=cw[:, tl, part, 0:1], in1=ACC[:, :w_],
                                                             op0=ALU.mult, op1=ALU.add), reads=["R", "cw", "ACC"], writes=["ACC"])
                P.op("dve", lambda e: e.scalar_tensor_tensor(out=ACC[:, :w_], in0=R[:, 2:w_ + 2], scalar=cw[:, tl, part, 2:3], in1=ACC[:, :w_],
                                                             op0=ALU.mult, op1=ALU.add), reads=["R", "cw", "ACC"], writes=["ACC"])
                if part == 0:
                    if c0 == 0:
                        P.op("act", lambda e: e.activation(out=a_ctx[:], in_=ACC[:, :CTX], func=AF.Copy), reads=["ACC"], writes=["a_ctx"])
                    else:
                        P.dma("act", a_dram[:, c0 - CTX:c0 - CTX + w_], ACC[:, :w_], reads=["ACC"], writes=["a_dram"])
                else:
                    dst = x1c if part == 1 else x2c
                    P.dma("act", dst[:, c0:c0 + w_], ACC[:, :w_], reads=["ACC"], writes=[dst.tensor.name])
        for o in range(2):
            for d in range(2):
                for b0 in range(0, SEQ, 512):
                    P.op("pe", lambda e: e.matmul(psm[:], w4[:, o, d, rows], h3l[:, b0:b0 + 512], start=True, stop=True),
                         reads=["w4", "h3"], writes=["psm"])
                    P.dma("sp", dblk[:], dl_in[rows, b0:b0 + 512], writes=["dblk"])
                    P.op("dve", lambda e: e.tensor_tensor(out=gblk[:], in0=psm[:], in1=dblk[:], op=ALU.mult), reads=["psm", "dblk"], writes=["gblk"])
                    if d == 0:
                        P.dma("act", g_dram[:, b0:b0 + 512], gblk[:], reads=["gblk"], writes=["g_dram"])
                    else:
                        P.op("pool", lambda e: e.tensor_copy(out=grev[:], in_=gblk[:, ::-1]), reads=["gblk"], writes=["grev"])
                        if b0 == 0:
                            P.dma("act", g_dram[:, NFFT - 511:NFFT], grev[:, 0:511], reads=["grev"], writes=["g_dram"])
                        else:
                            P.dma("act", g_dram[:, NFFT - b0 - 511:NFFT - b0 + 1], grev[:], reads=["grev"], writes=["g_dram"])
                P.op("pe", lambda e: e.matmul(psm[:, :CTX], w4[:, o, d, rows], h3c[:], start=True, stop=True), reads=["w4", "h3"], writes=["psm"])
                P.dma("sp", dblk[:, :CTX], dc_in[rows, :], writes=["dblk"])
                fc = fwc if d == 0 else bwc
                P.op("dve", lambda e: e.tensor_tensor(out=fc[:], in0=psm[:, :CTX], in1=dblk[:, :CTX], op=ALU.mult),
                     reads=["psm", "dblk"], writes=["fwc" if d == 0 else "bwc"])
            P.dma("act", g_dram[:, SEQ:SEQ + 1], zcol[:], reads=["zcol"], writes=["g_dram"], allow_slow_non_contiguous=True)
            run_fft_conv()
            P.op("dve", lambda e: e.memset(cc1[:], 0.0), writes=["cc1"])
            P.op("pool", lambda e: e.memset(cc2[:], 0.0), writes=["cc2"])
            for d in range(CTX):
                P.op("dve", lambda e: e.scalar_tensor_tensor(out=cc1[:, d:CTX], in0=a_ctx[:, 0:CTX - d], scalar=fwc[:, d:d + 1],
                                                             in1=cc1[:, d:CTX], op0=ALU.mult, op1=ALU.add),
                     reads=["a_ctx", "fwc", "cc1"], writes=["cc1"])
                if d >= 1:
                    P.op("dve", lambda e: e.scalar_tensor_tensor(out=cc2[:, 0:CTX - d], in0=a_ctx[:, d:CTX], scalar=bwc[:, d:d + 1],
                                                                  in1=cc2[:, 0:CTX - d], op0=ALU.mult, op1=ALU.add),
                         reads=["a_ctx", "bwc", "cc2"], writes=["cc2"])
            P.op("dve", lambda e: e.tensor_tensor(out=cc1[:], in0=cc1[:], in1=cc2[:], op=ALU.add), reads=["cc1", "cc2"], writes=["cc1"])
            gsrc = x1c if o == 0 else x2c
            for (c0, w_) in segs:
                P.dma("sp", GT[:, :w_], gsrc[:, c0:c0 + w_], reads=[gsrc.tensor.name], writes=["GT"])
                if c0 == 0:
                    csrc = cc1[:]; asrc = a_ctx[:]; ck = "cc1"; ak = "a_ctx"
                else:
                    P.dma("sp", CT[:, :w_], c_dram[:, c0 - CTX:c0 - CTX + w_], reads=["c_dram"], writes=["CT"])
                    P.dma("sp", R[:, :w_], a_dram[:, c0 - CTX:c0 - CTX + w_], reads=["a_dram"], writes=["R"])
                    csrc = CT[:, :w_]; asrc = R[:, :w_]; ck = "CT"; ak = "R"
                P.op("dve", lambda e: e.scalar_tensor_tensor(out=ACC[:, :w_], in0=asrc, scalar=hbias[:, tl, o:o + 1], in1=csrc,
                                                             op0=ALU.mult, op1=ALU.add), reads=[ak, ck, "hbias"], writes=["ACC"])
                P.op("pool", lambda e: e.tensor_tensor(out=ACC[:, :w_], in0=ACC[:, :w_], in1=GT[:, :w_], op=ALU.mult),
                     reads=["ACC", "GT"], writes=["ACC"])
                if o == 0:
                    if c0 == 0:
                        P.op("act", lambda e: e.activation(out=a_ctx[:], in_=ACC[:, :CTX], func=AF.Copy), reads=["ACC"], writes=["a_ctx"])
                    else:
                        P.dma("act", a_dram[:, c0 - CTX:c0 - CTX + w_], ACC[:, :w_], reads=["ACC"], writes=["a_dram"])
                else:
                    P.dma("act", y_out[rows, c0:c0 + w_], ACC[:, :w_], reads=["ACC"], is_output=True)
    return P.finish()


def hyena_consts(L):
    pos = np.arange(L, dtype=np.float32)
    t = pos / np.float32(max(L - 1, 1))
    ang = np.float32(2.0 * math.pi / L) * pos
    bands = np.linspace(1e-4, 15, 16, dtype=np.float32)
    z = np.concatenate([t[:, None], np.cos(ang[:, None] * bands), -np.sin(ang[:, None] * bands)], axis=-1).astype(np.float32)
    mx = math.log(1e-2) / 0.3
    mn = math.log(1e-2) / 1.5
    deltas = np.abs(np.linspace(mn, mx, HYW, dtype=np.float32))
    dec = np.exp(-t[None, :] * deltas[:, None]).astype(np.float32)
    return np.ascontiguousarray(z.T), dec


def fft_tabs():
    p = np.arange(128, dtype=np.float64)[:, None]; j = np.arange(128, dtype=np.float64)[None, :]
    a128 = 2 * np.pi * p * j / 128.0
    aN = 2 * np.pi * p * j / NFFT
    tabs = np.zeros((128, 12, 128), np.float64)
    tabs[:, 0] = np.cos(a128); tabs[:, 1] = -np.sin(a128); tabs[:, 2] = np.sin(a128)
    tabs[:, 3] = np.cos(aN); tabs[:, 4] = np.sin(aN)
    tabs[:, 5] = np.cos(a128) / NFFT; tabs[:, 6] = -np.sin(a128) / NFFT
    tabs[:, 7] = np.cos(a128); tabs[:, 8] = np.sin(a128)
    tabs[:, 9] = -np.sin(a128); tabs[:, 10] = np.cos(a128)
    return tabs.astype(np.float32)


def run_eh(zs, hy_conv, w1, b1, w2, b2, w3, b3, w4, freq, bias):
    zl, decl = hyena_consts(SEQ); zc, decc = hyena_consts(CTX)
    tabs = fft_tabs()
    bfr = np.zeros((HYF, 2, 3), np.float32)
    bfr[:, 0, 0] = b1; bfr[:, 0, 1] = b2; bfr[:, 0, 2] = b3
    bfr[:, 1, :] = np.asarray(freq, np.float32).T
    w4r = np.asarray(w4, np.float32).reshape(HYF, 2, 2, HYW)
    in_maps = []
    for j in range(NCORES):
        ch = slice(256 * j, 256 * (j + 1))
        cwj = np.zeros((128, 2, 3, 3), np.float32)
        for part in range(3):
            blk = np.asarray(hy_conv[:, part * HYW + 256 * j: part * HYW + 256 * (j + 1)], np.float32)
            cwj[:, :, part, :] = blk.T.reshape(2, 128, 3).transpose(1, 0, 2)
        hbj = np.ascontiguousarray(np.asarray(bias[:, ch], np.float32).T.reshape(2, 128, 2).transpose(1, 0, 2))
        in_maps.append({"zh": np.ascontiguousarray(zs[j][:768].reshape(3, 256, T)), "convw": cwj, "hbias": hbj,
                        "w1": np.asarray(w1, np.float32), "w2": np.asarray(w2, np.float32), "w3": np.asarray(w3, np.float32),
                        "bfr": bfr, "w4": np.ascontiguousarray(w4r[:, :, :, ch]), "zposl": zl, "zposc": zc,
                        "decl": np.ascontiguousarray(decl[ch]), "decc": np.ascontiguousarray(decc[ch]), "tabs": tabs})
    res = run(build_eh(), in_maps)
    return [r["yh"] for r in res]


DNC = 128
NCHK = T // DNC
DK = 128


def dn_masks():
    t = np.arange(128)[:, None]; i = np.arange(128)[None, :]
    m = np.zeros((10, 128, 128), np.float32)
    m[0] = (t <= i); m[1] = (t > i)
    m[2] = np.where(t >= i, 0.0, -30000.0)
    m[3] = np.where(i >= t, 0.0, -30000.0)
    m[4] = (t > i)
    m[5] = (t >= i); m[6] = (t < i)
    m[7] = np.where(t <= i, 0.0, -30000.0)
    m[8] = np.where(i <= t, 0.0, -30000.0)
    m[9] = (t < i)
    return np.ascontiguousarray(m.transpose(1, 0, 2))


def build_ed(dbg=None):
    P = Prog()
    dbg = dbg or {}
    nc = P.nc
    qkvz_in = P.dram_in("qkvz", [4, 2, 128, T])
    braw_in = P.dram_in("braw", [4, T]); araw_in = P.dram_in("araw", [4, T])
    cw_in = P.dram_in("dconv", [128, 3, 2, 5])
    aA_in = P.dram_in("aA", [4, 2])
    nw_in = P.dram_in("normw", [128, 128])
    mk_in = P.dram_in("masks", [128, 10, 128])
    id_in = P.dram_in("ident", [128, 128])
    y_out = P.dram_out("yd", [2, 128, T])
    qd = nc.dram_tensor("qd", [2, 128, T], F32, kind="Internal").ap()
    kd = nc.dram_tensor("kd", [2, 128, T], F32, kind="Internal").ap()
    vd = nc.dram_tensor("vd", [2, 128, T], F32, kind="Internal").ap()
    dsc = {0: qd, 1: kd, 2: vd}

    def ld(name, src, shape, q="sp"):
        t = P.sb(name, shape)
        P.dma(q, t[:], src, writes=[name])
        return t
    cw = ld("cw", cw_in, [128, 3, 2, 5]); aA = ld("aA", aA_in, [4, 2]); nw = ld("nw", nw_in, [128, 128])
    mk = ld("mk", mk_in, [128, 10, 128]); ident = ld("ident", id_in, [128, 128])
    ones128 = P.sb("ones128", [128, 128])
    P.op("dve", lambda e: e.memset(ones128[:], 1.0), writes=["ones128"])
    PS = [P.ps("PS%d" % i, [128, 512]) for i in range(8)]
    pctr = [0]

    def pslot():
        n = pctr[0]; pctr[0] += 1
        b, s = n % 8, (n // 8) % 4
        return PS[b][:, s * 128:(s + 1) * 128], ("PS", b, s)

    SG = 22 * DNC
    bg = P.sb("bg", [4, 2, SG])
    tA = P.sb("tA", [4, SG]); tB = P.sb("tB", [4, SG])
    nA = P.sb("nA", [4, 1])
    P.op("act", lambda e: e.activation(out=nA[:], in_=aA[:, 0:1], func=AF.Exp), reads=["aA"], writes=["nA"])
    P.op("dve", lambda e: e.tensor_scalar(out=nA[:], in0=nA[:], scalar1=-1.0, scalar2=None, op0=ALU.mult), reads=["nA"], writes=["nA"])
    BGT = P.sb("BGT", [128, NCHK, 2, 4])
    for sgi in range(3):
        s0 = sgi * SG
        P.dma("sp", tA[:], braw_in[:, s0:s0 + SG], writes=["tA"])
        P.op("act", lambda e: e.activation(out=bg[:, 0, :], in_=tA[:], func=AF.Sigmoid), reads=["tA"], writes=["bg"])
        P.dma("sp", tB[:], araw_in[:, s0:s0 + SG], writes=["tB"])
        P.op("dve", lambda e: e.tensor_scalar(out=tB[:], in0=tB[:], scalar1=aA[:, 1:2], scalar2=None, op0=ALU.add), reads=["tB", "aA"], writes=["tB"])
        P.op("act", lambda e: e.activation(out=tA[:], in_=tB[:], func=AF.Abs), reads=["tB", "bg"], writes=["tA"])
        P.op("act", lambda e: e.activation(out=tA[:], in_=tA[:], func=AF.Exp, scale=-1.0), reads=["tA"], writes=["tA"])
        P.op("dve", lambda e: e.tensor_scalar_add(out=tA[:], in0=tA[:], scalar1=1.0), reads=["tA"], writes=["tA"])
        P.op("act", lambda e: e.activation(out=tA[:], in_=tA[:], func=AF.Ln), reads=["tA"], writes=["tA"])
        P.op("dve", lambda e: e.tensor_scalar_max(out=tB[:], in0=tB[:], scalar1=0.0), reads=["tB"], writes=["tB"])
        P.op("dve", lambda e: e.tensor_tensor(out=tB[:], in0=tB[:], in1=tA[:], op=ALU.add), reads=["tA", "tB"], writes=["tB"])
        P.op("dve", lambda e: e.tensor_scalar(out=bg[:, 1, :], in0=tB[:], scalar1=nA[:, 0:1], scalar2=None, op0=ALU.mult),
             reads=["tB", "nA"], writes=["bg"])
        for cl in range(22):
            c = sgi * 22 + cl
            for w_ in range(2):
                pt, pk = pslot()
                P.op("pe", lambda e: e.matmul(pt[:, 0:4], bg[:, w_, cl * DNC:(cl + 1) * DNC], ident[0:4, 0:4], start=True, stop=True), reads=["bg", "ident"], writes=[pk])
                P.op("dve", lambda e: e.tensor_copy(out=BGT[:, c, w_, :], in_=pt[:, 0:4]), reads=[pk], writes=["BGT"])
    NBG = P.sb("NBG", [128, NCHK, 4])
    P.op("dve", lambda e: e.tensor_scalar(out=NBG[:], in0=BGT[:, :, 0, :], scalar1=-1.0, scalar2=None, op0=ALU.mult), reads=["BGT"], writes=["NBG"])

    SEGW = 2048
    R = P.sb("R", [128, SEGW + 4]); ACC = P.sb("ACC", [128, SEGW]); SQ = P.sb("SQ", [128, SEGW]); RS = P.sb("RS", [128, 512])
    segs = [(0, CTX)] + [(CTX + i * SEGW, SEGW) for i in range(SEQ // SEGW)]
    for hd in range(2):
        for part in range(3):
            for (c0, w_) in segs:
                first = c0 in (0, CTX); last = (c0 + w_) in (CTX, T)
                if first or last:
                    P.op("pool", lambda e: e.memset(R[:, 0:w_ + 4], 0.0), writes=["R"])
                lo = c0 - (0 if first else 2); hi = c0 + w_ + (0 if last else 2)
                off = 2 if first else 0
                P.dma("sp", R[:, off:off + hi - lo], qkvz_in[part, hd, :, lo:hi], writes=["R"])
                P.op("dve", lambda e: e.tensor_scalar(out=ACC[:, :w_], in0=R[:, 0:w_], scalar1=cw[:, part, hd, 0:1], scalar2=None, op0=ALU.mult),
                     reads=["R", "cw"], writes=["ACC"])
                for k in range(1, 5):
                    P.op("dve", lambda e: e.scalar_tensor_tensor(out=ACC[:, :w_], in0=R[:, k:k + w_], scalar=cw[:, part, hd, k:k + 1],
                                                                 in1=ACC[:, :w_], op0=ALU.mult, op1=ALU.add),
                         reads=["R", "cw", "ACC"], writes=["ACC"])
                P.op("act", lambda e: e.activation(out=ACC[:, :w_], in_=ACC[:, :w_], func=AF.Silu), reads=["ACC"], writes=["ACC"])
                if part < 2:
                    P.op("pool", lambda e: e.tensor_tensor(out=SQ[:, :w_], in0=ACC[:, :w_], in1=ACC[:, :w_], op=ALU.mult), reads=["ACC"], writes=["SQ"])
                    for b0 in range(0, w_, 512):
                        bw = min(512, w_ - b0)
                        pb = PS[pctr[0] % 8]; pbk = ("PS", pctr[0] % 8, 0); pctr[0] += 1
                        allk = [("PS", pbk[1], s_) for s_ in range(4)]
                        P.op("pe", lambda e: e.matmul(pb[:, :bw], ones128[:], SQ[:, b0:b0 + bw], start=True, stop=True),
                             reads=["SQ", "ones128"], writes=allk)
                        P.op("dve", lambda e: e.tensor_scalar_add(out=RS[:, :bw], in0=pb[:, :bw], scalar1=1e-6),
                             reads=allk, writes=["RS"])
                        P.op("act", lambda e: e.activation(out=RS[:, :bw], in_=RS[:, :bw], func=AF.Sqrt), reads=["RS"], writes=["RS"])
                        P.op("dve", lambda e: e.reciprocal(out=RS[:, :bw], in_=RS[:, :bw]), reads=["RS"], writes=["RS"])
                        if part == 0:
                            P.op("dve", lambda e: e.scalar_tensor_tensor(out=ACC[:, b0:b0 + bw], in0=ACC[:, b0:b0 + bw], scalar=DK ** -0.5,
                                                                         in1=RS[:, :bw], op0=ALU.mult, op1=ALU.mult),
                                 reads=["ACC", "RS"], writes=["ACC"])
                        else:
                            P.op("dve", lambda e: e.tensor_tensor(out=ACC[:, b0:b0 + bw], in0=ACC[:, b0:b0 + bw], in1=RS[:, :bw], op=ALU.mult),
                                 reads=["ACC", "RS"], writes=["ACC"])
                P.dma("act", dsc[part][hd, :, c0:c0 + w_], ACC[:, :w_], reads=["ACC"], writes=[dsc[part].tensor.name])

    if dbg.get('stop') == 'pre':
        P.dma('sp', y_out[0, :, 0:128], ident[:], reads=['ident'], is_output=True)
        return P.finish()
    Oh = [P.sb("O%d" % h, [128, NCHK, 128]) for h in range(2)]
    for h in range(2):
        P.op("pool", lambda e: e.memset(Oh[h][:], 0.0), writes=[("O", h, c) for c in range(NCHK)])

    def mm(lhsT, lk, rhs, rk):
        pt, pk = pslot()
        n = rhs.shape[-1]
        m = lhsT.shape[-1]
        pt = pt[0:m, 0:n]
        P.op("pe", lambda e: e.matmul(pt, lhsT, rhs, start=True, stop=True), reads=lk + rk, writes=[pk])
        return pt, pk

    def dvop(fn, reads, writes, eng="dve"):
        P.op(eng, fn, reads=reads, writes=writes)

    class St:
        pass
    streams = []
    for hd in range(2):
        for dr in range(2):
            st = St()
            sid = "s%d%d_" % (hd, dr)
            st.sid = sid; st.hd = hd; st.dr = dr
            def T_(name, shape=(128, 128), sid=sid):
                return P.sb(sid + name, list(shape))
            st.qT = [T_("qT%d" % i) for i in range(2)]; st.kT = [T_("kT%d" % i) for i in range(2)]; st.vT = [T_("vT%d" % i) for i in range(2)]
            st.ktm = T_("ktm"); st.bv = T_("bv"); st.kbg = T_("kbg"); st.kdec = T_("kdec"); st.gMC = T_("gMC")
            st.dec = T_("dec"); st.decT = T_("decT")
            st.Xs = [T_("X%d" % i) for i in range(2)]; st.Ys = [T_("Y%d" % i) for i in range(2)]
            st.Pm = [T_("Pm%d" % i) for i in range(2)]; st.PTm = [T_("PTm%d" % i) for i in range(2)]
            st.usb = T_("usb"); st.wTs = T_("wTs"); st.vnew = T_("vnew"); st.o1 = T_("o1"); st.qkm = T_("qkm")
            st.cols = T_("cols", (128, 8)); st.S = T_("S")
            st.order = list(range(NCHK)) if dr == 0 else [1, 0] + list(range(NCHK - 1, 1, -1))
            if 'nchunks' in dbg:
                st.order = st.order[:dbg['nchunks']]
            P.op("pool", lambda e: e.memset(st.S[:], 0.0), writes=[sid + "S"])
            streams.append(st)

    def chunk_gen(st, c, b):
        sid = st.sid; hd = st.hd; dr = st.dr
        def K_(n):
            return sid + n
        row = dr * 2 + hd
        mo = 5 * dr
        MC = mk[:, mo + 0, :]; MS = mk[:, mo + 1, :]; NEG = mk[:, mo + 2, :]; STRICT = mk[:, mo + 4, :]
        qT, kT, vT = st.qT[b], st.kT[b], st.vT[b]
        cols = st.cols; S = st.S
        t0 = c * DNC
        P.dma("sp", qT[:], qd[hd, :, t0:t0 + DNC], reads=["qd"], writes=[K_("qT%d" % b)])
        P.dma("sp", kT[:], kd[hd, :, t0:t0 + DNC], reads=["kd"], writes=[K_("kT%d" % b)])
        P.dma("sp", vT[:], vd[hd, :, t0:t0 + DNC], reads=["vd"], writes=[K_("vT%d" % b)])
        qk_, kk_, vk_ = [K_("qT%d" % b)], [K_("kT%d" % b)], [K_("vT%d" % b)]
        beta = BGT[:, c, 0, row:row + 1]; g = BGT[:, c, 1, row:row + 1]; nbeta = NBG[:, c, row:row + 1]
        yield
        pt, pk = mm(kT[:], kk_, ident[:], ["ident"])
        dvop(lambda e: e.activation(out=st.ktm[:], in_=pt, func=AF.Copy), [pk], [K_("ktm")], "act")
        pt, pk = mm(vT[:], vk_, ident[:], ["ident"])
        dvop(lambda e: e.tensor_scalar(out=st.bv[:], in0=pt, scalar1=beta, scalar2=None, op0=ALU.mult), [pk, "BGT"], [K_("bv")])
        yield
        pt, pk = mm(MC, ["mk"], g, ["BGT"])
        dvop(lambda e: e.tensor_copy(out=cols[:, 0:1], in_=pt[:, 0:1]), [pk], [K_("cols")])
        pt, pk = mm(ones128[:], ["ones128"], g, ["BGT"])
        dvop(lambda e: e.tensor_copy(out=cols[:, 3:4], in_=pt[:, 0:1]), [pk], [K_("cols")])
        dvop(lambda e: e.tensor_scalar(out=st.gMC[:], in0=MC, scalar1=g, scalar2=None, op0=ALU.mult), ["mk", "BGT"], [K_("gMC")])
        yield
        dvop(lambda e: e.activation(out=cols[:, 1:2], in_=cols[:, 0:1], func=AF.Exp), [K_("cols")], [K_("cols")], "act")
        dvop(lambda e: e.activation(out=cols[:, 4:5], in_=cols[:, 3:4], func=AF.Exp), [K_("cols")], [K_("cols")], "act")
        dvop(lambda e: e.activation(out=cols[:, 5:6], in_=cols[:, 0:1], func=AF.Exp, scale=-1.0, bias=cols[:, 3:4]), [K_("cols")], [K_("cols")], "act")
        pt, pk = mm(st.gMC[:], [K_("gMC")], MS, ["mk"])
        dvop(lambda e: e.tensor_tensor(out=st.dec[:], in0=pt, in1=NEG, op=ALU.add), [pk, "mk"], [K_("dec")])
        yield
        dvop(lambda e: e.tensor_tensor(out=cols[:, 2:3], in0=cols[:, 1:2], in1=beta, op=ALU.mult), [K_("cols"), "BGT"], [K_("cols")])
        dvop(lambda e: e.activation(out=st.dec[:], in_=st.dec[:], func=AF.Exp), [K_("dec")], [K_("dec")], "act")
        dvop(lambda e: e.tensor_scalar(out=st.kbg[:], in0=st.ktm[:], scalar1=cols[:, 2:3], scalar2=None, op0=ALU.mult), [K_("ktm"), K_("cols")], [K_("kbg")])
        dvop(lambda e: e.tensor_scalar(out=st.kdec[:], in0=st.ktm[:], scalar1=cols[:, 5:6], scalar2=0.0, op0=ALU.mult, op1=ALU.add),
             [K_("ktm"), K_("cols")], [K_("kdec")], "pool")
        yield
        pt, pk = mm(st.dec[:], [K_("dec")], ident[:], ["ident"])
        dvop(lambda e: e.tensor_copy(out=st.decT[:], in_=pt), [pk], [K_("decT")])
        X, Y = st.Xs[0], st.Ys[0]
        pt, pk = mm(kT[:], kk_, kT[:], kk_)
        dvop(lambda e: e.tensor_tensor(out=X[:], in0=pt, in1=st.dec[:], op=ALU.mult), [pk, K_("dec")], [K_("X0")])
        dvop(lambda e: e.scalar_tensor_tensor(out=X[:], in0=X[:], scalar=nbeta, in1=STRICT, op0=ALU.mult, op1=ALU.mult),
             [K_("X0"), "NBG", "mk"], [K_("X0")])
        yield
        pt, pk = mm(X[:], [K_("X0")], ident[:], ["ident"])
        dvop(lambda e: e.activation(out=Y[:], in_=pt, func=AF.Copy), [pk], [K_("Y0")], "act")
        dvop(lambda e: e.tensor_tensor(out=st.Pm[0][:], in0=X[:], in1=ident[:], op=ALU.add), [K_("X0"), "ident"], [K_("Pm0")])
        yield
        dvop(lambda e: e.tensor_tensor(out=st.PTm[0][:], in0=Y[:], in1=ident[:], op=ALU.add), [K_("Y0"), "ident"], [K_("PTm0")], "pool")
        cur = 0
        for lv in range(6):
            nx = 1 - cur
            ptx, pkx = mm(st.Ys[cur][:], [K_("Y%d" % cur)], st.Xs[cur][:], [K_("X%d" % cur)])
            pty, pky = mm(st.Xs[cur][:], [K_("X%d" % cur)], st.Ys[cur][:], [K_("Y%d" % cur)])
            dvop(lambda e: e.tensor_copy(out=st.Xs[nx][:], in_=ptx), [pkx], [K_("X%d" % nx)])
            dvop(lambda e: e.activation(out=st.Ys[nx][:], in_=pty, func=AF.Copy), [pky], [K_("Y%d" % nx)], "act")
            yield
            if lv < 5:
                ptp, pkp = mm(st.PTm[cur][:], [K_("PTm%d" % cur)], st.Xs[nx][:], [K_("X%d" % nx)])
                dvop(lambda e: e.tensor_tensor(out=st.Pm[nx][:], in0=ptp, in1=st.Pm[cur][:], op=ALU.add), [pkp, K_("Pm%d" % cur)], [K_("Pm%d" % nx)])
            ptq, pkq = mm(st.Pm[cur][:], [K_("Pm%d" % cur)], st.Ys[nx][:], [K_("Y%d" % nx)])
            dvop(lambda e: e.tensor_tensor(out=st.PTm[nx][:], in0=ptq, in1=st.PTm[cur][:], op=ALU.add), [pkq, K_("PTm%d" % cur)], [K_("PTm%d" % nx)])
            cur = nx
            yield
        TinvT = st.PTm[cur]; tk = [K_("PTm%d" % cur)]
        pt, pk = mm(TinvT[:], tk, st.bv[:], [K_("bv")])
        dvop(lambda e: e.tensor_copy(out=st.usb[:], in_=pt), [pk], [K_("usb")])
        pt, pk = mm(st.kbg[:], [K_("kbg")], TinvT[:], tk)
        dvop(lambda e: e.activation(out=st.wTs[:], in_=pt, func=AF.Copy), [pk], [K_("wTs")], "act")
        yield
        pt, pk = mm(st.wTs[:], [K_("wTs")], S[:], [K_("S")])
        dvop(lambda e: e.tensor_tensor(out=st.vnew[:], in0=st.usb[:], in1=pt, op=ALU.subtract), [K_("usb"), pk], [K_("vnew")])
        pt, pk = mm(qT[:], qk_, S[:], [K_("S")])
        dvop(lambda e: e.activation(out=st.o1[:], in_=pt, func=AF.Copy, scale=cols[:, 1:2]), [pk, K_("cols")], [K_("o1")], "act")
        pt, pk = mm(kT[:], kk_, qT[:], qk_)
        dvop(lambda e: e.tensor_tensor(out=st.qkm[:], in0=pt, in1=st.decT[:], op=ALU.mult), [pk, K_("decT")], [K_("qkm")])
        yield
        pt, pk = mm(st.qkm[:], [K_("qkm")], st.vnew[:], [K_("vnew")])
        dvop(lambda e: e.tensor_tensor(out=st.o1[:], in0=pt, in1=st.o1[:], op=ALU.add), [pk, K_("o1")], [K_("o1")])
        dvop(lambda e: e.tensor_tensor(out=Oh[hd][:, c, :], in0=Oh[hd][:, c, :], in1=st.o1[:], op=ALU.add), [("O", hd, c), K_("o1")], [("O", hd, c)], "pool")
        pt, pk = mm(st.kdec[:], [K_("kdec")], st.vnew[:], [K_("vnew")])
        dvop(lambda e: e.scalar_tensor_tensor(out=S[:], in0=S[:], scalar=cols[:, 4:5], in1=pt, op0=ALU.mult, op1=ALU.add),
             [K_("S"), K_("cols"), pk], [K_("S")])
        yield

    nsteps = len(streams[0].order)
    for k in range(nsteps):
        gens = [chunk_gen(st, st.order[k], k % 2) for st in streams]
        alive = list(gens)
        while alive:
            nxt = []
            for gen in alive:
                try:
                    next(gen)
                    nxt.append(gen)
                except StopIteration:
                    pass
            alive = nxt

    if dbg.get('stop') == 'scan':
        P.dma('sp', y_out[0, :, 0:128], ident[:], reads=['ident'], is_output=True)
        return P.finish()
    zt = [P.sb("zt%d" % i, [128, 128]) for i in range(3)]
    ob_ = [P.sb("ob%d" % i, [128, 128]) for i in range(3)]
    sqt = P.sb("sqt", [128, 128]); nrm = P.sb("nrm", [128, 128]); gz = P.sb("gz", [128, 128]); ncol = P.sb("ncol", [128, 2])
    for hd in range(2):
        O = Oh[hd]
        for c in range(NCHK):
            b = c % 3
            t0 = c * DNC
            P.dma("sp", zt[b][:], qkvz_in[3, hd, :, t0:t0 + DNC], writes=["zt%d" % b])
            dvop(lambda e: e.tensor_tensor(out=sqt[:], in0=O[:, c, :], in1=O[:, c, :], op=ALU.mult), [("O", hd, c)], ["sqt"], "pool")
            dvop(lambda e: e.reduce_sum(out=ncol[:, 0:1], in_=sqt[:], axis=AX.X), ["sqt"], ["ncol"])
            dvop(lambda e: e.tensor_scalar(out=ncol[:, 1:2], in0=ncol[:, 0:1], scalar1=1.0 / 128, scalar2=1e-6, op0=ALU.mult, op1=ALU.add),
                 ["ncol"], ["ncol"])
            dvop(lambda e: e.activation(out=ncol[:, 1:2], in_=ncol[:, 1:2], func=AF.Sqrt), ["ncol"], ["ncol"], "act")
            dvop(lambda e: e.reciprocal(out=ncol[:, 1:2], in_=ncol[:, 1:2]), ["ncol"], ["ncol"])
            dvop(lambda e: e.scalar_tensor_tensor(out=nrm[:], in0=O[:, c, :], scalar=ncol[:, 1:2], in1=nw[:], op0=ALU.mult, op1=ALU.mult),
                 [("O", hd, c), "ncol", "nw"], ["nrm"])
            pt, pk = mm(zt[b][:], ["zt%d" % b], ident[:], ["ident"])
            dvop(lambda e: e.activation(out=gz[:], in_=pt, func=AF.Silu), [pk], ["gz"], "act")
            dvop(lambda e: e.tensor_tensor(out=nrm[:], in0=nrm[:], in1=gz[:], op=ALU.mult), ["nrm", "gz"], ["nrm"])
            pt, pk = mm(nrm[:], ["nrm"], ident[:], ["ident"])
            dvop(lambda e: e.tensor_copy(out=ob_[b][:], in_=pt), [pk], ["ob%d" % b])
            P.dma("act", y_out[hd, :, t0:t0 + DNC], ob_[b][:], reads=["ob%d" % b], is_output=True)
    return P.finish()


def run_ed(zs, dn_conv, a_log, dt_bias, norm_w, dbg=None):
    masks = dn_masks()
    ident = np.eye(128, dtype=np.float32)
    nwr = np.ascontiguousarray(np.broadcast_to(np.asarray(norm_w, np.float32)[None, :], (128, 128)))
    in_maps = []
    for j in range(NCORES):
        zz = zs[j]
        qkvz = np.ascontiguousarray(zz[768:1792].reshape(4, 2, 128, T))
        ab = zz[1792:1800]
        braw = np.ascontiguousarray(ab[0:4]); araw = np.ascontiguousarray(ab[4:8])
        cwj = np.zeros((128, 3, 2, 5), np.float32)
        for part in range(3):
            blk = np.asarray(dn_conv[:, part * DNW + 256 * j: part * DNW + 256 * (j + 1)], np.float32)
            cwj[:, part, :, :] = blk.T.reshape(2, 128, 5).transpose(1, 0, 2)
        aA = np.zeros((4, 2), np.float32)
        for dr in range(2):
            for hd in range(2):
                aA[dr * 2 + hd, 0] = a_log[dr, 2 * j + hd]
                aA[dr * 2 + hd, 1] = dt_bias[dr, 2 * j + hd]
        in_maps.append({"qkvz": qkvz, "braw": braw, "araw": araw, "dconv": cwj, "aA": aA, "normw": nwr, "masks": masks, "ident": ident})
    res = run(build_ed(dbg), in_maps)
    return [r["yd"].reshape(256, T) for r in res]


GRID_W = 64


def lat_to_col_major(aT):
    out = aT.copy()
    lat = aT[:, CTX:]
    out[:, CTX:] = lat.reshape(lat.shape[0], SEQ // GRID_W, GRID_W).transpose(0, 2, 1).reshape(lat.shape[0], SEQ)
    return out


def lat_from_col_major(aT):
    out = aT.copy()
    lat = aT[:, CTX:]
    out[:, CTX:] = lat.reshape(lat.shape[0], GRID_W, SEQ // GRID_W).transpose(0, 2, 1).reshape(lat.shape[0], SEQ)
    return out


def kernel(x, c, ctx, c_ctx, ada_w, ada_b, ln_g, ln_b, ev_w_in, ev_w_out, hy_conv, hy_w1, hy_b1, hy_w2, hy_b2, hy_w3, hy_b3,
           hy_w4, hy_freq, hy_bias, dn_conv, dn_a_log, dn_dt_bias, dn_norm_w, s5_lam_re, s5_lam_im, s5_log_dt, s5_b_re, s5_b_im,
           s5_c_re, s5_c_im, s5_d, od_w_glu, moe_router, moe_w_in, moe_w_out):
    f32 = np.float32
    modT = run_k0({"c": c, "c_ctx": c_ctx, "ada_w": ada_w, "ada_b": ada_b})
    hT = np.ascontiguousarray(np.concatenate([np.asarray(ctx, f32)[0], np.asarray(x, f32)[0]], axis=0).T)
    for l in range(DEPTH):
        col = (l // 2) % 2 == 1
        i = l // 2
        mod_l = np.ascontiguousarray(modT[l])
        if l % 2 == 0:
            uT = run_mod(hT, mod_l)
            if col:
                uT = lat_to_col_major(uT)
            zs = run_ea1(uT, np.asarray(ev_w_in[i], f32))
            yh = run_eh(zs, hy_conv[i], hy_w1[i], hy_b1[i], hy_w2[i], hy_b2[i], hy_w3[i], hy_b3[i], hy_w4[i], hy_freq[i], hy_bias[i])
            yd = run_ed(zs, dn_conv[i], dn_a_log[i], dn_dt_bias[i], dn_norm_w[i])
            fT = np.concatenate(yh + yd, axis=0)
            if col:
                fT = lat_from_col_major(fT)
            h1T, h2T, affT = run_x3(False, fT, hT, mod_l, ev_w_out[i], ln_g[l, 0], ln_b[l, 0], moe_router[l])
        else:
            hp = lat_to_col_major(hT) if col else hT
            fT = run_o2(hp, mod_l, s5_lam_re[i], s5_lam_im[i], s5_log_dt[i], s5_b_re[i], s5_b_im[i], s5_c_re[i], s5_c_im[i], s5_d[i])
            if col:
                fT = lat_from_col_major(fT)
            h1T, h2T, affT = run_x3(True, fT, hT, mod_l, od_w_glu[i], ln_g[l, 0], ln_b[l, 0], moe_router[l])
        hT = run_x4(h1T, h2T, affT, mod_l, ln_g[l, 1], ln_b[l, 1], moe_w_in[l], moe_w_out[l])
    return np.ascontiguousarray(hT[:, CTX:].T)[None].astype(np.float32)
```

```python
import contextlib
import math
import numpy as np
import concourse.bass as bass
import concourse.mybir as mybir
from concourse.bass_utils import run_bass_kernel_spmd

F32 = mybir.dt.float32
BF16 = mybir.dt.bfloat16
AF = mybir.ActivationFunctionType
ALU = mybir.AluOpType
AX = mybir.AxisListType

NCORES = 8
D = 4096
SEQ = 8192
CTX = 256
T = SEQ + CTX
DEPTH = 4
KC = D // 128


class _Stop(Exception):
    pass


class Prog:
    NDSEM = 6

    def __init__(self):
        self.nc = bass.Bass("TRN2", target_bir_lowering=False)
        self.st = contextlib.ExitStack()
        nc = self.nc
        self.eng = {"pe": nc.tensor, "act": nc.scalar, "dve": nc.vector, "pool": nc.gpsimd, "sp": nc.sync}
        self.sem = {}
        self.cnt = {}
        for e in ("pe", "act", "dve", "pool"):
            self.sem[e] = self.st.enter_context(nc.semaphore("s_" + e))
            self.cnt[e] = 0
        self.dsem = {}
        self.dcnt = {}
        for q in ("sp", "pool", "act"):
            self.dsem[q] = [self.st.enter_context(nc.semaphore("d_%s%d" % (q, i))) for i in range(self.NDSEM)]
            self.dcnt[q] = 0
        self.waited = {}
        self.last_w = {}
        self.readers = {}
        self.out_events = []
        self.n_ins = 0

    def sb(self, name, shape, dt=F32):
        return self.st.enter_context(self.nc.sbuf_tensor("sb_" + name, list(shape), dt))

    def ps(self, name, shape, dt=F32):
        return self.st.enter_context(self.nc.psum_tensor("pp_" + name, list(shape), dt))

    def dram_in(self, name, shape, dt=F32):
        return self.nc.dram_tensor(name, list(shape), dt, kind="ExternalInput").ap()

    def dram_out(self, name, shape, dt=F32):
        return self.nc.dram_tensor(name, list(shape), dt, kind="ExternalOutput").ap()

    def _wait(self, e, ev):
        if ev is None:
            return
        sem, val = ev
        k = (e, sem.name)
        if self.waited.get(k, 0) >= val:
            return
        self.waited[k] = val
        self.eng[e].wait_ge(sem, val)

    def _deps(self, e, reads, writes, pe_acc=False):
        for k in reads:
            self._wait(e, self.last_w.get(k))
        for k in writes:
            lw = self.last_w.get(k)
            if not (pe_acc and lw is not None and lw[0] is self.sem["pe"]):
                self._wait(e, lw)
            for ev in self.readers.get(k, ()):
                self._wait(e, ev)

    def _record(self, ev, reads, writes):
        for k in reads:
            self.readers.setdefault(k, []).append(ev)
            if len(self.readers[k]) > 24:
                self.readers[k] = self.readers[k][-24:]
        for k in writes:
            self.last_w[k] = ev
            self.readers[k] = []

    def op(self, e, ins_fn, reads=(), writes=(), pe_acc=False):
        self._deps(e, reads, writes, pe_acc)
        ins = ins_fn(self.eng[e])
        self.cnt[e] += 1
        ins.then_inc(self.sem[e], 1)
        ev = (self.sem[e], self.cnt[e])
        self._record(ev, reads, writes)
        self.n_ins += 1
        return ev

    def dma(self, q, out, in_, reads=(), writes=(), is_output=False, **kw):
        n = self.dcnt[q]
        sem = self.dsem[q][n % self.NDSEM]
        prev = 16 * (n // self.NDSEM)
        if prev > 0:
            self._wait(q, (sem, prev))
        self._deps(q, reads, writes)
        ins = self.eng[q].dma_start(out=out, in_=in_, **kw)
        ins.then_inc(sem, 16)
        self.dcnt[q] = n + 1
        ev = (sem, prev + 16)
        self._record(ev, reads, writes)
        if is_output:
            self.out_events.append(ev)
        self.n_ins += 1
        return ev

    def finish(self):
        for ev in self.out_events:
            self._wait("sp", ev)
        for e in ("pe", "act", "dve", "pool"):
            if self.cnt[e]:
                self._wait("sp", (self.sem[e], self.cnt[e]))
        for q in ("sp", "pool", "act"):
            n = self.dcnt[q]
            for i in range(self.NDSEM):
                uses = (n - i + self.NDSEM - 1) // self.NDSEM if n > i else 0
                if uses:
                    self._wait("sp", (self.dsem[q][i], 16 * uses))
        self.st.close()
        return self.nc


def run(prog_nc, in_maps):
    res = run_bass_kernel_spmd(prog_nc, in_maps, core_ids=list(range(NCORES)))
    return res.results


NCH_ADA = 6 * D // 128
NCH_ADA_CORE = NCH_ADA // NCORES


def build_k0():
    P = Prog()
    nc = P.nc
    cols_core = NCH_ADA_CORE * 128
    c_in = P.dram_in("c2", [2, 128, KC])
    w_in = P.dram_in("ada_w", [DEPTH, D, cols_core])
    b_in = P.dram_in("ada_b", [DEPTH, 1, cols_core])
    out = P.dram_out("modT", [DEPTH, 128, NCH_ADA_CORE, 2])

    craw = P.sb("craw", [128, 2, KC])
    S = P.sb("S", [128, KC, 2])
    ones = P.sb("ones", [1, 2])
    bias = P.sb("bias", [1, DEPTH, cols_core])
    wt = [P.sb("wt%d" % i, [128, KC, 512]) for i in range(2)]
    ot = [P.sb("ot%d" % i, [128, NCH_ADA_CORE, 2]) for i in range(2)]
    pst = [P.ps("ps%d" % i, [128, 2]) for i in range(4)]

    P.dma("sp", craw[:, 0, :], c_in[0], writes=["craw"])
    P.dma("sp", craw[:, 1, :], c_in[1], writes=["craw"])
    P.dma("sp", bias[:], b_in.rearrange("l o c -> o l c"), writes=["bias"])
    P.op("dve", lambda e: e.memset(ones[:], 1.0), writes=["ones"])
    for s in range(2):
        P.op("act", lambda e: e.activation(out=S[:, :, s], in_=craw[:, s, :], func=AF.Silu),
             reads=["craw"], writes=["S"])
    wv = w_in.rearrange("l (p kc) c -> l p kc c", kc=KC)
    blk = 0
    for l in range(DEPTH):
        o = ot[l % 2]
        for cb in range(cols_core // 512):
            w = wt[blk % 2]
            wk = "wt%d" % (blk % 2)
            P.dma("sp" if blk % 2 == 0 else "act", w[:], wv[l, :, :, cb * 512:(cb + 1) * 512], writes=[wk])
            for sub in range(4):
                ch = cb * 4 + sub
                pt = pst[ch % 4]
                pk = "ps%d" % (ch % 4)
                for kc in range(KC):
                    P.op("pe", lambda e: e.matmul(pt[:], w[:, kc, sub * 128:(sub + 1) * 128], S[:, kc, :],
                                                  start=(kc == 0), stop=False),
                         reads=[wk, "S"], writes=[pk], pe_acc=True)
                c0 = ch * 128
                P.op("pe", lambda e: e.matmul(pt[:], bias[:, l, c0:c0 + 128], ones[:], start=False, stop=True),
                     reads=["bias", "ones"], writes=[pk], pe_acc=True)
                P.op("dve", lambda e: e.tensor_copy(out=o[:, ch, :], in_=pt[:]), reads=[pk], writes=["ot%d" % (l % 2)])
            blk += 1
        P.dma("sp", out[l], o[:], reads=["ot%d" % (l % 2)], is_output=True)
    return P.finish()


def run_k0(inputs):
    cols_core = NCH_ADA_CORE * 128
    c2 = np.stack([np.asarray(inputs["c"], np.float32).reshape(128, KC),
                   np.asarray(inputs["c_ctx"], np.float32).reshape(128, KC)])
    ada_w = inputs["ada_w"]
    ada_b = inputs["ada_b"]
    in_maps = []
    for j in range(NCORES):
        sl = slice(j * cols_core, (j + 1) * cols_core)
        in_maps.append({"c2": c2,
                        "ada_w": np.ascontiguousarray(ada_w[:, :, sl]),
                        "ada_b": np.ascontiguousarray(ada_b[:, None, sl])})
    res = run(build_k0(), in_maps)
    return np.concatenate([r["modT"] for r in res], axis=2)


NTC = CTX // NCORES
NTL = SEQ // NCORES
NT = NTC + NTL


def emit_modulate(P, dst, src, mod, sc1, c, k_shift, k_scale, dkey, skey):
    for (a, b, s) in ((0, NTC, 1), (NTC, NT, 0)):
        P.op("dve", lambda e: e.tensor_scalar(out=dst[:, a:b], in0=src[:, a:b],
                                              scalar1=sc1[:, k_scale * KC + c, s:s + 1],
                                              scalar2=mod[:, k_shift * KC + c, s:s + 1],
                                              op0=ALU.mult, op1=ALU.add),
             reads=[skey, "sc1", "mod"], writes=[dkey])


def load_mod(P, mod_in):
    mod = P.sb("mod", [128, NCH_ADA, 2])
    sc1 = P.sb("sc1", [128, NCH_ADA, 2])
    P.dma("sp", mod[:], mod_in, writes=["mod"])
    P.op("dve", lambda e: e.tensor_scalar_add(out=sc1[:], in0=mod[:], scalar1=1.0), reads=["mod"], writes=["sc1"])
    return mod, sc1


def build_mod(k_shift=0, k_scale=1):
    P = Prog()
    h_in = P.dram_in("hT", [D, NT])
    mod_in = P.dram_in("modT", [128, NCH_ADA, 2])
    u_out = P.dram_out("uT", [D, NT], BF16)
    mod, sc1 = load_mod(P, mod_in)
    ht = [P.sb("ht%d" % i, [128, NT]) for i in range(3)]
    ut = [P.sb("ut%d" % i, [128, NT], BF16) for i in range(3)]
    for c in range(KC):
        i = c % 3
        P.dma("sp", ht[i][:], h_in[c * 128:(c + 1) * 128, :], writes=["ht%d" % i])
        emit_modulate(P, ut[i], ht[i], mod, sc1, c, k_shift, k_scale, "ut%d" % i, "ht%d" % i)
        P.dma("act", u_out[c * 128:(c + 1) * 128, :], ut[i][:], reads=["ut%d" % i], is_output=True)
    return P.finish()


def tok_slices(j):
    return slice(NTC * j, NTC * (j + 1)), slice(CTX + NTL * j, CTX + NTL * (j + 1))


def shard_tokens(aT):
    out = []
    for j in range(NCORES):
        sc, sl = tok_slices(j)
        out.append(np.ascontiguousarray(np.concatenate([aT[:, sc], aT[:, sl]], axis=1)))
    return out


def unshard_tokens(parts):
    rows = parts[0].shape[0]
    full = np.empty((rows, T), parts[0].dtype)
    for j in range(NCORES):
        sc, sl = tok_slices(j)
        full[:, sc] = parts[j][:, :NTC]
        full[:, sl] = parts[j][:, NTC:]
    return full


def run_mod(hT, modT_l):
    hs = shard_tokens(hT)
    res = run(build_mod(), [{"hT": hs[j], "modT": modT_l} for j in range(NCORES)])
    return unshard_tokens([r["uT"] for r in res])


EA_NCOLS = 1920
HYW = 2048
DNW = 2048
TB = 256


def build_ea1(ncols=EA_NCOLS):
    P = Prog()
    nm = ncols // 128
    u_in = P.dram_in("uT", [D, T], BF16)
    w_in = P.dram_in("w", [D, ncols])
    z_out = P.dram_out("zT", [ncols, T])
    W = P.sb("W", [128, KC, ncols], BF16)
    for kc in range(KC):
        P.dma("pool", W[:, kc, :], w_in[kc * 128:(kc + 1) * 128, :], writes=[("W", kc)])
    ub = [P.sb("ub%d" % i, [128, KC, TB], BF16) for i in range(2)]
    st = [P.sb("st%d" % i, [128, nm, TB]) for i in range(2)]
    ps = [P.ps("ps%d" % i, [128, TB]) for i in range(4)]
    uv = u_in.rearrange("(kc p) t -> p kc t", p=128)
    zv = z_out.rearrange("(m p) t -> p m t", p=128)
    nblk = T // TB
    ev = 0
    for tb in range(nblk):
        i = tb % 2
        t0 = tb * TB
        P.dma("sp", ub[i][:], uv[:, :, t0:t0 + TB], writes=["ub%d" % i])
        for m in range(nm):
            pt = ps[ev % 4]
            pk = "ps%d" % (ev % 4)
            for kc in range(KC):
                P.op("pe", lambda e: e.matmul(pt[:], W[:, kc, m * 128:(m + 1) * 128], ub[i][:, kc, :],
                                              start=(kc == 0), stop=(kc == KC - 1)),
                     reads=[("W", kc), "ub%d" % i], writes=[pk], pe_acc=True)
            if ev % 2 == 0:
                P.op("dve", lambda e: e.tensor_copy(out=st[i][:, m, :], in_=pt[:]), reads=[pk], writes=["st%d" % i])
            else:
                P.op("act", lambda e: e.activation(out=st[i][:, m, :], in_=pt[:], func=AF.Copy),
                     reads=[pk], writes=["st%d" % i])
            ev += 1
        P.dma("act", zv[:, :, t0:t0 + TB], st[i][:], reads=["st%d" % i], is_output=True)
    return P.finish()


def ea_cols(j):
    cols = []
    for part in range(3):
        cols += list(range(part * HYW + 256 * j, part * HYW + 256 * (j + 1)))
    base = 3 * HYW
    for part in range(4):
        cols += list(range(base + part * DNW + 256 * j, base + part * DNW + 256 * (j + 1)))
    base = 3 * HYW + 4 * DNW
    for part in range(4):
        cols += [base + part * 16 + 2 * j, base + part * 16 + 2 * j + 1]
    return np.array(cols)


def run_ea1(uT, w_in_l):
    in_maps = []
    for j in range(NCORES):
        cols = ea_cols(j)
        w = np.zeros((D, EA_NCOLS), np.float32)
        w[:, :len(cols)] = w_in_l[:, cols]
        in_maps.append({"uT": uT, "w": w})
    res = run(build_ea1(), in_maps)
    return [r["zT"] for r in res]


ALPHA = (2 * DEPTH) ** 0.25
LN_EPS = 1e-5
NBK = 3
BK = NT // NBK
NEXP = 16


def emit_layernorm(P, zb, outb, g, b, ones128, ps_s, ps_s2, sq, tmp, zkey, okey, width):
    for m in range(KC):
        P.op("pe", lambda e: e.matmul(ps_s[:, :width], ones128[:], zb[:, m, :width], start=(m == 0), stop=(m == KC - 1)),
             reads=[zkey, "ones128"], writes=["ps_s"], pe_acc=True)
    for m in range(KC):
        s = sq[m % 2]
        sk = "sq%d" % (m % 2)
        P.op("act", lambda e: e.activation(out=s[:, :width], in_=zb[:, m, :width], func=AF.Square), reads=[zkey], writes=[sk])
        P.op("pe", lambda e: e.matmul(ps_s2[:, :width], ones128[:], s[:, :width], start=(m == 0), stop=(m == KC - 1)),
             reads=[sk, "ones128"], writes=["ps_s2"], pe_acc=True)
    mean, msq, rstd = tmp
    P.op("dve", lambda e: e.tensor_scalar(out=mean[:, :width], in0=ps_s[:, :width], scalar1=1.0 / D, scalar2=None, op0=ALU.mult),
         reads=["ps_s"], writes=["mean"])
    P.op("dve", lambda e: e.tensor_tensor(out=msq[:, :width], in0=mean[:, :width], in1=mean[:, :width], op=ALU.mult),
         reads=["mean"], writes=["msq"])
    P.op("dve", lambda e: e.scalar_tensor_tensor(out=rstd[:, :width], in0=ps_s2[:, :width], scalar=1.0 / D, in1=msq[:, :width],
                                                 op0=ALU.mult, op1=ALU.subtract),
         reads=["ps_s2", "msq"], writes=["rstd"])
    P.op("dve", lambda e: e.tensor_scalar_add(out=rstd[:, :width], in0=rstd[:, :width], scalar1=LN_EPS),
         reads=["rstd"], writes=["rstd"])
    P.op("act", lambda e: e.activation(out=rstd[:, :width], in_=rstd[:, :width], func=AF.Sqrt),
         reads=["rstd"], writes=["rstd"])
    P.op("dve", lambda e: e.reciprocal(out=rstd[:, :width], in_=rstd[:, :width]),
         reads=["rstd"], writes=["rstd"])
    for m in range(KC):
        eng = "dve" if m % 2 == 0 else "pool"
        P.op(eng, lambda e: e.tensor_tensor(out=outb[:, m, :width], in0=zb[:, m, :width], in1=mean[:, :width], op=ALU.subtract),
             reads=[zkey, "mean"], writes=[(okey, m)])
        P.op(eng, lambda e: e.tensor_tensor(out=outb[:, m, :width], in0=outb[:, m, :width], in1=rstd[:, :width], op=ALU.mult),
             reads=[(okey, m), "rstd"], writes=[(okey, m)])
        P.op(eng, lambda e: e.tensor_scalar(out=outb[:, m, :width], in0=outb[:, m, :width], scalar1=g[:, m:m + 1],
                                            scalar2=b[:, m:m + 1], op0=ALU.mult, op1=ALU.add),
             reads=[(okey, m), "lng", "lnb"], writes=[(okey, m)])


def blk_streams(bk):
    if bk == 0:
        return [(0, NTC, 1), (NTC, BK, 0)]
    return [(0, BK, 0)]


def build_x3(glu):
    P = Prog()
    nout = 2 * D if glu else D
    f_in = P.dram_in("fT", [D, NT])
    h_in = P.dram_in("hT", [D, NT])
    mod_in = P.dram_in("modT", [128, NCH_ADA, 2])
    w_in = P.dram_in("w", [D, nout])
    g_in = P.dram_in("lng", [128, KC])
    b_in = P.dram_in("lnb", [128, KC])
    wr_in = P.dram_in("wr", [128, KC, NEXP])
    h1_out = P.dram_out("h1T", [D, NT])
    h2_out = P.dram_out("h2T", [D, NT], BF16)
    aff_out = P.dram_out("affT", [NEXP, NT])

    mod, sc1 = load_mod(P, mod_in)
    g = P.sb("lng", [128, KC]); b = P.sb("lnb", [128, KC]); wr = P.sb("wr", [128, KC, NEXP])
    P.dma("sp", g[:], g_in, writes=["lng"]); P.dma("sp", b[:], b_in, writes=["lnb"]); P.dma("sp", wr[:], wr_in, writes=["wr"])
    ones128 = P.sb("ones128", [128, 128])
    P.op("dve", lambda e: e.memset(ones128[:], 1.0), writes=["ones128"])
    fb = P.sb("fb", [128, KC, NT], BF16)
    ystg = [P.sb("ystg%d" % i, [128, BK]) for i in range(3)]
    yscr = P.nc.dram_tensor("yscr3", [D, NT], F32, kind="Internal").ap()
    yv = yscr.rearrange("(kc p) t -> p kc t", p=128)
    hb = P.sb("hb", [128, KC, BK])
    zb = P.sb("zb", [128, KC, BK])
    h2b = fb[:].rearrange("p a b -> p (a b)")[:, 0:KC * BK].rearrange("p (k t) -> p k t", t=BK)
    nw = 4 if glu else 2
    wm = [P.sb("wm%d" % i, [128, KC, 128], BF16) for i in range(nw)]
    sq = [P.sb("sq%d" % i, [128, BK]) for i in range(2)]
    tmp = (P.sb("mean", [128, BK]), P.sb("msq", [128, BK]), P.sb("rstd", [128, BK]))
    ex = P.sb("ex", [NEXP, BK]); rs = P.sb("rs", [NEXP, BK]); af = P.sb("af", [NEXP, BK])
    psA = [P.ps("psA%d" % i, [128, BK]) for i in range(2)]
    psB = [P.ps("psB%d" % i, [128, BK]) for i in range(2)] if glu else None
    ps_s = P.ps("ps_s", [128, BK]); ps_s2 = P.ps("ps_s2", [128, BK])
    ps_r = P.ps("ps_r", [NEXP, BK])
    fv = f_in.rearrange("(kc p) t -> p kc t", p=128)
    hv = h_in.rearrange("(kc p) t -> p kc t", p=128)
    wv = w_in.rearrange("(kc p) c -> p kc c", p=128)
    h1v = h1_out.rearrange("(kc p) t -> p kc t", p=128)
    h2v = h2_out.rearrange("(kc p) t -> p kc t", p=128)
    wi = 0
    for bk in range(NBK):
        c0 = bk * BK
        P.dma("pool", fb[:, :, c0:c0 + BK], fv[:, :, c0:c0 + BK], writes=[("fb", bk)])
    it = 0
    for m in range(KC):
        wa = wm[wi % nw]; wak = "wm%d" % (wi % nw); wi += 1
        P.dma("pool", wa[:], wv[:, :, m * 128:(m + 1) * 128], writes=[wak])
        if glu:
            wb = wm[wi % nw]; wbk = "wm%d" % (wi % nw); wi += 1
            P.dma("pool", wb[:], wv[:, :, D + m * 128:D + (m + 1) * 128], writes=[wbk])
        for bk in range(NBK):
            c0 = bk * BK
            pa = psA[it % 2]; pak = "psA%d" % (it % 2)
            for kc in range(KC):
                P.op("pe", lambda e: e.matmul(pa[:], wa[:, kc, :], fb[:, kc, c0:c0 + BK], start=(kc == 0), stop=(kc == KC - 1)),
                     reads=[wak, ("fb", bk)], writes=[pak], pe_acc=True)
            ys = ystg[it % 3]; ysk = "ystg%d" % (it % 3)
            if glu:
                pb = psB[it % 2]; pbk = "psB%d" % (it % 2)
                for kc in range(KC):
                    P.op("pe", lambda e: e.matmul(pb[:], wb[:, kc, :], fb[:, kc, c0:c0 + BK], start=(kc == 0), stop=(kc == KC - 1)),
                         reads=[wbk, ("fb", bk)], writes=[pbk], pe_acc=True)
                P.op("act", lambda e: e.activation(out=ys[:], in_=pb[:], func=AF.Sigmoid), reads=[pbk], writes=[ysk])
                P.op("dve", lambda e: e.tensor_tensor(out=ys[:], in0=pa[:], in1=ys[:], op=ALU.mult), reads=[pak, ysk], writes=[ysk])
            else:
                if it % 2 == 0:
                    P.op("act", lambda e: e.activation(out=ys[:], in_=pa[:], func=AF.Copy), reads=[pak], writes=[ysk])
                else:
                    P.op("dve", lambda e: e.tensor_copy(out=ys[:], in_=pa[:]), reads=[pak], writes=[ysk])
            P.dma("sp", yv[:, m, c0:c0 + BK], ys[:], reads=[ysk], writes=["yscr"])
            it += 1
    for bk in range(NBK):
        c0 = bk * BK
        P.dma("sp", zb[:], yv[:, :, c0:c0 + BK], reads=["yscr"], writes=["zb"] + ([("fb", q) for q in range(NBK)] if bk == 0 else []))
        P.dma("act", hb[:], hv[:, :, c0:c0 + BK], writes=["hb"] + [("hb", m_) for m_ in range(KC)])
        P.op("act", lambda e: e.activation(out=hb[:], in_=hb[:], func=AF.Copy, scale=ALPHA), reads=["hb"], writes=["hb"])
        for m in range(KC):
            for (a, bb, s) in blk_streams(bk):
                P.op("dve", lambda e: e.scalar_tensor_tensor(out=zb[:, m, a:bb], in0=zb[:, m, a:bb],
                                                             scalar=mod[:, 2 * KC + m, s:s + 1], in1=hb[:, m, a:bb],
                                                             op0=ALU.mult, op1=ALU.add),
                     reads=["zb", "hb", "mod"], writes=["zb"])
        emit_layernorm(P, zb, hb, g, b, ones128, ps_s, ps_s2, sq, tmp, "zb", "hb", BK)
        P.dma("sp", h1v[:, :, c0:c0 + BK], hb[:], reads=[("hb", m) for m in range(KC)] + ["hb"], is_output=True)
        for m in range(KC):
            for (a, bb, s) in blk_streams(bk):
                P.op("pool", lambda e: e.tensor_scalar(out=zb[:, m, a:bb], in0=hb[:, m, a:bb],
                                                       scalar1=sc1[:, 4 * KC + m, s:s + 1], scalar2=mod[:, 3 * KC + m, s:s + 1],
                                                       op0=ALU.mult, op1=ALU.add),
                     reads=[("hb", m), "sc1", "mod"], writes=[("zb2", m), "zb"])
            P.op("act", lambda e: e.activation(out=h2b[:, m, :], in_=zb[:, m, :], func=AF.Copy), reads=[("zb2", m)], writes=["h2b"])
            P.op("pe", lambda e: e.matmul(ps_r[:], wr[:, m, :], zb[:, m, :], start=(m == 0), stop=(m == KC - 1)),
                 reads=["wr", ("zb2", m)], writes=["ps_r"], pe_acc=True)
        P.dma("act", h2v[:, :, c0:c0 + BK], h2b, reads=["h2b"], is_output=True)
        P.op("act", lambda e: e.activation(out=ex[:], in_=ps_r[:], func=AF.Exp), reads=["ps_r"], writes=["ex"])
        P.op("pe", lambda e: e.matmul(ps_r[:], ones128[0:NEXP, 0:NEXP], ex[:], start=True, stop=True),
             reads=["ex", "ones128"], writes=["ps_r"])
        P.op("dve", lambda e: e.reciprocal(out=rs[:], in_=ps_r[:]), reads=["ps_r"], writes=["rs"])
        P.op("dve", lambda e: e.tensor_tensor(out=af[:], in0=ex[:], in1=rs[:], op=ALU.mult), reads=["ex", "rs"], writes=["af"])
        P.dma("sp", aff_out[:, c0:c0 + BK], af[:], reads=["af"], is_output=True)
    return P.finish()


def pvec(v):
    return np.ascontiguousarray(np.asarray(v, np.float32).reshape(KC, 128).T)


def run_x3(glu, fT, hT, modT_l, w, lng, lnb, wrouter):
    fs = shard_tokens(fT); hs = shard_tokens(hT)
    wr = np.ascontiguousarray(np.asarray(wrouter, np.float32).reshape(KC, 128, NEXP).transpose(1, 0, 2))
    g = pvec(lng); b = pvec(lnb)
    w = np.ascontiguousarray(w, dtype=np.float32)
    in_maps = [{"fT": fs[j], "hT": hs[j], "modT": modT_l, "w": w, "lng": g, "lnb": b, "wr": wr} for j in range(NCORES)]
    res = run(build_x3(glu), in_maps)
    return (unshard_tokens([r["h1T"] for r in res]), unshard_tokens([r["h2T"] for r in res]),
            unshard_tokens([r["affT"] for r in res]))


EFF = 384
NKF = EFF // 128
CAP_LAT = 2 * SEQ // NEXP
CAP_CTX = 2 * CTX // NEXP
NBIS = 28


def emit_threshold(P, aff_t, width, cap, blockones, lo, mid, cmp_t, cnt, ge, ps_c, tag):
    P.op("dve", lambda e: e.memset(lo[:], 0.0), writes=[tag + "lo"])
    for it in range(NBIS):
        h = 0.5 ** (it + 1)
        P.op("dve", lambda e: e.tensor_scalar_add(out=mid[:], in0=lo[:], scalar1=h), reads=[tag + "lo"], writes=[tag + "mid"])
        P.op("dve", lambda e: e.tensor_scalar(out=cmp_t[:, :width], in0=aff_t[:, :width], scalar1=mid[:, 0:1], scalar2=None,
                                              op0=ALU.is_ge),
             reads=[tag + "aff", tag + "mid"], writes=[tag + "cmp"])
        P.op("dve", lambda e: e.reduce_sum(out=cnt[:], in_=cmp_t[:, :width], axis=AX.X), reads=[tag + "cmp"], writes=[tag + "cnt"])
        P.op("pe", lambda e: e.matmul(ps_c, blockones[:], cnt[:], start=True, stop=True),
             reads=[tag + "cnt", "blockones"], writes=["ps_c"])
        P.op("dve", lambda e: e.tensor_scalar(out=ge[:], in0=ps_c, scalar1=float(cap) - 0.5, scalar2=None, op0=ALU.is_ge),
             reads=["ps_c"], writes=[tag + "ge"])
        P.op("dve", lambda e: e.scalar_tensor_tensor(out=lo[:], in0=ge[:], scalar=h, in1=lo[:], op0=ALU.mult, op1=ALU.add),
             reads=[tag + "ge", tag + "lo"], writes=[tag + "lo"])


def build_x4():
    P = Prog()
    h2_in = P.dram_in("h2T", [D, NT], BF16)
    h1_in = P.dram_in("h1T", [D, NT])
    affj_in = P.dram_in("affj", [NEXP, NT])
    affl_in = P.dram_in("affl", [128, SEQ // 8])
    affc_in = P.dram_in("affc", [128, CTX // 8])
    mod_in = P.dram_in("modT", [128, NCH_ADA, 2])
    g_in = P.dram_in("lng", [128, KC])
    b_in = P.dram_in("lnb", [128, KC])
    wi_in = P.dram_in("w_in", [NEXP, D, 2 * EFF])
    wo_in = P.dram_in("w_out", [NEXP, EFF, D])
    sel_in = P.dram_in("sel", [NEXP, NEXP, 128])
    bo_in = P.dram_in("blockones", [128, 128])
    pick_in = P.dram_in("pick", [128, NEXP])
    h_out = P.dram_out("hT", [D, NT])

    mod, sc1 = load_mod(P, mod_in)
    g = P.sb("lng", [128, KC]); b = P.sb("lnb", [128, KC])
    P.dma("sp", g[:], g_in, writes=["lng"]); P.dma("sp", b[:], b_in, writes=["lnb"])
    sel = P.sb("sel", [NEXP, NEXP, 128]); blockones = P.sb("blockones", [128, 128]); pick = P.sb("pick", [128, NEXP])
    P.dma("sp", sel[:], sel_in, writes=["sel"]); P.dma("sp", blockones[:], bo_in, writes=["blockones"])
    P.dma("sp", pick[:], pick_in, writes=["pick"])
    ones128 = P.sb("ones128", [128, 128])
    P.op("dve", lambda e: e.memset(ones128[:], 1.0), writes=["ones128"])

    zb = P.sb("zb", [128, KC, BK])
    zflat = zb[:].rearrange("p a b -> p (a b)")
    affl = zflat[:, 0:SEQ // 8]; cmp_t = zflat[:, SEQ // 8:2 * (SEQ // 8)]
    affc = P.sb("affc", [128, CTX // 8])
    P.dma("sp", affl, affl_in, writes=["Laff"]); P.dma("sp", affc[:], affc_in, writes=["Caff"])
    lo_l = P.sb("lo_l", [128, 1]); lo_c = P.sb("lo_c", [128, 1]); mid = P.sb("mid", [128, 1]); cnt = P.sb("cnt", [128, 1])
    ge = P.sb("ge", [128, 1])
    ps_small = P.ps("ps_small", [128, 4])
    ps_c = ps_small[:, 0:1]
    emit_threshold(P, affl, SEQ // 8, CAP_LAT, blockones, lo_l, mid, cmp_t, cnt, ge, ps_c, "L")
    emit_threshold(P, affc, CTX // 8, CAP_CTX, blockones, lo_c, mid, cmp_t, cnt, ge, ps_c, "C")
    thr = P.sb("thr", [NEXP, 2])
    ps_t = ps_small[0:NEXP, 1:3]
    P.op("pe", lambda e: e.matmul(ps_t[:, 0:1], pick[:], lo_l[:], start=True, stop=True), reads=["pick", "Llo"], writes=["ps_c"])
    P.op("pe", lambda e: e.matmul(ps_t[:, 1:2], pick[:], lo_c[:], start=True, stop=True), reads=["pick", "Clo"], writes=["ps_c"])
    P.op("dve", lambda e: e.tensor_copy(out=thr[:], in_=ps_t), reads=["ps_c"], writes=["thr"])
    affj = P.sb("affj", [NEXP, NT]); G = P.sb("G", [NEXP, NT])
    P.dma("sp", affj[:], affj_in, writes=["affj"])
    for (a, bb, s) in ((0, NTC, 1), (NTC, NT, 0)):
        P.op("dve", lambda e: e.scalar_tensor_tensor(out=G[:, a:bb], in0=affj[:, a:bb], scalar=thr[:, s:s + 1], in1=affj[:, a:bb],
                                                     op0=ALU.is_ge, op1=ALU.mult),
             reads=["affj", "thr"], writes=["G"])

    h2all = P.sb("h2all", [128, KC, NT], BF16)
    act = h2all[:].rearrange("p a b -> p (a b)")[:, 0:NEXP * NKF * BK].rearrange("p (k t) -> p k t", t=BK)
    actc = [P.sb("actc%d" % i, [128, BK], BF16) for i in range(3)]
    gbs = [P.sb("gbs%d" % i, [128, BK]) for i in range(NBK)]
    h1c = [P.sb("h1c%d" % i, [128, BK]) for i in range(2)]
    wa = [P.sb("wa%d" % i, [128, KC, 128], BF16) for i in range(3)]
    wo = [P.sb("wo%d" % i, [128, NEXP * NKF, 128], BF16) for i in range(2)]
    sq = [P.sb("sq%d" % i, [128, BK]) for i in range(2)]
    tmp = (P.sb("mean", [128, BK]), P.sb("msq", [128, BK]), P.sb("rstd", [128, BK]))
    st = [P.sb("silu%d" % i, [128, BK]) for i in range(2)]
    psg = P.ps("psg", [128, BK])
    psa = [P.ps("psa%d" % i, [128, BK]) for i in range(2)]
    psb = [P.ps("psb%d" % i, [128, BK]) for i in range(2)]
    ps_s = P.ps("ps_s", [128, BK]); ps_s2 = P.ps("ps_s2", [128, BK])
    actd = P.nc.dram_tensor("actd", [NEXP * NKF * 128, NT], BF16, kind="Internal").ap()
    h2v = h2_in.rearrange("(kc p) t -> p kc t", p=128)
    h1v = h1_in.rearrange("(kc p) t -> p kc t", p=128)
    hov = h_out.rearrange("(kc p) t -> p kc t", p=128)
    wiv = wi_in.rearrange("e (kc p) c -> e p kc c", p=128)
    wov = wo_in.rearrange("e (kf p) c -> p e kf c", p=128)
    actv = actd.rearrange("(k p) t -> p k t", p=128)
    for bk in range(NBK):
        c0 = bk * BK
        P.dma("sp", h2all[:, :, c0:c0 + BK], h2v[:, :, c0:c0 + BK], writes=[("h2all", bk)])
    wai = 0
    it = 0
    for ex in range(NEXP):
        for kf in range(NKF):
            wts = []
            for half in range(2):
                w = wa[wai % 3]; wk = "wa%d" % (wai % 3); wai += 1
                cc = half * EFF + kf * 128
                P.dma("pool", w[:], wiv[ex, :, :, cc:cc + 128], writes=[wk])
                wts.append((w, wk))
            for bk in range(NBK):
                c0 = bk * BK
                if kf == 0:
                    P.op("pe", lambda e: e.matmul(psg[:], sel[:, ex, :], G[:, c0:c0 + BK], start=True, stop=True),
                         reads=["sel", "G"], writes=["psg"])
                    P.op("act", lambda e: e.activation(out=gbs[bk][:], in_=psg[:], func=AF.Copy), reads=["psg"], writes=["gbs%d" % bk])
                i2 = it % 2
                pa, pak = psa[i2], "psa%d" % i2
                pb, pbk = psb[i2], "psb%d" % i2
                for (pt, ptk, (w, wk)) in ((pa, pak, wts[0]), (pb, pbk, wts[1])):
                    for kc in range(KC):
                        P.op("pe", lambda e: e.matmul(pt[:], w[:, kc, :], h2all[:, kc, c0:c0 + BK], start=(kc == 0), stop=(kc == KC - 1)),
                             reads=[wk, ("h2all", bk)], writes=[ptk], pe_acc=True)
                s_t, sk = st[i2], "silu%d" % i2
                ac = actc[it % 3]; ack = "actc%d" % (it % 3)
                P.op("act", lambda e: e.activation(out=s_t[:], in_=pa[:], func=AF.Silu), reads=[pak], writes=[sk])
                P.op("dve", lambda e: e.tensor_tensor(out=s_t[:], in0=pb[:], in1=s_t[:], op=ALU.mult), reads=[pbk, sk], writes=[sk])
                P.op("dve", lambda e: e.tensor_tensor(out=ac[:], in0=s_t[:], in1=gbs[bk][:], op=ALU.mult), reads=[sk, "gbs%d" % bk], writes=[ack])
                P.dma("sp", actd[(ex * NKF + kf) * 128:(ex * NKF + kf + 1) * 128, c0:c0 + BK], ac[:], reads=[ack], writes=["actd"])
                it += 1
    for bk in range(NBK):
        c0 = bk * BK
        P.dma("sp", act, actv[:, :, c0:c0 + BK], reads=["actd"], writes=["act"] + [("h2all", q) for q in range(NBK)])
        for m in range(KC):
            w = wo[m % 2]; wk = "wo%d" % (m % 2)
            P.dma("pool", w[:].rearrange("p (e kf) c -> p e kf c", kf=NKF), wov[:, :, :, m * 128:(m + 1) * 128], writes=[wk])
            hc = h1c[m % 2]; hk = "h1c%d" % (m % 2)
            P.dma("act", hc[:], h1v[:, m, c0:c0 + BK], writes=[hk])
            P.op("act", lambda e: e.activation(out=hc[:], in_=hc[:], func=AF.Copy, scale=ALPHA), reads=[hk], writes=[hk])
            pa, pak = psa[m % 2], "psa%d" % (m % 2)
            nk = NEXP * NKF
            for k in range(nk):
                P.op("pe", lambda e: e.matmul(pa[:], w[:, k, :], act[:, k, :], start=(k == 0), stop=(k == nk - 1)),
                     reads=[wk, "act"], writes=[pak], pe_acc=True)
            for (a_, bb, s_) in blk_streams(bk):
                P.op("dve", lambda e: e.scalar_tensor_tensor(out=zb[:, m, a_:bb], in0=pa[:, a_:bb],
                                                             scalar=mod[:, 5 * KC + m, s_:s_ + 1], in1=hc[:, a_:bb],
                                                             op0=ALU.mult, op1=ALU.add),
                     reads=[pak, hk, "mod"], writes=["zb"])
        emit_layernorm(P, zb, zb, g, b, ones128, ps_s, ps_s2, sq, tmp, "zb", "zb", BK)
        P.dma("sp", hov[:, :, c0:c0 + BK], zb[:], reads=[("zb", m) for m in range(KC)] + ["zb"], writes=["zb"], is_output=True)
    return P.finish()


def moe_consts():
    sel = np.zeros((NEXP, NEXP, 128), np.float32)
    for e in range(NEXP):
        sel[e, e, :] = 1.0
    bo = np.kron(np.eye(NEXP, dtype=np.float32), np.ones((8, 8), np.float32))
    pick = np.zeros((128, NEXP), np.float32)
    for e in range(NEXP):
        pick[8 * e, e] = 1.0
    return sel, bo, pick


def run_x4(h1T, h2T, affT, modT_l, lng, lnb, w_in, w_out):
    h1s = shard_tokens(h1T); h2s = shard_tokens(h2T); affs = shard_tokens(affT)
    affl = np.ascontiguousarray(affT[:, CTX:].reshape(128, SEQ // 8))
    affc = np.ascontiguousarray(affT[:, :CTX].reshape(128, CTX // 8))
    sel, bo, pick = moe_consts()
    g = pvec(lng); b = pvec(lnb)
    w_in = np.ascontiguousarray(w_in, dtype=np.float32); w_out = np.ascontiguousarray(w_out, dtype=np.float32)
    in_maps = [{"h2T": h2s[j], "h1T": h1s[j], "affj": affs[j], "affl": affl, "affc": affc, "modT": modT_l, "lng": g, "lnb": b,
                "w_in": w_in, "w_out": w_out, "sel": sel, "blockones": bo, "pick": pick} for j in range(NCORES)]
    res = run(build_x4(), in_maps)
    return unshard_tokens([r["hT"] for r in res])


S5G = 16
S5P = 64
NGC = 32
NOCT = 4
NA = T // 64
TWO_PI = 6.283180
GELU_C = 0.7978845608028654


def emit_cmul(P, yr, yi, xr, xi, c, s, conj, keys):
    ykr, yki, xkr, xki, tk = keys
    P.op("dve", lambda e: e.tensor_tensor(out=yr, in0=xr, in1=c, op=ALU.mult), reads=[xkr, tk], writes=[ykr])
    P.op("pool", lambda e: e.tensor_tensor(out=yi, in0=xi, in1=s, op=ALU.mult), reads=[xki, tk], writes=[yki])
    P.op("dve", lambda e: e.tensor_tensor(out=yr, in0=yr, in1=yi, op=(ALU.add if conj else ALU.subtract)),
         reads=[ykr, yki], writes=[ykr])
    P.op("pool", lambda e: e.tensor_tensor(out=yi, in0=xi, in1=c, op=ALU.mult), reads=[xki, tk, ykr], writes=[yki])
    P.op("dve", lambda e: e.tensor_tensor(out=xr, in0=xr, in1=s, op=ALU.mult), reads=[xkr, tk], writes=[xkr])
    P.op("pool", lambda e: e.tensor_tensor(out=yi, in0=yi, in1=xr, op=(ALU.subtract if conj else ALU.add)),
         reads=[yki, xkr], writes=[yki])


def kk2(k):
    return [k + "A", k + "B"]


def emit_cmul2(P, yr, yi, xr, xi, c, s, conj, keys, na):
    ykr, yki, xkr, xki, tk = keys
    for (eng, sl, sfx) in (("dve", slice(0, na), "A"), ("pool", slice(na, None), "B")):
        Yr, Yi, Xr, Xi, C, S = (t[:, sl, :] for t in (yr, yi, xr, xi, c, s))
        a_, b_, c_, d_ = ykr + sfx, yki + sfx, xkr + sfx, xki + sfx
        P.op(eng, lambda e: e.tensor_tensor(out=Yr, in0=Xr, in1=C, op=ALU.mult), reads=[c_, tk], writes=[a_])
        P.op(eng, lambda e: e.tensor_tensor(out=Yi, in0=Xi, in1=S, op=ALU.mult), reads=[d_, tk], writes=[b_])
        P.op(eng, lambda e: e.tensor_tensor(out=Yr, in0=Yr, in1=Yi, op=(ALU.add if conj else ALU.subtract)), reads=[a_, b_], writes=[a_])
        P.op(eng, lambda e: e.tensor_tensor(out=Yi, in0=Xi, in1=C, op=ALU.mult), reads=[d_, tk, a_], writes=[b_])
        P.op(eng, lambda e: e.tensor_tensor(out=Xr, in0=Xr, in1=S, op=ALU.mult), reads=[c_, tk], writes=[c_])
        P.op(eng, lambda e: e.tensor_tensor(out=Yi, in0=Yi, in1=Xr, op=(ALU.subtract if conj else ALU.add)), reads=[b_, c_], writes=[b_])


def emit_sincos(P, ph, tmp_i, out_s, out_c, key):
    I32 = mybir.dt.int32
    P.op("dve", lambda e: e.tensor_copy(out=tmp_i, in_=ph), reads=[key + "ph"], writes=[key + "i"])
    P.op("dve", lambda e: e.tensor_tensor(out=out_s, in0=ph, in1=tmp_i, op=ALU.subtract), reads=[key + "ph", key + "i"], writes=[key + "s"])
    P.op("act", lambda e: e.activation(out=out_s, in_=out_s, func=AF.Sin, scale=TWO_PI), reads=[key + "s"], writes=[key + "s"])
    P.op("dve", lambda e: e.tensor_scalar_add(out=ph, in0=ph, scalar1=0.25), reads=[key + "ph", key + "s"], writes=[key + "ph"])
    P.op("dve", lambda e: e.tensor_copy(out=tmp_i, in_=ph), reads=[key + "ph"], writes=[key + "i"])
    P.op("dve", lambda e: e.tensor_tensor(out=out_c, in0=ph, in1=tmp_i, op=ALU.subtract), reads=[key + "ph", key + "i"], writes=[key + "c"])
    P.op("act", lambda e: e.activation(out=out_c, in_=out_c, func=AF.Sin, scale=TWO_PI), reads=[key + "c"], writes=[key + "c"])


def build_o2(n_oct=NOCT, n_grp=8, stages=(1, 1, 1, 1, 1, 1)):
    P = Prog()
    nc = P.nc
    I32 = mybir.dt.int32
    h_in = P.dram_in("hT", [NOCT * 128, T])
    modo_in = P.dram_in("modo", [128, NOCT, 2, 2])
    dsk_in = P.dram_in("dsk", [128, NOCT])
    lre_in = P.dram_in("lam_re", [128, NGC]); lim_in = P.dram_in("lam_im", [128, NGC]); ldt_in = P.dram_in("log_dt", [128, NGC])
    bre_in = P.dram_in("bre", [NOCT, 128, 8, 128]); bim_in = P.dram_in("bim", [NOCT, 128, 8, 128])
    cre_in = P.dram_in("cre", [128, NGC, S5G]); cim_in = P.dram_in("cim", [128, NGC, S5G])
    av_in = P.dram_in("avals", [128, NA]); bv_in = P.dram_in("bvals", [128, 64])
    f_out = P.dram_out("fT", [NOCT * 128, T])
    yscr = nc.dram_tensor("yscr", [128, T], F32, kind="Internal").ap()

    def ld(name, src, shape, dt=F32, q="sp"):
        t = P.sb(name, shape, dt)
        P.dma(q, t[:], src, writes=[name])
        return t
    modo = ld("modo", modo_in, [128, NOCT, 2, 2]); dsk = ld("dsk", dsk_in, [128, NOCT])
    lre = ld("lre", lre_in, [128, NGC]); lim = ld("lim", lim_in, [128, NGC]); ldt = ld("ldt", ldt_in, [128, NGC])
    cre = ld("cre", cre_in, [128, NGC, S5G]); cim = ld("cim", cim_in, [128, NGC, S5G])
    avals = ld("avals", av_in, [128, NA]); bvals = ld("bvals", bv_in, [128, 64])
    sc1 = P.sb("sc1o", [128, NOCT, 2])
    P.op("dve", lambda e: e.tensor_scalar_add(out=sc1[:], in0=modo[:, :, 1, :], scalar1=1.0), reads=["modo"], writes=["sc1o"])

    def sm(name, dt=F32):
        return P.sb(name, [128, NGC], dt)
    dtv = sm("dtv"); r = sm("r"); fq = sm("fq"); F1 = sm("F1"); ti = sm("ti", I32)
    lbs = sm("lbs"); lbc = sm("lbc"); ph = sm("ph"); den = sm("den"); fre = sm("fre"); fim = sm("fim"); t1 = sm("t1"); t2 = sm("t2")
    K = "prm"
    def dv(fn, reads, writes):
        P.op("dve", fn, reads=reads, writes=writes)
    dv(lambda e: e.tensor_scalar_min(out=lre[:], in0=lre[:], scalar1=-1e-4), ["lre"], ["lre"])
    P.op("act", lambda e: e.activation(out=dtv[:], in_=ldt[:], func=AF.Exp), reads=["ldt"], writes=["dtv"])
    dv(lambda e: e.tensor_tensor(out=r[:], in0=lre[:], in1=dtv[:], op=ALU.mult), ["lre", "dtv"], ["r"])
    P.op("act", lambda e: e.activation(out=r[:], in_=r[:], func=AF.Exp), reads=["r"], writes=["r"])
    dv(lambda e: e.tensor_tensor(out=fq[:], in0=lim[:], in1=dtv[:], op=ALU.mult), ["lim", "dtv"], ["fq"])
    dv(lambda e: e.tensor_scalar(out=fq[:], in0=fq[:], scalar1=1.0 / (2 * math.pi), scalar2=None, op0=ALU.mult), ["fq"], ["fq"])
    dv(lambda e: e.tensor_scalar(out=t1[:], in0=fq[:], scalar1=64.0, scalar2=None, op0=ALU.mult), ["fq"], ["t1"])
    dv(lambda e: e.tensor_copy(out=ti[:], in_=t1[:]), ["t1"], ["ti"])
    dv(lambda e: e.tensor_tensor(out=F1[:], in0=t1[:], in1=ti[:], op=ALU.subtract), ["t1", "ti"], ["F1"])
    dv(lambda e: e.tensor_copy(out=ph[:], in_=fq[:]), ["fq"], ["lbph"])
    emit_sincos(P, ph[:], ti[:], lbs[:], lbc[:], "lb")
    dv(lambda e: e.tensor_tensor(out=lbs[:], in0=lbs[:], in1=r[:], op=ALU.mult), ["lbs", "r"], ["lbs"])
    dv(lambda e: e.tensor_tensor(out=lbc[:], in0=lbc[:], in1=r[:], op=ALU.mult), ["lbc", "r"], ["lbc"])
    dv(lambda e: e.tensor_scalar_add(out=lbc[:], in0=lbc[:], scalar1=-1.0), ["lbc"], ["lbc"])
    dv(lambda e: e.tensor_tensor(out=den[:], in0=lre[:], in1=lre[:], op=ALU.mult), ["lre"], ["den"])
    dv(lambda e: e.tensor_tensor(out=t1[:], in0=lim[:], in1=lim[:], op=ALU.mult), ["lim", "F1"], ["t1"])
    dv(lambda e: e.tensor_tensor(out=den[:], in0=den[:], in1=t1[:], op=ALU.add), ["den", "t1"], ["den"])
    dv(lambda e: e.reciprocal(out=den[:], in_=den[:]), ["den"], ["den"])
    dv(lambda e: e.tensor_tensor(out=fre[:], in0=lbc[:], in1=lre[:], op=ALU.mult), ["lbc", "lre"], ["fre"])
    dv(lambda e: e.tensor_tensor(out=t1[:], in0=lbs[:], in1=lim[:], op=ALU.mult), ["lbs", "lim", "den"], ["t1"])
    dv(lambda e: e.tensor_tensor(out=fre[:], in0=fre[:], in1=t1[:], op=ALU.add), ["fre", "t1"], ["fre"])
    dv(lambda e: e.tensor_tensor(out=fre[:], in0=fre[:], in1=den[:], op=ALU.mult), ["fre", "den"], ["fre"])
    dv(lambda e: e.tensor_tensor(out=fim[:], in0=lbs[:], in1=lre[:], op=ALU.mult), ["lbs", "lre"], ["fim"])
    dv(lambda e: e.tensor_tensor(out=t2[:], in0=lbc[:], in1=lim[:], op=ALU.mult), ["lbc", "lim"], ["t2"])
    dv(lambda e: e.tensor_tensor(out=fim[:], in0=fim[:], in1=t2[:], op=ALU.subtract), ["fim", "t2"], ["fim"])
    dv(lambda e: e.tensor_tensor(out=fim[:], in0=fim[:], in1=den[:], op=ALU.mult), ["fim", "den"], ["fim"])
    gre = P.sb("gre", [128, NGC, S5G]); gimn = P.sb("gimn", [128, NGC, S5G]); gt = P.sb("gt", [128, NGC, S5G])
    freb = fre[:].unsqueeze(2).to_broadcast([128, NGC, S5G]); fimb = fim[:].unsqueeze(2).to_broadcast([128, NGC, S5G])
    dv(lambda e: e.tensor_tensor(out=gre[:], in0=cre[:], in1=freb, op=ALU.mult), ["cre", "fre"], ["gre"])
    dv(lambda e: e.tensor_tensor(out=gt[:], in0=cim[:], in1=fimb, op=ALU.mult), ["cim", "fim"], ["gt"])
    dv(lambda e: e.tensor_tensor(out=gre[:], in0=gre[:], in1=gt[:], op=ALU.subtract), ["gre", "gt"], ["gre"])
    dv(lambda e: e.tensor_tensor(out=gimn[:], in0=cre[:], in1=fimb, op=ALU.mult), ["cre", "fim"], ["gimn"])
    dv(lambda e: e.tensor_tensor(out=gt[:], in0=cim[:], in1=freb, op=ALU.mult), ["cim", "fre", "gre"], ["gt"])
    dv(lambda e: e.tensor_tensor(out=gimn[:], in0=gimn[:], in1=gt[:], op=ALU.add), ["gimn", "gt"], ["gimn"])
    dv(lambda e: e.tensor_scalar(out=gimn[:], in0=gimn[:], scalar1=-1.0, scalar2=None, op0=ALU.mult), ["gimn"], ["gimn"])
    greb = P.sb("greb", [128, NGC, S5G], BF16); gimb = P.sb("gimb", [128, NGC, S5G], BF16)
    dv(lambda e: e.tensor_copy(out=greb[:], in_=gre[:]), ["gre"], ["greb"])
    dv(lambda e: e.tensor_copy(out=gimb[:], in_=gimn[:]), ["gimn"], ["gimb"])

    U = P.sb("U", [128, T], BF16)
    BR = P.sb("BR", [128, T], BF16); BI = P.sb("BI", [128, T], BF16); TR = P.sb("TR", [128, T]); TI = P.sb("TI", [128, T])
    ZR = P.sb("ZR", [128, T], BF16); ZI = P.sb("ZI", [128, T], BF16)
    Yg = P.sb("Yg", [S5G, 2112])
    bpr = P.sb("bpr", [128, 8, 128], BF16); bpi = P.sb("bpi", [128, 8, 128], BF16)
    eac = P.sb("eac", [128, 8, NA]); eas = P.sb("eas", [128, 8, NA]); ebc = P.sb("ebc", [128, 8, 64]); ebs = P.sb("ebs", [128, 8, 64])
    pha = P.sb("pha", [128, 8, NA]); phb = P.sb("phb", [128, 8, 64]); pia = P.sb("pia", [128, 8, NA], I32); pib = P.sb("pib", [128, 8, 64], I32)
    psr = [P.ps("psr%d" % i, [128, 512]) for i in range(2)]
    psi = [P.ps("psi%d" % i, [128, 512]) for i in range(2)]
    psy = [P.ps("psy%d" % i, [S5G, 512]) for i in range(2)]

    NSPL = 128

    def v3(t):
        return t[:].rearrange("p (a b) -> p a b", b=64)
    blocks = [(i * 512, min(512, T - i * 512)) for i in range((T + 511) // 512)]
    for oc in range(n_oct):
        for sg in range(4):
            c0 = sg * 2112
            hraw = TR[:, 0:2112]
            P.dma("sp", hraw, h_in[oc * 128:(oc + 1) * 128, c0:c0 + 2112], writes=kk2("TR"))
            pieces = [(0, CTX, 1), (CTX, 2112, 0)] if sg == 0 else [(0, 2112, 0)]
            for (a, bb, s) in pieces:
                P.op("dve", lambda e: e.tensor_scalar(out=U[:, c0 + a:c0 + bb], in0=hraw[:, a:bb], scalar1=sc1[:, oc, s:s + 1],
                                                      scalar2=modo[:, oc, 0, s:s + 1], op0=ALU.mult, op1=ALU.add),
                     reads=kk2("TR") + ["sc1o", "modo"], writes=["U"])
        P.dma("pool", bpr[:], bre_in[oc], writes=["bpr"]); P.dma("pool", bpi[:], bim_in[oc], writes=["bpi"])
        g0 = oc * 8
        dv(lambda e: e.tensor_tensor(out=pha[:], in0=F1[:, g0:g0 + 8].unsqueeze(2).to_broadcast([128, 8, NA]),
                                     in1=avals[:].unsqueeze(1).to_broadcast([128, 8, NA]), op=ALU.mult),
           ["F1", "avals"], ["EAph"])
        emit_sincos(P, pha[:], pia[:], eas[:], eac[:], "EA")
        dv(lambda e: e.tensor_tensor(out=phb[:], in0=fq[:, g0:g0 + 8].unsqueeze(2).to_broadcast([128, 8, 64]),
                                     in1=bvals[:].unsqueeze(1).to_broadcast([128, 8, 64]), op=ALU.mult),
           ["fq", "bvals"], ["EBph"])
        emit_sincos(P, phb[:], pib[:], ebs[:], ebc[:], "EB")
        for gl in range(n_grp):
            g = g0 + gl
            for bi_, (c0, w) in enumerate(blocks):
                pr, prk = psr[bi_ % 2], "psr%d" % (bi_ % 2)
                pi, pik = psi[bi_ % 2], "psi%d" % (bi_ % 2)
                P.op("pe", lambda e: e.matmul(pr[:, :w], bpr[:, gl, :], U[:, c0:c0 + w], start=True, stop=True),
                     reads=["bpr", "U"], writes=[prk])
                P.op("pe", lambda e: e.matmul(pi[:, :w], bpi[:, gl, :], U[:, c0:c0 + w], start=True, stop=True),
                     reads=["bpi", "U"], writes=[pik])
                P.op("act", lambda e: e.activation(out=BR[:, c0:c0 + w], in_=pr[:, :w], func=AF.Copy), reads=[prk], writes=kk2("BR"))
                P.op("act", lambda e: e.activation(out=BI[:, c0:c0 + w], in_=pi[:, :w], func=AF.Copy), reads=[pik], writes=kk2("BI"))
            ebcb = ebc[:, gl, :].unsqueeze(1).to_broadcast([128, NA, 64]); ebsb = ebs[:, gl, :].unsqueeze(1).to_broadcast([128, NA, 64])
            eacb = eac[:, gl, :].unsqueeze(2).to_broadcast([128, NA, 64]); easb = eas[:, gl, :].unsqueeze(2).to_broadcast([128, NA, 64])
            if stages[0]:
                emit_cmul2(P, v3(ZR), v3(ZI), v3(BR), v3(BI), ebcb, ebsb, True, ("ZR", "ZI", "BR", "BI", "EBc"), NSPL)
            if stages[1]:
                emit_cmul2(P, v3(BR), v3(BI), v3(ZR), v3(ZI), eacb, easb, True, ("BR", "BI", "ZR", "ZI", "EAc"), NSPL)
            rf = r[0:64, g:g + 1]; rb = r[64:128, g:g + 1]
            for (src, dst, sk, dk) in (((BR, TR, "BR", "TR"), (BI, TI, "BI", "TI")) if stages[2] else ()):
                P.op("dve", lambda e: e.tensor_tensor_scan(out=dst[0:64, :], data0=rf.to_broadcast([64, T]), data1=src[0:64, :],
                                                           initial=0.0, op0=ALU.mult, op1=ALU.add),
                     reads=kk2(sk) + ["r"], writes=kk2(dk))
                P.op("dve", lambda e: e.tensor_tensor_scan(out=dst[64:128, 0:CTX][:, ::-1], data0=rb.to_broadcast([64, CTX]),
                                                           data1=src[64:128, 0:CTX][:, ::-1], initial=0.0, op0=ALU.mult, op1=ALU.add),
                     reads=kk2(sk) + ["r"], writes=kk2(dk))
                P.op("dve", lambda e: e.tensor_tensor_scan(out=dst[64:128, CTX:T][:, ::-1], data0=rb.to_broadcast([64, SEQ]),
                                                           data1=src[64:128, CTX:T][:, ::-1], initial=dst[64:128, 0:1],
                                                           op0=ALU.mult, op1=ALU.add),
                     reads=kk2(sk) + ["r"] + kk2(dk), writes=kk2(dk))
            if stages[3]:
                emit_cmul2(P, v3(BR), v3(BI), v3(TR), v3(TI), ebcb, ebsb, False, ("BR", "BI", "TR", "TI", "EBc"), NSPL)
                emit_cmul2(P, v3(ZR), v3(ZI), v3(BR), v3(BI), eacb, easb, False, ("ZR", "ZI", "BR", "BI", "EAc"), NSPL)
            for sg in (range(4) if stages[4] else ()):
                for sb_ in range(5):
                    c0 = sg * 2112 + sb_ * 512
                    w = min(512, sg * 2112 + 2112 - c0)
                    if w <= 0:
                        continue
                    py, pyk = psy[sb_ % 2], "psy%d" % (sb_ % 2)
                    P.op("pe", lambda e: e.matmul(py[:, :w], greb[:, g, :], ZR[:, c0:c0 + w], start=True, stop=False),
                         reads=["greb"] + kk2("ZR"), writes=[pyk])
                    P.op("pe", lambda e: e.matmul(py[:, :w], gimb[:, g, :], ZI[:, c0:c0 + w], start=False, stop=True),
                         reads=["gimb"] + kk2("ZI"), writes=[pyk], pe_acc=True)
                    P.op("dve", lambda e: e.tensor_copy(out=Yg[:, sb_ * 512:sb_ * 512 + w], in_=py[:, :w]), reads=[pyk], writes=["Yg"])
                P.dma("sp", yscr[gl * S5G:(gl + 1) * S5G, sg * 2112:(sg + 1) * 2112], Yg[:], reads=["Yg"], writes=["yscr"])
        for sg in (range(4) if stages[5] else ()):
            c0 = sg * 2112
            X = TR[:, 0:2112]; X2 = TI[:, 0:2112]
            P.dma("sp", X, yscr[:, c0:c0 + 2112], reads=["yscr"], writes=kk2("TR"))
            P.op("dve", lambda e: e.scalar_tensor_tensor(out=X, in0=U[:, c0:c0 + 2112], scalar=dsk[:, oc:oc + 1], in1=X,
                                                         op0=ALU.mult, op1=ALU.add), reads=["U", "dsk"] + kk2("TR"), writes=kk2("TR"))
            P.op("pool", lambda e: e.tensor_tensor(out=X2, in0=X, in1=X, op=ALU.mult), reads=kk2("TR"), writes=kk2("TI"))
            P.op("pool", lambda e: e.tensor_scalar(out=X2, in0=X2, scalar1=0.044715 * GELU_C, scalar2=GELU_C, op0=ALU.mult, op1=ALU.add),
                 reads=kk2("TI"), writes=kk2("TI"))
            P.op("dve", lambda e: e.tensor_tensor(out=X2, in0=X2, in1=X, op=ALU.mult), reads=kk2("TI") + kk2("TR"), writes=kk2("TI"))
            P.op("act", lambda e: e.activation(out=X2, in_=X2, func=AF.Tanh), reads=kk2("TI"), writes=kk2("TI"))
            P.op("dve", lambda e: e.tensor_scalar(out=X2, in0=X2, scalar1=1.0, scalar2=0.5, op0=ALU.add, op1=ALU.mult),
                 reads=kk2("TI"), writes=kk2("TI"))
            P.op("dve", lambda e: e.tensor_tensor(out=X, in0=X, in1=X2, op=ALU.mult), reads=kk2("TI") + kk2("TR"), writes=kk2("TR"))
            P.dma("sp", f_out[oc * 128:(oc + 1) * 128, c0:c0 + 2112], X, reads=kk2("TR"), writes=kk2("TR"), is_output=True)
    return P.finish()


def s5_consts():
    a = np.arange(NA, dtype=np.float32)
    ab = np.where(a < 4, 3 - a, 135 - a).astype(np.float32)
    avals = np.concatenate([np.tile(a, (64, 1)), np.tile(ab, (64, 1))], axis=0)
    b = np.arange(64, dtype=np.float32)
    bvals = np.concatenate([np.tile(b, (64, 1)), np.tile(63 - b, (64, 1))], axis=0)
    return np.ascontiguousarray(avals), np.ascontiguousarray(bvals)


def run_o2(hT, modT_l, lam_re, lam_im, log_dt, b_re, b_im, c_re, c_im, d_skip, **bkw):
    avals, bvals = s5_consts()
    in_maps = []
    for j in range(NCORES):
        gs = slice(NGC * j, NGC * (j + 1))
        ch = slice(512 * j, 512 * (j + 1))
        def dp(a):
            return np.ascontiguousarray(np.asarray(a[:, gs], np.float32).transpose(0, 2, 1).reshape(128, NGC))
        ldt = np.ascontiguousarray(np.broadcast_to(np.asarray(log_dt[:, gs], np.float32)[:, None, :], (2, 64, NGC)).reshape(128, NGC))
        def bpad(bm):
            bm = np.asarray(bm[:, gs], np.float32)
            out = np.zeros((NOCT, 128, 8, 128), np.float32)
            for oc in range(NOCT):
                for gl in range(8):
                    blk = bm[:, oc * 8 + gl]
                    out[oc, gl * 16:(gl + 1) * 16, gl, :] = blk.transpose(2, 0, 1).reshape(16, 128)
            return out
        def cpad(cm):
            cm = np.asarray(cm[:, gs], np.float32)
            return np.ascontiguousarray(cm.transpose(0, 3, 1, 2).reshape(128, NGC, S5G))
        modo = np.zeros((128, NOCT, 2, 2), np.float32)
        for oc in range(NOCT):
            chunk = 4 * j + oc
            modo[:, oc, 0, :] = modT_l[:, 0 * KC + chunk, :]
            modo[:, oc, 1, :] = modT_l[:, 1 * KC + chunk, :]
        dsk = np.ascontiguousarray(np.asarray(d_skip[ch], np.float32).reshape(NOCT, 128).T)
        in_maps.append({"hT": np.ascontiguousarray(hT[ch]), "modo": modo, "dsk": dsk, "lam_re": dp(lam_re), "lam_im": dp(lam_im),
                        "log_dt": ldt, "bre": bpad(b_re), "bim": bpad(b_im), "cre": cpad(c_re), "cim": cpad(c_im),
                        "avals": avals, "bvals": bvals})
    res = run(build_o2(**bkw), in_maps)
    return np.concatenate([r["fT"] for r in res], axis=0)


CB = 16
NFFT = 16384
HYF = 64


def build_eh():
    P = Prog()
    nc = P.nc
    I32 = mybir.dt.int32
    zh_in = P.dram_in("zh", [3, 256, T])
    cw_in = P.dram_in("convw", [128, 2, 3, 3])
    hb_in = P.dram_in("hbias", [128, 2, 2])
    w1_in = P.dram_in("w1", [33, HYF]); w2_in = P.dram_in("w2", [HYF, HYF]); w3_in = P.dram_in("w3", [HYF, HYF])
    bfr_in = P.dram_in("bfr", [HYF, 2, 3])
    w4_in = P.dram_in("w4", [HYF, 2, 2, 256])
    zl_in = P.dram_in("zposl", [33, SEQ]); zc_in = P.dram_in("zposc", [33, CTX])
    dl_in = P.dram_in("decl", [256, SEQ]); dc_in = P.dram_in("decc", [256, CTX])
    tabs_in = P.dram_in("tabs", [128, 12, 128])
    y_out = P.dram_out("yh", [256, T])
    x1c = nc.dram_tensor("x1c", [128, T], F32, kind="Internal").ap()
    x2c = nc.dram_tensor("x2c", [128, T], F32, kind="Internal").ap()
    a_dram = nc.dram_tensor("a_dram", [128, SEQ], F32, kind="Internal").ap()
    g_dram = nc.dram_tensor("g_dram", [128, NFFT], F32, kind="Internal").ap()
    c_dram = nc.dram_tensor("c_dram", [128, SEQ], F32, kind="Internal").ap()

    def ld(name, src, shape, q="sp"):
        t = P.sb(name, shape)
        P.dma(q, t[:], src, writes=[name])
        return t
    cw = ld("cw", cw_in, [128, 2, 3, 3]); hbias = ld("hbias", hb_in, [128, 2, 2])
    w1 = ld("w1", w1_in, [33, HYF]); w2 = ld("w2", w2_in, [HYF, HYF]); w3 = ld("w3", w3_in, [HYF, HYF])
    bfr = ld("bfr", bfr_in, [HYF, 2, 3]); w4 = ld("w4", w4_in, [HYF, 2, 2, 256])
    tabs = ld("tabs", tabs_in, [128, 12, 128])
    T1 = tabs[:, 0:2, :].rearrange("p a b -> p (a b)")
    TI1a = tabs[:, 7:9, :].rearrange("p a b -> p (a b)")
    TI1b = tabs[:, 9:11, :].rearrange("p a b -> p (a b)")
    C128 = tabs[:, 0, :]; NEGS = tabs[:, 1, :]; S128 = tabs[:, 2, :]
    twc = tabs[:, 3, :]; tws = tabs[:, 4, :]
    CN = tabs[:, 5, 0:64]; NSN = tabs[:, 6, 0:64]
    frs = P.sb("frs", [HYF, 3])
    P.op("dve", lambda e: e.tensor_scalar(out=frs[:], in0=bfr[:, 1, :], scalar1=1.0 / (2 * math.pi), scalar2=None, op0=ALU.mult),
         reads=["bfr"], writes=["frs"])

    h3l = P.sb("h3l", [HYF, SEQ]); h3c = P.sb("h3c", [HYF, CTX])
    zp = P.sb("zp", [33, 512]); ha = P.sb("ha", [HYF, 512]); hbt = P.sb("hbt", [HYF, 512]); hi_ = P.sb("hi", [HYF, 512], I32)
    psm = P.ps("psm", [128, 512])

    def mlp_layer(li, wt, wk, src, srck, dst, dstk, w_):
        kdim = 33 if li == 0 else HYF
        P.op("pe", lambda e: e.matmul(psm[0:HYF, :w_], wt[0:kdim, :], src, start=True, stop=True), reads=[wk, srck], writes=["psm"])
        P.op("dve", lambda e: e.tensor_scalar(out=hbt[:, :w_], in0=psm[0:HYF, :w_], scalar1=bfr[:, 0, li:li + 1],
                                              scalar2=frs[:, li:li + 1], op0=ALU.add, op1=ALU.mult),
             reads=["psm", "bfr", "frs"], writes=["hbt"])
        P.op("dve", lambda e: e.tensor_copy(out=hi_[:, :w_], in_=hbt[:, :w_]), reads=["hbt"], writes=["hi"])
        P.op("dve", lambda e: e.tensor_tensor(out=hbt[:, :w_], in0=hbt[:, :w_], in1=hi_[:, :w_], op=ALU.subtract),
             reads=["hbt", "hi"], writes=["hbt"])
        P.op("act", lambda e: e.activation(out=dst, in_=hbt[:, :w_], func=AF.Sin, scale=TWO_PI), reads=["hbt"], writes=[dstk])

    for (zsrc, L, h3) in ((zl_in, SEQ, h3l), (zc_in, CTX, h3c)):
        for b0 in range(0, L, 512):
            w_ = min(512, L - b0)
            P.dma("sp", zp[:, :w_], zsrc[:, b0:b0 + w_], writes=["zp"])
            mlp_layer(0, w1, "w1", zp[:, :w_], "zp", ha[:, :w_], "ha", w_)
            mlp_layer(1, w2, "w2", ha[:, :w_], "ha", ha[:, :w_], "ha", w_)
            mlp_layer(2, w3, "w3", ha[:, :w_], "ha", h3[:, b0:b0 + w_], "h3", w_)

    CBH = 8

    class BS:
        pass
    bsets = []
    for si in range(2):
        bs = BS(); bs.k = "f%d_" % si
        def T_(name, shape, k=bs.k):
            return P.sb(k + name, shape)
        bs.Xt = T_("Xt", [128, CBH, 128]); bs.A = T_("A", [128, CBH, 256])
        bs.P1r = T_("P1r", [128, CBH, 128]); bs.P1i = T_("P1i", [128, CBH, 128])
        bs.P2r = T_("P2r", [128, CBH, 128]); bs.P2i = T_("P2i", [128, CBH, 128])
        bs.Hr = T_("Hr", [128, CBH, 128]); bs.Hi = T_("Hi", [128, CBH, 128])
        bs.Yo = T_("Yo", [64, CBH, 128])
        bs.ps1 = P.ps(bs.k + "ps1", [128, 512]); bs.psxr = P.ps(bs.k + "psxr", [128, 512]); bs.psxi = P.ps(bs.k + "psxi", [128, 512])
        bsets.append(bs)
    twcb = twc.unsqueeze(1).to_broadcast([128, CBH, 128]); twsb = tws.unsqueeze(1).to_broadcast([128, CBH, 128])

    def fl(t):
        return t[:].rearrange("p c k -> p (c k)")

    def fft_fwd(bs, src_dram, ch0, kdim, outr, outi, ork, oik):
        k = bs.k
        P.dma("sp", bs.Xt[0:kdim], src_dram[ch0:ch0 + CBH, 0:kdim * 128].rearrange("c (n2 n1) -> n2 c n1", n1=128),
              reads=[src_dram.tensor.name], writes=[k + "Xt"])
        yield
        for c2 in range(CBH // 2):
            for h in range(2):
                ch = 2 * c2 + h
                P.op("pe", lambda e: e.matmul(bs.ps1[:, h * 256:(h + 1) * 256], bs.Xt[0:kdim, ch, :], T1[0:kdim, :], start=True, stop=True),
                     reads=[k + "Xt", "tabs"], writes=[k + "ps1"])
            P.op("act", lambda e: e.activation(out=bs.A[:, 2 * c2:2 * c2 + 2, :].rearrange("p c k -> p (c k)"), in_=bs.ps1[:], func=AF.Copy),
                 reads=[k + "ps1"], writes=[k + "A"])
            yield
        emit_cmul(P, bs.P1r[:], bs.P1i[:], bs.A[:, :, 0:128], bs.A[:, :, 128:256], twcb, twsb, True,
                  (k + "P1r", k + "P1i", k + "A", k + "A", "tabs"))
        yield
        for b4 in range(CBH * 128 // 512):
            sl = slice(b4 * 512, (b4 + 1) * 512)
            P.op("pe", lambda e: e.matmul(bs.psxr[:], C128, fl(bs.P1r)[:, sl], start=True, stop=False), reads=["tabs", k + "P1r"], writes=[k + "psxr"])
            P.op("pe", lambda e: e.matmul(bs.psxr[:], S128, fl(bs.P1i)[:, sl], start=False, stop=True), reads=["tabs", k + "P1i"], writes=[k + "psxr"],
                 pe_acc=True)
            P.op("pe", lambda e: e.matmul(bs.psxi[:], C128, fl(bs.P1i)[:, sl], start=True, stop=False), reads=["tabs", k + "P1i"], writes=[k + "psxi"])
            P.op("pe", lambda e: e.matmul(bs.psxi[:], NEGS, fl(bs.P1r)[:, sl], start=False, stop=True), reads=["tabs", k + "P1r"], writes=[k + "psxi"],
                 pe_acc=True)
            P.op("act", lambda e: e.activation(out=fl(outr)[:, sl], in_=bs.psxr[:], func=AF.Copy), reads=[k + "psxr"], writes=[ork])
            P.op("dve", lambda e: e.tensor_copy(out=fl(outi)[:, sl], in_=bs.psxi[:]), reads=[k + "psxi"], writes=[oik])
            yield

    def fft_conv_batch(bs, ch0):
        k = bs.k
        yield from fft_fwd(bs, g_dram, ch0, 128, bs.Hr, bs.Hi, k + "Hr", k + "Hi")
        yield from fft_fwd(bs, a_dram, ch0, 64, bs.P2r, bs.P2i, k + "P2r", k + "P2i")
        P.op("pool", lambda e: e.tensor_tensor(out=bs.P1r[:], in0=bs.P2r[:], in1=bs.Hr[:], op=ALU.mult), reads=[k + "P2r", k + "Hr"], writes=[k + "P1r"])
        P.op("dve", lambda e: e.tensor_tensor(out=bs.P1i[:], in0=bs.P2i[:], in1=bs.Hi[:], op=ALU.mult), reads=[k + "P2i", k + "Hi"], writes=[k + "P1i"])
        P.op("dve", lambda e: e.tensor_tensor(out=bs.P1r[:], in0=bs.P1r[:], in1=bs.P1i[:], op=ALU.subtract), reads=[k + "P1r", k + "P1i"], writes=[k + "P1r"])
        yield
        P.op("pool", lambda e: e.tensor_tensor(out=bs.P1i[:], in0=bs.P2i[:], in1=bs.Hr[:], op=ALU.mult), reads=[k + "P2i", k + "Hr", k + "P1r"], writes=[k + "P1i"])
        P.op("dve", lambda e: e.tensor_tensor(out=bs.P2r[:], in0=bs.P2r[:], in1=bs.Hi[:], op=ALU.mult), reads=[k + "P2r", k + "Hi"], writes=[k + "P2r"])
        P.op("dve", lambda e: e.tensor_tensor(out=bs.P1i[:], in0=bs.P1i[:], in1=bs.P2r[:], op=ALU.add), reads=[k + "P1i", k + "P2r"], writes=[k + "P1i"])
        yield
        for c2 in range(CBH // 2):
            for h in range(2):
                ch = 2 * c2 + h
                P.op("pe", lambda e: e.matmul(bs.ps1[:, h * 256:(h + 1) * 256], bs.P1r[:, ch, :], TI1a, start=True, stop=False),
                     reads=[k + "P1r", "tabs"], writes=[k + "ps1"])
                P.op("pe", lambda e: e.matmul(bs.ps1[:, h * 256:(h + 1) * 256], bs.P1i[:, ch, :], TI1b, start=False, stop=True),
                     reads=[k + "P1i", "tabs"], writes=[k + "ps1"], pe_acc=True)
            P.op("act", lambda e: e.activation(out=bs.A[:, 2 * c2:2 * c2 + 2, :].rearrange("p c k -> p (c k)"), in_=bs.ps1[:], func=AF.Copy),
                 reads=[k + "ps1"], writes=[k + "A"])
            yield
        emit_cmul(P, bs.P2r[:], bs.P2i[:], bs.A[:, :, 0:128], bs.A[:, :, 128:256], twcb, twsb, False,
                  (k + "P2r", k + "P2i", k + "A", k + "A", "tabs"))
        yield
        for b4 in range(CBH * 128 // 512):
            sl = slice(b4 * 512, (b4 + 1) * 512)
            pt = bs.ps1[0:64, :]
            P.op("pe", lambda e: e.matmul(pt, CN, fl(bs.P2r)[:, sl], start=True, stop=False), reads=["tabs", k + "P2r"], writes=[k + "ps1"])
            P.op("pe", lambda e: e.matmul(pt, NSN, fl(bs.P2i)[:, sl], start=False, stop=True), reads=["tabs", k + "P2i"], writes=[k + "ps1"], pe_acc=True)
            P.op("act", lambda e: e.activation(out=bs.Yo[:].rearrange("p c k -> p (c k)")[:, sl], in_=pt, func=AF.Copy),
                 reads=[k + "ps1"], writes=[k + "Yo"])
            yield
        P.dma("sp", c_dram[ch0:ch0 + CBH, :].rearrange("c (m1 m2) -> m1 c m2", m2=128), bs.Yo[:], reads=[k + "Yo"], writes=["c_dram"])
        yield

    def run_fft_conv():
        batches = list(range(0, 128, CBH))
        for i0 in range(0, len(batches), 2):
            gens = [fft_conv_batch(bsets[j], batches[i0 + j]) for j in range(2) if i0 + j < len(batches)]
            alive = gens
            while alive:
                nxt = []
                for gen in alive:
                    try:
                        next(gen)
                        nxt.append(gen)
                    except StopIteration:
                        pass
                alive = nxt

    SEGW = 2048
    R = P.sb("R", [128, SEGW + 2]); ACC = P.sb("ACC", [128, SEGW]); GT = P.sb("GT", [128, SEGW]); CT = P.sb("CT", [128, SEGW])
    a_ctx = P.sb("a_ctx", [128, CTX]); cc1 = P.sb("cc1", [128, CTX]); cc2 = P.sb("cc2", [128, CTX])
    fwc = P.sb("fwc", [128, CTX]); bwc = P.sb("bwc", [128, CTX])
    gblk = P.sb("gblk", [128, 512]); grev = P.sb("grev", [128, 512]); dblk = P.sb("dblk", [128, 512]); zcol = P.sb("zcol", [128, 1])
    P.op("dve", lambda e: e.memset(zcol[:], 0.0), writes=["zcol"])
    segs = [(0, CTX)] + [(CTX + i * SEGW, SEGW) for i in range(SEQ // SEGW)]

    for tl in range(2):
        rows = slice(tl * 128, (tl + 1) * 128)
        for part in range(3):
            for (c0, w_) in segs:
                first = c0 in (0, CTX); last = (c0 + w_) in (CTX, T)
                if first or last:
                    P.op("dve", lambda e: e.memset(R[:, 0:w_ + 2], 0.0), writes=["R"])
                lo = c0 - (0 if first else 1); hi = c0 + w_ + (0 if last else 1)
                P.dma("sp", R[:, (1 if first else 0):(1 if first else 0) + hi - lo], zh_in[part, rows, lo:hi], writes=["R"])
                P.op("dve", lambda e: e.tensor_scalar(out=ACC[:, :w_], in0=R[:, 1:w_ + 1], scalar1=cw[:, tl, part, 1:2], scalar2=None,
                                                      op0=ALU.mult), reads=["R", "cw"], writes=["ACC"])
                P.op("dve", lambda e: e.scalar_tensor_tensor(out=ACC[:, :w_], in0=R[:, 0:w_], scalar=cw[:, tl, part, 0:1], in1=ACC[:, :w_],
                                                             op0=ALU.mult, op1=ALU.add), reads=["R", "cw", "ACC"], writes=["ACC"])
                P.op("dve", lambda e: e.scalar_tensor_tensor(out=ACC[:, :w_], in0=R[:, 2:w_ + 2], scalar=cw[:, tl, part, 2:3], in1=ACC[:, :w_],
                                                             op0=ALU.mult, op1=ALU.add), reads=["R", "cw", "ACC"], writes=["ACC"])
                if part == 0:
                    if c0 == 0:
                        P.op("act", lambda e: e.activation(out=a_ctx[:], in_=ACC[:, :CTX], func=AF.Copy), reads=["ACC"], writes=["a_ctx"])
                    else:
                        P.dma("act", a_dram[:, c0 - CTX:c0 - CTX + w_], ACC[:, :w_], reads=["ACC"], writes=["a_dram"])
                else:
                    dst = x1c if part == 1 else x2c
                    P.dma("act", dst[:, c0:c0 + w_], ACC[:, :w_], reads=["ACC"], writes=[dst.tensor.name])
        for o in range(2):
            for d in range(2):
                for b0 in range(0, SEQ, 512):
                    P.op("pe", lambda e: e.matmul(psm[:], w4[:, o, d, rows], h3l[:, b0:b0 + 512], start=True, stop=True),
                         reads=["w4", "h3"], writes=["psm"])
                    P.dma("sp", dblk[:], dl_in[rows, b0:b0 + 512], writes=["dblk"])
                    P.op("dve", lambda e: e.tensor_tensor(out=gblk[:], in0=psm[:], in1=dblk[:], op=ALU.mult), reads=["psm", "dblk"], writes=["gblk"])
                    if d == 0:
                        P.dma("act", g_dram[:, b0:b0 + 512], gblk[:], reads=["gblk"], writes=["g_dram"])
                    else:
                        P.op("pool", lambda e: e.tensor_copy(out=grev[:], in_=gblk[:, ::-1]), reads=["gblk"], writes=["grev"])
                        if b0 == 0:
                            P.dma("act", g_dram[:, NFFT - 511:NFFT], grev[:, 0:511], reads=["grev"], writes=["g_dram"])
                        else:
                            P.dma("act", g_dram[:, NFFT - b0 - 511:NFFT - b0 + 1], grev[:], reads=["grev"], writes=["g_dram"])
                P.op("pe", lambda e: e.matmul(psm[:, :CTX], w4[:, o, d, rows], h3c[:], start=True, stop=True), reads=["w4", "h3"], writes=["psm"])
                P.dma("sp", dblk[:, :CTX], dc_in[rows, :], writes=["dblk"])
                fc = fwc if d == 0 else bwc
                P.op("dve", lambda e: e.tensor_tensor(out=fc[:], in0=psm[:, :CTX], in1=dblk[:, :CTX], op=ALU.mult),
                     reads=["psm", "dblk"], writes=["fwc" if d == 0 else "bwc"])
            P.dma("act", g_dram[:, SEQ:SEQ + 1], zcol[:], reads=["zcol"], writes=["g_dram"], allow_slow_non_contiguous=True)
            run_fft_conv()
            P.op("dve", lambda e: e.memset(cc1[:], 0.0), writes=["cc1"])
            P.op("pool", lambda e: e.memset(cc2[:], 0.0), writes=["cc2"])
            for d in range(CTX):
                P.op("dve", lambda e: e.scalar_tensor_tensor(out=cc1[:, d:CTX], in0=a_ctx[:, 0:CTX - d], scalar=fwc[:, d:d + 1],
                                                             in1=cc1[:, d:CTX], op0=ALU.mult, op1=ALU.add),
                     reads=["a_ctx", "fwc", "cc1"], writes=["cc1"])
                if d >= 1:
                    P.op("dve", lambda e: e.scalar_tensor_tensor(out=cc2[:, 0:CTX - d], in0=a_ctx[:, d:CTX], scalar=bwc[:, d:d + 1],
                                                                  in1=cc2[:, 0:CTX - d], op0=ALU.mult, op1=ALU.add),
                         reads=["a_ctx", "bwc", "cc2"], writes=["cc2"])
            P.op("dve", lambda e: e.tensor_tensor(out=cc1[:], in0=cc1[:], in1=cc2[:], op=ALU.add), reads=["cc1", "cc2"], writes=["cc1"])
            gsrc = x1c if o == 0 else x2c
            for (c0, w_) in segs:
                P.dma("sp", GT[:, :w_], gsrc[:, c0:c0 + w_], reads=[gsrc.tensor.name], writes=["GT"])
                if c0 == 0:
                    csrc = cc1[:]; asrc = a_ctx[:]; ck = "cc1"; ak = "a_ctx"
                else:
                    P.dma("sp", CT[:, :w_], c_dram[:, c0 - CTX:c0 - CTX + w_], reads=["c_dram"], writes=["CT"])
                    P.dma("sp", R[:, :w_], a_dram[:, c0 - CTX:c0 - CTX + w_], reads=["a_dram"], writes=["R"])
                    csrc = CT[:, :w_]; asrc = R[:, :w_]; ck = "CT"; ak = "R"
                P.op("dve", lambda e: e.scalar_tensor_tensor(out=ACC[:, :w_], in0=asrc, scalar=hbias[:, tl, o:o + 1], in1=csrc,
                                                             op0=ALU.mult, op1=ALU.add), reads=[ak, ck, "hbias"], writes=["ACC"])
                P.op("pool", lambda e: e.tensor_tensor(out=ACC[:, :w_], in0=ACC[:, :w_], in1=GT[:, :w_], op=ALU.mult),
                     reads=["ACC", "GT"], writes=["ACC"])
                if o == 0:
                    if c0 == 0:
                        P.op("act", lambda e: e.activation(out=a_ctx[:], in_=ACC[:, :CTX], func=AF.Copy), reads=["ACC"], writes=["a_ctx"])
                    else:
                        P.dma("act", a_dram[:, c0 - CTX:c0 - CTX + w_], ACC[:, :w_], reads=["ACC"], writes=["a_dram"])
                else:
                    P.dma("act", y_out[rows, c0:c0 + w_], ACC[:, :w_], reads=["ACC"], is_output=True)
    return P.finish()


def hyena_consts(L):
    pos = np.arange(L, dtype=np.float32)
    t = pos / np.float32(max(L - 1, 1))
    ang = np.float32(2.0 * math.pi / L) * pos
    bands = np.linspace(1e-4, 15, 16, dtype=np.float32)
    z = np.concatenate([t[:, None], np.cos(ang[:, None] * bands), -np.sin(ang[:, None] * bands)], axis=-1).astype(np.float32)
    mx = math.log(1e-2) / 0.3
    mn = math.log(1e-2) / 1.5
    deltas = np.abs(np.linspace(mn, mx, HYW, dtype=np.float32))
    dec = np.exp(-t[None, :] * deltas[:, None]).astype(np.float32)
    return np.ascontiguousarray(z.T), dec


def fft_tabs():
    p = np.arange(128, dtype=np.float64)[:, None]; j = np.arange(128, dtype=np.float64)[None, :]
    a128 = 2 * np.pi * p * j / 128.0
    aN = 2 * np.pi * p * j / NFFT
    tabs = np.zeros((128, 12, 128), np.float64)
    tabs[:, 0] = np.cos(a128); tabs[:, 1] = -np.sin(a128); tabs[:, 2] = np.sin(a128)
    tabs[:, 3] = np.cos(aN); tabs[:, 4] = np.sin(aN)
    tabs[:, 5] = np.cos(a128) / NFFT; tabs[:, 6] = -np.sin(a128) / NFFT
    tabs[:, 7] = np.cos(a128); tabs[:, 8] = np.sin(a128)
    tabs[:, 9] = -np.sin(a128); tabs[:, 10] = np.cos(a128)
    return tabs.astype(np.float32)


def run_eh(zs, hy_conv, w1, b1, w2, b2, w3, b3, w4, freq, bias):
    zl, decl = hyena_consts(SEQ); zc, decc = hyena_consts(CTX)
    tabs = fft_tabs()
    bfr = np.zeros((HYF, 2, 3), np.float32)
    bfr[:, 0, 0] = b1; bfr[:, 0, 1] = b2; bfr[:, 0, 2] = b3
    bfr[:, 1, :] = np.asarray(freq, np.float32).T
    w4r = np.asarray(w4, np.float32).reshape(HYF, 2, 2, HYW)
    in_maps = []
    for j in range(NCORES):
        ch = slice(256 * j, 256 * (j + 1))
        cwj = np.zeros((128, 2, 3, 3), np.float32)
        for part in range(3):
            blk = np.asarray(hy_conv[:, part * HYW + 256 * j: part * HYW + 256 * (j + 1)], np.float32)
            cwj[:, :, part, :] = blk.T.reshape(2, 128, 3).transpose(1, 0, 2)
        hbj = np.ascontiguousarray(np.asarray(bias[:, ch], np.float32).T.reshape(2, 128, 2).transpose(1, 0, 2))
        in_maps.append({"zh": np.ascontiguousarray(zs[j][:768].reshape(3, 256, T)), "convw": cwj, "hbias": hbj,
                        "w1": np.asarray(w1, np.float32), "w2": np.asarray(w2, np.float32), "w3": np.asarray(w3, np.float32),
                        "bfr": bfr, "w4": np.ascontiguousarray(w4r[:, :, :, ch]), "zposl": zl, "zposc": zc,
                        "decl": np.ascontiguousarray(decl[ch]), "decc": np.ascontiguousarray(decc[ch]), "tabs": tabs})
    res = run(build_eh(), in_maps)
    return [r["yh"] for r in res]


DNC = 128
NCHK = T // DNC
DK = 128


def dn_masks():
    t = np.arange(128)[:, None]; i = np.arange(128)[None, :]
    m = np.zeros((10, 128, 128), np.float32)
    m[0] = (t <= i); m[1] = (t > i)
    m[2] = np.where(t >= i, 0.0, -30000.0)
    m[3] = np.where(i >= t, 0.0, -30000.0)
    m[4] = (t > i)
    m[5] = (t >= i); m[6] = (t < i)
    m[7] = np.where(t <= i, 0.0, -30000.0)
    m[8] = np.where(i <= t, 0.0, -30000.0)
    m[9] = (t < i)
    return np.ascontiguousarray(m.transpose(1, 0, 2))


def build_ed(dbg=None):
    P = Prog()
    dbg = dbg or {}
    nc = P.nc
    qkvz_in = P.dram_in("qkvz", [4, 2, 128, T])
    braw_in = P.dram_in("braw", [4, T]); araw_in = P.dram_in("araw", [4, T])
    cw_in = P.dram_in("dconv", [128, 3, 2, 5])
    aA_in = P.dram_in("aA", [4, 2])
    nw_in = P.dram_in("normw", [128, 128])
    mk_in = P.dram_in("masks", [128, 10, 128])
    id_in = P.dram_in("ident", [128, 128])
    y_out = P.dram_out("yd", [2, 128, T])
    qd = nc.dram_tensor("qd", [2, 128, T], F32, kind="Internal").ap()
    kd = nc.dram_tensor("kd", [2, 128, T], F32, kind="Internal").ap()
    vd = nc.dram_tensor("vd", [2, 128, T], F32, kind="Internal").ap()
    dsc = {0: qd, 1: kd, 2: vd}

    def ld(name, src, shape, q="sp"):
        t = P.sb(name, shape)
        P.dma(q, t[:], src, writes=[name])
        return t
    cw = ld("cw", cw_in, [128, 3, 2, 5]); aA = ld("aA", aA_in, [4, 2]); nw = ld("nw", nw_in, [128, 128])
    mk = ld("mk", mk_in, [128, 10, 128]); ident = ld("ident", id_in, [128, 128])
    ones128 = P.sb("ones128", [128, 128])
    P.op("dve", lambda e: e.memset(ones128[:], 1.0), writes=["ones128"])
    PS = [P.ps("PS%d" % i, [128, 512]) for i in range(8)]
    pctr = [0]

    def pslot():
        n = pctr[0]; pctr[0] += 1
        b, s = n % 8, (n // 8) % 4
        return PS[b][:, s * 128:(s + 1) * 128], ("PS", b, s)

    SG = 22 * DNC
    bg = P.sb("bg", [4, 2, SG])
    tA = P.sb("tA", [4, SG]); tB = P.sb("tB", [4, SG])
    nA = P.sb("nA", [4, 1])
    P.op("act", lambda e: e.activation(out=nA[:], in_=aA[:, 0:1], func=AF.Exp), reads=["aA"], writes=["nA"])
    P.op("dve", lambda e: e.tensor_scalar(out=nA[:], in0=nA[:], scalar1=-1.0, scalar2=None, op0=ALU.mult), reads=["nA"], writes=["nA"])
    BGT = P.sb("BGT", [128, NCHK, 2, 4])
    for sgi in range(3):
        s0 = sgi * SG
        P.dma("sp", tA[:], braw_in[:, s0:s0 + SG], writes=["tA"])
        P.op("act", lambda e: e.activation(out=bg[:, 0, :], in_=tA[:], func=AF.Sigmoid), reads=["tA"], writes=["bg"])
        P.dma("sp", tB[:], araw_in[:, s0:s0 + SG], writes=["tB"])
        P.op("dve", lambda e: e.tensor_scalar(out=tB[:], in0=tB[:], scalar1=aA[:, 1:2], scalar2=None, op0=ALU.add), reads=["tB", "aA"], writes=["tB"])
        P.op("act", lambda e: e.activation(out=tA[:], in_=tB[:], func=AF.Abs), reads=["tB", "bg"], writes=["tA"])
        P.op("act", lambda e: e.activation(out=tA[:], in_=tA[:], func=AF.Exp, scale=-1.0), reads=["tA"], writes=["tA"])
        P.op("dve", lambda e: e.tensor_scalar_add(out=tA[:], in0=tA[:], scalar1=1.0), reads=["tA"], writes=["tA"])
        P.op("act", lambda e: e.activation(out=tA[:], in_=tA[:], func=AF.Ln), reads=["tA"], writes=["tA"])
        P.op("dve", lambda e: e.tensor_scalar_max(out=tB[:], in0=tB[:], scalar1=0.0), reads=["tB"], writes=["tB"])
        P.op("dve", lambda e: e.tensor_tensor(out=tB[:], in0=tB[:], in1=tA[:], op=ALU.add), reads=["tA", "tB"], writes=["tB"])
        P.op("dve", lambda e: e.tensor_scalar(out=bg[:, 1, :], in0=tB[:], scalar1=nA[:, 0:1], scalar2=None, op0=ALU.mult),
             reads=["tB", "nA"], writes=["bg"])
        for cl in range(22):
            c = sgi * 22 + cl
            for w_ in range(2):
                pt, pk = pslot()
                P.op("pe", lambda e: e.matmul(pt[:, 0:4], bg[:, w_, cl * DNC:(cl + 1) * DNC], ident[0:4, 0:4], start=True, stop=True), reads=["bg", "ident"], writes=[pk])
                P.op("dve", lambda e: e.tensor_copy(out=BGT[:, c, w_, :], in_=pt[:, 0:4]), reads=[pk], writes=["BGT"])
    NBG = P.sb("NBG", [128, NCHK, 4])
    P.op("dve", lambda e: e.tensor_scalar(out=NBG[:], in0=BGT[:, :, 0, :], scalar1=-1.0, scalar2=None, op0=ALU.mult), reads=["BGT"], writes=["NBG"])

    SEGW = 2048
    R = P.sb("R", [128, SEGW + 4]); ACC = P.sb("ACC", [128, SEGW]); SQ = P.sb("SQ", [128, SEGW]); RS = P.sb("RS", [128, 512])
    segs = [(0, CTX)] + [(CTX + i * SEGW, SEGW) for i in range(SEQ // SEGW)]
    for hd in range(2):
        for part in range(3):
            for (c0, w_) in segs:
                first = c0 in (0, CTX); last = (c0 + w_) in (CTX, T)
                if first or last:
                    P.op("pool", lambda e: e.memset(R[:, 0:w_ + 4], 0.0), writes=["R"])
                lo = c0 - (0 if first else 2); hi = c0 + w_ + (0 if last else 2)
                off = 2 if first else 0
                P.dma("sp", R[:, off:off + hi - lo], qkvz_in[part, hd, :, lo:hi], writes=["R"])
                P.op("dve", lambda e: e.tensor_scalar(out=ACC[:, :w_], in0=R[:, 0:w_], scalar1=cw[:, part, hd, 0:1], scalar2=None, op0=ALU.mult),
                     reads=["R", "cw"], writes=["ACC"])
                for k in range(1, 5):
                    P.op("dve", lambda e: e.scalar_tensor_tensor(out=ACC[:, :w_], in0=R[:, k:k + w_], scalar=cw[:, part, hd, k:k + 1],
                                                                 in1=ACC[:, :w_], op0=ALU.mult, op1=ALU.add),
                         reads=["R", "cw", "ACC"], writes=["ACC"])
                P.op("act", lambda e: e.activation(out=ACC[:, :w_], in_=ACC[:, :w_], func=AF.Silu), reads=["ACC"], writes=["ACC"])
                if part < 2:
                    P.op("pool", lambda e: e.tensor_tensor(out=SQ[:, :w_], in0=ACC[:, :w_], in1=ACC[:, :w_], op=ALU.mult), reads=["ACC"], writes=["SQ"])
                    for b0 in range(0, w_, 512):
                        bw = min(512, w_ - b0)
                        pb = PS[pctr[0] % 8]; pbk = ("PS", pctr[0] % 8, 0); pctr[0] += 1
                        allk = [("PS", pbk[1], s_) for s_ in range(4)]
                        P.op("pe", lambda e: e.matmul(pb[:, :bw], ones128[:], SQ[:, b0:b0 + bw], start=True, stop=True),
                             reads=["SQ", "ones128"], writes=allk)
                        P.op("dve", lambda e: e.tensor_scalar_add(out=RS[:, :bw], in0=pb[:, :bw], scalar1=1e-6),
                             reads=allk, writes=["RS"])
                        P.op("act", lambda e: e.activation(out=RS[:, :bw], in_=RS[:, :bw], func=AF.Sqrt), reads=["RS"], writes=["RS"])
                        P.op("dve", lambda e: e.reciprocal(out=RS[:, :bw], in_=RS[:, :bw]), reads=["RS"], writes=["RS"])
                        if part == 0:
                            P.op("dve", lambda e: e.scalar_tensor_tensor(out=ACC[:, b0:b0 + bw], in0=ACC[:, b0:b0 + bw], scalar=DK ** -0.5,
                                                                         in1=RS[:, :bw], op0=ALU.mult, op1=ALU.mult),
                                 reads=["ACC", "RS"], writes=["ACC"])
                        else:
                            P.op("dve", lambda e: e.tensor_tensor(out=ACC[:, b0:b0 + bw], in0=ACC[:, b0:b0 + bw], in1=RS[:, :bw], op=ALU.mult),
                                 reads=["ACC", "RS"], writes=["ACC"])
                P.dma("act", dsc[part][hd, :, c0:c0 + w_], ACC[:, :w_], reads=["ACC"], writes=[dsc[part].tensor.name])

    if dbg.get('stop') == 'pre':
        P.dma('sp', y_out[0, :, 0:128], ident[:], reads=['ident'], is_output=True)
        return P.finish()
    Oh = [P.sb("O%d" % h, [128, NCHK, 128]) for h in range(2)]
    for h in range(2):
        P.op("pool", lambda e: e.memset(Oh[h][:], 0.0), writes=[("O", h, c) for c in range(NCHK)])

    def mm(lhsT, lk, rhs, rk):
        pt, pk = pslot()
        n = rhs.shape[-1]
        m = lhsT.shape[-1]
        pt = pt[0:m, 0:n]
        P.op("pe", lambda e: e.matmul(pt, lhsT, rhs, start=True, stop=True), reads=lk + rk, writes=[pk])
        return pt, pk

    def dvop(fn, reads, writes, eng="dve"):
        P.op(eng, fn, reads=reads, writes=writes)

    class St:
        pass
    streams = []
    for hd in range(2):
        for dr in range(2):
            st = St()
            sid = "s%d%d_" % (hd, dr)
            st.sid = sid; st.hd = hd; st.dr = dr
            def T_(name, shape=(128, 128), sid=sid):
                return P.sb(sid + name, list(shape))
            st.qT = [T_("qT%d" % i) for i in range(2)]; st.kT = [T_("kT%d" % i) for i in range(2)]; st.vT = [T_("vT%d" % i) for i in range(2)]
            st.ktm = T_("ktm"); st.bv = T_("bv"); st.kbg = T_("kbg"); st.kdec = T_("kdec"); st.gMC = T_("gMC")
            st.dec = T_("dec"); st.decT = T_("decT")
            st.Xs = [T_("X%d" % i) for i in range(2)]; st.Ys = [T_("Y%d" % i) for i in range(2)]
            st.Pm = [T_("Pm%d" % i) for i in range(2)]; st.PTm = [T_("PTm%d" % i) for i in range(2)]
            st.usb = T_("usb"); st.wTs = T_("wTs"); st.vnew = T_("vnew"); st.o1 = T_("o1"); st.qkm = T_("qkm")
            st.cols = T_("cols", (128, 8)); st.S = T_("S")
            st.order = list(range(NCHK)) if dr == 0 else [1, 0] + list(range(NCHK - 1, 1, -1))
            if 'nchunks' in dbg:
                st.order = st.order[:dbg['nchunks']]
            P.op("pool", lambda e: e.memset(st.S[:], 0.0), writes=[sid + "S"])
            streams.append(st)

    def chunk_gen(st, c, b):
        sid = st.sid; hd = st.hd; dr = st.dr
        def K_(n):
            return sid + n
        row = dr * 2 + hd
        mo = 5 * dr
        MC = mk[:, mo + 0, :]; MS = mk[:, mo + 1, :]; NEG = mk[:, mo + 2, :]; STRICT = mk[:, mo + 4, :]
        qT, kT, vT = st.qT[b], st.kT[b], st.vT[b]
        cols = st.cols; S = st.S
        t0 = c * DNC
        P.dma("sp", qT[:], qd[hd, :, t0:t0 + DNC], reads=["qd"], writes=[K_("qT%d" % b)])
        P.dma("sp", kT[:], kd[hd, :, t0:t0 + DNC], reads=["kd"], writes=[K_("kT%d" % b)])
        P.dma("sp", vT[:], vd[hd, :, t0:t0 + DNC], reads=["vd"], writes=[K_("vT%d" % b)])
        qk_, kk_, vk_ = [K_("qT%d" % b)], [K_("kT%d" % b)], [K_("vT%d" % b)]
        beta = BGT[:, c, 0, row:row + 1]; g = BGT[:, c, 1, row:row + 1]; nbeta = NBG[:, c, row:row + 1]
        yield
        pt, pk = mm(kT[:], kk_, ident[:], ["ident"])
        dvop(lambda e: e.activation(out=st.ktm[:], in_=pt, func=AF.Copy), [pk], [K_("ktm")], "act")
        pt, pk = mm(vT[:], vk_, ident[:], ["ident"])
        dvop(lambda e: e.tensor_scalar(out=st.bv[:], in0=pt, scalar1=beta, scalar2=None, op0=ALU.mult), [pk, "BGT"], [K_("bv")])
        yield
        pt, pk = mm(MC, ["mk"], g, ["BGT"])
        dvop(lambda e: e.tensor_copy(out=cols[:, 0:1], in_=pt[:, 0:1]), [pk], [K_("cols")])
        pt, pk = mm(ones128[:], ["ones128"], g, ["BGT"])
        dvop(lambda e: e.tensor_copy(out=cols[:, 3:4], in_=pt[:, 0:1]), [pk], [K_("cols")])
        dvop(lambda e: e.tensor_scalar(out=st.gMC[:], in0=MC, scalar1=g, scalar2=None, op0=ALU.mult), ["mk", "BGT"], [K_("gMC")])
        yield
        dvop(lambda e: e.activation(out=cols[:, 1:2], in_=cols[:, 0:1], func=AF.Exp), [K_("cols")], [K_("cols")], "act")
        dvop(lambda e: e.activation(out=cols[:, 4:5], in_=cols[:, 3:4], func=AF.Exp), [K_("cols")], [K_("cols")], "act")
        dvop(lambda e: e.activation(out=cols[:, 5:6], in_=cols[:, 0:1], func=AF.Exp, scale=-1.0, bias=cols[:, 3:4]), [K_("cols")], [K_("cols")], "act")
        pt, pk = mm(st.gMC[:], [K_("gMC")], MS, ["mk"])
        dvop(lambda e: e.tensor_tensor(out=st.dec[:], in0=pt, in1=NEG, op=ALU.add), [pk, "mk"], [K_("dec")])
        yield
        dvop(lambda e: e.tensor_tensor(out=cols[:, 2:3], in0=cols[:, 1:2], in1=beta, op=ALU.mult), [K_("cols"), "BGT"], [K_("cols")])
        dvop(lambda e: e.activation(out=st.dec[:], in_=st.dec[:], func=AF.Exp), [K_("dec")], [K_("dec")], "act")
        dvop(lambda e: e.tensor_scalar(out=st.kbg[:], in0=st.ktm[:], scalar1=cols[:, 2:3], scalar2=None, op0=ALU.mult), [K_("ktm"), K_("cols")], [K_("kbg")])
        dvop(lambda e: e.tensor_scalar(out=st.kdec[:], in0=st.ktm[:], scalar1=cols[:, 5:6], scalar2=0.0, op0=ALU.mult, op1=ALU.add),
             [K_("ktm"), K_("cols")], [K_("kdec")], "pool")
        yield
        pt, pk = mm(st.dec[:], [K_("dec")], ident[:], ["ident"])
        dvop(lambda e: e.tensor_copy(out=st.decT[:], in_=pt), [pk], [K_("decT")])
        X, Y = st.Xs[0], st.Ys[0]
        pt, pk = mm(kT[:], kk_, kT[:], kk_)
        dvop(lambda e: e.tensor_tensor(out=X[:], in0=pt, in1=st.dec[:], op=ALU.mult), [pk, K_("dec")], [K_("X0")])
        dvop(lambda e: e.scalar_tensor_tensor(out=X[:], in0=X[:], scalar=nbeta, in1=STRICT, op0=ALU.mult, op1=ALU.mult),
             [K_("X0"), "NBG", "mk"], [K_("X0")])
        yield
        pt, pk = mm(X[:], [K_("X0")], ident[:], ["ident"])
        dvop(lambda e: e.activation(out=Y[:], in_=pt, func=AF.Copy), [pk], [K_("Y0")], "act")
        dvop(lambda e: e.tensor_tensor(out=st.Pm[0][:], in0=X[:], in1=ident[:], op=ALU.add), [K_("X0"), "ident"], [K_("Pm0")])
        yield
        dvop(lambda e: e.tensor_tensor(out=st.PTm[0][:], in0=Y[:], in1=ident[:], op=ALU.add), [K_("Y0"), "ident"], [K_("PTm0")], "pool")
        cur = 0
        for lv in range(6):
            nx = 1 - cur
            ptx, pkx = mm(st.Ys[cur][:], [K_("Y%d" % cur)], st.Xs[cur][:], [K_("X%d" % cur)])
            pty, pky = mm(st.Xs[cur][:], [K_("X%d" % cur)], st.Ys[cur][:], [K_("Y%d" % cur)])
            dvop(lambda e: e.tensor_copy(out=st.Xs[nx][:], in_=ptx), [pkx], [K_("X%d" % nx)])
            dvop(lambda e: e.activation(out=st.Ys[nx][:], in_=pty, func=AF.Copy), [pky], [K_("Y%d" % nx)], "act")
            yield
            if lv < 5:
                ptp, pkp = mm(st.PTm[cur][:], [K_("PTm%d" % cur)], st.Xs[nx][:], [K_("X%d" % nx)])
                dvop(lambda e: e.tensor_tensor(out=st.Pm[nx][:], in0=ptp, in1=st.Pm[cur][:], op=ALU.add), [pkp, K_("Pm%d" % cur)], [K_("Pm%d" % nx)])
            ptq, pkq = mm(st.Pm[cur][:], [K_("Pm%d" % cur)], st.Ys[nx][:], [K_("Y%d" % nx)])
            dvop(lambda e: e.tensor_tensor(out=st.PTm[nx][:], in0=ptq, in1=st.PTm[cur][:], op=ALU.add), [pkq, K_("PTm%d" % cur)], [K_("PTm%d" % nx)])
            cur = nx
            yield
        TinvT = st.PTm[cur]; tk = [K_("PTm%d" % cur)]
        pt, pk = mm(TinvT[:], tk, st.bv[:], [K_("bv")])
        dvop(lambda e: e.tensor_copy(out=st.usb[:], in_=pt), [pk], [K_("usb")])
        pt, pk = mm(st.kbg[:], [K_("kbg")], TinvT[:], tk)
        dvop(lambda e: e.activation(out=st.wTs[:], in_=pt, func=AF.Copy), [pk], [K_("wTs")], "act")
        yield
        pt, pk = mm(st.wTs[:], [K_("wTs")], S[:], [K_("S")])
        dvop(lambda e: e.tensor_tensor(out=st.vnew[:], in0=st.usb[:], in1=pt, op=ALU.subtract), [K_("usb"), pk], [K_("vnew")])
        pt, pk = mm(qT[:], qk_, S[:], [K_("S")])
        dvop(lambda e: e.activation(out=st.o1[:], in_=pt, func=AF.Copy, scale=cols[:, 1:2]), [pk, K_("cols")], [K_("o1")], "act")
        pt, pk = mm(kT[:], kk_, qT[:], qk_)
        dvop(lambda e: e.tensor_tensor(out=st.qkm[:], in0=pt, in1=st.decT[:], op=ALU.mult), [pk, K_("decT")], [K_("qkm")])
        yield
        pt, pk = mm(st.qkm[:], [K_("qkm")], st.vnew[:], [K_("vnew")])
        dvop(lambda e: e.tensor_tensor(out=st.o1[:], in0=pt, in1=st.o1[:], op=ALU.add), [pk, K_("o1")], [K_("o1")])
        dvop(lambda e: e.tensor_tensor(out=Oh[hd][:, c, :], in0=Oh[hd][:, c, :], in1=st.o1[:], op=ALU.add), [("O", hd, c), K_("o1")], [("O", hd, c)], "pool")
        pt, pk = mm(st.kdec[:], [K_("kdec")], st.vnew[:], [K_("vnew")])
        dvop(lambda e: e.scalar_tensor_tensor(out=S[:], in0=S[:], scalar=cols[:, 4:5], in1=pt, op0=ALU.mult, op1=ALU.add),
             [K_("S"), K_("cols"), pk], [K_("S")])
        yield

    nsteps = len(streams[0].order)
    for k in range(nsteps):
        gens = [chunk_gen(st, st.order[k], k % 2) for st in streams]
        alive = list(gens)
        while alive:
            nxt = []
            for gen in alive:
                try:
                    next(gen)
                    nxt.append(gen)
                except StopIteration:
                    pass
            alive = nxt

    if dbg.get('stop') == 'scan':
        P.dma('sp', y_out[0, :, 0:128], ident[:], reads=['ident'], is_output=True)
        return P.finish()
    zt = [P.sb("zt%d" % i, [128, 128]) for i in range(3)]
    ob_ = [P.sb("ob%d" % i, [128, 128]) for i in range(3)]
    sqt = P.sb("sqt", [128, 128]); nrm = P.sb("nrm", [128, 128]); gz = P.sb("gz", [128, 128]); ncol = P.sb("ncol", [128, 2])
    for hd in range(2):
        O = Oh[hd]
        for c in range(NCHK):
            b = c % 3
            t0 = c * DNC
            P.dma("sp", zt[b][:], qkvz_in[3, hd, :, t0:t0 + DNC], writes=["zt%d" % b])
            dvop(lambda e: e.tensor_tensor(out=sqt[:], in0=O[:, c, :], in1=O[:, c, :], op=ALU.mult), [("O", hd, c)], ["sqt"], "pool")
            dvop(lambda e: e.reduce_sum(out=ncol[:, 0:1], in_=sqt[:], axis=AX.X), ["sqt"], ["ncol"])
            dvop(lambda e: e.tensor_scalar(out=ncol[:, 1:2], in0=ncol[:, 0:1], scalar1=1.0 / 128, scalar2=1e-6, op0=ALU.mult, op1=ALU.add),
                 ["ncol"], ["ncol"])
            dvop(lambda e: e.activation(out=ncol[:, 1:2], in_=ncol[:, 1:2], func=AF.Sqrt), ["ncol"], ["ncol"], "act")
            dvop(lambda e: e.reciprocal(out=ncol[:, 1:2], in_=ncol[:, 1:2]), ["ncol"], ["ncol"])
            dvop(lambda e: e.scalar_tensor_tensor(out=nrm[:], in0=O[:, c, :], scalar=ncol[:, 1:2], in1=nw[:], op0=ALU.mult, op1=ALU.mult),
                 [("O", hd, c), "ncol", "nw"], ["nrm"])
            pt, pk = mm(zt[b][:], ["zt%d" % b], ident[:], ["ident"])
            dvop(lambda e: e.activation(out=gz[:], in_=pt, func=AF.Silu), [pk], ["gz"], "act")
            dvop(lambda e: e.tensor_tensor(out=nrm[:], in0=nrm[:], in1=gz[:], op=ALU.mult), ["nrm", "gz"], ["nrm"])
            pt, pk = mm(nrm[:], ["nrm"], ident[:], ["ident"])
            dvop(lambda e: e.tensor_copy(out=ob_[b][:], in_=pt), [pk], ["ob%d" % b])
            P.dma("act", y_out[hd, :, t0:t0 + DNC], ob_[b][:], reads=["ob%d" % b], is_output=True)
    return P.finish()


def run_ed(zs, dn_conv, a_log, dt_bias, norm_w, dbg=None):
    masks = dn_masks()
    ident = np.eye(128, dtype=np.float32)
    nwr = np.ascontiguousarray(np.broadcast_to(np.asarray(norm_w, np.float32)[None, :], (128, 128)))
    in_maps = []
    for j in range(NCORES):
        zz = zs[j]
        qkvz = np.ascontiguousarray(zz[768:1792].reshape(4, 2, 128, T))
        ab = zz[1792:1800]
        braw = np.ascontiguousarray(ab[0:4]); araw = np.ascontiguousarray(ab[4:8])
        cwj = np.zeros((128, 3, 2, 5), np.float32)
        for part in range(3):
            blk = np.asarray(dn_conv[:, part * DNW + 256 * j: part * DNW + 256 * (j + 1)], np.float32)
            cwj[:, part, :, :] = blk.T.reshape(2, 128, 5).transpose(1, 0, 2)
        aA = np.zeros((4, 2), np.float32)
        for dr in range(2):
            for hd in range(2):
                aA[dr * 2 + hd, 0] = a_log[dr, 2 * j + hd]
                aA[dr * 2 + hd, 1] = dt_bias[dr, 2 * j + hd]
        in_maps.append({"qkvz": qkvz, "braw": braw, "araw": araw, "dconv": cwj, "aA": aA, "normw": nwr, "masks": masks, "ident": ident})
    res = run(build_ed(dbg), in_maps)
    return [r["yd"].reshape(256, T) for r in res]


GRID_W = 64


def lat_to_col_major(aT):
    out = aT.copy()
    lat = aT[:, CTX:]
    out[:, CTX:] = lat.reshape(lat.shape[0], SEQ // GRID_W, GRID_W).transpose(0, 2, 1).reshape(lat.shape[0], SEQ)
    return out


def lat_from_col_major(aT):
    out = aT.copy()
    lat = aT[:, CTX:]
    out[:, CTX:] = lat.reshape(lat.shape[0], GRID_W, SEQ // GRID_W).transpose(0, 2, 1).reshape(lat.shape[0], SEQ)
    return out


def kernel(x, c, ctx, c_ctx, ada_w, ada_b, ln_g, ln_b, ev_w_in, ev_w_out, hy_conv, hy_w1, hy_b1, hy_w2, hy_b2, hy_w3, hy_b3,
           hy_w4, hy_freq, hy_bias, dn_conv, dn_a_log, dn_dt_bias, dn_norm_w, s5_lam_re, s5_lam_im, s5_log_dt, s5_b_re, s5_b_im,
           s5_c_re, s5_c_im, s5_d, od_w_glu, moe_router, moe_w_in, moe_w_out):
    f32 = np.float32
    modT = run_k0({"c": c, "c_ctx": c_ctx, "ada_w": ada_w, "ada_b": ada_b})
    hT = np.ascontiguousarray(np.concatenate([np.asarray(ctx, f32)[0], np.asarray(x, f32)[0]], axis=0).T)
    for l in range(DEPTH):
        col = (l // 2) % 2 == 1
        i = l // 2
        mod_l = np.ascontiguousarray(modT[l])
        if l % 2 == 0:
            uT = run_mod(hT, mod_l)
            if col:
                uT = lat_to_col_major(uT)
            zs = run_ea1(uT, np.asarray(ev_w_in[i], f32))
            yh = run_eh(zs, hy_conv[i], hy_w1[i], hy_b1[i], hy_w2[i], hy_b2[i], hy_w3[i], hy_b3[i], hy_w4[i], hy_freq[i], hy_bias[i])
            yd = run_ed(zs, dn_conv[i], dn_a_log[i], dn_dt_bias[i], dn_norm_w[i])
            fT = np.concatenate(yh + yd, axis=0)
            if col:
                fT = lat_from_col_major(fT)
            h1T, h2T, affT = run_x3(False, fT, hT, mod_l, ev_w_out[i], ln_g[l, 0], ln_b[l, 0], moe_router[l])
        else:
            hp = lat_to_col_major(hT) if col else hT
            fT = run_o2(hp, mod_l, s5_lam_re[i], s5_lam_im[i], s5_log_dt[i], s5_b_re[i], s5_b_im[i], s5_c_re[i], s5_c_im[i], s5_d[i])
            if col:
                fT = lat_from_col_major(fT)
            h1T, h2T, affT = run_x3(True, fT, hT, mod_l, od_w_glu[i], ln_g[l, 0], ln_b[l, 0], moe_router[l])
        hT = run_x4(h1T, h2T, affT, mod_l, ln_g[l, 1], ln_b[l, 1], moe_w_in[l], moe_w_out[l])
    return np.ascontiguousarray(hT[:, CTX:].T)[None].astype(np.float32)
```
